# Optimizing a Trainium2 kernel written in Bass

```python
import math
import jax
import jax.numpy as jnp
from jax import lax
import numpy as np

D_MODEL = 2048
BATCH = 2
SEQ = 4096
DEPTH = 4
DEC_BATCH = 8
DEC_SEQ = 1
PAST_LEN = 16384
PAGE_SIZE = 128

N_A = DEPTH // 2
N_B = DEPTH - N_A
MIX_W = D_MODEL
MEM_TOKENS = 256
MEM_HEADS = 4
MEM_W = MIX_W // 4
MEM_DH = MEM_W // MEM_HEADS
RWKV_W = MIX_W - MEM_W
RWKV_N = 64
RWKV_HEADS = RWKV_W // RWKV_N
LORA_W = max(32, int(round(D_MODEL ** 0.5 * 1.8 / 32)) * 32)
LORA_A = max(32, int(round(D_MODEL ** 0.5 * 1.8 / 32)) * 32)
LORA_G = max(32, int(round(D_MODEL ** 0.8 * 0.6 / 32)) * 32)
A_IN = 3 * RWKV_W + MEM_W
DIFF_W = MIX_W - MEM_W
DIFF_DV = 128
DIFF_HEADS = DIFF_W // DIFF_DV
DIFF_DH = DIFF_DV // 2
ROT_DIM = DIFF_DH // 4
ROPE_THETA = 500000.0
D_FF = 11 * D_MODEL // 4
Q_BLOCK = 128
NORM_EPS = 1e-6
LNX_EPS = 64e-5
SUBLN_EPS = 1e-5

kernel_name = 'yoco_rwkv7_diffattn_macaron_step'


def rms_norm(x, g, eps=NORM_EPS):
    xf = x.astype(jnp.float32)
    y = xf * lax.rsqrt(jnp.mean(xf * xf, axis=-1, keepdims=True) + eps)
    return (y * g.astype(jnp.float32)).astype(x.dtype)


def swiglu_half(x, g, w13, w2):
    gate, up = jnp.split(rms_norm(x, g) @ w13, 2, axis=-1)
    return x + 0.5 * ((jax.nn.silu(gate) * up) @ w2)


def rope_tables(pos):
    inv = ROPE_THETA ** (-jnp.arange(0, ROT_DIM, 2, dtype=jnp.float32) / ROT_DIM)
    ang = pos.astype(jnp.float32)[:, None] * inv[None, :]
    return jnp.cos(ang), jnp.sin(ang)


def partial_rope(x, cos, sin):
    half = ROT_DIM // 2
    c, s = cos[:, None, None, :], sin[:, None, None, :]
    xf = x.astype(jnp.float32)
    x1, x2, xp = xf[..., :half], xf[..., half:ROT_DIM], xf[..., ROT_DIM:]
    return jnp.concatenate([x1 * c - x2 * s, x2 * c + x1 * s, xp], axis=-1).astype(x.dtype)


def memory_kv(mem, g_mem, w_kv, g_k):
    b, m, _ = mem.shape
    k, v = jnp.split(rms_norm(mem, g_mem) @ w_kv, 2, axis=-1)
    k = rms_norm(k.reshape(b, m, MEM_HEADS, MEM_DH), g_k)
    return k, v.reshape(b, m, MEM_HEADS, MEM_DH)


def memory_attend(q, mk, mv, g_q):
    b, t, _ = q.shape
    qh = rms_norm(q.reshape(b, t, MEM_HEADS, MEM_DH), g_q)
    s = jnp.einsum('bthd,bmhd->bhtm', qh, mk).astype(jnp.float32) * (MEM_DH ** -0.5)
    p = jax.nn.softmax(s, axis=-1)
    o = jnp.einsum('bhtm,bmhd->bthd', p, mv.astype(jnp.float32))
    return o.reshape(b, t, MEM_W).astype(q.dtype)


def wkv_scan(s0, r, decay, k, v, a_vec, b_vec):
    def step(s, inp):
        rt, wt, kt, vt, at, bt = inp
        sa = jnp.einsum('bhij,bhj->bhi', s, at)
        s = s * wt[:, :, None, :] + sa[..., None] * bt[:, :, None, :] + vt[..., None] * kt[:, :, None, :]
        return s, jnp.einsum('bhij,bhj->bhi', s, rt)
    xs = tuple(jnp.moveaxis(t, 1, 0) for t in (r, decay, k, v, a_vec, b_vec))
    s_fin, ys = lax.scan(step, s0, xs)
    return jnp.moveaxis(ys, 0, 1), s_fin


def rwkv7_mix(xn, shift_prev, s0, w_in, mu, w0, w1, w2, a0, a1, a2, g1, g2, k_k, k_a, r_k, lnx_w, lnx_b):
    b, t, _ = xn.shape
    f32 = jnp.float32
    x_prev = jnp.concatenate([shift_prev[:, None, :].astype(xn.dtype), xn[:, :-1]], axis=1)
    xx = x_prev - xn
    proj = jnp.concatenate([xn, xx], axis=-1) @ w_in
    r, k, v, q_mem = jnp.split(proj, [RWKV_W, 2 * RWKV_W, 3 * RWKV_W], axis=-1)
    xw, xa, xg = xn + xx * mu[0], xn + xx * mu[1], xn + xx * mu[2]
    w = -jax.nn.softplus(-(w0 + jnp.tanh(xw @ w1) @ w2)) - 0.5
    decay = jnp.exp(-jnp.exp(w.astype(f32)))
    a = jax.nn.sigmoid(a0 + (xa @ a1) @ a2)
    g = jax.nn.sigmoid(xg @ g1) @ g2
    heads = lambda z: z.reshape(b, t, RWKV_HEADS, RWKV_N).astype(f32)
    kk = heads(k * k_k)
    kk = kk / jnp.maximum(jnp.sqrt(jnp.sum(kk * kk, axis=-1, keepdims=True)), 1e-12)
    k = k * (1.0 + (a - 1.0) * k_a)
    rh, kh, vh, ah = heads(r), heads(k), heads(v), heads(a)
    y, s_fin = wkv_scan(s0.astype(f32), rh, heads(decay), kh, vh, -kk, kk * ah)
    mean = jnp.mean(y, axis=-1, keepdims=True)
    var = jnp.mean(jnp.square(y - mean), axis=-1, keepdims=True)
    y = ((y - mean) * lax.rsqrt(var + LNX_EPS)).reshape(b, t, RWKV_W)
    y = y * lnx_w.astype(f32) + lnx_b.astype(f32)
    y = y + (jnp.sum(rh * kh * r_k.astype(f32), axis=-1, keepdims=True) * vh).reshape(b, t, RWKV_W)
    return (y * g.astype(f32)).astype(xn.dtype), q_mem, s_fin


def shared_kv(h, g, w_kv, g_k, cos, sin):
    b, t, _ = h.shape
    k, v = jnp.split(rms_norm(h, g) @ w_kv, 2, axis=-1)
    k = partial_rope(rms_norm(k.reshape(b, t, DIFF_HEADS, 2, DIFF_DH), g_k), cos, sin)
    return k, v.reshape(b, t, DIFF_HEADS, DIFF_DV)


def diff_attention_prompt(q, k, v, lam):
    b, t = q.shape[:2]
    nb = t // Q_BLOCK
    scale = DIFF_DH ** -0.5
    key_pos = jnp.arange(t)
    qb = jnp.moveaxis(q.reshape(b, nb, Q_BLOCK, DIFF_HEADS, 2, DIFF_DH), 1, 0)
    vf = v.astype(jnp.float32)

    def block(args):
        qi, i = args
        s = jnp.einsum('bqhcd,bkhcd->bhcqk', qi, k).astype(jnp.float32) * scale
        q_pos = i * Q_BLOCK + jnp.arange(Q_BLOCK)
        s = jnp.where(key_pos[None, :] <= q_pos[:, None], s, -jnp.inf)
        p = jax.nn.softmax(s, axis=-1)
        att = p[:, :, 0] - lam * p[:, :, 1]
        return jnp.einsum('bhqk,bkhd->bqhd', att, vf)

    o = lax.map(block, (qb, jnp.arange(nb)))
    return jnp.moveaxis(o, 0, 1).reshape(b, t, DIFF_HEADS, DIFF_DV).astype(q.dtype)


def diff_attention_sample(q, past_k, past_v, k_new, v_new, lam):
    scale = DIFF_DH ** -0.5
    t = q.shape[1]
    p_len = past_k.shape[1]
    s_past = jnp.einsum('bqhcd,bkhcd->bhcqk', q, past_k).astype(jnp.float32) * scale
    s_new = jnp.einsum('bqhcd,bkhcd->bhcqk', q, k_new).astype(jnp.float32) * scale
    s_new = jnp.where(jnp.tril(jnp.ones((t, t), dtype=bool)), s_new, -jnp.inf)
    p = jax.nn.softmax(jnp.concatenate([s_past, s_new], axis=-1), axis=-1)
    att = p[:, :, 0] - lam * p[:, :, 1]
    o = jnp.einsum('bhqk,bkhd->bqhd', att[..., :p_len], past_v.astype(jnp.float32))
    o = o + jnp.einsum('bhqk,bkhd->bqhd', att[..., p_len:], v_new.astype(jnp.float32))
    return o.astype(q.dtype)


def setup_inputs(seed: int = 0) -> dict:
    key = jax.random.key(seed)
    ks = iter(jax.random.split(key, 64))
    f32 = jnp.float32

    def nrm(shape, scale=1.0):
        return jax.random.normal(next(ks), shape, f32) * scale

    def gain(shape):
        return 1.0 + nrm(shape, 0.02)

    n_pages = PAST_LEN // PAGE_SIZE
    n_used = DEC_BATCH * n_pages
    n_pool = n_used + max(1, n_used // 4)
    perm = jax.random.permutation(next(ks), n_pool).astype(jnp.int32)
    page_table = perm[:n_used].reshape(DEC_BATCH, n_pages)
    d = D_MODEL
    return {
        'x_prompt': nrm((BATCH, SEQ, d)),
        'x_sample': nrm((DEC_BATCH, DEC_SEQ, d)),
        'mem_prompt': nrm((BATCH, MEM_TOKENS, d)),
        'state_wkv': nrm((N_A, DEC_BATCH, RWKV_HEADS, RWKV_N, RWKV_N), 0.5),
        'state_shift': nrm((N_A, DEC_BATCH, d)),
        'cache_mem_k': nrm((DEPTH, DEC_BATCH, MEM_TOKENS, MEM_HEADS, MEM_DH)),
        'cache_mem_v': nrm((DEPTH, DEC_BATCH, MEM_TOKENS, MEM_HEADS, MEM_DH)),
        'cache_k': nrm((n_pool, PAGE_SIZE, DIFF_HEADS, 2, DIFF_DH)),
        'cache_v': nrm((n_pool, PAGE_SIZE, DIFF_HEADS, DIFF_DV)),
        'page_table': page_table,
        'ffn_norm': gain((DEPTH, 2, d)),
        'ffn_w13': nrm((DEPTH, 2, d, 2 * D_FF), d ** -0.5),
        'ffn_w2': nrm((DEPTH, 2, D_FF, d), D_FF ** -0.5),
        'mix_norm': gain((DEPTH, d)),
        'w_out': nrm((DEPTH, MIX_W, d), MIX_W ** -0.5),
        'mem_norm': gain((DEPTH, d)),
        'mem_w_kv': nrm((DEPTH, d, 2 * MEM_W), d ** -0.5),
        'mem_q_norm': gain((DEPTH, MEM_DH)),
        'mem_k_norm': gain((DEPTH, MEM_DH)),
        'a_w_in': nrm((N_A, 2 * d, A_IN), (2 * d) ** -0.5),
        'a_mu': jax.random.uniform(next(ks), (N_A, 3, d), f32),
        'a_w0': jax.random.uniform(next(ks), (N_A, RWKV_W), f32, -6.0, 1.0),
        'a_w1': nrm((N_A, d, LORA_W), d ** -0.5),
        'a_w2': nrm((N_A, LORA_W, RWKV_W), 0.5 * LORA_W ** -0.5),
        'a_a0': nrm((N_A, RWKV_W), 0.1),
        'a_a1': nrm((N_A, d, LORA_A), d ** -0.5),
        'a_a2': nrm((N_A, LORA_A, RWKV_W), 0.5 * LORA_A ** -0.5),
        'a_g1': nrm((N_A, d, LORA_G), d ** -0.5),
        'a_g2': nrm((N_A, LORA_G, RWKV_W), LORA_G ** -0.5),
        'a_k_k': 0.85 + nrm((N_A, RWKV_W), 0.02),
        'a_k_a': gain((N_A, RWKV_W)),
        'a_r_k': nrm((N_A, RWKV_HEADS, RWKV_N), 0.1),
        'a_lnx_w': gain((N_A, RWKV_W)),
        'a_lnx_b': nrm((N_A, RWKV_W), 0.02),
        'kv_norm': gain((d,)),
        'kv_w': nrm((d, 2 * DIFF_W), d ** -0.5),
        'k_norm': gain((DIFF_DH,)),
        'b_w_in': nrm((N_B, d, DIFF_W + MEM_W), d ** -0.5),
        'b_q_norm': gain((N_B, DIFF_DH)),
        'b_lam': nrm((N_B, 4, DIFF_DH), 0.1),
        'b_subln': gain((N_B, DIFF_DV)),
    }


def reference(x_prompt, x_sample, mem_prompt, state_wkv, state_shift, cache_mem_k, cache_mem_v, cache_k, cache_v, page_table,
              ffn_norm, ffn_w13, ffn_w2, mix_norm, w_out, mem_norm, mem_w_kv, mem_q_norm, mem_k_norm,
              a_w_in, a_mu, a_w0, a_w1, a_w2, a_a0, a_a1, a_a2, a_g1, a_g2, a_k_k, a_k_a, a_r_k, a_lnx_w, a_lnx_b,
              kv_norm, kv_w, k_norm, b_w_in, b_q_norm, b_lam, b_subln):
    f32 = jnp.float32

    def ffn(x, i, j):
        return swiglu_half(x, ffn_norm[i, j], ffn_w13[i, j], ffn_w2[i, j])

    def layer_a(x, i, shift_prev, s0, mk, mv):
        x = ffn(x, i, 0)
        xn = rms_norm(x, mix_norm[i])
        y, q_mem, s_new = rwkv7_mix(xn, shift_prev, s0, a_w_in[i], a_mu[i], a_w0[i], a_w1[i], a_w2[i],
                                    a_a0[i], a_a1[i], a_a2[i], a_g1[i], a_g2[i], a_k_k[i], a_k_a[i],
                                    a_r_k[i], a_lnx_w[i], a_lnx_b[i])
        o_mem = memory_attend(q_mem, mk, mv, mem_q_norm[i])
        x = x + jnp.concatenate([y, o_mem], axis=-1) @ w_out[i]
        return ffn(x, i, 1), xn[:, -1], s_new

    def layer_b(x, j, cos, sin, mk, mv, attend):
        i = N_A + j
        b, t, _ = x.shape
        x = ffn(x, i, 0)
        xn = rms_norm(x, mix_norm[i])
        q, q_mem = jnp.split(xn @ b_w_in[j], [DIFF_W], axis=-1)
        q = partial_rope(rms_norm(q.reshape(b, t, DIFF_HEADS, 2, DIFF_DH), b_q_norm[j]), cos, sin)
        lam_init = 0.8 - 0.6 * math.exp(-0.3 * i)
        lq = b_lam[j].astype(f32)
        lam = jnp.exp(jnp.sum(lq[0] * lq[1])) - jnp.exp(jnp.sum(lq[2] * lq[3])) + lam_init
        o = rms_norm(attend(q, lam), b_subln[j], SUBLN_EPS) * (1.0 - lam_init)
        o_mem = memory_attend(q_mem, mk, mv, mem_q_norm[i])
        x = x + jnp.concatenate([o.reshape(b, t, DIFF_W), o_mem], axis=-1) @ w_out[i]
        return ffn(x, i, 1)

    mk_p, mv_p = [], []
    for i in range(DEPTH):
        mk, mv = memory_kv(mem_prompt, mem_norm[i], mem_w_kv[i], mem_k_norm[i])
        mk_p.append(mk)
        mv_p.append(mv)
    x = x_prompt
    wkv_p, shift_p = [], []
    for i in range(N_A):
        x, sh, s = layer_a(x, i, jnp.zeros((BATCH, D_MODEL), x.dtype),
                           jnp.zeros((BATCH, RWKV_HEADS, RWKV_N, RWKV_N), f32), mk_p[i], mv_p[i])
        wkv_p.append(s)
        shift_p.append(sh)
    cos_p, sin_p = rope_tables(jnp.arange(SEQ))
    k_rows_p, v_rows_p = shared_kv(x, kv_norm, kv_w, k_norm, cos_p, sin_p)
    attend_p = lambda q, lam: diff_attention_prompt(q, k_rows_p, v_rows_p, lam)
    for j in range(N_B):
        x = layer_b(x, j, cos_p, sin_p, mk_p[N_A + j], mv_p[N_A + j], attend_p)
    y_prompt = x

    x = x_sample
    wkv_s, shift_s = [], []
    for i in range(N_A):
        x, sh, s = layer_a(x, i, state_shift[i], state_wkv[i], cache_mem_k[i], cache_mem_v[i])
        wkv_s.append(s)
        shift_s.append(sh)
    cos_s, sin_s = rope_tables(PAST_LEN + jnp.arange(DEC_SEQ))
    k_rows_s, v_rows_s = shared_kv(x, kv_norm, kv_w, k_norm, cos_s, sin_s)
    n_pages = PAST_LEN // PAGE_SIZE
    past_k = jnp.take(cache_k, page_table, axis=0).reshape(DEC_BATCH, n_pages * PAGE_SIZE, DIFF_HEADS, 2, DIFF_DH)
    past_v = jnp.take(cache_v, page_table, axis=0).reshape(DEC_BATCH, n_pages * PAGE_SIZE, DIFF_HEADS, DIFF_DV)
    attend_s = lambda q, lam: diff_attention_sample(q, past_k, past_v, k_rows_s, v_rows_s, lam)
    for j in range(N_B):
        x = layer_b(x, j, cos_s, sin_s, cache_mem_k[N_A + j], cache_mem_v[N_A + j], attend_s)
    y_sample = x

    wkv_prompt = jnp.stack(wkv_p)
    shift_prompt = jnp.stack(shift_p)
    wkv_sample = jnp.stack(wkv_s)
    shift_sample = jnp.stack(shift_s)
    mem_k_prompt = jnp.stack(mk_p)
    mem_v_prompt = jnp.stack(mv_p)
    return (y_prompt, y_sample, wkv_prompt, shift_prompt, wkv_sample, shift_sample,
            k_rows_p, v_rows_p, k_rows_s, v_rows_s, mem_k_prompt, mem_v_prompt)
```

```python
import numpy as np
from contextlib import ExitStack
import concourse.bass as bass
import concourse.mybir as mybir

F32 = mybir.dt.float32
BF16 = mybir.dt.bfloat16
I32 = mybir.dt.int32
AF = mybir.ActivationFunctionType
ALU = mybir.AluOpType
AX = mybir.AxisListType

ENGS = ("pe", "act", "dve", "pool", "sp")
DMA_POOL = {"sp": 24, "pool": 24, "act": 8}


class Res:
    __slots__ = ("name", "w", "rs")

    def __init__(self, name):
        self.name = name
        self.w = None
        self.rs = {}


class Buf:
    def __init__(self, name, t):
        self.name = name
        self.t = t
        self._res = {}

    def r(self, key=None):
        x = self._res.get(key)
        if x is None:
            x = Res(f"{self.name}[{key}]")
            self._res[key] = x
        return x

    def __getitem__(self, idx):
        return self.t[idx]


class Prog:
    def __init__(self, nc):
        self.nc = nc
        self.es = ExitStack()
        self.ops = {e: [] for e in ENGS}
        self.cnt = {e: 0 for e in ENGS}
        self.dcnt = {e: 0 for e in DMA_POOL}
        self.waited = {e: {} for e in ENGS}
        self.sems = {}
        self.final = {}
        self.nbuf = 0
        self.bar = {}

    def sb(self, name, shape, dt=F32):
        t = self.es.enter_context(self.nc.sbuf_tensor(name, list(shape), dt))
        return Buf(name, t)

    def ps(self, name, shape, dt=F32):
        t = self.es.enter_context(self.nc.psum_tensor(name, list(shape), dt))
        return Buf(name, t)

    def dram(self, name, shape, dt=F32, kind="Internal"):
        t = self.nc.dram_tensor(name, list(shape), dt, kind=kind)
        return Buf(name, t)

    def _sem(self, key):
        s = self.sems.get(key)
        if s is None:
            nm = "s_" + "_".join(str(k) for k in key) if isinstance(key, tuple) else "s_" + str(key)
            s = self.es.enter_context(self.nc.semaphore(nm))
            self.sems[key] = s
        return s

    def add(self, eng, fn, reads=(), writes=(), dma=False, inc=None):
        deps = {}

        def need(k, v):
            if deps.get(k, 0) < v:
                deps[k] = v

        for k, v in self.bar.items():
            need(k, v)
        for r in reads:
            if r.w is not None:
                need(*r.w)
        for r in writes:
            if r.w is not None:
                need(*r.w)
            for k, v in r.rs.items():
                need(k, v)
        if dma:
            n = self.dcnt[eng]
            self.dcnt[eng] = n + 1
            P = DMA_POOL[eng]
            key = ("d", eng, n % P)
            val = 16 * (n // P + 1)
            if n >= P:
                need(key, val - 16)
            tag = (key, val)
            self.final[key] = val
        else:
            self.cnt[eng] += 1
            key = ("e", eng)
            tag = (key, self.cnt[eng])
        wl = []
        wd = self.waited[eng]
        for k, v in deps.items():
            if k == ("e", "pe") and eng == "pe" and not dma:
                continue
            if wd.get(k, 0) >= v:
                continue
            wd[k] = v
            wl.append((k, v))
        for r in writes:
            r.w = tag
            r.rs = {}
        for r in reads:
            if r.rs.get(tag[0], 0) < tag[1]:
                r.rs[tag[0]] = tag[1]
        self.ops[eng].append((fn, wl, tag, dma))
        return tag

    def barrier(self):
        for e in ENGS:
            if self.cnt[e]:
                self.bar[("e", e)] = self.cnt[e]
        for k, v in self.final.items():
            self.bar[k] = v

    def view(self, name, ap):
        return Buf(name, ap)

    def emit(self):
        nc = self.nc
        for k in list(self.final):
            self._sem(k)
        for e in ENGS:
            self._sem(("e", e))
        for e in ENGS:
            for (_, wl, tag, _) in self.ops[e]:
                for k, _ in wl:
                    self._sem(k)
        engobj = {"pe": "tensor", "act": "scalar", "dve": "vector", "pool": "gpsimd", "sp": "sync"}
        with nc.Block() as block:
            for e in ENGS:
                ops = self.ops[e]
                finals = dict(self.final) if e == "sp" else {}

                def body(eng, ops=ops, e=e, finals=finals):
                    for (fn, wl, tag, dma) in ops:
                        for k, v in wl:
                            eng.wait_ge(self.sems[k], v)
                        ins = fn(eng)
                        if dma:
                            ins.then_inc(self.sems[tag[0]], 16)
                        else:
                            ins.then_inc(self.sems[tag[0]], 1)
                    for k, v in finals.items():
                        eng.wait_ge(self.sems[k], v)

                getattr(block, engobj[e])(body)

    def close(self):
        self.es.close()


from concourse.bass_utils import run_bass_kernel_spmd

D = 2048
KC = 16
DFF = 5632
NT = 1024
NCORE = 8
MEMT = 256
NBLK = 4
TA = 256
CH = 64
ARENA = 90112


def fm(v):
    v = np.asarray(v, dtype=np.float32)
    lead = v.shape[:-1]
    n = v.shape[-1] // 128
    w = v.reshape(lead + (n, 128))
    w = np.moveaxis(w, -1, 0)
    return np.ascontiguousarray(w.reshape(128, -1))


class PV:
    def __init__(self):
        self.cols, self.off, self.n = [], {}, 0

    def add(self, name, arr):
        arr = np.asarray(arr, dtype=np.float32)
        assert arr.shape[0] == 128
        self.off[name] = (self.n, arr.shape[1])
        self.cols.append(arr)
        self.n += arr.shape[1]

    def build(self):
        return np.ascontiguousarray(np.concatenate(self.cols, axis=1))


def pack_params(inp):
    pv = PV()
    pv.add("ffn_norm", fm(inp["ffn_norm"].reshape(8, D)))
    pv.add("mix_norm", fm(inp["mix_norm"]))
    pv.add("mem_norm", fm(inp["mem_norm"]))
    pv.add("kv_norm", fm(inp["kv_norm"].reshape(1, D)))
    pv.add("mem_q_norm", fm(inp["mem_q_norm"]))
    pv.add("a_mu", fm(inp["a_mu"].reshape(6, D)))
    for nm in ("a_w0", "a_a0", "a_k_k", "a_k_a"):
        pv.add(nm, fm(inp[nm]))
    pv.add("a_r_k", fm(inp["a_r_k"].reshape(2, 1536)))
    pv.add("a_lnx_w", fm(inp["a_lnx_w"]))
    pv.add("a_lnx_b", fm(inp["a_lnx_b"]))
    pv.add("b_q_norm", np.ascontiguousarray(np.tile(np.asarray(inp["b_q_norm"], np.float32), (1, 2)).T))
    pv.add("b_subln", np.ascontiguousarray(np.asarray(inp["b_subln"], np.float32).T))
    return pv


def host_consts():
    c = {}
    c["ident"] = np.eye(128, dtype=np.float32)
    su = np.triu(np.ones((64, 64), np.float32), 1)
    iu = np.triu(np.ones((64, 64), np.float32), 0)
    m4 = np.tile(np.concatenate([su, iu], 1), (1, 4))
    sl8 = np.tile(su.T, (1, 8))
    i8 = np.tile(np.eye(64, dtype=np.float32), (1, 8))
    c["c64"] = np.ascontiguousarray(np.concatenate([m4, sl8, i8], 1))
    blk = np.zeros((128, 128), np.float32)
    blk[:64, :64] = 1
    blk[64:, 64:] = 1
    hs = np.zeros((128, 2), np.float32)
    hs[:64, 0] = 1
    hs[64:, 1] = 1
    pidx = np.arange(128, dtype=np.float32)[:, None]
    negm = np.full((128, 1), -1e30, np.float32)
    negm[0, 0] = 0.0
    bm = np.zeros((128, 12), np.float32)
    for r_ in range(24):
        bm[r_, r_ // 2] = 1.0
    c["c128"] = np.ascontiguousarray(np.concatenate([blk, hs, pidx, negm, bm], 1))
    pos = np.concatenate([np.arange(4096), [16384]]).astype(np.float32)
    inv = (np.float32(500000.0) ** (-np.arange(0, 16, 2, dtype=np.float32) / np.float32(16))).astype(np.float32)
    ang = (pos[:, None] * inv[None, :]).astype(np.float32)
    cs_, sn_ = np.cos(ang).astype(np.float32), np.sin(ang).astype(np.float32)
    c["cs_tok"] = np.ascontiguousarray(np.concatenate([cs_, sn_], 1))
    C = np.ones((128, 4097), np.float32)
    S = np.zeros((128, 4097), np.float32)
    rot = np.zeros((128, 128), np.float32)
    for p in range(128):
        dd = p % 64
        if dd < 8:
            C[p] = cs_[:, dd]; S[p] = -sn_[:, dd]; rot[p + 8, p] = 1
        elif dd < 16:
            C[p] = cs_[:, dd - 8]; S[p] = sn_[:, dd - 8]; rot[p - 8, p] = 1
    c["ropeC"], c["ropeS"], c["rot"] = C, S, rot
    dm = np.zeros((128, 4, 512), np.float32)
    for r in range(4):
        dm[:, r, :] = (np.arange(128)[:, None] + 128 * r <= np.arange(512)[None, :])
    c["dmask"] = np.ascontiguousarray(dm.reshape(128, 2048))
    return c


class KB:
    def __init__(self, pvoff, npv, stage):
        self.nc = nc = bass.Bass("TRN2", target_bir_lowering=False)
        self.P = P = Prog(nc)
        self.pvoff = pvoff
        self.stage = stage
        dt_in = lambda name, shape, dt=F32: nc.dram_tensor(name, list(shape), dt, kind="ExternalInput").ap()
        dt_out = lambda name, shape, dt=F32: nc.dram_tensor(name, list(shape), dt, kind="ExternalOutput").ap()
        self.din = dict(
            xp=dt_in("xp", [NBLK * NT, D]), xs=dt_in("xs", [KC, 128]), memp=dt_in("memp", [MEMT, D]),
            pv=dt_in("pv", [128, npv]), ident=dt_in("ident", [128, 128]), c64=dt_in("c64", [64, 1536]),
            c128=dt_in("c128", [128, 144]),
            ffn_w13=dt_in("ffn_w13", [8, D, 2 * DFF]), ffn_w2=dt_in("ffn_w2", [8, DFF, D]),
            mem_w_kv=dt_in("mem_w_kv", [4, D, 1024]), mem_k_norm=dt_in("mem_k_norm", [4, 128]),
            a_w_in=dt_in("a_w_in", [2, 2 * D, 5120]), a_w1=dt_in("a_w1", [2, D, 96]), a_w2=dt_in("a_w2", [2, 96, 1536]),
            a_a1=dt_in("a_a1", [2, D, 96]), a_a2=dt_in("a_a2", [2, 96, 1536]),
            a_g1=dt_in("a_g1", [2, D, 256]), a_g2=dt_in("a_g2", [2, 256, 1536]),
            a_lnx_w=dt_in("a_lnx_w", [2, 1536]), a_lnx_b=dt_in("a_lnx_b", [2, 1536]),
            w_out=dt_in("w_out", [4, D, D]),
            kv_w=dt_in("kv_w", [D, 3072]), k_norm=dt_in("k_norm", [1, 64]), b_w_in=dt_in("b_w_in", [2, D, D]), b_lam=dt_in("b_lam", [2, 256]),
            cs_tok=dt_in("cs_tok", [4097, 16]), ropeC=dt_in("ropeC", [128, 4097]), ropeS=dt_in("ropeS", [128, 4097]),
            rot=dt_in("rot", [128, 128]), dmask=dt_in("dmask", [128, 2048]),
            stw=dt_in("stw", [2, 24, 64, 64]), sts=dt_in("sts", [2, KC, 128]),
            cmk=dt_in("cmk", [4, MEMT, 512]), cmv=dt_in("cmv", [4, MEMT, 512]),
            ck=dt_in("ck", [1280 * 128, 1536]), cv=dt_in("cv", [1280 * 128, 1536]), pt=dt_in("pt", [1, 128], I32),
        )
        self.dout = dict(
            yp=dt_out("yp", [NBLK * NT, D]), ys=dt_out("ys", [KC, 128]),
            shp=dt_out("shp", [2, KC, 128]), shs=dt_out("shs", [2, KC, 128]),
            mkp=dt_out("mkp", [4, MEMT, 512]), mvp=dt_out("mvp", [4, MEMT, 512]),
            wkvp=dt_out("wkvp", [2, 24, 64, 64]),
            krp=dt_out("krp", [NBLK * NT, 1536]), vrp=dt_out("vrp", [NBLK * NT, 1536]),
            krs=dt_out("krs", [1, 1536]), vrs=dt_out("vrs", [1, 1536]),
            wkvs=dt_out("wkvs", [2, 24, 64, 64]),
        )
        self.mk_scr = P.dram("mk_scr", [4, 128, 4 * MEMT], BF16)
        self.mv_scr = P.dram("mv_scr", [4, MEMT, 512], BF16)
        self.mks_scr = P.dram("mks_scr", [4, 128, 4 * MEMT], BF16)
        self.mvs_scr = P.dram("mvs_scr", [4, MEMT, 512], BF16)
        self.ks_scr = P.dram("ks_scr", [1, 1536], F32)
        self.vs_scr = P.dram("vs_scr", [1, 1536], F32)
        self.q_scr = P.dram("q_scr", [2, 12, 128], F32)
        self.kt_scr = P.dram("kt_scr", [12, 128, NBLK * NT], BF16)
        self.v_scr = P.dram("v_scr", [NBLK * NT, 1536], BF16)
        self.xT = P.sb("xT", [128, KC, NT], F32)
        self.xsT = P.sb("xsT", [128, KC], F32)
        self.xn = P.sb("xn", [128, KC, NT + 1], BF16)
        self.xns = P.sb("xns", [128, KC, 2], BF16)
        self.pv = P.sb("pv_sb", [128, npv], F32)
        self.omu = P.sb("omu", [128, 96], F32)
        self.neg = P.sb("negp", [128, 48], F32)
        self.ident = P.sb("ident_sb", [128, 128], F32)
        self.identb = P.sb("identb", [128, 128], BF16)
        self.onesb = P.sb("onesb", [128, 128], BF16)
        self.c128 = P.sb("c128_sb", [128, 144], F32)
        self.onesf = P.sb("onesf", [128, 128], F32)
        self.blk64b = P.sb("blk64b", [128, 128], BF16)
        self.c64 = P.sb("c64_sb", [64, 1536], BF16)
        self.rmask = P.sb("rmask", [128, TA], F32)
        self.prevcol = P.sb("prevcol", [128, 2, KC], BF16)
        self.xlast = P.sb("xlast", [128, KC], F32)
        self.xslast = P.sb("xslast", [128, KC], F32)
        self.Sf = P.sb("Sf", [128, 2, 12, 64], F32)
        self.small = P.sb("small", [128, 64], F32)
        self.rotb = P.sb("rotb", [128, 128], BF16)
        self.dmask = P.sb("dmask_sb", [128, 2048], BF16)
        self.lamv = P.sb("lamv", [128, 4], F32)
        self.colst = P.sb("colst", [KC, 128], F32)
        self.psb = [P.ps(f"ps{i}", [128, 512], F32) for i in range(8)]
        self.arena = P.sb("arena", [128, ARENA // 2], BF16)
        self.aoff = 0
        self.cur = {}
        self.wi = 0

    def phase(self, name):
        self.P.barrier()
        self.aoff = 0
        self.cur = {}
        self.pname = name

    def av(self, name, shape, dt=F32, parts=128):
        esz = 4 if dt in (F32, I32) else 2
        n = int(np.prod(shape))
        nbytes = ((n * esz + 63) // 64) * 64
        assert self.aoff + nbytes <= ARENA, (self.pname, name, self.aoff, nbytes)
        o2 = self.aoff // 2
        ap = self.arena.t[0:parts, o2:o2 + (n * esz) // 2]
        if dt in (F32, I32):
            ap = ap.bitcast(dt)
        if len(shape) == 2:
            ap = ap.rearrange("p (a b) -> p a b", a=shape[0])
        elif len(shape) == 3:
            ap = ap.rearrange("p (a b c) -> p a b c", a=shape[0], b=shape[1])
        self.aoff += nbytes
        b = self.P.view(f"{self.pname}.{name}", ap)
        self.cur[name] = b
        return b

    def pvc(self, name, i=0, n=1):
        o, w = self.pvoff[name]
        return self.pv[:, o + i:o + i + n]

    def psbf(self, i):
        return self.psb[i].t[:].bitcast(BF16)

    def setup(self):
        P, d = self.P, self.din
        P.add("sp", lambda e: e.dma_start(out=self.ident[:], in_=d["ident"]), writes=[self.ident.r()], dma=True)
        P.add("sp", lambda e: e.dma_start(out=self.pv[:], in_=d["pv"]), writes=[self.pv.r()], dma=True)
        P.add("pool", lambda e: e.dma_start(out=self.c64[:], in_=d["c64"]), writes=[self.c64.r()], dma=True)
        P.add("sp", lambda e: e.dma_start(out=self.c128[:], in_=d["c128"]), writes=[self.c128.r()], dma=True)
        P.add("dve", lambda e: e.memset(self.onesb[:], 1.0), writes=[self.onesb.r()])
        P.add("dve", lambda e: e.memset(self.onesf[:], 1.0), writes=[self.onesf.r()])
        P.add("dve", lambda e: e.tensor_copy(out=self.identb[:], in_=self.ident[:]), reads=[self.ident.r()], writes=[self.identb.r()])
        P.add("dve", lambda e: e.tensor_copy(out=self.blk64b[:], in_=self.c128[:, 0:128]), reads=[self.c128.r()], writes=[self.blk64b.r()])
        P.add("dve", lambda e: e.memset(self.prevcol[:], 0.0), writes=[self.prevcol.r()])
        P.add("dve", lambda e: e.memset(self.Sf[:], 0.0), writes=[self.Sf.r()])
        P.add("dve", lambda e: e.memset(self.rmask[:], 1.0), writes=[self.rmask.r()])
        P.add("dve", lambda e: e.memset(self.rmask[:].rearrange("p (c t) -> p c t", t=CH)[:, :, 0:1], 0.0), reads=[self.rmask.r()], writes=[self.rmask.r()])
        P.add("pool", lambda e: e.dma_start(out=self.rotb[:], in_=d["rot"]), writes=[self.rotb.r()], dma=True)
        P.add("pool", lambda e: e.dma_start(out=self.dmask[:], in_=d["dmask"]), writes=[self.dmask.r()], dma=True)
        self.lam_setup()
        mo = self.pvoff["a_mu"][0]
        P.add("dve", lambda e: e.tensor_scalar(out=self.omu[:], in0=self.pv[:, mo:mo + 96], scalar1=-1.0, scalar2=1.0, op0=ALU.mult, op1=ALU.add),
              reads=[self.pv.r()], writes=[self.omu.r()])
        wo = self.pvoff["a_w0"][0]
        ko = self.pvoff["a_k_a"][0]
        P.add("dve", lambda e: e.tensor_scalar(out=self.neg[:, 0:24], in0=self.pv[:, wo:wo + 24], scalar1=-1.0, scalar2=None, op0=ALU.mult),
              reads=[self.pv.r()], writes=[self.neg.r()])
        P.add("dve", lambda e: e.tensor_scalar(out=self.neg[:, 24:48], in0=self.pv[:, ko:ko + 24], scalar1=-1.0, scalar2=1.0, op0=ALU.mult, op1=ALU.add),
              reads=[self.pv.r()], writes=[self.neg.r()])

    def load_x(self, blk):
        P, d = self.P, self.din
        xT, psb = self.xT, self.psb
        self.phase("io")
        xin = [self.av(f"xin{i}", [D], F32) for i in range(2)]
        for tb in range(NT // 128):
            xi = xin[tb % 2]
            r0 = blk * NT + tb * 128
            P.add("sp", lambda e, xi=xi, r0=r0: e.dma_start(out=xi[:], in_=d["xp"][r0:r0 + 128, :]), writes=[xi.r()], dma=True)
            for k4 in range(4):
                pb = psb[(tb * 4 + k4) % 2]
                for j in range(4):
                    kc = k4 * 4 + j
                    P.add("pe", lambda e, pb=pb, xi=xi, j=j, kc=kc: e.transpose(
                        out=pb[:, j * 128:(j + 1) * 128], in_=xi[:, kc * 128:(kc + 1) * 128], identity=self.ident[:]),
                        reads=[xi.r(), self.ident.r()], writes=[pb.r()])
                P.add("act", lambda e, pb=pb, k4=k4, tb=tb: e.copy(
                    out=xT[:, k4 * 4:(k4 + 1) * 4, tb * 128:(tb + 1) * 128],
                    in_=pb[:].rearrange("p (a b) -> p a b", a=4)),
                    reads=[pb.r()], writes=[xT.r(tb // 4)])
        if blk == 0:
            xi = xin[0]
            P.add("sp", lambda e: e.dma_start(out=xi[0:KC, 0:128], in_=d["xs"]), writes=[xi.r()], dma=True)
            pb = psb[2]
            P.add("pe", lambda e: e.transpose(out=pb[:, 0:KC], in_=xi[0:KC, 0:128], identity=self.ident[0:KC, 0:KC]),
                  reads=[xi.r(), self.ident.r()], writes=[pb.r()])
            P.add("act", lambda e: e.copy(out=self.xsT[:], in_=pb[:, 0:KC]), reads=[pb.r()], writes=[self.xsT.r()])

    def store_x(self, blk):
        P, o = self.P, self.dout
        xT, psb = self.xT, self.psb
        self.phase("io")
        xin = [self.av(f"xin{i}", [D], F32) for i in range(2)]
        for tb in range(NT // 128):
            xi = xin[tb % 2]
            for k4 in range(4):
                pb = psb[(tb * 4 + k4) % 2]
                for j in range(4):
                    kc = k4 * 4 + j
                    P.add("pe", lambda e, pb=pb, j=j, kc=kc, tb=tb: e.transpose(
                        out=pb[:, j * 128:(j + 1) * 128], in_=xT[:, kc, tb * 128:(tb + 1) * 128], identity=self.ident[:]),
                        reads=[xT.r(tb // 4), self.ident.r()], writes=[pb.r()])
                P.add("act", lambda e, pb=pb, k4=k4, xi=xi: e.copy(out=xi[:, k4 * 512:(k4 + 1) * 512], in_=pb[:]),
                      reads=[pb.r()], writes=[xi.r()])
            r0 = blk * NT + tb * 128
            P.add("sp", lambda e, xi=xi, r0=r0: e.dma_start(out=o["yp"][r0:r0 + 128, :], in_=xi[:]), reads=[xi.r()], dma=True)
        if blk == NBLK - 1:
            self.store_col(self.xsT, o["ys"], self.xsT.r())

    def store_col(self, src, dst, res):
        P = self.P
        pb = self.psb[2]
        cs = self.colst
        P.add("pe", lambda e: e.transpose(out=pb[0:KC, 0:128], in_=src[:, 0:KC], identity=self.ident[:]),
              reads=[res, self.ident.r()], writes=[pb.r()])
        P.add("act", lambda e: e.copy(out=cs[0:KC, 0:128], in_=pb[0:KC, 0:128]), reads=[pb.r()], writes=[cs.r()])
        P.add("sp", lambda e: e.dma_start(out=dst, in_=cs[0:KC, 0:128]), reads=[cs.r()], writes=[cs.r("o")], dma=True)

    def tiles(self):
        return [(0, 512), (512, 512)]

    def rmsnorm(self, gname, gi, sample, last_out=None, shift_li=None):
        P = self.P
        xT, xn, psb = self.xT, self.xn, self.psb
        sq = [self.cur.get("sq0") or self.av("sq0", [512], BF16), self.cur.get("sq1") or self.av("sq1", [512], BF16)]
        rstd = self.cur.get("rstd") or self.av("rstd", [512], F32)
        if shift_li is not None:
            P.add("dve", lambda e: e.tensor_copy(out=xn[:, :, 0], in_=self.prevcol[:, shift_li, :]),
                  reads=[self.prevcol.r()], writes=[xn.r("prev")])
        for ti, (t0, n) in enumerate(self.tiles()):
            pb = psb[7]
            for kc in range(KC):
                s = sq[kc % 2]
                P.add("act", lambda e, s=s, kc=kc, t0=t0, n=n: e.activation(out=s[:, :n], in_=xT[:, kc, t0:t0 + n], func=AF.Square),
                      reads=[xT.r(ti)], writes=[s.r()])
                P.add("pe", lambda e, s=s, kc=kc, n=n, pb=pb: e.matmul(pb[:, :n], lhsT=self.onesb[:], rhs=s[:, :n], start=(kc == 0), stop=(kc == KC - 1)),
                      reads=[s.r(), self.onesb.r()], writes=[pb.r()])
            P.add("act", lambda e, pb=pb, n=n: e.activation(out=rstd[:, :n], in_=pb[:, :n], func=AF.Sqrt, scale=1.0 / D, bias=1e-6),
                  reads=[pb.r()], writes=[rstd.r()])
            P.add("dve", lambda e, n=n: e.reciprocal(out=rstd[:, :n], in_=rstd[:, :n]), reads=[rstd.r()], writes=[rstd.r()])
            for kc in range(KC):
                P.add("dve", lambda e, kc=kc, t0=t0, n=n: e.scalar_tensor_tensor(
                    out=xn[:, kc, 1 + t0:1 + t0 + n], in0=xT[:, kc, t0:t0 + n], scalar=self.pvc(gname, gi * KC + kc), in1=rstd[:, :n],
                    op0=ALU.mult, op1=ALU.mult),
                    reads=[xT.r(ti), self.pv.r(), rstd.r()], writes=[xn.r(ti)])
            if ti == 1 and last_out is not None:
                go = self.pvoff[gname][0] + gi * KC
                P.add("dve", lambda e: e.tensor_tensor(out=last_out[:], in0=xT[:, :, NT - 1], in1=self.pv[:, go:go + KC], op=ALU.mult),
                      reads=[xT.r(1), self.pv.r()], writes=[last_out.r()])
                P.add("dve", lambda e: e.tensor_scalar(out=last_out[:], in0=last_out[:], scalar1=rstd[:, 511:512], scalar2=None, op0=ALU.mult),
                      reads=[last_out.r(), rstd.r()], writes=[last_out.r()])
        if shift_li is not None:
            P.add("dve", lambda e: e.tensor_copy(out=self.prevcol[:, shift_li, :], in_=xn[:, :, NT]),
                  reads=[xn.r(1)], writes=[self.prevcol.r()])
        if not sample:
            return
        sm = self.small
        P.add("dve", lambda e: e.tensor_tensor(out=sm[:, 0:KC], in0=self.xsT[:], in1=self.xsT[:], op=ALU.mult),
              reads=[self.xsT.r()], writes=[sm.r("a")])
        P.add("dve", lambda e: e.tensor_reduce(out=sm[:, 16:17], in_=sm[:, 0:KC], axis=AX.X, op=ALU.add),
              reads=[sm.r("a")], writes=[sm.r("b")])
        P.add("dve", lambda e: e.tensor_copy(out=sq[0][:, 0:1], in_=sm[:, 16:17]), reads=[sm.r("b")], writes=[sq[0].r()])
        pb = psb[7]
        P.add("pe", lambda e: e.matmul(pb[:, 0:1], lhsT=self.onesb[:], rhs=sq[0][:, 0:1], start=True, stop=True),
              reads=[sq[0].r(), self.onesb.r()], writes=[pb.r()])
        P.add("act", lambda e: e.activation(out=sm[:, 17:18], in_=pb[:, 0:1], func=AF.Sqrt, scale=1.0 / D, bias=1e-6),
              reads=[pb.r()], writes=[sm.r("c")])
        P.add("dve", lambda e: e.reciprocal(out=sm[:, 17:18], in_=sm[:, 17:18]), reads=[sm.r("c")], writes=[sm.r("c")])
        go = self.pvoff[gname][0] + gi * KC
        dst = self.xslast
        P.add("dve", lambda e: e.scalar_tensor_tensor(out=dst[:], in0=self.xsT[:], scalar=sm[:, 17:18], in1=self.pv[:, go:go + KC], op0=ALU.mult, op1=ALU.mult),
              reads=[self.xsT.r(), sm.r("c"), self.pv.r()], writes=[dst.r()])
        P.add("dve", lambda e: e.tensor_copy(out=self.xns[:, :, 1], in_=dst[:]), reads=[dst.r()], writes=[self.xns.r("cur")])

    def ffn(self, li, lj, sample):
        P, d = self.P, self.din
        xT, xn, psb = self.xT, self.xn, self.psb
        self.phase("ffn")
        hq = self.av("hq", [11, NT], BF16)
        hqs = self.av("hqs", [32], BF16)
        wb = [self.av(f"wb{i}", [8192], BF16) for i in range(3)]
        sil = [self.av(f"sil{i}", [512], F32) for i in range(2)]
        fi = li * 2 + lj
        self.rmsnorm("ffn_norm", fi, sample)
        w13v = d["ffn_w13"][fi].rearrange("(kc p) c -> p kc c", p=128)
        w2v = d["ffn_w2"][fi].rearrange("(fc p) c -> p fc c", p=128)
        tiles = self.tiles()
        ps_s = psb[4]
        wi = 0
        for q in range(4):
            for jl in range(11):
                j = q * 11 + jl
                b = wb[wi % 3]
                wi += 1
                bv = b[:, :KC * 256].rearrange("p (k c) -> p k c", k=KC)
                P.add("pool", lambda e, bv=bv, j=j: e.dma_start(out=bv[:, :, 0:128], in_=w13v[:, :, j * 128:(j + 1) * 128]),
                      writes=[b.r()], dma=True)
                P.add("pool", lambda e, bv=bv, j=j: e.dma_start(out=bv[:, :, 128:256], in_=w13v[:, :, DFF + j * 128:DFF + (j + 1) * 128]),
                      writes=[b.r()], dma=True)
                for ti, (t0, n) in enumerate(tiles):
                    pg, pu = psb[ti * 2], psb[ti * 2 + 1]
                    for kc in range(KC):
                        P.add("pe", lambda e, pg=pg, bv=bv, kc=kc, t0=t0, n=n: e.matmul(
                            pg[:, :n], lhsT=bv[:, kc, 0:128], rhs=xn[:, kc, 1 + t0:1 + t0 + n], start=(kc == 0), stop=(kc == KC - 1)),
                            reads=[b.r(), xn.r(ti)], writes=[pg.r()])
                    for kc in range(KC):
                        P.add("pe", lambda e, pu=pu, bv=bv, kc=kc, t0=t0, n=n: e.matmul(
                            pu[:, :n], lhsT=bv[:, kc, 128:256], rhs=xn[:, kc, 1 + t0:1 + t0 + n], start=(kc == 0), stop=(kc == KC - 1)),
                            reads=[b.r(), xn.r(ti)], writes=[pu.r()])
                    s = sil[ti]
                    P.add("act", lambda e, s=s, pg=pg, n=n: e.activation(out=s[:, :n], in_=pg[:, :n], func=AF.Silu),
                          reads=[pg.r()], writes=[s.r()])
                    P.add("dve", lambda e, s=s, pu=pu, jl=jl, t0=t0, n=n: e.tensor_tensor(
                        out=hq[:, jl, t0:t0 + n], in0=s[:, :n], in1=pu[:, :n], op=ALU.mult),
                        reads=[s.r(), pu.r()], writes=[hq.r(ti)])
                if sample:
                    for half in range(2):
                        for kc in range(KC):
                            P.add("pe", lambda e, bv=bv, kc=kc, half=half: e.matmul(
                                ps_s[:, half:half + 1], lhsT=bv[:, kc, half * 128:(half + 1) * 128], rhs=self.xns[:, kc, 1:2],
                                start=(kc == 0), stop=(kc == KC - 1)),
                                reads=[b.r(), self.xns.r("cur")], writes=[ps_s.r()])
                    sm = self.small
                    P.add("act", lambda e: e.activation(out=sm[:, 20:21], in_=ps_s[:, 0:1], func=AF.Silu), reads=[ps_s.r()], writes=[sm.r("s")])
                    P.add("dve", lambda e, jl=jl: e.tensor_tensor(out=hqs[:, jl:jl + 1], in0=sm[:, 20:21], in1=ps_s[:, 1:2], op=ALU.mult),
                          reads=[sm.r("s"), ps_s.r()], writes=[hqs.r()])
            for dq in range(4):
                b = wb[wi % 3]
                wi += 1
                bv = b[:, :11 * 512].rearrange("p (k c) -> p k c", k=11)
                P.add("pool", lambda e, bv=bv, q=q, dq=dq: e.dma_start(out=bv, in_=w2v[:, q * 11:(q + 1) * 11, dq * 512:(dq + 1) * 512]),
                      writes=[b.r()], dma=True)
                for dl in range(4):
                    dc = dq * 4 + dl
                    for ti, (t0, n) in enumerate(tiles):
                        pa = psb[5 + (dc * 2 + ti) % 2]
                        for f in range(11):
                            P.add("pe", lambda e, pa=pa, bv=bv, f=f, dl=dl, t0=t0, n=n: e.matmul(
                                pa[:, :n], lhsT=bv[:, f, dl * 128:(dl + 1) * 128], rhs=hq[:, f, t0:t0 + n], start=(f == 0), stop=(f == 10)),
                                reads=[b.r(), hq.r(ti)], writes=[pa.r()])
                        P.add("dve", lambda e, pa=pa, dc=dc, t0=t0, n=n: e.scalar_tensor_tensor(
                            out=xT[:, dc, t0:t0 + n], in0=pa[:, :n], scalar=0.5, in1=xT[:, dc, t0:t0 + n], op0=ALU.mult, op1=ALU.add),
                            reads=[pa.r(), xT.r(ti)], writes=[xT.r(ti)])
                    if sample:
                        for f in range(11):
                            P.add("pe", lambda e, bv=bv, f=f, dl=dl: e.matmul(
                                ps_s[:, 2:3], lhsT=bv[:, f, dl * 128:(dl + 1) * 128], rhs=hqs[:, f:f + 1], start=(f == 0), stop=(f == 10)),
                                reads=[b.r(), hqs.r()], writes=[ps_s.r()])
                        P.add("dve", lambda e, dc=dc: e.scalar_tensor_tensor(
                            out=self.xsT[:, dc:dc + 1], in0=ps_s[:, 2:3], scalar=0.5, in1=self.xsT[:, dc:dc + 1], op0=ALU.mult, op1=ALU.add),
                            reads=[ps_s.r(), self.xsT.r()], writes=[self.xsT.r()])

    def mem_all(self):
        P, d, o = self.P, self.din, self.dout
        psb = self.psb
        self.phase("mem")
        memT = self.av("memT", [KC, MEMT], F32)
        memn = self.av("memn", [KC, MEMT], BF16)
        xin0 = self.av("xin0", [D], F32)
        xin = [xin0, xin0]
        wbuf = [self.av(f"wm{i}", [KC, 512], BF16) for i in range(2)]
        kf = [self.av(f"kf{i}", [512], F32) for i in range(2)]
        kb16 = [self.av(f"kb{i}", [512], BF16) for i in range(2)]
        sqv = [self.av(f"sqv{i}", [512], F32) for i in range(2)]
        gk = self.av("gk", [128], F32)
        mkT = self.av("mkT", [4, MEMT], BF16)
        sq = [self.av("sq0", [512], BF16), self.av("sq1", [512], BF16)]
        rstd = self.av("rstd", [512], F32)
        for tb in range(MEMT // 128):
            xi = xin[tb % 2]
            P.add("sp", lambda e, xi=xi, tb=tb: e.dma_start(out=xi[:], in_=d["memp"][tb * 128:(tb + 1) * 128, :]), writes=[xi.r()], dma=True)
            for k4 in range(4):
                pb = psb[(tb * 4 + k4) % 2]
                for j in range(4):
                    kc = k4 * 4 + j
                    P.add("pe", lambda e, pb=pb, xi=xi, j=j, kc=kc: e.transpose(
                        out=pb[:, j * 128:(j + 1) * 128], in_=xi[:, kc * 128:(kc + 1) * 128], identity=self.ident[:]),
                        reads=[xi.r(), self.ident.r()], writes=[pb.r()])
                P.add("act", lambda e, pb=pb, k4=k4, tb=tb: e.copy(
                    out=memT[:, k4 * 4:(k4 + 1) * 4, tb * 128:(tb + 1) * 128], in_=pb[:].rearrange("p (a b) -> p a b", a=4)),
                    reads=[pb.r()], writes=[memT.r()])
        pb = psb[7]
        n = MEMT
        for kc in range(KC):
            s = sq[kc % 2]
            P.add("act", lambda e, s=s, kc=kc: e.activation(out=s[:, :n], in_=memT[:, kc, :], func=AF.Square), reads=[memT.r()], writes=[s.r()])
            P.add("pe", lambda e, s=s, kc=kc: e.matmul(pb[:, :n], lhsT=self.onesb[:], rhs=s[:, :n], start=(kc == 0), stop=(kc == KC - 1)),
                  reads=[s.r(), self.onesb.r()], writes=[pb.r()])
        P.add("act", lambda e: e.activation(out=rstd[:, :n], in_=pb[:, :n], func=AF.Sqrt, scale=1.0 / D, bias=1e-6), reads=[pb.r()], writes=[rstd.r()])
        P.add("dve", lambda e: e.reciprocal(out=rstd[:, :n], in_=rstd[:, :n]), reads=[rstd.r()], writes=[rstd.r()])
        for kc in range(KC):
            P.add("dve", lambda e, kc=kc: e.tensor_tensor(out=memT[:, kc, :], in0=memT[:, kc, :], in1=rstd[:, :n], op=ALU.mult),
                  reads=[memT.r(), rstd.r()], writes=[memT.r()])
        sm = self.small
        for li in range(4):
            for kc in range(KC):
                P.add("dve", lambda e, kc=kc, li=li: e.tensor_scalar(out=memn[:, kc, :], in0=memT[:, kc, :], scalar1=self.pvc("mem_norm", li * KC + kc), scalar2=None, op0=ALU.mult),
                      reads=[memT.r(), self.pv.r()], writes=[memn.r()])
            P.add("sp", lambda e, li=li: e.dma_start(out=gk[:], in_=d["mem_k_norm"][li, :].partition_broadcast(128)), writes=[gk.r()], dma=True)
            wv = d["mem_w_kv"][li].rearrange("(kc p) c -> p kc c", p=128)
            for cb in range(2):
                b = wbuf[cb]
                P.add("pool", lambda e, b=b, cb=cb, wv=wv: e.dma_start(out=b[:], in_=wv[:, :, cb * 512:(cb + 1) * 512]), writes=[b.r()], dma=True)
                for mt in range(2):
                    pb = psb[mt]
                    for kc in range(KC):
                        P.add("pe", lambda e, pb=pb, b=b, kc=kc, mt=mt: e.matmul(
                            pb[:, :], lhsT=memn[:, kc, mt * 128:(mt + 1) * 128], rhs=b[:, kc, :], start=(kc == 0), stop=(kc == KC - 1)),
                            reads=[b.r(), memn.r()], writes=[pb.r()])
                    k_f = kf[mt]
                    k_b = kb16[mt]
                    if cb == 0:
                        sv = sqv[mt]
                        c0 = 24 + 4 * mt
                        P.add("act", lambda e, pb=pb, sv=sv: e.activation(out=sv[:, :], in_=pb[:, :], func=AF.Square), reads=[pb.r()], writes=[sv.r()])
                        P.add("dve", lambda e, sv=sv, c0=c0: e.tensor_reduce(out=sm[:, c0:c0 + 4], in_=sv[:, :].rearrange("p (h d) -> p h d", h=4), axis=AX.X, op=ALU.add),
                              reads=[sv.r()], writes=[sm.r(("mk", mt))])
                        P.add("act", lambda e, c0=c0: e.activation(out=sm[:, c0:c0 + 4], in_=sm[:, c0:c0 + 4], func=AF.Sqrt, scale=1.0 / 128, bias=1e-6),
                              reads=[sm.r(("mk", mt))], writes=[sm.r(("mk", mt))])
                        P.add("dve", lambda e, c0=c0: e.reciprocal(out=sm[:, c0:c0 + 4], in_=sm[:, c0:c0 + 4]), reads=[sm.r(("mk", mt))], writes=[sm.r(("mk", mt))])
                        for h in range(4):
                            P.add("dve", lambda e, pb=pb, k_f=k_f, h=h, c0=c0: e.scalar_tensor_tensor(
                                out=k_f[:, h * 128:(h + 1) * 128], in0=pb[:, h * 128:(h + 1) * 128], scalar=sm[:, c0 + h:c0 + h + 1], in1=gk[:, :],
                                op0=ALU.mult, op1=ALU.mult),
                                reads=[pb.r(), sm.r(("mk", mt)), gk.r()], writes=[k_f.r()])
                        P.add("sp", lambda e, k_f=k_f, mt=mt, li=li: e.dma_start(out=o["mkp"][li, mt * 128:(mt + 1) * 128, :], in_=k_f[:, :]), reads=[k_f.r()], writes=[k_f.r("o")], dma=True)
                        P.add("act", lambda e, k_f=k_f, k_b=k_b: e.copy(out=k_b[:, :], in_=k_f[:, :]), reads=[k_f.r()], writes=[k_b.r()])
                        pt = self.psbf(2 + mt)
                        for h in range(4):
                            P.add("pe", lambda e, pt=pt, k_b=k_b, h=h: e.transpose(out=pt[:, h * 128:(h + 1) * 128], in_=k_b[:, h * 128:(h + 1) * 128], identity=self.identb[:]),
                                  reads=[k_b.r(), self.identb.r()], writes=[psb[2 + mt].r()])
                        P.add("act", lambda e, pt=pt, mt=mt: e.copy(out=mkT[:, :, mt * 128:(mt + 1) * 128], in_=pt[:, 0:512].rearrange("p (h m) -> p h m", h=4)),
                              reads=[psb[2 + mt].r()], writes=[mkT.r()])
                    else:
                        P.add("act", lambda e, pb=pb, k_f=k_f: e.copy(out=k_f[:, :], in_=pb[:, :]), reads=[pb.r()], writes=[k_f.r()])
                        P.add("sp", lambda e, k_f=k_f, mt=mt, li=li: e.dma_start(out=o["mvp"][li, mt * 128:(mt + 1) * 128, :], in_=k_f[:, :]), reads=[k_f.r()], writes=[k_f.r("o")], dma=True)
                        P.add("dve", lambda e, k_f=k_f, k_b=k_b: e.tensor_copy(out=k_b[:, :], in_=k_f[:, :]), reads=[k_f.r()], writes=[k_b.r()])
                        P.add("sp", lambda e, k_b=k_b, mt=mt, li=li: e.dma_start(out=self.mv_scr.t.ap()[li, mt * 128:(mt + 1) * 128, :], in_=k_b[:, :]),
                              reads=[k_b.r()], writes=[k_b.r("o"), self.mv_scr.r(li)], dma=True)
            P.add("sp", lambda e, li=li: e.dma_start(out=self.mk_scr.t.ap()[li], in_=mkT[:].rearrange("p h m -> p (h m)")),
                  reads=[mkT.r()], writes=[mkT.r("o"), self.mk_scr.r(li)], dma=True)

    def mem_sample(self):
        P, d = self.P, self.din
        psb = self.psb
        self.phase("mems")
        kb16 = [self.av(f"kb{i}", [512], BF16) for i in range(2)]
        mkT = self.av("mkT", [4, MEMT], BF16)
        for li in range(4):
            for mt in range(2):
                k_b = kb16[mt]
                P.add("pool", lambda e, k_b=k_b, li=li, mt=mt: e.dma_start(out=k_b[:, :], in_=d["cmk"][li, mt * 128:(mt + 1) * 128, :]), writes=[k_b.r()], dma=True)
                pt = self.psbf(2 + mt)
                for h in range(4):
                    P.add("pe", lambda e, pt=pt, k_b=k_b, h=h: e.transpose(out=pt[:, h * 128:(h + 1) * 128], in_=k_b[:, h * 128:(h + 1) * 128], identity=self.identb[:]),
                          reads=[k_b.r(), self.identb.r()], writes=[psb[2 + mt].r()])
                P.add("act", lambda e, pt=pt, mt=mt: e.copy(out=mkT[:, :, mt * 128:(mt + 1) * 128], in_=pt[:, 0:512].rearrange("p (h m) -> p h m", h=4)),
                      reads=[psb[2 + mt].r()], writes=[mkT.r()])
            P.add("sp", lambda e, li=li: e.dma_start(out=self.mks_scr.t.ap()[li], in_=mkT[:].rearrange("p h m -> p (h m)")), reads=[mkT.r()], writes=[self.mks_scr.r(li)], dma=True)
            for mt in range(2):
                k_b = kb16[mt]
                P.add("pool", lambda e, k_b=k_b, li=li, mt=mt: e.dma_start(out=k_b[:, :], in_=d["cmv"][li, mt * 128:(mt + 1) * 128, :]), writes=[k_b.r()], dma=True)
                P.add("sp", lambda e, k_b=k_b, li=li, mt=mt: e.dma_start(out=self.mvs_scr.t.ap()[li, mt * 128:(mt + 1) * 128, :], in_=k_b[:, :]), reads=[k_b.r()], writes=[self.mvs_scr.r(li)], dma=True)

    def mixerA(self, li, blk, sample, smp=False):
        P, d = self.P, self.din
        xn, xT, psb = self.xn, self.xT, self.psb
        self.phase("mixA")
        N = CH if smp else TA
        NC = N // CH
        av = self.av
        wbig = av("wbig", [3 * 32 * 128], BF16)
        wX = [wbig[:, i * 4096:(i + 1) * 4096].rearrange("p (k c) -> p k c", k=32) for i in range(3)]
        l1w = av("l1w", [KC, 96], BF16)
        l1a = av("l1a", [KC, 96], BF16)
        self._xw_off = self.aoff
        class _V:
            pass
        l1gv = _V()
        l1gv_ap = wbig[:, 8192:8192 + KC * 128].rearrange("p (k c) -> p k c", k=KC)
        l1gv.__class__ = type("LV", (), {"__getitem__": lambda s, idx: l1gv_ap[idx], "r": lambda s, key=None: wbig.r(2)})
        xw = [av("xw0", [N], BF16), av("xw1", [N], BF16)]
        xtmp = [av("xt0", [N], BF16), av("xt1", [N], BF16)]
        th = av("th", [N], BF16)
        ah = av("ah", [N], BF16)
        gh = av("gh", [2, N], BF16)
        w2s = av("w2s", [128], BF16)
        a2s = av("a2s", [128], BF16)
        g2s = av("g2s", [2, 128], BF16)
        r_f = av("r_f", [N], F32)
        k_f = av("k_f", [N], F32)
        v_b = av("v_b", [N], BF16)
        ew = av("ew", [N], F32)
        asig = av("asig", [N], F32)
        kk = av("kk", [N], F32)
        kmod = av("kmod", [N], F32)
        bb = av("bb", [N], F32)
        cum = av("cum", [N], F32)
        e1 = av("e1", [N], F32)
        e2 = av("e2", [N], F32)
        rk = av("rk", [N], F32)
        sqk = av("sqk", [N], BF16)
        AR = av("AR", [NC, 128], BF16)
        Bt = av("Bt", [N], BF16)
        Kt = av("Kt", [N], BF16)
        ARh = [av("AR0", [NC, 128], BF16), av("AR1", [NC, 128], BF16)]
        Bth = [av("Bt0", [N], BF16), av("Bt1", [N], BF16)]
        Kth = [av("Kt0", [N], BF16), av("Kt1", [N], BF16)]
        Bh = av("Bh", [N], BF16)
        Kh = av("Kh", [N], BF16)
        tot = av("tot", [NC], F32)
        gtot = av("gtot", [NC], F32)
        Sb = av("Sb", [64], BF16)
        BhT = av("BhT", [NC, 128], BF16, parts=64)
        KhT = av("KhT", [NC, 128], BF16, parts=64)
        Vt = av("Vt", [NC, 128], BF16, parts=64)
        g_t = av("g_t", [NC, 128], BF16, parts=64)
        NNs = av("NNs", [NC, 512], BF16, parts=64)
        NabT = av("NabT", [2 * NC, 64], BF16, parts=64)
        Tm = [av("Tm0", [2 * NC, 64], BF16, parts=64), av("Tm1", [2 * NC, 64], BF16, parts=64)]
        Xs = [av("Xa", [2 * NC, 64], BF16, parts=64), av("Xb", [2 * NC, 64], BF16, parts=64)]
        XTs = [av("XTa", [2 * NC, 64], BF16, parts=64), av("XTb", [2 * NC, 64], BF16, parts=64)]
        y_t = av("y_t", [NC, 128], F32, parts=64)
        W1b = av("W1b", [128], BF16, parts=64)
        Ub = av("Ub", [128], BF16, parts=64)
        pt1 = self.P.view("mixA.pt1", self.arena.t[0:64, self._xw_off // 2:self._xw_off // 2 + NC * 256].bitcast(F32).rearrange("p (a b) -> p a b", a=NC))
        bon = av("bon", [2 * NC], F32, parts=64)
        st = av("st", [64], F32, parts=64)
        lnw = av("lnw", [128], F32, parts=64)
        lnb = av("lnb", [128], F32, parts=64)
        mixout = av("mixout", [KC, N], BF16)
        mkT = av("mkT", [4, MEMT], BF16)
        mv = av("mv", [2, 512], BF16)
        qf, qn, pT, rz = r_f, v_b, [Bt, Kt], k_f
        m4 = self.c64[:, 0:512]
        sl8 = self.c64[:, 512:1024]
        i8 = self.c64[:, 1024:1536]
        hsel = self.c128[:, 128:130]
        SC = 128 ** -0.5

        win = d["a_w_in"][li].rearrange("(k p) c -> p k c", p=128)
        mks, mvs = (self.mks_scr, self.mvs_scr) if smp else (self.mk_scr, self.mv_scr)
        P.add("sp", lambda e: e.dma_start(out=mkT[:].rearrange("p h m -> p (h m)"), in_=mks.t.ap()[li]), reads=[mks.r(li)], writes=[mkT.r()], dma=True)
        P.add("sp", lambda e: e.dma_start(out=mv[:], in_=mvs.t.ap()[li].rearrange("(t p) c -> p t c", p=128)), reads=[mvs.r(li)], writes=[mv.r()], dma=True)
        if smp:
            xst = av("xst", [KC, N + 1], BF16)
            tmask = av("tmask", [N], F32)
            so = av("so_s", [12, 128], F32, parts=64)
            xi = av("sxi", [128], F32)
            P.add("dve", lambda e: e.memset(xst[:], 0.0), writes=[xst.r()])
            P.add("dve", lambda e: e.memset(tmask[:], 0.0), writes=[tmask.r()])
            P.add("dve", lambda e: e.memset(tmask[:, 0:1], 1.0), reads=[tmask.r()], writes=[tmask.r()])
            P.add("sp", lambda e: e.dma_start(out=xi[0:KC, :], in_=d["sts"][li]), writes=[xi.r()], dma=True)
            P.add("pe", lambda e: e.transpose(out=psb[2][:, 0:KC], in_=xi[0:KC, 0:128], identity=self.ident[0:KC, 0:KC]), reads=[xi.r(), self.ident.r()], writes=[psb[2].r()])
            P.add("act", lambda e: e.copy(out=xst[:, :, 0], in_=psb[2][:, 0:KC]), reads=[psb[2].r(), xst.r()], writes=[xst.r()])
            P.add("dve", lambda e: e.tensor_copy(out=xst[:, :, 1], in_=self.xns[:, :, 1]), reads=[self.xns.r("cur"), xst.r()], writes=[xst.r()])
            s_in = av("s_in", [24 * 64], F32, parts=64)
            P.add("sp", lambda e: e.dma_start(out=s_in[:].rearrange("p (h j) -> p h j", h=24), in_=d["stw"][li].rearrange("h i j -> i h j")), writes=[s_in.r()], dma=True)
            for gi in range(12):
                pbt = psb[gi % 2]
                P.add("pe", lambda e, gi=gi, pbt=pbt: e.transpose(out=pbt[:, 0:64], in_=s_in[:, gi * 128:(gi + 1) * 128], identity=self.ident[0:64, 0:64]),
                      reads=[s_in.r(), self.ident.r()], writes=[pbt.r()])
                P.add("act", lambda e, gi=gi, pbt=pbt: e.copy(out=self.Sf[:, li, gi, :], in_=pbt[:, 0:64]), reads=[pbt.r(), self.Sf.r((li, gi)), self.Sf.r()], writes=[self.Sf.r((li, gi))])

        def proj(ps, w, xres, cur, prv):
            for kc in range(32):
                rhs = cur(kc) if kc < 16 else prv(kc - 16)
                P.add("pe", lambda e, kc=kc, rhs=rhs: e.matmul(ps, lhsT=w[0][:, kc, :], rhs=rhs, start=(kc == 0), stop=(kc == 31)),
                      reads=[w[1]] + xres, writes=[w[2]])

        def loadw(i, c0):
            P.add("pool", lambda e: e.dma_start(out=wX[i], in_=win[:, :, c0:c0 + 128]), writes=[wbig.r(i)], dma=True)
            P.add("dve", lambda e: e.tensor_tensor(out=wX[i][:, 0:16, :], in0=wX[i][:, 0:16, :], in1=wX[i][:, 16:32, :], op=ALU.subtract),
                  reads=[wbig.r(i)], writes=[wbig.r(i)])

        import os
        dbg = int(os.environ.get('KDBG', '99'))
        if dbg <= 0:
            return
        for ti in range(1 if smp else ((NT // N) if dbg >= 60 else 1)):
            t0 = ti * N
            if smp:
                cur = lambda kc: xst[:, kc, 1:1 + N]
                prv = lambda kc: xst[:, kc, 0:N]
                xres = [xst.r()]
                xkey = 0
            else:
                cur = lambda kc, t0=t0: xn[:, kc, 1 + t0:1 + t0 + N]
                prv = lambda kc, t0=t0: xn[:, kc, t0:t0 + N]
                xres = [xn.r(t0 // 512)] + ([xn.r("prev")] if t0 == 0 else ([xn.r(t0 // 512 - 1)] if t0 % 512 == 0 else []))
                xkey = t0 // 512
            moff = self.pvoff["a_mu"][0] + li * 48
            for which, (wsrc, l1, ncol, dst, func) in enumerate((("a_w1", l1w, 96, th, AF.Tanh), ("a_a1", l1a, 96, ah, AF.Copy), ("a_g1", l1gv, 128, gh, AF.Sigmoid))):
                nh = 2 if which == 2 else 1
                for hh in range(nh):
                    src = d[wsrc][li].rearrange("(k p) c -> p k c", p=128)[:, :, hh * ncol:(hh + 1) * ncol]
                    l1r = wbig.r(2) if which == 2 else l1.r()
                    P.add("pool", lambda e, l1=l1, src=src: e.dma_start(out=l1[:], in_=src), writes=[l1r], dma=True)
                    pb = psb[3]
                    for kc in range(KC):
                        xt_, xw_ = xtmp[kc % 2], xw[kc % 2]
                        mcol = moff + which * 16 + kc
                        ocol = li * 48 + which * 16 + kc
                        pk, ck = prv(kc), cur(kc)
                        P.add("dve", lambda e, xt_=xt_, pk=pk, mcol=mcol: e.tensor_scalar(out=xt_[:], in0=pk, scalar1=self.pv[:, mcol:mcol + 1], scalar2=None, op0=ALU.mult),
                              reads=xres + [self.pv.r()], writes=[xt_.r()])
                        P.add("dve", lambda e, xt_=xt_, xw_=xw_, ck=ck, ocol=ocol: e.scalar_tensor_tensor(out=xw_[:], in0=ck, scalar=self.omu[:, ocol:ocol + 1], in1=xt_[:], op0=ALU.mult, op1=ALU.add),
                              reads=xres + [self.omu.r(), xt_.r()], writes=[xw_.r()])
                        P.add("pe", lambda e, l1=l1, xw_=xw_, kc=kc, ncol=ncol: e.matmul(pb[0:ncol, 0:N], lhsT=l1[:, kc, 0:ncol], rhs=xw_[:], start=(kc == 0), stop=(kc == KC - 1)),
                              reads=[l1r, xw_.r()], writes=[pb.r()])
                    o_ap = dst[0:ncol, :] if which < 2 else dst[:, hh, :]
                    P.add("act", lambda e, o_ap=o_ap, func=func, ncol=ncol: e.activation(out=o_ap, in_=pb[0:ncol, 0:N], func=func), reads=[pb.r()], writes=[dst.r()])
            if dbg <= 1:
                return
            for gi in range(12 if dbg >= 60 else 1):
                c0 = gi * 128
                for i in range(3):
                    loadw(i, i * 1536 + c0)
                P.add("pool", lambda e, c0=c0: e.dma_start(out=w2s[0:96, :], in_=d["a_w2"][li][:, c0:c0 + 128]), writes=[w2s.r()], dma=True)
                P.add("pool", lambda e, c0=c0: e.dma_start(out=a2s[0:96, :], in_=d["a_a2"][li][:, c0:c0 + 128]), writes=[a2s.r()], dma=True)
                P.add("pool", lambda e, c0=c0: e.dma_start(out=g2s[:], in_=d["a_g2"][li].rearrange("(k p) c -> p k c", p=128)[:, :, c0:c0 + 128]), writes=[g2s.r()], dma=True)
                P.add("sp", lambda e, c0=c0: e.dma_start(out=lnw[:], in_=d["a_lnx_w"][li, c0:c0 + 128].partition_broadcast(64)), writes=[lnw.r()], dma=True)
                P.add("sp", lambda e, c0=c0: e.dma_start(out=lnb[:], in_=d["a_lnx_b"][li, c0:c0 + 128].partition_broadcast(64)), writes=[lnb.r()], dma=True)
                for i in range(3):
                    proj(psb[i][:, 0:N], (wX[i], wbig.r(i), psb[i].r()), xres, cur, prv)
                P.add("act", lambda e: e.copy(out=r_f[:], in_=psb[0][:, 0:N]), reads=[psb[0].r()], writes=[r_f.r()])
                P.add("act", lambda e: e.copy(out=k_f[:], in_=psb[1][:, 0:N]), reads=[psb[1].r()], writes=[k_f.r()])
                P.add("act", lambda e: e.copy(out=v_b[:], in_=psb[2][:, 0:N]), reads=[psb[2].r()], writes=[v_b.r()])
                if smp:
                    for bf_ in (r_f, k_f, v_b):
                        P.add("dve", lambda e, bf_=bf_: e.tensor_tensor(out=bf_[:], in0=bf_[:], in1=tmask[:], op=ALU.mult), reads=[bf_.r(), tmask.r()], writes=[bf_.r()])
                pb = psb[3]
                P.add("pe", lambda e: e.matmul(pb[:, 0:N], lhsT=w2s[0:96, :], rhs=th[0:96, :], start=True, stop=True), reads=[w2s.r(), th.r()], writes=[pb.r()])
                P.add("pe", lambda e: e.matmul(pb[:, N:2 * N], lhsT=a2s[0:96, :], rhs=ah[0:96, :], start=True, stop=True), reads=[a2s.r(), ah.r()], writes=[pb.r()])
                pcol = li * 12 + gi
                P.add("act", lambda e, pcol=pcol: e.activation(out=ew[:], in_=pb[:, 0:N], func=AF.Exp, scale=-1.0, bias=self.neg[:, pcol:pcol + 1]), reads=[pb.r(), self.neg.r()], writes=[ew.r()])
                P.add("act", lambda e: e.activation(out=ew[:], in_=ew[:], func=AF.Ln, bias=1.0), reads=[ew.r()], writes=[ew.r()])
                P.add("act", lambda e: e.activation(out=ew[:], in_=ew[:], func=AF.Exp, scale=-1.0, bias=-0.5), reads=[ew.r()], writes=[ew.r()])
                if smp:
                    P.add("dve", lambda e: e.tensor_tensor(out=ew[:], in0=ew[:], in1=tmask[:], op=ALU.mult), reads=[ew.r(), tmask.r()], writes=[ew.r()])
                P.add("act", lambda e, pcol=pcol: e.activation(out=asig[:], in_=pb[:, N:2 * N], func=AF.Sigmoid, bias=self.pvc("a_a0", pcol)), reads=[pb.r(), self.pv.r()], writes=[asig.r()])
                pg = psb[4]
                for c in range(NC):
                    for k2 in range(2):
                        P.add("pe", lambda e, c=c, k2=k2: e.matmul(pg[0:64, c * 128:(c + 1) * 128], lhsT=gh[:, k2, c * 64:(c + 1) * 64], rhs=g2s[:, k2, :], start=(k2 == 0), stop=(k2 == 1)),
                              reads=[gh.r(), g2s.r()], writes=[pg.r()])
                P.add("act", lambda e: e.copy(out=g_t[:].rearrange("p c f -> p (c f)"), in_=pg[0:64, 0:NC * 128]), reads=[pg.r()], writes=[g_t.r()])
                if dbg <= 2:
                    return
                V = lambda fn, reads, writes: P.add("dve", fn, reads=reads, writes=writes)
                A = lambda fn, reads, writes: P.add("act", fn, reads=reads, writes=writes)
                V(lambda e, pcol=pcol: e.tensor_scalar(out=kk[:], in0=k_f[:], scalar1=self.pvc("a_k_k", pcol), scalar2=None, op0=ALU.mult), [k_f.r(), self.pv.r()], [kk.r()])
                V(lambda e: e.tensor_tensor(out=sqk[:], in0=kk[:], in1=kk[:], op=ALU.mult), [kk.r()], [sqk.r()])
                pn = psb[5]
                P.add("pe", lambda e: e.matmul(pn[:, 0:N], lhsT=self.blk64b[:], rhs=sqk[:], start=True, stop=True), reads=[sqk.r(), self.blk64b.r()], writes=[pn.r()])
                A(lambda e: e.activation(out=e1[:], in_=pn[:, 0:N], func=AF.Sqrt), [pn.r()], [e1.r()])
                V(lambda e: e.tensor_scalar(out=e1[:], in0=e1[:], scalar1=1e-12, scalar2=None, op0=ALU.max), [e1.r()], [e1.r()])
                V(lambda e: e.reciprocal(out=e1[:], in_=e1[:]), [e1.r()], [e1.r()])
                V(lambda e: e.tensor_tensor(out=kk[:], in0=kk[:], in1=e1[:], op=ALU.mult), [kk.r(), e1.r()], [kk.r()])
                V(lambda e, pcol=pcol: e.tensor_scalar(out=e2[:], in0=asig[:], scalar1=self.pvc("a_k_a", pcol), scalar2=self.neg[:, 24 + pcol:25 + pcol], op0=ALU.mult, op1=ALU.add),
                  [asig.r(), self.pv.r(), self.neg.r()], [e2.r()])
                V(lambda e: e.tensor_tensor(out=kmod[:], in0=k_f[:], in1=e2[:], op=ALU.mult), [k_f.r(), e2.r()], [kmod.r()])
                V(lambda e: e.tensor_tensor(out=bb[:], in0=kk[:], in1=asig[:], op=ALU.mult), [kk.r(), asig.r()], [bb.r()])
                V(lambda e, pcol=pcol: e.scalar_tensor_tensor(out=rk[:], in0=r_f[:], scalar=self.pvc("a_r_k", pcol), in1=kmod[:], op0=ALU.mult, op1=ALU.mult),
                  [r_f.r(), kmod.r(), self.pv.r()], [rk.r()])
                for c in range(NC):
                    P.add("pe", lambda e, c=c: e.matmul(pn[0:64, N + 2 * c:N + 2 * c + 2], lhsT=rk[:, c * 64:(c + 1) * 64], rhs=hsel, start=True, stop=True),
                          reads=[rk.r(), self.c128.r()], writes=[pn.r()])
                A(lambda e: e.copy(out=bon[:], in_=pn[0:64, N:N + 2 * NC]), [pn.r()], [bon.r()])
                V(lambda e: e.tensor_tensor_scan(out=cum[:], data0=self.rmask[:, 0:N], data1=ew[:], initial=0.0, op0=ALU.mult, op1=ALU.add), [ew.r(), self.rmask.r()], [cum.r()])
                c3 = lambda b: b[:].rearrange("p (c t) -> p c t", t=CH)
                V(lambda e: e.tensor_copy(out=tot[:], in_=c3(cum)[:, :, CH - 1]), [cum.r()], [tot.r()])
                A(lambda e: e.activation(out=e1[:], in_=cum[:], func=AF.Exp, scale=-1.0), [cum.r()], [e1.r()])
                V(lambda e: e.tensor_tensor(out=AR[:, :, 64:128], in0=c3(r_f), in1=c3(e1), op=ALU.mult), [r_f.r(), e1.r()], [AR.r()])
                A(lambda e: e.activation(out=e2[:], in_=cum[:], func=AF.Exp), [cum.r()], [e2.r()])
                V(lambda e: e.tensor_tensor(out=Bt[:], in0=bb[:], in1=e2[:], op=ALU.mult), [bb.r(), e2.r()], [Bt.r()])
                V(lambda e: e.tensor_tensor(out=Kt[:], in0=kmod[:], in1=e2[:], op=ALU.mult), [kmod.r(), e2.r()], [Kt.r()])
                V(lambda e: e.tensor_tensor(out=e1[:], in0=ew[:], in1=cum[:], op=ALU.subtract), [ew.r(), cum.r(), AR.r()], [e1.r()])
                A(lambda e: e.activation(out=e1[:], in_=e1[:], func=AF.Exp), [e1.r()], [e1.r()])
                V(lambda e: e.scalar_tensor_tensor(out=AR[:, :, 0:64], in0=c3(kk), scalar=-1.0, in1=c3(e1), op0=ALU.mult, op1=ALU.mult), [kk.r(), e1.r()], [AR.r()])
                V(lambda e: e.tensor_tensor(out=c3(e2), in0=c3(cum), in1=tot[:].unsqueeze(2).to_broadcast([128, NC, CH]), op=ALU.subtract), [cum.r(), tot.r(), Bt.r(), Kt.r()], [e2.r()])
                A(lambda e: e.activation(out=e2[:], in_=e2[:], func=AF.Exp), [e2.r()], [e2.r()])
                V(lambda e: e.tensor_tensor(out=Bh[:], in0=bb[:], in1=e2[:], op=ALU.mult), [bb.r(), e2.r()], [Bh.r()])
                V(lambda e: e.tensor_tensor(out=Kh[:], in0=kmod[:], in1=e2[:], op=ALU.mult), [kmod.r(), e2.r()], [Kh.r()])
                A(lambda e: e.activation(out=gtot[:], in_=tot[:], func=AF.Exp, scale=-1.0), [tot.r()], [gtot.r()])
                for h in range(2):
                    hm = self.c128[:, 128 + h:129 + h]
                    V(lambda e, h=h, hm=hm: e.tensor_scalar(out=ARh[h][:].rearrange("p c f -> p (c f)"), in0=AR[:].rearrange("p c f -> p (c f)"), scalar1=hm, scalar2=None, op0=ALU.mult), [AR.r(), self.c128.r()], [ARh[h].r()])
                    V(lambda e, h=h, hm=hm: e.tensor_scalar(out=Bth[h][:], in0=Bt[:], scalar1=hm, scalar2=None, op0=ALU.mult), [Bt.r(), self.c128.r()], [Bth[h].r()])
                    V(lambda e, h=h, hm=hm: e.tensor_scalar(out=Kth[h][:], in0=Kt[:], scalar1=hm, scalar2=None, op0=ALU.mult), [Kt.r(), self.c128.r()], [Kth[h].r()])
                if dbg <= 3:
                    return
                for (srcb, dstb) in ((Bh, BhT), (Kh, KhT), (v_b, Vt)):
                    ptb = self.psbf(6)
                    for c in range(NC):
                        P.add("pe", lambda e, c=c, srcb=srcb, ptb=ptb: e.transpose(out=ptb[0:64, c * 128:(c + 1) * 128], in_=srcb[:, c * 64:(c + 1) * 64], identity=self.identb[:]),
                              reads=[srcb.r(), self.identb.r()], writes=[psb[6].r()])
                    A(lambda e, dstb=dstb, ptb=ptb: e.copy(out=dstb[:].rearrange("p c f -> p (c f)"), in_=ptb[0:64, 0:NC * 128]), [psb[6].r()], [dstb.r()])
                if dbg <= 4:
                    return
                for c in range(NC):
                    pb7 = psb[7]
                    for h in range(2):
                        hp = slice(64 * h, 64 * h + 64)
                        P.add("pe", lambda e, c=c, h=h, hp=hp: e.matmul(pb7[0:64, (2 * h) * 128:(2 * h + 1) * 128], lhsT=Bth[h][:, c * 64:(c + 1) * 64], rhs=AR[:, c, :], start=True, stop=True),
                              reads=[Bth[h].r(), AR.r()], writes=[pb7.r()])
                        P.add("pe", lambda e, c=c, h=h, hp=hp: e.matmul(pb7[0:64, (2 * h + 1) * 128:(2 * h + 2) * 128], lhsT=Kth[h][:, c * 64:(c + 1) * 64], rhs=AR[:, c, :], start=True, stop=True),
                              reads=[Kth[h].r(), AR.r()], writes=[pb7.r()])
                    V(lambda e, c=c: e.tensor_tensor(out=NNs[:, c, :], in0=pb7[0:64, :], in1=m4, op=ALU.mult), [pb7.r(), self.c64.r()], [NNs.r()])
                pb4 = psb[4]
                for c in range(NC):
                    for h in range(2):
                        hp = slice(64 * h, 64 * h + 64)
                        m = c * 2 + h
                        P.add("pe", lambda e, c=c, h=h, m=m: e.matmul(pb4[0:64, m * 64:(m + 1) * 64], lhsT=ARh[h][:, c, 0:64], rhs=Bt[:, c * 64:(c + 1) * 64], start=True, stop=True),
                              reads=[Bt.r(), ARh[h].r()], writes=[pb4.r()])
                V(lambda e: e.tensor_tensor(out=NabT[:].rearrange("p m s -> p (m s)"), in0=pb4[0:64, 0:2 * NC * 64], in1=sl8[:, 0:2 * NC * 64], op=ALU.mult), [pb4.r(), self.c64.r()], [NabT.r()])
                if dbg <= 5:
                    return
                nm = 2 * NC
                X0 = lambda m: NNs[:, m // 2, (m % 2) * 256:(m % 2) * 256 + 64]
                NN5 = NNs[:].rearrange("p c (h q t) -> p c h q t", h=2, q=4)
                V(lambda e: e.tensor_tensor(out=Tm[0][:].rearrange("p (c h) s -> p c h s", h=2), in0=NN5[:, :, :, 0, :],
                                            in1=i8[:, 0:nm * 64].rearrange("p (c h s) -> p c h s", h=2, s=64), op=ALU.add), [NNs.r(), self.c64.r()], [Tm[0].r()])
                for lvl in range(5):
                    Xc = X0 if lvl == 0 else (lambda m, b=Xs[lvl % 2]: b[:, m, :])
                    XTc = (lambda m: NabT[:, m, :]) if lvl == 0 else (lambda m, b=XTs[lvl % 2]: b[:, m, :])
                    xr = [NNs.r(), NabT.r()] if lvl == 0 else [Xs[lvl % 2].r(), XTs[lvl % 2].r()]
                    Xn, XTn = Xs[(lvl + 1) % 2], XTs[(lvl + 1) % 2]
                    pX, pXT, pT_ = psb[4], psb[5], psb[6]
                    if lvl < 4:
                        for m in range(nm):
                            P.add("pe", lambda e, m=m, Xc=Xc, XTc=XTc: e.matmul(pX[0:64, m * 64:(m + 1) * 64], lhsT=XTc(m), rhs=Xc(m), start=True, stop=True), reads=xr, writes=[pX.r()])
                        A(lambda e, Xn=Xn: e.copy(out=Xn[:].rearrange("p m s -> p (m s)"), in_=pX[0:64, 0:nm * 64]), [pX.r()], [Xn.r()])
                    for m in range(nm):
                        P.add("pe", lambda e, m=m, Xc=Xc, XTc=XTc: e.matmul(pXT[0:64, m * 64:(m + 1) * 64], lhsT=Xc(m), rhs=XTc(m), start=True, stop=True), reads=xr, writes=[pXT.r()])
                    V(lambda e, XTn=XTn: e.tensor_copy(out=XTn[:].rearrange("p m s -> p (m s)"), in_=pXT[0:64, 0:nm * 64]), [pXT.r()], [XTn.r()])
                    Tc, Tn = Tm[lvl % 2], Tm[(lvl + 1) % 2]
                    for m in range(nm):
                        P.add("pe", lambda e, m=m, XTn=XTn, Tc=Tc: e.matmul(pT_[0:64, m * 64:(m + 1) * 64], lhsT=XTn[:, m, :], rhs=Tc[:, m, :], start=True, stop=True),
                              reads=[XTn.r(), Tc.r()], writes=[pT_.r()])
                    V(lambda e, Tc=Tc, Tn=Tn: e.tensor_tensor(out=Tn[:].rearrange("p m s -> p (m s)"), in0=pT_[0:64, 0:nm * 64], in1=Tc[:].rearrange("p m s -> p (m s)"), op=ALU.add),
                      [pT_.r(), Tc.r()], [Tn.r()])
                TF = Tm[1]
                if dbg <= 6:
                    return
                Sg = self.Sf[:, li, gi, :]
                sres = self.Sf.r((li, gi))
                A(lambda e, Sg=Sg: e.copy(out=Sb[:], in_=Sg), [sres, self.Sf.r()], [Sb.r()])
                for c in range(NC):
                    ps1, ps2, psY, psS = psb[0], psb[1], psb[2], psb[3]
                    for h in range(2):
                        hp = slice(64 * h, 64 * h + 64)
                        hs = slice(64 * h, 64 * h + 64)
                        P.add("pe", lambda e, c=c, h=h, hs=hs: e.matmul(ps1[0:64, hs], lhsT=ARh[h][:, c, 0:64], rhs=Sb[:, :], start=True, stop=False), reads=[ARh[h].r(), Sb.r()], writes=[ps1.r()])
                        P.add("pe", lambda e, c=c, h=h, hs=hs: e.matmul(ps1[0:64, hs], lhsT=NNs[:, c, h * 256 + 128:h * 256 + 192], rhs=Vt[:, c, hs], start=False, stop=True), reads=[NNs.r(), Vt.r()], writes=[ps1.r()])
                    A(lambda e: e.copy(out=W1b[:], in_=ps1[0:64, 0:128]), [ps1.r()], [W1b.r()])
                    for h in range(2):
                        hs = slice(64 * h, 64 * h + 64)
                        P.add("pe", lambda e, c=c, h=h, hs=hs: e.matmul(ps2[0:64, hs], lhsT=TF[:, c * 2 + h, :], rhs=W1b[:, hs], start=True, stop=True), reads=[TF.r(), W1b.r()], writes=[ps2.r()])
                    V(lambda e: e.tensor_copy(out=Ub[:], in_=ps2[0:64, 0:128]), [ps2.r()], [Ub.r()])
                    for h in range(2):
                        hp = slice(64 * h, 64 * h + 64)
                        hs = slice(64 * h, 64 * h + 64)
                        P.add("pe", lambda e, c=c, h=h, hs=hs: e.matmul(psY[0:64, hs], lhsT=ARh[h][:, c, 64:128], rhs=Sb[:, :], start=True, stop=False), reads=[ARh[h].r(), Sb.r()], writes=[psY.r()])
                        P.add("pe", lambda e, c=c, h=h, hs=hs: e.matmul(psY[0:64, hs], lhsT=NNs[:, c, h * 256 + 64:h * 256 + 128], rhs=Ub[:, hs], start=False, stop=False), reads=[NNs.r(), Ub.r()], writes=[psY.r()])
                        P.add("pe", lambda e, c=c, h=h, hs=hs: e.matmul(psY[0:64, hs], lhsT=NNs[:, c, h * 256 + 192:h * 256 + 256], rhs=Vt[:, c, hs], start=False, stop=True), reads=[NNs.r(), Vt.r()], writes=[psY.r()])
                    A(lambda e, c=c: e.copy(out=y_t[:, c, :], in_=psY[0:64, 0:128]), [psY.r()], [y_t.r()])
                    P.add("pe", lambda e, c=c: e.matmul(psS[:, 0:128], lhsT=BhT[:, c, :], rhs=Ub[:], start=True, stop=False), reads=[BhT.r(), Ub.r()], writes=[psS.r()])
                    P.add("pe", lambda e, c=c: e.matmul(psS[:, 0:128], lhsT=KhT[:, c, :], rhs=Vt[:, c, :], start=False, stop=True), reads=[KhT.r(), Vt.r()], writes=[psS.r()])
                    for h in range(2):
                        hp = slice(64 * h, 64 * h + 64)
                        V(lambda e, c=c, hp=hp, h=h, gi=gi: e.scalar_tensor_tensor(out=self.Sf[hp, li, gi, :], in0=self.Sf[hp, li, gi, :], scalar=gtot[hp, c:c + 1], in1=psS[hp, 64 * h:64 * h + 64], op0=ALU.mult, op1=ALU.add),
                          [sres, psS.r(), gtot.r(), Sb.r()], [sres])
                    A(lambda e, Sg=Sg: e.copy(out=Sb[:], in_=Sg), [sres], [Sb.r()])
                if dbg <= 7:
                    return
                y3 = y_t[:].rearrange("p c (h i) -> p (c h) i", h=2)
                p3 = pt1[:].rearrange("p c (h i) -> p (c h) i", h=2)
                v3 = Vt[:].rearrange("p c (h i) -> p (c h) i", h=2)
                bc = lambda a: a.unsqueeze(2).to_broadcast([64, nm, 64])
                V(lambda e: e.tensor_reduce(out=st[:, 0:nm], in_=y3, axis=AX.X, op=ALU.add), [y_t.r()], [st.r()])
                V(lambda e: e.tensor_tensor(out=p3, in0=y3, in1=y3, op=ALU.mult), [y_t.r()], [pt1.r()])
                V(lambda e: e.tensor_reduce(out=st[:, 8:8 + nm], in_=p3, axis=AX.X, op=ALU.add), [pt1.r(), st.r()], [st.r()])
                V(lambda e: e.tensor_scalar(out=st[:, 16:16 + nm], in0=st[:, 0:nm], scalar1=1.0 / 64, scalar2=None, op0=ALU.mult), [st.r()], [st.r()])
                V(lambda e: e.tensor_tensor(out=st[:, 24:24 + nm], in0=st[:, 16:16 + nm], in1=st[:, 16:16 + nm], op=ALU.mult), [st.r()], [st.r()])
                V(lambda e: e.scalar_tensor_tensor(out=st[:, 32:32 + nm], in0=st[:, 8:8 + nm], scalar=1.0 / 64, in1=st[:, 24:24 + nm], op0=ALU.mult, op1=ALU.subtract), [st.r()], [st.r()])
                A(lambda e: e.activation(out=st[:, 40:40 + nm], in_=st[:, 32:32 + nm], func=AF.Sqrt, bias=64e-5), [st.r()], [st.r()])
                V(lambda e: e.reciprocal(out=st[:, 40:40 + nm], in_=st[:, 40:40 + nm]), [st.r()], [st.r()])
                V(lambda e: e.tensor_tensor(out=y3, in0=y3, in1=bc(st[:, 16:16 + nm]), op=ALU.subtract), [y_t.r(), st.r()], [y_t.r()])
                V(lambda e: e.tensor_tensor(out=y3, in0=y3, in1=bc(st[:, 40:40 + nm]), op=ALU.mult), [y_t.r(), st.r()], [y_t.r()])
                lb = lambda a: a[:].unsqueeze(1).to_broadcast([64, NC, 128])
                V(lambda e: e.tensor_tensor(out=y_t[:], in0=y_t[:], in1=lb(lnw), op=ALU.mult), [y_t.r(), lnw.r()], [y_t.r()])
                V(lambda e: e.tensor_tensor(out=y_t[:], in0=y_t[:], in1=lb(lnb), op=ALU.add), [y_t.r(), lnb.r()], [y_t.r()])
                V(lambda e: e.tensor_tensor(out=p3, in0=v3, in1=bc(bon[:, 0:nm]), op=ALU.mult), [Vt.r(), bon.r()], [pt1.r()])
                V(lambda e: e.tensor_tensor(out=y_t[:], in0=y_t[:], in1=pt1[:], op=ALU.add), [y_t.r(), pt1.r()], [y_t.r()])
                V(lambda e: e.tensor_tensor(out=BhT[:], in0=y_t[:], in1=g_t[:], op=ALU.mult), [y_t.r(), g_t.r()], [BhT.r()])
                ptb = self.psbf(6)
                for c in range(NC):
                    P.add("pe", lambda e, c=c: e.transpose(out=ptb[:, c * 64:(c + 1) * 64], in_=BhT[:, c, :], identity=self.identb[0:64, 0:64]),
                          reads=[BhT.r(), self.identb.r()], writes=[psb[6].r()])
                A(lambda e, gi=gi: e.copy(out=mixout[:, gi, :], in_=ptb[:, 0:N]), [psb[6].r()], [mixout.r()])
                if dbg == 50:
                    for nm_, bf_ in (("r_f", r_f), ("k_f", k_f), ("v_b", v_b), ("ew", ew), ("asig", asig), ("kk", kk), ("kmod", kmod), ("bb", bb), ("cum", cum),
                                     ("AR", AR), ("Bt", Bt), ("Kt", Kt), ("Bh", Bh), ("Kh", Kh), ("NNs", NNs), ("NabT", NabT), ("TF", TF), ("y_t", y_t), ("g_t", g_t),
                                     ("bon", bon), ("yo", BhT), ("Vt", Vt), ("gtot", gtot), ("th", th), ("ah", ah), ("gh", gh)):
                        self.dump(nm_, bf_)
                    self.dump("Sg", self.Sf, ap=self.Sf[:, li, gi, :])
                    self.dump("mix0", mixout, ap=mixout[:, 0, :])
                    return
            if dbg <= 8:
                return
            for h in range(4):
                loadw(0, 4608 + h * 128)
                proj(psb[0][:, 0:N], (wX[0], wbig.r(0), psb[0].r()), xres, cur, prv)
                self.mem_attend(psb[0][:, 0:N], psb[0].r(), li, h, N, qf, qn, sqk, pT, rz, mkT, mv, mixout[:, 12 + h, :], mixout.r())
            if dbg <= 9:
                return
            if dbg == 60 and ti == 3:
                self.dump('mixout', mixout)
            wo = wbig[:, 0:KC * 512].rearrange("p (k c) -> p k c", k=KC)
            wsrc = d["w_out"][li].rearrange("(k p) c -> p k c", p=128)
            for dq in range(4):
                P.add("pool", lambda e, dq=dq: e.dma_start(out=wo, in_=wsrc[:, :, dq * 512:(dq + 1) * 512]), writes=[wbig.r(0), wbig.r(1)], dma=True)
                for dl in range(4):
                    dc = dq * 4 + dl
                    pa = psb[1 + dc % 2]
                    for kc in range(KC):
                        P.add("pe", lambda e, pa=pa, kc=kc, dl=dl: e.matmul(pa[:, 0:N], lhsT=wo[:, kc, dl * 128:(dl + 1) * 128], rhs=mixout[:, kc, :], start=(kc == 0), stop=(kc == KC - 1)),
                              reads=[wbig.r(0), wbig.r(1), mixout.r()], writes=[pa.r()])
                    if smp:
                        P.add("dve", lambda e, pa=pa, dc=dc: e.tensor_tensor(out=self.xsT[:, dc:dc + 1], in0=pa[:, 0:1], in1=self.xsT[:, dc:dc + 1], op=ALU.add),
                              reads=[pa.r(), self.xsT.r()], writes=[self.xsT.r()])
                    else:
                        P.add("dve", lambda e, pa=pa, dc=dc, t0=t0: e.tensor_tensor(out=xT[:, dc, t0:t0 + N], in0=pa[:, 0:N], in1=xT[:, dc, t0:t0 + N], op=ALU.add),
                              reads=[pa.r(), xT.r(xkey)], writes=[xT.r(xkey)])
        if smp:
            for gi in range(12):
                pbt = psb[gi % 2]
                P.add("pe", lambda e, gi=gi, pbt=pbt: e.transpose(out=pbt[0:64, 0:128], in_=self.Sf[:, li, gi, :], identity=self.ident[:]),
                      reads=[self.Sf.r((li, gi)), self.ident.r()], writes=[pbt.r()])
                P.add("act", lambda e, gi=gi, pbt=pbt: e.copy(out=so[:, gi, :], in_=pbt[0:64, 0:128]), reads=[pbt.r()], writes=[so.r()])
            P.add("sp", lambda e: e.dma_start(out=self.dout["wkvs"][li].rearrange("(g h) i j -> i g h j", h=2), in_=so[:].rearrange("p g (h j) -> p g h j", h=2)), reads=[so.r()], dma=True)

    def mem_attend(self, psq, psq_res, li, h, N, qf, qn, sqb, pT, rz, mkT, mv, out_ap, out_res):
        P, psb = self.P, self.psb
        SC = 128 ** -0.5
        P.add("act", lambda e: e.copy(out=qf[:, 0:N], in_=psq), reads=[psq_res], writes=[qf.r()])
        P.add("dve", lambda e: e.tensor_tensor(out=sqb[:, 0:N], in0=qf[:, 0:N], in1=qf[:, 0:N], op=ALU.mult), reads=[qf.r()], writes=[sqb.r()])
        pn = psb[5]
        P.add("pe", lambda e: e.matmul(pn[:, 0:N], lhsT=self.onesb[:], rhs=sqb[:, 0:N], start=True, stop=True), reads=[sqb.r(), self.onesb.r()], writes=[pn.r()])
        P.add("act", lambda e: e.activation(out=rz[:, 0:N], in_=pn[:, 0:N], func=AF.Sqrt, scale=1.0 / 128, bias=1e-6), reads=[pn.r()], writes=[rz.r()])
        P.add("dve", lambda e: e.reciprocal(out=rz[:, 0:N], in_=rz[:, 0:N]), reads=[rz.r()], writes=[rz.r()])
        P.add("dve", lambda e: e.scalar_tensor_tensor(out=qn[:, 0:N], in0=qf[:, 0:N], scalar=self.pvc("mem_q_norm", li), in1=rz[:, 0:N], op0=ALU.mult, op1=ALU.mult),
              reads=[qf.r(), rz.r(), self.pv.r()], writes=[qn.r()])
        for mt in range(2):
            pS = psb[6 + mt]
            P.add("pe", lambda e, mt=mt, pS=pS: e.matmul(pS[:, 0:N], lhsT=mkT[:, h, mt * 128:(mt + 1) * 128], rhs=qn[:, 0:N], start=True, stop=True), reads=[mkT.r(), qn.r()], writes=[pS.r()])
            P.add("act", lambda e, mt=mt, pS=pS: e.activation(out=pT[mt][:, 0:N], in_=pS[:, 0:N], func=AF.Exp, scale=SC), reads=[pS.r()], writes=[pT[mt].r()])
        pO, pZ = psb[4], psb[5]
        for mt in range(2):
            P.add("pe", lambda e, mt=mt: e.matmul(pO[:, 0:N], lhsT=mv[:, mt, h * 128:(h + 1) * 128], rhs=pT[mt][:, 0:N], start=(mt == 0), stop=(mt == 1)), reads=[mv.r(), pT[mt].r()], writes=[pO.r()])
        for mt in range(2):
            P.add("pe", lambda e, mt=mt: e.matmul(pZ[:, 0:N], lhsT=self.onesb[:], rhs=pT[mt][:, 0:N], start=(mt == 0), stop=(mt == 1)), reads=[self.onesb.r(), pT[mt].r()], writes=[pZ.r()])
        P.add("dve", lambda e: e.reciprocal(out=rz[:, 0:N], in_=pZ[:, 0:N]), reads=[pZ.r()], writes=[rz.r()])
        P.add("dve", lambda e: e.tensor_tensor(out=out_ap, in0=pO[:, 0:N], in1=rz[:, 0:N], op=ALU.mult), reads=[pO.r(), rz.r()], writes=[out_res])

    def lam_setup(self):
        import math
        P, d = self.P, self.din
        sm = self.small
        row = self.colst
        for lj in range(2):
            lam_init = 0.8 - 0.6 * math.exp(-0.3 * (2 + lj))
            P.add("sp", lambda e, lj=lj: e.dma_start(out=row[0:1, 0:128], in_=d["b_lam"][lj:lj + 1, 0:128]), writes=[row.r()], dma=True)
            P.add("sp", lambda e, lj=lj: e.dma_start(out=row[1:2, 0:128], in_=d["b_lam"][lj:lj + 1, 128:256]), writes=[row.r()], dma=True)
            P.add("dve", lambda e: e.tensor_tensor(out=row[0:2, 0:64], in0=row[0:2, 0:64], in1=row[0:2, 64:128], op=ALU.mult), reads=[row.r()], writes=[row.r()])
            P.add("dve", lambda e: e.tensor_reduce(out=sm[0:2, 40:41], in_=row[0:2, 0:64], axis=AX.X, op=ALU.add), reads=[row.r()], writes=[sm.r("lam")])
            P.add("act", lambda e: e.activation(out=sm[0:2, 40:41], in_=sm[0:2, 40:41], func=AF.Exp), reads=[sm.r("lam")], writes=[sm.r("lam")])
            P.add("dve", lambda e: e.memset(sm[0:2, 41:42], 1.0), reads=[sm.r("lam")], writes=[sm.r("lam2")])
            P.add("dve", lambda e: e.memset(self.colst[0:2, 0:128], -1.0), reads=[row.r(), sm.r("lam")], writes=[row.r()])
            P.add("dve", lambda e: e.memset(self.colst[0:1, 0:128], 1.0), reads=[row.r()], writes=[row.r()])
            pb = self.psb[0]
            P.add("pe", lambda e: e.matmul(pb[:, 0:1], lhsT=row[0:2, 0:128], rhs=sm[0:2, 40:41], start=True, stop=True), reads=[row.r(), sm.r("lam")], writes=[pb.r()])
            P.add("dve", lambda e, lj=lj, lam_init=lam_init: e.tensor_scalar(out=self.lamv[:, 2 * lj:2 * lj + 1], in0=pb[:, 0:1], scalar1=-1.0, scalar2=-lam_init, op0=ALU.mult, op1=ALU.add),
                  reads=[pb.r()], writes=[self.lamv.r()])
            P.add("dve", lambda e, lj=lj, lam_init=lam_init: e.memset(self.lamv[:, 2 * lj + 1:2 * lj + 2], 1.0 - lam_init), reads=[self.lamv.r()], writes=[self.lamv.r()])

    def shared_kv(self, blk, sample):
        P, d, o = self.P, self.din, self.dout
        xn, psb = self.xn, self.psb
        self.phase("kv")
        av = self.av
        self.rmsnorm("kv_norm", 0, sample)
        wkv = [av("wkv0", [KC, 512], BF16), av("wkv1", [KC, 512], BF16)]
        kf = [av("kf0", [512], F32), av("kf1", [512], F32)]
        sqv = av("sqv", [512], F32)
        kb = [av("kb0", [512], BF16), av("kb1", [512], BF16)]
        kts = [av("kts0", [512], BF16), av("kts1", [512], BF16)]
        st = av("stk", [16], F32)
        gk = av("gk64", [64], F32)
        cs = av("cs", [8, 16], F32)
        css = av("css", [16], F32)
        rt = av("rt", [8, 16], F32)
        rt2 = av("rt2", [8, 16], F32)
        P.add("sp", lambda e: e.dma_start(out=gk[:], in_=d["k_norm"][0, :].partition_broadcast(128)), writes=[gk.r()], dma=True)
        P.add("sp", lambda e: e.dma_start(out=cs[:], in_=d["cs_tok"][blk * NT:(blk + 1) * NT, :].rearrange("(t p) f -> p t f", p=128)), writes=[cs.r()], dma=True)
        if sample:
            P.add("sp", lambda e: e.dma_start(out=css[0:1, :], in_=d["cs_tok"][4096:4097, :]), writes=[css.r()], dma=True)
        wv = d["kv_w"].rearrange("(kc p) c -> p kc c", p=128)

        def post(pb, npart, cb, csap, kf_, kb_, out_rows_k, out_rows_v, tb):
            V = lambda fn, reads, writes: P.add("dve", fn, reads=reads, writes=writes)
            pp = slice(0, npart)
            if cb < 3:
                P.add("act", lambda e: e.activation(out=sqv[pp, :], in_=pb[pp, :], func=AF.Square), reads=[pb.r()], writes=[sqv.r()])
                V(lambda e: e.tensor_reduce(out=st[pp, 0:8], in_=sqv[pp, :].rearrange("p (g d) -> p g d", g=8), axis=AX.X, op=ALU.add), [sqv.r()], [st.r()])
                P.add("act", lambda e: e.activation(out=st[pp, 0:8], in_=st[pp, 0:8], func=AF.Sqrt, scale=1.0 / 64, bias=1e-6), reads=[st.r()], writes=[st.r()])
                V(lambda e: e.reciprocal(out=st[pp, 0:8], in_=st[pp, 0:8]), [st.r()], [st.r()])
                k3 = kf_[pp, :].rearrange("p (g d) -> p g d", g=8)
                V(lambda e: e.tensor_tensor(out=k3, in0=pb[pp, :].rearrange("p (g d) -> p g d", g=8), in1=st[pp, 0:8].unsqueeze(2).to_broadcast([npart, 8, 64]), op=ALU.mult), [pb.r(), st.r()], [kf_.r()])
                V(lambda e: e.tensor_tensor(out=k3, in0=k3, in1=gk[pp, :].unsqueeze(1).to_broadcast([npart, 8, 64]), op=ALU.mult), [kf_.r(), gk.r()], [kf_.r()])
                cb_ = csap[:, 0:8].unsqueeze(1).to_broadcast([npart, 8, 8])
                sb_ = csap[:, 8:16].unsqueeze(1).to_broadcast([npart, 8, 8])
                x1, x2 = k3[:, :, 0:8], k3[:, :, 8:16]
                V(lambda e: e.tensor_tensor(out=rt[pp, :, 0:8], in0=x1, in1=cb_, op=ALU.mult), [kf_.r(), cs.r(), css.r()], [rt.r()])
                V(lambda e: e.tensor_tensor(out=rt[pp, :, 8:16], in0=x2, in1=cb_, op=ALU.mult), [kf_.r(), cs.r(), css.r()], [rt.r()])
                V(lambda e: e.tensor_tensor(out=rt2[pp, :, 0:8], in0=x2, in1=sb_, op=ALU.mult), [kf_.r(), cs.r(), css.r()], [rt2.r()])
                V(lambda e: e.tensor_tensor(out=rt2[pp, :, 8:16], in0=x1, in1=sb_, op=ALU.mult), [kf_.r(), cs.r(), css.r()], [rt2.r()])
                V(lambda e: e.tensor_tensor(out=k3[:, :, 0:8], in0=rt[pp, :, 0:8], in1=rt2[pp, :, 0:8], op=ALU.subtract), [rt.r(), rt2.r(), kf_.r()], [kf_.r()])
                V(lambda e: e.tensor_tensor(out=k3[:, :, 8:16], in0=rt[pp, :, 8:16], in1=rt2[pp, :, 8:16], op=ALU.add), [rt.r(), rt2.r(), kf_.r()], [kf_.r()])
                P.add("sp", lambda e: e.dma_start(out=out_rows_k[:, cb * 512:(cb + 1) * 512], in_=kf_[pp, :]), reads=[kf_.r()], dma=True)
                if tb is None:
                    P.add("sp", lambda e: e.dma_start(out=self.ks_scr.t.ap()[:, cb * 512:(cb + 1) * 512], in_=kf_[pp, :]), reads=[kf_.r()], writes=[self.ks_scr.r(cb)], dma=True)
                if tb is not None:
                    P.add("act", lambda e: e.copy(out=kb_[:, :], in_=kf_[:, :]), reads=[kf_.r()], writes=[kb_.r()])
                    ptb = self.psbf(4 + tb % 2)
                    kt_ = kts[tb % 2]
                    for hh in range(4):
                        P.add("pe", lambda e, hh=hh: e.transpose(out=ptb[:, hh * 128:(hh + 1) * 128], in_=kb_[:, hh * 128:(hh + 1) * 128], identity=self.identb[:]),
                              reads=[kb_.r(), self.identb.r()], writes=[psb[4 + tb % 2].r()])
                    P.add("act", lambda e: e.copy(out=kt_[:, :], in_=ptb[:, 0:512]), reads=[psb[4 + tb % 2].r()], writes=[kt_.r()])
                    k0 = blk * NT + tb * 128
                    P.add("sp", lambda e: e.dma_start(out=self.kt_scr.t.ap()[cb * 4:(cb + 1) * 4, :, k0:k0 + 128].rearrange("h p k -> p h k"), in_=kt_[:, :].rearrange("p (h k) -> p h k", h=4)),
                          reads=[kt_.r()], writes=[self.kt_scr.r(blk)], dma=True)
            else:
                vc = cb - 3
                P.add("act", lambda e: e.copy(out=kf_[pp, :], in_=pb[pp, :]), reads=[pb.r()], writes=[kf_.r()])
                P.add("sp", lambda e: e.dma_start(out=out_rows_v[:, vc * 512:(vc + 1) * 512], in_=kf_[pp, :]), reads=[kf_.r()], dma=True)
                if tb is None:
                    P.add("sp", lambda e: e.dma_start(out=self.vs_scr.t.ap()[:, vc * 512:(vc + 1) * 512], in_=kf_[pp, :]), reads=[kf_.r()], writes=[self.vs_scr.r(vc)], dma=True)
                if tb is not None:
                    V(lambda e: e.tensor_copy(out=kb_[:, :], in_=kf_[:, :]), [kf_.r()], [kb_.r()])
                    r0 = blk * NT + tb * 128
                    P.add("sp", lambda e: e.dma_start(out=self.v_scr.t.ap()[r0:r0 + 128, vc * 512:(vc + 1) * 512], in_=kb_[:, :]), reads=[kb_.r()], writes=[self.v_scr.r(blk)], dma=True)

        for cb in range(6):
            w = wkv[cb % 2]
            P.add("pool", lambda e, w=w, cb=cb: e.dma_start(out=w[:], in_=wv[:, :, cb * 512:(cb + 1) * 512]), writes=[w.r()], dma=True)
            for tb in range(8):
                pb = psb[tb % 4]
                for kc in range(KC):
                    P.add("pe", lambda e, pb=pb, w=w, kc=kc, tb=tb: e.matmul(pb[:, :], lhsT=xn[:, kc, 1 + tb * 128:1 + (tb + 1) * 128], rhs=w[:, kc, :], start=(kc == 0), stop=(kc == KC - 1)),
                          reads=[w.r(), xn.r(tb // 4)], writes=[pb.r()])
                r0 = blk * NT + tb * 128
                post(pb, 128, cb, cs[:, tb, :], kf[tb % 2], kb[tb % 2], o["krp"][r0:r0 + 128, :], o["vrp"][r0:r0 + 128, :], tb)
            if sample:
                pb = psb[6]
                for kc in range(KC):
                    P.add("pe", lambda e, pb=pb, w=w, kc=kc: e.matmul(pb[0:1, :], lhsT=self.xns[:, kc, 1:2], rhs=w[:, kc, :], start=(kc == 0), stop=(kc == KC - 1)),
                          reads=[w.r(), self.xns.r("cur")], writes=[pb.r()])
                post(pb, 1, cb, css[0:1, :], kf[0], kb[0], o["krs"], o["vrs"], None)

    def mixerB(self, lj, blk, sample):
        P, d = self.P, self.din
        xn, xT, psb = self.xn, self.xT, self.psb
        li = 2 + lj
        self.phase("mixB")
        av = self.av
        N = 512
        nkeys = (blk + 1) * NT
        KT = [av("KT0", [NBLK * NT], BF16), av("KT1", [NBLK * NT], BF16)]
        VV = [av("VV0", [NBLK * 8, 128], BF16), av("VV1", [NBLK * 8, 128], BF16)]
        wq = [av("wq0", [KC, 128], BF16), av("wq1", [KC, 128], BF16)]
        qf = av("qf", [N], F32)
        sqb = av("sqb", [N], BF16)
        rz = av("rz", [N], F32)
        qn = av("qn", [N], BF16)
        rC = av("rC", [N], F32)
        rS = av("rS", [N], F32)
        qm = [av("qm0", [N], BF16), av("qm1", [N], BF16)]
        pT = [av(f"pT{i}", [N], BF16) for i in range(4)]
        o0 = av("o0", [N], F32)
        o1 = av("o1", [N], F32)
        mixout = av("mixout", [KC, N], BF16)
        mkT = av("mkT", [4, MEMT], BF16)
        mv = av("mv", [2, 512], BF16)
        wo = av("wo", [KC, 256], BF16)
        SC = 64 ** -0.5
        V = lambda fn, reads, writes: P.add("dve", fn, reads=reads, writes=writes)
        A = lambda fn, reads, writes: P.add("act", fn, reads=reads, writes=writes)
        P.add("sp", lambda e: e.dma_start(out=mkT[:].rearrange("p h m -> p (h m)"), in_=self.mk_scr.t.ap()[li]), reads=[self.mk_scr.r(li)], writes=[mkT.r()], dma=True)
        P.add("sp", lambda e: e.dma_start(out=mv[:], in_=self.mv_scr.t.ap()[li].rearrange("(t p) c -> p t c", p=128)), reads=[self.mv_scr.r(li)], writes=[mv.r()], dma=True)
        win = d["b_w_in"][lj].rearrange("(k p) c -> p k c", p=128)
        kvres = [self.kt_scr.r(bb_) for bb_ in range(blk + 1)] + [self.v_scr.r(bb_) for bb_ in range(blk + 1)]
        for qt in range(2):
            t0 = qt * N
            q0 = blk * NT + t0
            xres = [xn.r(qt)]
            P.add("sp", lambda e, q0=q0: e.dma_start(out=rC[:], in_=d["ropeC"][:, q0:q0 + N]), writes=[rC.r()], dma=True)
            P.add("sp", lambda e, q0=q0: e.dma_start(out=rS[:], in_=d["ropeS"][:, q0:q0 + N]), writes=[rS.r()], dma=True)
            nkt = (q0 + N) // 128
            for hd in range(12):
                it = qt * 12 + hd
                w = wq[it % 2]
                kt_, vv_ = KT[it % 2], VV[it % 2]
                P.add("pool", lambda e, w=w, hd=hd: e.dma_start(out=w[:], in_=win[:, :, hd * 128:(hd + 1) * 128]), writes=[w.r()], dma=True)
                P.add("sp", lambda e, kt_=kt_, hd=hd, nkt=nkt: e.dma_start(out=kt_[:, 0:nkt * 128], in_=self.kt_scr.t.ap()[hd, :, 0:nkt * 128]), reads=kvres, writes=[kt_.r()], dma=True)
                P.add("sp", lambda e, vv_=vv_, hd=hd, nkt=nkt: e.dma_start(out=vv_[:, 0:nkt, :], in_=self.v_scr.t.ap()[0:nkt * 128, hd * 128:(hd + 1) * 128].rearrange("(t p) c -> p t c", p=128)),
                      reads=kvres, writes=[vv_.r()], dma=True)
                pq = psb[0]
                for kc in range(KC):
                    P.add("pe", lambda e, w=w, kc=kc, t0=t0: e.matmul(pq[:, 0:N], lhsT=w[:, kc, :], rhs=xn[:, kc, 1 + t0:1 + t0 + N], start=(kc == 0), stop=(kc == KC - 1)),
                          reads=[w.r()] + xres, writes=[pq.r()])
                A(lambda e: e.copy(out=qf[:], in_=pq[:, 0:N]), [pq.r()], [qf.r()])
                V(lambda e: e.tensor_tensor(out=sqb[:], in0=qf[:], in1=qf[:], op=ALU.mult), [qf.r()], [sqb.r()])
                pn = psb[1]
                P.add("pe", lambda e: e.matmul(pn[:, 0:N], lhsT=self.blk64b[:], rhs=sqb[:], start=True, stop=True), reads=[sqb.r(), self.blk64b.r()], writes=[pn.r()])
                A(lambda e: e.activation(out=rz[:], in_=pn[:, 0:N], func=AF.Sqrt, scale=1.0 / 64, bias=1e-6), [pn.r()], [rz.r()])
                V(lambda e: e.reciprocal(out=rz[:], in_=rz[:]), [rz.r()], [rz.r()])
                V(lambda e: e.scalar_tensor_tensor(out=qf[:], in0=qf[:], scalar=self.pvc("b_q_norm", lj), in1=rz[:], op0=ALU.mult, op1=ALU.mult), [qf.r(), rz.r(), self.pv.r()], [qf.r()])
                V(lambda e: e.tensor_copy(out=qn[:], in_=qf[:]), [qf.r()], [qn.r()])
                P.add("pe", lambda e: e.matmul(pn[:, 0:N], lhsT=self.rotb[:], rhs=qn[:], start=True, stop=True), reads=[qn.r(), self.rotb.r()], writes=[pn.r()])
                V(lambda e: e.tensor_tensor(out=rz[:], in0=pn[:, 0:N], in1=rS[:], op=ALU.mult), [pn.r(), rS.r(), rz.r()], [rz.r()])
                V(lambda e: e.tensor_tensor(out=qf[:], in0=qf[:], in1=rC[:], op=ALU.mult), [qf.r(), rC.r()], [qf.r()])
                V(lambda e: e.tensor_tensor(out=qf[:], in0=qf[:], in1=rz[:], op=ALU.add), [qf.r(), rz.r()], [qf.r()])
                for c in range(2):
                    V(lambda e, c=c: e.tensor_scalar(out=qm[c][:], in0=qf[:], scalar1=self.c128[:, 128 + c:129 + c], scalar2=None, op0=ALU.mult), [qf.r(), self.c128.r()], [qm[c].r()])
                pO = [psb[2], psb[3]]
                pZ = [psb[4], psb[5]]
                for kt in range(nkt):
                    r = kt - (q0 // 128)
                    for c in range(2):
                        pS = psb[6 + c]
                        pt_ = pT[(kt % 2) * 2 + c]
                        P.add("pe", lambda e, kt=kt, c=c, pS=pS, kt_=kt_: e.matmul(pS[:, 0:N], lhsT=kt_[:, kt * 128:(kt + 1) * 128], rhs=qm[c][:], start=True, stop=True),
                              reads=[kt_.r(), qm[c].r()], writes=[pS.r()])
                        A(lambda e, pS=pS, pt_=pt_: e.activation(out=pt_[:], in_=pS[:, 0:N], func=AF.Exp, scale=SC), [pS.r()], [pt_.r()])
                        if r >= 0:
                            P.add("pool", lambda e, pt_=pt_, r=r: e.tensor_tensor(out=pt_[:], in0=pt_[:], in1=self.dmask[:, r * 512:(r + 1) * 512], op=ALU.mult),
                                  reads=[pt_.r(), self.dmask.r()], writes=[pt_.r()])
                        P.add("pe", lambda e, kt=kt, c=c, pt_=pt_, vv_=vv_, nkt=nkt: e.matmul(pO[c][:, 0:N], lhsT=vv_[:, kt, :], rhs=pt_[:], start=(kt == 0), stop=(kt == nkt - 1)),
                              reads=[vv_.r(), pt_.r()], writes=[pO[c].r()])
                        P.add("pe", lambda e, kt=kt, c=c, pt_=pt_, nkt=nkt: e.matmul(pZ[c][:, 0:N], lhsT=self.onesb[:], rhs=pt_[:], start=(kt == 0), stop=(kt == nkt - 1)),
                              reads=[self.onesb.r(), pt_.r()], writes=[pZ[c].r()])
                V(lambda e: e.reciprocal(out=rz[:], in_=pZ[0][:, 0:N]), [pZ[0].r()], [rz.r()])
                V(lambda e: e.tensor_tensor(out=o0[:], in0=pO[0][:, 0:N], in1=rz[:], op=ALU.mult), [pO[0].r(), rz.r()], [o0.r()])
                V(lambda e: e.reciprocal(out=rz[:], in_=pZ[1][:, 0:N]), [pZ[1].r(), o0.r()], [rz.r()])
                V(lambda e: e.tensor_tensor(out=o1[:], in0=pO[1][:, 0:N], in1=rz[:], op=ALU.mult), [pO[1].r(), rz.r()], [o1.r()])
                V(lambda e: e.scalar_tensor_tensor(out=o0[:], in0=o1[:], scalar=self.lamv[:, 2 * lj:2 * lj + 1], in1=o0[:], op0=ALU.mult, op1=ALU.add), [o0.r(), o1.r(), self.lamv.r()], [o0.r()])
                V(lambda e: e.tensor_tensor(out=sqb[:], in0=o0[:], in1=o0[:], op=ALU.mult), [o0.r()], [sqb.r()])
                P.add("pe", lambda e: e.matmul(pn[:, 0:N], lhsT=self.onesb[:], rhs=sqb[:], start=True, stop=True), reads=[sqb.r(), self.onesb.r()], writes=[pn.r()])
                A(lambda e: e.activation(out=rz[:], in_=pn[:, 0:N], func=AF.Sqrt, scale=1.0 / 128, bias=1e-5), [pn.r()], [rz.r()])
                V(lambda e: e.reciprocal(out=rz[:], in_=rz[:]), [rz.r()], [rz.r()])
                V(lambda e: e.scalar_tensor_tensor(out=o0[:], in0=o0[:], scalar=self.pvc("b_subln", lj), in1=rz[:], op0=ALU.mult, op1=ALU.mult), [o0.r(), rz.r(), self.pv.r()], [o0.r()])
                V(lambda e, hd=hd: e.tensor_scalar(out=mixout[:, hd, :], in0=o0[:], scalar1=self.lamv[:, 2 * lj + 1:2 * lj + 2], scalar2=None, op0=ALU.mult), [o0.r(), self.lamv.r()], [mixout.r()])
                import os
                if int(os.environ.get('KDBG', '99')) == 70:
                    self.dump("qf", qf); self.dump("qm0", qm[0]); self.dump("o0", o0); self.dump("o1", o1); self.dump("lamv", self.lamv); self.dump("KT", kt_, ap=kt_[:, 0:512])
                    self.dump("VV", vv_, ap=vv_[:, 0:4, :]); self.dump("mix0", mixout, ap=mixout[:, 0, :]); self.dump("pT0", pT[0]); self.dump("rz", rz)
                    return
            for h in range(4):
                it = h
                w = wq[it % 2]
                P.add("pool", lambda e, w=w, h=h: e.dma_start(out=w[:], in_=win[:, :, 1536 + h * 128:1536 + (h + 1) * 128]), writes=[w.r()], dma=True)
                pq = psb[0]
                for kc in range(KC):
                    P.add("pe", lambda e, w=w, kc=kc, t0=t0: e.matmul(pq[:, 0:N], lhsT=w[:, kc, :], rhs=xn[:, kc, 1 + t0:1 + t0 + N], start=(kc == 0), stop=(kc == KC - 1)),
                          reads=[w.r()] + xres, writes=[pq.r()])
                self.mem_attend(pq[:, 0:N], pq.r(), li, h, N, qf, qn, sqb, [pT[0], pT[1]], rz, mkT, mv, mixout[:, 12 + h, :], mixout.r())
            wsrc = d["w_out"][li].rearrange("(k p) c -> p k c", p=128)
            for dq in range(8):
                P.add("pool", lambda e, dq=dq: e.dma_start(out=wo[:], in_=wsrc[:, :, dq * 256:(dq + 1) * 256]), writes=[wo.r()], dma=True)
                for dl in range(2):
                    dc = dq * 2 + dl
                    pa = psb[1 + dc % 2]
                    for kc in range(KC):
                        P.add("pe", lambda e, pa=pa, kc=kc, dl=dl: e.matmul(pa[:, 0:N], lhsT=wo[:, kc, dl * 128:(dl + 1) * 128], rhs=mixout[:, kc, :], start=(kc == 0), stop=(kc == KC - 1)),
                              reads=[wo.r(), mixout.r()], writes=[pa.r()])
                    P.add("dve", lambda e, pa=pa, dc=dc, t0=t0: e.tensor_tensor(out=xT[:, dc, t0:t0 + N], in0=pa[:, 0:N], in1=xT[:, dc, t0:t0 + N], op=ALU.add),
                          reads=[pa.r(), xT.r(qt)], writes=[xT.r(qt)])

    def mixerB_s(self, lj):
        P, d = self.P, self.din
        psb = self.psb
        li = 2 + lj
        self.phase("mixBs")
        av = self.av
        N = 64
        NPG = 128
        V = lambda fn, reads, writes: P.add("dve", fn, reads=reads, writes=writes)
        A = lambda fn, reads, writes: P.add("act", fn, reads=reads, writes=writes)
        xst = av("xst", [KC, N], BF16)
        wq = [av("wq0", [KC, 128], BF16), av("wq1", [KC, 128], BF16)]
        qs = av("qs", [16], F32)
        qsb = av("qsb", [16], BF16)
        rz = av("rz", [N], F32)
        rcs = av("rcs", [2], F32)
        qrow = av("qrow", [128], F32)
        qbc = av("qbc", [1536], F32)
        ptb = av("ptb", [128], I32)
        ptf = av("ptf", [128], F32)
        idx = av("idx", [128], I32)
        kp = [av(f"kp{i}", [1536], F32) for i in range(3)]
        prod = av("prod", [1536], F32)
        S = av("S", [NPG + 1, 24], F32)
        mx = av("mx", [24], F32)
        mxT = av("mxT", [128], F32)
        dg = av("dg", [24], F32)
        mxb = av("mxb", [24], F32)
        o24 = av("o24", [12, 128], F32)
        od = av("od", [128], F32)
        zc = av("zc", [2], F32)
        oT = av("oT", [24], F32)
        of = av("of", [16], F32)
        sqb = av("sqb", [N], BF16)
        qf = av("qf", [N], F32)
        qn = av("qn", [N], BF16)
        pT = [av("pT0", [N], BF16), av("pT1", [N], BF16)]
        mixs = av("mixs", [KC, N], BF16)
        mkT = av("mkT", [4, MEMT], BF16)
        mv = av("mv", [2, 512], BF16)
        wo = av("wo", [KC, 256], BF16)
        P.add("sp", lambda e: e.dma_start(out=mkT[:].rearrange("p h m -> p (h m)"), in_=self.mks_scr.t.ap()[li]), reads=[self.mks_scr.r(li)], writes=[mkT.r()], dma=True)
        P.add("sp", lambda e: e.dma_start(out=mv[:], in_=self.mvs_scr.t.ap()[li].rearrange("(t p) c -> p t c", p=128)), reads=[self.mvs_scr.r(li)], writes=[mv.r()], dma=True)
        P.add("sp", lambda e: e.dma_start(out=rcs[:, 0:1], in_=d["ropeC"][:, 4096:4097], allow_slow_non_contiguous=True), writes=[rcs.r()], dma=True)
        P.add("sp", lambda e: e.dma_start(out=rcs[:, 1:2], in_=d["ropeS"][:, 4096:4097], allow_slow_non_contiguous=True), writes=[rcs.r()], dma=True)
        V(lambda e: e.memset(xst[:], 0.0), [], [xst.r()])
        V(lambda e: e.tensor_copy(out=xst[:, :, 0], in_=self.xns[:, :, 1]), [self.xns.r("cur"), xst.r()], [xst.r()])
        V(lambda e: e.memset(mixs[:], 0.0), [], [mixs.r()])
        win = d["b_w_in"][lj].rearrange("(k p) c -> p k c", p=128)
        pq = psb[0]
        for hd in range(12):
            w = wq[hd % 2]
            P.add("pool", lambda e, w=w, hd=hd: e.dma_start(out=w[:], in_=win[:, :, hd * 128:(hd + 1) * 128]), writes=[w.r()], dma=True)
            for kc in range(KC):
                P.add("pe", lambda e, w=w, kc=kc, hd=hd: e.matmul(pq[:, hd:hd + 1], lhsT=w[:, kc, :], rhs=xst[:, kc, 0:1], start=(kc == 0), stop=(kc == KC - 1)),
                      reads=[w.r(), xst.r()], writes=[pq.r()])
        A(lambda e: e.copy(out=qs[:, 0:12], in_=pq[:, 0:12]), [pq.r()], [qs.r()])
        V(lambda e: e.tensor_tensor(out=qsb[:, 0:12], in0=qs[:, 0:12], in1=qs[:, 0:12], op=ALU.mult), [qs.r()], [qsb.r()])
        pn = psb[1]
        P.add("pe", lambda e: e.matmul(pn[:, 0:12], lhsT=self.blk64b[:], rhs=qsb[:, 0:12], start=True, stop=True), reads=[qsb.r(), self.blk64b.r()], writes=[pn.r()])
        A(lambda e: e.activation(out=rz[:, 0:12], in_=pn[:, 0:12], func=AF.Sqrt, scale=1.0 / 64, bias=1e-6), [pn.r()], [rz.r()])
        V(lambda e: e.reciprocal(out=rz[:, 0:12], in_=rz[:, 0:12]), [rz.r()], [rz.r()])
        V(lambda e: e.scalar_tensor_tensor(out=qs[:, 0:12], in0=qs[:, 0:12], scalar=self.pvc("b_q_norm", lj), in1=rz[:, 0:12], op0=ALU.mult, op1=ALU.mult), [qs.r(), rz.r(), self.pv.r()], [qs.r()])
        V(lambda e: e.tensor_copy(out=qsb[:, 0:12], in_=qs[:, 0:12]), [qs.r(), qsb.r()], [qsb.r()])
        P.add("pe", lambda e: e.matmul(pn[:, 0:12], lhsT=self.rotb[:], rhs=qsb[:, 0:12], start=True, stop=True), reads=[qsb.r(), self.rotb.r()], writes=[pn.r()])
        V(lambda e: e.tensor_scalar(out=rz[:, 0:12], in0=pn[:, 0:12], scalar1=rcs[:, 1:2], scalar2=None, op0=ALU.mult), [pn.r(), rcs.r(), rz.r()], [rz.r()])
        V(lambda e: e.scalar_tensor_tensor(out=qs[:, 0:12], in0=qs[:, 0:12], scalar=rcs[:, 0:1], in1=rz[:, 0:12], op0=ALU.mult, op1=ALU.add), [qs.r(), rz.r(), rcs.r()], [qs.r()])
        V(lambda e: e.tensor_scalar(out=qs[:, 0:12], in0=qs[:, 0:12], scalar1=64 ** -0.5, scalar2=None, op0=ALU.mult), [qs.r()], [qs.r()])
        P.add("pe", lambda e: e.transpose(out=pn[0:12, 0:128], in_=qs[:, 0:12], identity=self.ident[:]), reads=[qs.r(), self.ident.r()], writes=[pn.r()])
        A(lambda e: e.copy(out=qrow[0:12, :], in_=pn[0:12, 0:128]), [pn.r()], [qrow.r()])
        P.add("sp", lambda e: e.dma_start(out=self.q_scr.t.ap()[lj], in_=qrow[0:12, :]), reads=[qrow.r()], writes=[self.q_scr.r(lj)], dma=True)
        P.add("sp", lambda e: e.dma_start(out=qbc[:], in_=self.q_scr.t.ap()[lj].rearrange("h f -> (h f)").partition_broadcast(128)), reads=[self.q_scr.r(lj)], writes=[qbc.r()], dma=True)
        P.add("sp", lambda e: e.dma_start(out=ptb[:], in_=d["pt"][0, :].partition_broadcast(128)), writes=[ptb.r()], dma=True)
        V(lambda e: e.tensor_copy(out=ptf[:], in_=ptb[:]), [ptb.r()], [ptf.r()])
        V(lambda e: e.tensor_scalar(out=ptf[:], in0=ptf[:], scalar1=128.0, scalar2=self.c128[:, 130:131], op0=ALU.mult, op1=ALU.add), [ptf.r(), self.c128.r()], [ptf.r()])
        V(lambda e: e.tensor_copy(out=idx[:], in_=ptf[:]), [ptf.r()], [idx.r()])
        q3 = qbc[:].rearrange("p (g d) -> p g d", d=64)
        for pg in range(NPG + 1):
            kb_ = kp[pg % 3]
            if pg < NPG:
                P.add("pool", lambda e, kb_=kb_, pg=pg: e.indirect_dma_start(out=kb_[:], out_offset=None, in_=d["ck"], in_offset=bass.IndirectOffsetOnAxis(ap=idx[:, pg:pg + 1], axis=0)),
                      reads=[idx.r()], writes=[kb_.r()], dma=True)
            else:
                V(lambda e, kb_=kb_: e.memset(kb_[:], 0.0), [], [kb_.r()])
                P.add("sp", lambda e, kb_=kb_: e.dma_start(out=kb_[0:1, :], in_=self.ks_scr.t.ap()), reads=[self.ks_scr.r(0), self.ks_scr.r(1), self.ks_scr.r(2), kb_.r()], writes=[kb_.r()], dma=True)
            V(lambda e, kb_=kb_: e.tensor_tensor(out=prod[:], in0=kb_[:], in1=qbc[:], op=ALU.mult), [kb_.r(), qbc.r()], [prod.r()])
            V(lambda e, pg=pg: e.tensor_reduce(out=S[:, pg, :], in_=prod[:].rearrange("p (g d) -> p g d", d=64), axis=AX.X, op=ALU.add), [prod.r()], [S.r()])
        V(lambda e: e.tensor_scalar(out=S[:, NPG, :], in0=S[:, NPG, :], scalar1=self.c128[:, 131:132], scalar2=None, op0=ALU.add), [S.r(), self.c128.r()], [S.r()])
        V(lambda e: e.tensor_reduce(out=mx[:], in_=S[:].rearrange("p g c -> p c g"), axis=AX.X, op=ALU.max), [S.r()], [mx.r()])
        P.add("pe", lambda e: e.transpose(out=pn[0:24, 0:128], in_=mx[:, 0:24], identity=self.ident[:]), reads=[mx.r(), self.ident.r()], writes=[pn.r()])
        A(lambda e: e.copy(out=mxT[0:24, :], in_=pn[0:24, 0:128]), [pn.r()], [mxT.r()])
        V(lambda e: e.tensor_reduce(out=zc[0:24, 0:1], in_=mxT[0:24, :], axis=AX.X, op=ALU.max), [mxT.r()], [zc.r()])
        V(lambda e: e.tensor_scalar(out=dg[0:24, 0:24], in0=self.ident[0:24, 0:24], scalar1=zc[0:24, 0:1], scalar2=None, op0=ALU.mult), [zc.r(), self.ident.r()], [dg.r()])
        P.add("pe", lambda e: e.matmul(pn[:, 0:24], lhsT=self.onesf[0:24, :], rhs=dg[0:24, 0:24], start=True, stop=True), reads=[dg.r(), self.onesf.r()], writes=[pn.r()])
        A(lambda e: e.copy(out=mxb[:], in_=pn[:, 0:24]), [pn.r()], [mxb.r()])
        V(lambda e: e.tensor_tensor(out=S[:], in0=S[:], in1=mxb[:].unsqueeze(1).to_broadcast([128, NPG + 1, 24]), op=ALU.subtract), [S.r(), mxb.r()], [S.r()])
        A(lambda e: e.activation(out=S[:].rearrange("p g c -> p (g c)"), in_=S[:].rearrange("p g c -> p (g c)"), func=AF.Exp), [S.r()], [S.r()])
        pO = [psb[1], psb[2], psb[3]]
        pZ = psb[4]
        for pg in range(NPG + 1):
            vb_ = kp[pg % 3]
            if pg < NPG:
                P.add("pool", lambda e, vb_=vb_, pg=pg: e.indirect_dma_start(out=vb_[:], out_offset=None, in_=d["cv"], in_offset=bass.IndirectOffsetOnAxis(ap=idx[:, pg:pg + 1], axis=0)),
                      reads=[idx.r()], writes=[vb_.r()], dma=True)
            else:
                V(lambda e, vb_=vb_: e.memset(vb_[:], 0.0), [], [vb_.r()])
                P.add("sp", lambda e, vb_=vb_: e.dma_start(out=vb_[0:1, :], in_=self.vs_scr.t.ap()), reads=[self.vs_scr.r(0), self.vs_scr.r(1), self.vs_scr.r(2), vb_.r()], writes=[vb_.r()], dma=True)
            for cb in range(3):
                P.add("pe", lambda e, vb_=vb_, pg=pg, cb=cb: e.matmul(pO[cb][0:24, :], lhsT=S[:, pg, :], rhs=vb_[:, cb * 512:(cb + 1) * 512], start=(pg == 0), stop=(pg == NPG)),
                      reads=[S.r(), vb_.r()], writes=[pO[cb].r()])
            P.add("pe", lambda e, pg=pg: e.matmul(pZ[0:24, 0:1], lhsT=S[:, pg, :], rhs=self.onesf[:, 0:1], start=(pg == 0), stop=(pg == NPG)),
                  reads=[S.r(), self.onesf.r()], writes=[pZ.r()])
        V(lambda e: e.reciprocal(out=zc[0:24, 1:2], in_=pZ[0:24, 0:1]), [pZ.r(), zc.r()], [zc.r()])
        for cb in range(3):
            A(lambda e, cb=cb: e.copy(out=o24[0:24, cb * 4:(cb + 1) * 4, :], in_=pO[cb][0:24, :].rearrange("p (h d) -> p h d", h=4)), [pO[cb].r()], [o24.r()])
        V(lambda e: e.tensor_tensor(out=o24[0:24, :, :], in0=o24[0:24, :, :], in1=self.c128[0:24, 132:144].unsqueeze(2).to_broadcast([24, 12, 128]), op=ALU.mult), [o24.r(), self.c128.r()], [o24.r()])
        V(lambda e: e.tensor_reduce(out=od[0:24, :], in_=o24[0:24, :, :].rearrange("p h d -> p d h"), axis=AX.X, op=ALU.add), [o24.r()], [od.r()])
        V(lambda e: e.tensor_scalar(out=od[0:24, :], in0=od[0:24, :], scalar1=zc[0:24, 1:2], scalar2=None, op0=ALU.mult), [od.r(), zc.r()], [od.r()])
        P.add("pe", lambda e: e.transpose(out=pn[:, 0:24], in_=od[0:24, :], identity=self.ident[0:24, 0:24]), reads=[od.r(), self.ident.r()], writes=[pn.r()])
        A(lambda e: e.copy(out=oT[:], in_=pn[:, 0:24]), [pn.r()], [oT.r()])
        o3 = oT[:].rearrange("p (h c) -> p h c", c=2)
        V(lambda e: e.scalar_tensor_tensor(out=of[:, 0:12], in0=o3[:, :, 1], scalar=self.lamv[:, 2 * lj:2 * lj + 1], in1=o3[:, :, 0], op0=ALU.mult, op1=ALU.add), [oT.r(), self.lamv.r()], [of.r()])
        V(lambda e: e.tensor_tensor(out=qsb[:, 0:12], in0=of[:, 0:12], in1=of[:, 0:12], op=ALU.mult), [of.r(), qsb.r()], [qsb.r()])
        P.add("pe", lambda e: e.matmul(pn[:, 0:12], lhsT=self.onesb[:], rhs=qsb[:, 0:12], start=True, stop=True), reads=[qsb.r(), self.onesb.r()], writes=[pn.r()])
        A(lambda e: e.activation(out=rz[:, 0:12], in_=pn[:, 0:12], func=AF.Sqrt, scale=1.0 / 128, bias=1e-5), [pn.r()], [rz.r()])
        V(lambda e: e.reciprocal(out=rz[:, 0:12], in_=rz[:, 0:12]), [rz.r()], [rz.r()])
        V(lambda e: e.scalar_tensor_tensor(out=of[:, 0:12], in0=of[:, 0:12], scalar=self.pvc("b_subln", lj), in1=rz[:, 0:12], op0=ALU.mult, op1=ALU.mult), [of.r(), rz.r(), self.pv.r()], [of.r()])
        V(lambda e: e.tensor_scalar(out=mixs[:, 0:12, 0], in0=of[:, 0:12], scalar1=self.lamv[:, 2 * lj + 1:2 * lj + 2], scalar2=None, op0=ALU.mult), [of.r(), self.lamv.r(), mixs.r()], [mixs.r()])
        for h in range(4):
            w = wq[h % 2]
            P.add("pool", lambda e, w=w, h=h: e.dma_start(out=w[:], in_=win[:, :, 1536 + h * 128:1536 + (h + 1) * 128]), writes=[w.r()], dma=True)
            for kc in range(KC):
                P.add("pe", lambda e, w=w, kc=kc: e.matmul(pq[:, 0:N], lhsT=w[:, kc, :], rhs=xst[:, kc, :], start=(kc == 0), stop=(kc == KC - 1)), reads=[w.r(), xst.r()], writes=[pq.r()])
            self.mem_attend(pq[:, 0:N], pq.r(), li, h, N, qf, qn, sqb, pT, rz, mkT, mv, mixs[:, 12 + h, :], mixs.r())
        wsrc = d["w_out"][li].rearrange("(k p) c -> p k c", p=128)
        for dq in range(8):
            P.add("pool", lambda e, dq=dq: e.dma_start(out=wo[:], in_=wsrc[:, :, dq * 256:(dq + 1) * 256]), writes=[wo.r()], dma=True)
            for dl in range(2):
                dc = dq * 2 + dl
                pa = psb[5 + dc % 2]
                for kc in range(KC):
                    P.add("pe", lambda e, pa=pa, kc=kc, dl=dl: e.matmul(pa[:, 0:N], lhsT=wo[:, kc, dl * 128:(dl + 1) * 128], rhs=mixs[:, kc, :], start=(kc == 0), stop=(kc == KC - 1)),
                          reads=[wo.r(), mixs.r()], writes=[pa.r()])
                V(lambda e, pa=pa, dc=dc: e.tensor_tensor(out=self.xsT[:, dc:dc + 1], in0=pa[:, 0:1], in1=self.xsT[:, dc:dc + 1], op=ALU.add), [pa.r(), self.xsT.r()], [self.xsT.r()])

    def dump(self, name, buf, ap=None):
        ap = buf[:] if ap is None else ap
        shape = list(ap.shape)
        dt_ = ap.dtype
        t = self.nc.dram_tensor("dbg_" + name, shape, dt_, kind="ExternalOutput").ap()
        rs = list(buf._res.values()) or [buf.r()]
        self.P.add("sp", lambda e: e.dma_start(out=t, in_=ap), reads=rs, dma=True)

    def store_wkv(self, li):
        P, o = self.P, self.dout
        self.phase("wkv")
        so = self.av("so", [12, 128], F32, parts=64)
        for gi in range(12):
            pb = self.psb[gi % 2]
            P.add("pe", lambda e, gi=gi, pb=pb: e.transpose(out=pb[0:64, 0:128], in_=self.Sf[:, li, gi, :], identity=self.ident[:]),
                  reads=[self.Sf.r((li, gi)), self.Sf.r(), self.ident.r()], writes=[pb.r()])
            P.add("act", lambda e, gi=gi, pb=pb: e.copy(out=so[:, gi, :], in_=pb[0:64, 0:128]), reads=[pb.r()], writes=[so.r()])
        P.add("sp", lambda e: e.dma_start(out=o["wkvp"][li].rearrange("(g h) i j -> i g h j", h=2), in_=so[:].rearrange("p g (h j) -> p g h j", h=2)), reads=[so.r()], dma=True)


def build_program(pvoff, npv, stage):
    kb = KB(pvoff, npv, stage)
    P = kb.P
    kb.setup()
    kb.mem_all()
    kb.mem_sample()
    for blk in range(NBLK):
        last = blk == NBLK - 1
        sample = last
        kb.load_x(blk)
        for li in range(2):
            kb.ffn(li, 0, sample)
            kb.phase("norm")
            kb.rmsnorm("mix_norm", li, sample, last_out=kb.xlast if last else None, shift_li=li)
            if last:
                kb.store_col(kb.xlast, kb.dout["shp"][li], kb.xlast.r())
                kb.store_col(kb.xslast, kb.dout["shs"][li], kb.xslast.r())
            kb.mixerA(li, blk, False)
            if last:
                kb.store_wkv(li)
                kb.mixerA(li, blk, False, smp=True)
            kb.ffn(li, 1, sample)
        kb.shared_kv(blk, sample)
        for lj in range(2):
            kb.ffn(2 + lj, 0, sample)
            kb.phase("norm")
            kb.rmsnorm("mix_norm", 2 + lj, sample)
            kb.mixerB(lj, blk, False)
            if last:
                kb.mixerB_s(lj)
            kb.ffn(2 + lj, 1, sample)
        kb.store_x(blk)
    P.barrier()
    P.emit()
    P.close()
    return kb.nc


STAGE = 3


def kernel(**inp):
    inp = {k: np.asarray(v) for k, v in inp.items()}
    pvp = pack_params(inp)
    pv = pvp.build()
    nc = build_program(pvp.off, pvp.n, STAGE)
    hc = host_consts()
    w13 = inp["ffn_w13"].reshape(8, D, 2 * DFF)
    w2 = inp["ffn_w2"].reshape(8, DFF, D)
    shared = dict(
        pv=pv, ident=hc["ident"], c64=hc["c64"], c128=hc["c128"], ffn_w13=w13, ffn_w2=w2,
        mem_w_kv=inp["mem_w_kv"], mem_k_norm=inp["mem_k_norm"],
        a_w_in=inp["a_w_in"], a_w1=inp["a_w1"], a_w2=inp["a_w2"], a_a1=inp["a_a1"], a_a2=inp["a_a2"],
        a_g1=inp["a_g1"], a_g2=inp["a_g2"], a_lnx_w=inp["a_lnx_w"], a_lnx_b=inp["a_lnx_b"], w_out=inp["w_out"],
        kv_w=inp["kv_w"], k_norm=inp["k_norm"].reshape(1, 64), b_w_in=inp["b_w_in"], b_lam=inp["b_lam"].reshape(2, 256),
        cs_tok=hc["cs_tok"], ropeC=hc["ropeC"], ropeS=hc["ropeS"], rot=hc["rot"], dmask=hc["dmask"],
        ck=inp["cache_k"].reshape(1280 * 128, 1536), cv=inp["cache_v"].reshape(1280 * 128, 1536),
    )
    xps = [np.ascontiguousarray(inp["x_prompt"][b]) for b in range(2)]
    mps = [np.ascontiguousarray(inp["mem_prompt"][b]) for b in range(2)]
    in_maps = []
    for c in range(NCORE):
        b = c % 2
        m = dict(shared)
        m.update(xp=xps[b], xs=np.ascontiguousarray(inp["x_sample"][c, 0].reshape(KC, 128)), memp=mps[b],
                 stw=np.ascontiguousarray(inp["state_wkv"][:, c]), sts=np.ascontiguousarray(inp["state_shift"][:, c].reshape(2, KC, 128)),
                 cmk=np.ascontiguousarray(inp["cache_mem_k"][:, c].reshape(4, MEMT, 512)), cmv=np.ascontiguousarray(inp["cache_mem_v"][:, c].reshape(4, MEMT, 512)),
                 pt=np.ascontiguousarray(inp["page_table"][c].reshape(1, 128).astype(np.int32)))
        in_maps.append(m)
    res = run_bass_kernel_spmd(nc, in_maps, core_ids=list(range(NCORE)))
    R = res.results
    f32 = np.float32
    y_prompt = np.stack([R[b]["yp"] for b in range(2)]).astype(f32)
    y_sample = np.stack([R[c]["ys"].reshape(1, D) for c in range(8)]).astype(f32)
    wkv_prompt = np.stack([np.stack([R[b]["wkvp"][l] for b in range(2)]) for l in range(2)]).astype(f32)
    shift_prompt = np.stack([np.stack([R[b]["shp"][l].reshape(D) for b in range(2)]) for l in range(2)]).astype(f32)
    wkv_sample = np.stack([np.stack([R[c]["wkvs"][l] for c in range(8)]) for l in range(2)]).astype(f32)
    shift_sample = np.stack([np.stack([R[c]["shs"][l].reshape(D) for c in range(8)]) for l in range(2)]).astype(f32)
    k_rows_p = np.stack([R[b]["krp"].reshape(4096, 12, 2, 64) for b in range(2)]).astype(f32)
    v_rows_p = np.stack([R[b]["vrp"].reshape(4096, 12, 128) for b in range(2)]).astype(f32)
    k_rows_s = np.stack([R[c]["krs"].reshape(1, 12, 2, 64) for c in range(8)]).astype(f32)
    v_rows_s = np.stack([R[c]["vrs"].reshape(1, 12, 128) for c in range(8)]).astype(f32)
    mem_k = np.stack([np.stack([R[b]["mkp"][l].reshape(MEMT, 4, 128) for b in range(2)]) for l in range(4)]).astype(f32)
    mem_v = np.stack([np.stack([R[b]["mvp"][l].reshape(MEMT, 4, 128) for b in range(2)]) for l in range(4)]).astype(f32)
    return (y_prompt, y_sample, wkv_prompt, shift_prompt, wkv_sample, shift_sample,
            k_rows_p, v_rows_p, k_rows_s, v_rows_s, mem_k, mem_v)
```

```python
import numpy as np
from contextlib import ExitStack
import concourse.bass as bass
import concourse.mybir as mybir

F32 = mybir.dt.float32
BF16 = mybir.dt.bfloat16
I32 = mybir.dt.int32
AF = mybir.ActivationFunctionType
ALU = mybir.AluOpType
AX = mybir.AxisListType

ENGS = ("pe", "act", "dve", "pool", "sp")
DMA_POOL = {"sp": 24, "pool": 24, "act": 8}


class Res:
    __slots__ = ("name", "w", "rs")

    def __init__(self, name):
        self.name = name
        self.w = None
        self.rs = {}


class Buf:
    def __init__(self, name, t):
        self.name = name
        self.t = t
        self._res = {}

    def r(self, key=None):
        x = self._res.get(key)
        if x is None:
            x = Res(f"{self.name}[{key}]")
            self._res[key] = x
        return x

    def __getitem__(self, idx):
        return self.t[idx]


class Prog:
    def __init__(self, nc):
        self.nc = nc
        self.es = ExitStack()
        self.ops = {e: [] for e in ENGS}
        self.cnt = {e: 0 for e in ENGS}
        self.dcnt = {e: 0 for e in DMA_POOL}
        self.waited = {e: {} for e in ENGS}
        self.sems = {}
        self.final = {}
        self.nbuf = 0
        self.bar = {}

    def sb(self, name, shape, dt=F32):
        t = self.es.enter_context(self.nc.sbuf_tensor(name, list(shape), dt))
        return Buf(name, t)

    def ps(self, name, shape, dt=F32):
        t = self.es.enter_context(self.nc.psum_tensor(name, list(shape), dt))
        return Buf(name, t)

    def dram(self, name, shape, dt=F32, kind="Internal"):
        t = self.nc.dram_tensor(name, list(shape), dt, kind=kind)
        return Buf(name, t)

    def _sem(self, key):
        s = self.sems.get(key)
        if s is None:
            nm = "s_" + "_".join(str(k) for k in key) if isinstance(key, tuple) else "s_" + str(key)
            s = self.es.enter_context(self.nc.semaphore(nm))
            self.sems[key] = s
        return s

    def add(self, eng, fn, reads=(), writes=(), dma=False, inc=None):
        deps = {}

        def need(k, v):
            if deps.get(k, 0) < v:
                deps[k] = v

        for k, v in self.bar.items():
            need(k, v)
        for r in reads:
            if r.w is not None:
                need(*r.w)
        for r in writes:
            if r.w is not None:
                need(*r.w)
            for k, v in r.rs.items():
                need(k, v)
        if dma:
            n = self.dcnt[eng]
            self.dcnt[eng] = n + 1
            P = DMA_POOL[eng]
            key = ("d", eng, n % P)
            val = 16 * (n // P + 1)
            if n >= P:
                need(key, val - 16)
            tag = (key, val)
            self.final[key] = val
        else:
            self.cnt[eng] += 1
            key = ("e", eng)
            tag = (key, self.cnt[eng])
        wl = []
        wd = self.waited[eng]
        for k, v in deps.items():
            if k == ("e", "pe") and eng == "pe" and not dma:
                continue
            if wd.get(k, 0) >= v:
                continue
            wd[k] = v
            wl.append((k, v))
        for r in writes:
            r.w = tag
            r.rs = {}
        for r in reads:
            if r.rs.get(tag[0], 0) < tag[1]:
                r.rs[tag[0]] = tag[1]
        self.ops[eng].append((fn, wl, tag, dma))
        return tag

    def barrier(self):
        for e in ENGS:
            if self.cnt[e]:
                self.bar[("e", e)] = self.cnt[e]
        for k, v in self.final.items():
            self.bar[k] = v

    def view(self, name, ap):
        return Buf(name, ap)

    def emit(self):
        nc = self.nc
        for k in list(self.final):
            self._sem(k)
        for e in ENGS:
            self._sem(("e", e))
        for e in ENGS:
            for (_, wl, tag, _) in self.ops[e]:
                for k, _ in wl:
                    self._sem(k)
        engobj = {"pe": "tensor", "act": "scalar", "dve": "vector", "pool": "gpsimd", "sp": "sync"}
        with nc.Block() as block:
            for e in ENGS:
                ops = self.ops[e]
                finals = dict(self.final) if e == "sp" else {}

                def body(eng, ops=ops, e=e, finals=finals):
                    for (fn, wl, tag, dma) in ops:
                        for k, v in wl:
                            eng.wait_ge(self.sems[k], v)
                        ins = fn(eng)
                        if dma:
                            ins.then_inc(self.sems[tag[0]], 16)
                        else:
                            ins.then_inc(self.sems[tag[0]], 1)
                    for k, v in finals.items():
                        eng.wait_ge(self.sems[k], v)

                getattr(block, engobj[e])(body)

    def close(self):
        self.es.close()


from concourse.bass_utils import run_bass_kernel_spmd

D = 2048
KC = 16
DFF = 5632
NT = 1024
NCORE = 8
MEMT = 256
NBLK = 4
TA = 256
CH = 64
ARENA = 90112


def fm(v):
    v = np.asarray(v, dtype=np.float32)
    lead = v.shape[:-1]
    n = v.shape[-1] // 128
    w = v.reshape(lead + (n, 128))
    w = np.moveaxis(w, -1, 0)
    return np.ascontiguousarray(w.reshape(128, -1))


class PV:
    def __init__(self):
        self.cols, self.off, self.n = [], {}, 0

    def add(self, name, arr):
        arr = np.asarray(arr, dtype=np.float32)
        assert arr.shape[0] == 128
        self.off[name] = (self.n, arr.shape[1])
        self.cols.append(arr)
        self.n += arr.shape[1]

    def build(self):
        return np.ascontiguousarray(np.concatenate(self.cols, axis=1))


def pack_params(inp):
    pv = PV()
    pv.add("ffn_norm", fm(inp["ffn_norm"].reshape(8, D)))
    pv.add("mix_norm", fm(inp["mix_norm"]))
    pv.add("mem_norm", fm(inp["mem_norm"]))
    pv.add("kv_norm", fm(inp["kv_norm"].reshape(1, D)))
    pv.add("mem_q_norm", fm(inp["mem_q_norm"]))
    pv.add("a_mu", fm(inp["a_mu"].reshape(6, D)))
    for nm in ("a_w0", "a_a0", "a_k_k", "a_k_a"):
        pv.add(nm, fm(inp[nm]))
    pv.add("a_r_k", fm(inp["a_r_k"].reshape(2, 1536)))
    pv.add("a_lnx_w", fm(inp["a_lnx_w"]))
    pv.add("a_lnx_b", fm(inp["a_lnx_b"]))
    pv.add("b_q_norm", np.ascontiguousarray(np.tile(np.asarray(inp["b_q_norm"], np.float32), (1, 2)).T))
    pv.add("b_subln", np.ascontiguousarray(np.asarray(inp["b_subln"], np.float32).T))
    return pv


def host_consts():
    c = {}
    c["ident"] = np.eye(128, dtype=np.float32)
    su = np.triu(np.ones((64, 64), np.float32), 1)
    iu = np.triu(np.ones((64, 64), np.float32), 0)
    m4 = np.tile(np.concatenate([su, iu], 1), (1, 4))
    sl8 = np.tile(su.T, (1, 8))
    i8 = np.tile(np.eye(64, dtype=np.float32), (1, 8))
    c["c64"] = np.ascontiguousarray(np.concatenate([m4, sl8, i8], 1))
    blk = np.zeros((128, 128), np.float32)
    blk[:64, :64] = 1
    blk[64:, 64:] = 1
    hs = np.zeros((128, 2), np.float32)
    hs[:64, 0] = 1
    hs[64:, 1] = 1
    pidx = np.arange(128, dtype=np.float32)[:, None]
    negm = np.full((128, 1), -1e30, np.float32)
    negm[0, 0] = 0.0
    bm = np.zeros((128, 12), np.float32)
    for r_ in range(24):
        bm[r_, r_ // 2] = 1.0
    c["c128"] = np.ascontiguousarray(np.concatenate([blk, hs, pidx, negm, bm], 1))
    pos = np.concatenate([np.arange(4096), [16384]]).astype(np.float32)
    inv = (np.float32(500000.0) ** (-np.arange(0, 16, 2, dtype=np.float32) / np.float32(16))).astype(np.float32)
    ang = (pos[:, None] * inv[None, :]).astype(np.float32)
    cs_, sn_ = np.cos(ang).astype(np.float32), np.sin(ang).astype(np.float32)
    c["cs_tok"] = np.ascontiguousarray(np.concatenate([cs_, sn_], 1))
    C = np.ones((128, 4097), np.float32)
    S = np.zeros((128, 4097), np.float32)
    rot = np.zeros((128, 128), np.float32)
    for p in range(128):
        dd = p % 64
        if dd < 8:
            C[p] = cs_[:, dd]; S[p] = -sn_[:, dd]; rot[p + 8, p] = 1
        elif dd < 16:
            C[p] = cs_[:, dd - 8]; S[p] = sn_[:, dd - 8]; rot[p - 8, p] = 1
    c["ropeC"], c["ropeS"], c["rot"] = C, S, rot
    dm = np.zeros((128, 4, 512), np.float32)
    for r in range(4):
        dm[:, r, :] = (np.arange(128)[:, None] + 128 * r <= np.arange(512)[None, :])
    c["dmask"] = np.ascontiguousarray(dm.reshape(128, 2048))
    return c


class KB:
    def __init__(self, pvoff, npv, stage):
        self.nc = nc = bass.Bass("TRN2", target_bir_lowering=False)
        self.P = P = Prog(nc)
        self.pvoff = pvoff
        self.stage = stage
        dt_in = lambda name, shape, dt=F32: nc.dram_tensor(name, list(shape), dt, kind="ExternalInput").ap()
        dt_out = lambda name, shape, dt=F32: nc.dram_tensor(name, list(shape), dt, kind="ExternalOutput").ap()
        self.din = dict(
            xp=dt_in("xp", [NBLK * NT, D]), xs=dt_in("xs", [KC, 128]), memp=dt_in("memp", [MEMT, D]),
            pv=dt_in("pv", [128, npv]), ident=dt_in("ident", [128, 128]), c64=dt_in("c64", [64, 1536]),
            c128=dt_in("c128", [128, 144]),
            ffn_w13=dt_in("ffn_w13", [8, D, 2 * DFF]), ffn_w2=dt_in("ffn_w2", [8, DFF, D]),
            mem_w_kv=dt_in("mem_w_kv", [4, D, 1024]), mem_k_norm=dt_in("mem_k_norm", [4, 128]),
            a_w_in=dt_in("a_w_in", [2, 2 * D, 5120]), a_w1=dt_in("a_w1", [2, D, 96]), a_w2=dt_in("a_w2", [2, 96, 1536]),
            a_a1=dt_in("a_a1", [2, D, 96]), a_a2=dt_in("a_a2", [2, 96, 1536]),
            a_g1=dt_in("a_g1", [2, D, 256]), a_g2=dt_in("a_g2", [2, 256, 1536]),
            a_lnx_w=dt_in("a_lnx_w", [2, 1536]), a_lnx_b=dt_in("a_lnx_b", [2, 1536]),
            w_out=dt_in("w_out", [4, D, D]),
            kv_w=dt_in("kv_w", [D, 3072]), k_norm=dt_in("k_norm", [1, 64]), b_w_in=dt_in("b_w_in", [2, D, D]), b_lam=dt_in("b_lam", [2, 256]),
            cs_tok=dt_in("cs_tok", [4097, 16]), ropeC=dt_in("ropeC", [128, 4097]), ropeS=dt_in("ropeS", [128, 4097]),
            rot=dt_in("rot", [128, 128]), dmask=dt_in("dmask", [128, 2048]),
            stw=dt_in("stw", [2, 24, 64, 64]), sts=dt_in("sts", [2, KC, 128]),
            cmk=dt_in("cmk", [4, MEMT, 512]), cmv=dt_in("cmv", [4, MEMT, 512]),
            ck=dt_in("ck", [1280 * 128, 1536]), cv=dt_in("cv", [1280 * 128, 1536]), pt=dt_in("pt", [1, 128], I32),
        )
        self.dout = dict(
            yp=dt_out("yp", [NBLK * NT, D]), ys=dt_out("ys", [KC, 128]),
            shp=dt_out("shp", [2, KC, 128]), shs=dt_out("shs", [2, KC, 128]),
            mkp=dt_out("mkp", [4, MEMT, 512]), mvp=dt_out("mvp", [4, MEMT, 512]),
            wkvp=dt_out("wkvp", [2, 24, 64, 64]),
            krp=dt_out("krp", [NBLK * NT, 1536]), vrp=dt_out("vrp", [NBLK * NT, 1536]),
            krs=dt_out("krs", [1, 1536]), vrs=dt_out("vrs", [1, 1536]),
            wkvs=dt_out("wkvs", [2, 24, 64, 64]),
        )
        self.mk_scr = P.dram("mk_scr", [4, 128, 4 * MEMT], BF16)
        self.mv_scr = P.dram("mv_scr", [4, MEMT, 512], BF16)
        self.mks_scr = P.dram("mks_scr", [4, 128, 4 * MEMT], BF16)
        self.mvs_scr = P.dram("mvs_scr", [4, MEMT, 512], BF16)
        self.ks_scr = P.dram("ks_scr", [1, 1536], F32)
        self.vs_scr = P.dram("vs_scr", [1, 1536], F32)
        self.q_scr = P.dram("q_scr", [2, 12, 128], F32)
        self.kt_scr = P.dram("kt_scr", [12, 128, NBLK * NT], BF16)
        self.v_scr = P.dram("v_scr", [NBLK * NT, 1536], BF16)
        self.xT = P.sb("xT", [128, KC, NT], F32)
        self.xsT = P.sb("xsT", [128, KC], F32)
        self.xn = P.sb("xn", [128, KC, NT + 1], BF16)
        self.xns = P.sb("xns", [128, KC, 2], BF16)
        self.pv = P.sb("pv_sb", [128, npv], F32)
        self.omu = P.sb("omu", [128, 96], F32)
        self.neg = P.sb("negp", [128, 48], F32)
        self.ident = P.sb("ident_sb", [128, 128], F32)
        self.identb = P.sb("identb", [128, 128], BF16)
        self.onesb = P.sb("onesb", [128, 128], BF16)
        self.c128 = P.sb("c128_sb", [128, 144], F32)
        self.onesf = P.sb("onesf", [128, 128], F32)
        self.blk64b = P.sb("blk64b", [128, 128], BF16)
        self.c64 = P.sb("c64_sb", [64, 1536], BF16)
        self.rmask = P.sb("rmask", [128, TA], F32)
        self.prevcol = P.sb("prevcol", [128, 2, KC], BF16)
        self.xlast = P.sb("xlast", [128, KC], F32)
        self.xslast = P.sb("xslast", [128, KC], F32)
        self.Sf = P.sb("Sf", [128, 2, 12, 64], F32)
        self.small = P.sb("small", [128, 64], F32)
        self.rotb = P.sb("rotb", [128, 128], BF16)
        self.dmask = P.sb("dmask_sb", [128, 2048], BF16)
        self.lamv = P.sb("lamv", [128, 4], F32)
        self.colst = P.sb("colst", [KC, 128], F32)
        self.psb = [P.ps(f"ps{i}", [128, 512], F32) for i in range(8)]
        self.arena = P.sb("arena", [128, ARENA // 2], BF16)
        self.aoff = 0
        self.cur = {}
        self.wi = 0

    def phase(self, name):
        self.P.barrier()
        self.aoff = 0
        self.cur = {}
        self.pname = name

    def av(self, name, shape, dt=F32, parts=128):
        esz = 4 if dt in (F32, I32) else 2
        n = int(np.prod(shape))
        nbytes = ((n * esz + 63) // 64) * 64
        assert self.aoff + nbytes <= ARENA, (self.pname, name, self.aoff, nbytes)
        o2 = self.aoff // 2
        ap = self.arena.t[0:parts, o2:o2 + (n * esz) // 2]
        if dt in (F32, I32):
            ap = ap.bitcast(dt)
        if len(shape) == 2:
            ap = ap.rearrange("p (a b) -> p a b", a=shape[0])
        elif len(shape) == 3:
            ap = ap.rearrange("p (a b c) -> p a b c", a=shape[0], b=shape[1])
        self.aoff += nbytes
        b = self.P.view(f"{self.pname}.{name}", ap)
        self.cur[name] = b
        return b

    def pvc(self, name, i=0, n=1):
        o, w = self.pvoff[name]
        return self.pv[:, o + i:o + i + n]

    def psbf(self, i):
        return self.psb[i].t[:].bitcast(BF16)

    def setup(self):
        P, d = self.P, self.din
        P.add("sp", lambda e: e.dma_start(out=self.ident[:], in_=d["ident"]), writes=[self.ident.r()], dma=True)
        P.add("sp", lambda e: e.dma_start(out=self.pv[:], in_=d["pv"]), writes=[self.pv.r()], dma=True)
        P.add("pool", lambda e: e.dma_start(out=self.c64[:], in_=d["c64"]), writes=[self.c64.r()], dma=True)
        P.add("sp", lambda e: e.dma_start(out=self.c128[:], in_=d["c128"]), writes=[self.c128.r()], dma=True)
        P.add("dve", lambda e: e.memset(self.onesb[:], 1.0), writes=[self.onesb.r()])
        P.add("dve", lambda e: e.memset(self.onesf[:], 1.0), writes=[self.onesf.r()])
        P.add("dve", lambda e: e.tensor_copy(out=self.identb[:], in_=self.ident[:]), reads=[self.ident.r()], writes=[self.identb.r()])
        P.add("dve", lambda e: e.tensor_copy(out=self.blk64b[:], in_=self.c128[:, 0:128]), reads=[self.c128.r()], writes=[self.blk64b.r()])
        P.add("dve", lambda e: e.memset(self.prevcol[:], 0.0), writes=[self.prevcol.r()])
        P.add("dve", lambda e: e.memset(self.Sf[:], 0.0), writes=[self.Sf.r()])
        P.add("dve", lambda e: e.memset(self.rmask[:], 1.0), writes=[self.rmask.r()])
        P.add("dve", lambda e: e.memset(self.rmask[:].rearrange("p (c t) -> p c t", t=CH)[:, :, 0:1], 0.0), reads=[self.rmask.r()], writes=[self.rmask.r()])
        P.add("pool", lambda e: e.dma_start(out=self.rotb[:], in_=d["rot"]), writes=[self.rotb.r()], dma=True)
        P.add("pool", lambda e: e.dma_start(out=self.dmask[:], in_=d["dmask"]), writes=[self.dmask.r()], dma=True)
        self.lam_setup()
        mo = self.pvoff["a_mu"][0]
        P.add("dve", lambda e: e.tensor_scalar(out=self.omu[:], in0=self.pv[:, mo:mo + 96], scalar1=-1.0, scalar2=1.0, op0=ALU.mult, op1=ALU.add),
              reads=[self.pv.r()], writes=[self.omu.r()])
        wo = self.pvoff["a_w0"][0]
        ko = self.pvoff["a_k_a"][0]
        P.add("dve", lambda e: e.tensor_scalar(out=self.neg[:, 0:24], in0=self.pv[:, wo:wo + 24], scalar1=-1.0, scalar2=None, op0=ALU.mult),
              reads=[self.pv.r()], writes=[self.neg.r()])
        P.add("dve", lambda e: e.tensor_scalar(out=self.neg[:, 24:48], in0=self.pv[:, ko:ko + 24], scalar1=-1.0, scalar2=1.0, op0=ALU.mult, op1=ALU.add),
              reads=[self.pv.r()], writes=[self.neg.r()])

    def load_x(self, blk):
        P, d = self.P, self.din
        xT, psb = self.xT, self.psb
        self.phase("io")
        xin = [self.av(f"xin{i}", [D], F32) for i in range(2)]
        for tb in range(NT // 128):
            xi = xin[tb % 2]
            r0 = blk * NT + tb * 128
            P.add("sp", lambda e, xi=xi, r0=r0: e.dma_start(out=xi[:], in_=d["xp"][r0:r0 + 128, :]), writes=[xi.r()], dma=True)
            for k4 in range(4):
                pb = psb[(tb * 4 + k4) % 2]
                for j in range(4):
                    kc = k4 * 4 + j
                    P.add("pe", lambda e, pb=pb, xi=xi, j=j, kc=kc: e.transpose(
                        out=pb[:, j * 128:(j + 1) * 128], in_=xi[:, kc * 128:(kc + 1) * 128], identity=self.ident[:]),
                        reads=[xi.r(), self.ident.r()], writes=[pb.r()])
                P.add("act", lambda e, pb=pb, k4=k4, tb=tb: e.copy(
                    out=xT[:, k4 * 4:(k4 + 1) * 4, tb * 128:(tb + 1) * 128],
                    in_=pb[:].rearrange("p (a b) -> p a b", a=4)),
                    reads=[pb.r()], writes=[xT.r(tb // 4)])
        if blk == 0:
            xi = xin[0]
            P.add("sp", lambda e: e.dma_start(out=xi[0:KC, 0:128], in_=d["xs"]), writes=[xi.r()], dma=True)
            pb = psb[2]
            P.add("pe", lambda e: e.transpose(out=pb[:, 0:KC], in_=xi[0:KC, 0:128], identity=self.ident[0:KC, 0:KC]),
                  reads=[xi.r(), self.ident.r()], writes=[pb.r()])
            P.add("act", lambda e: e.copy(out=self.xsT[:], in_=pb[:, 0:KC]), reads=[pb.r()], writes=[self.xsT.r()])

    def store_x(self, blk):
        P, o = self.P, self.dout
        xT, psb = self.xT, self.psb
        self.phase("io")
        xin = [self.av(f"xin{i}", [D], F32) for i in range(2)]
        for tb in range(NT // 128):
            xi = xin[tb % 2]
            for k4 in range(4):
                pb = psb[(tb * 4 + k4) % 2]
                for j in range(4):
                    kc = k4 * 4 + j
                    P.add("pe", lambda e, pb=pb, j=j, kc=kc, tb=tb: e.transpose(
                        out=pb[:, j * 128:(j + 1) * 128], in_=xT[:, kc, tb * 128:(tb + 1) * 128], identity=self.ident[:]),
                        reads=[xT.r(tb // 4), self.ident.r()], writes=[pb.r()])
                P.add("act", lambda e, pb=pb, k4=k4, xi=xi: e.copy(out=xi[:, k4 * 512:(k4 + 1) * 512], in_=pb[:]),
                      reads=[pb.r()], writes=[xi.r()])
            r0 = blk * NT + tb * 128
            P.add("sp", lambda e, xi=xi, r0=r0: e.dma_start(out=o["yp"][r0:r0 + 128, :], in_=xi[:]), reads=[xi.r()], dma=True)
        if blk == NBLK - 1:
            self.store_col(self.xsT, o["ys"], self.xsT.r())

    def store_col(self, src, dst, res):
        P = self.P
        pb = self.psb[2]
        cs = self.colst
        P.add("pe", lambda e: e.transpose(out=pb[0:KC, 0:128], in_=src[:, 0:KC], identity=self.ident[:]),
              reads=[res, self.ident.r()], writes=[pb.r()])
        P.add("act", lambda e: e.copy(out=cs[0:KC, 0:128], in_=pb[0:KC, 0:128]), reads=[pb.r()], writes=[cs.r()])
        P.add("sp", lambda e: e.dma_start(out=dst, in_=cs[0:KC, 0:128]), reads=[cs.r()], writes=[cs.r("o")], dma=True)

    def tiles(self):
        return [(0, 512), (512, 512)]

    def rmsnorm(self, gname, gi, sample, last_out=None, shift_li=None):
        P = self.P
        xT, xn, psb = self.xT, self.xn, self.psb
        sq = [self.cur.get("sq0") or self.av("sq0", [512], BF16), self.cur.get("sq1") or self.av("sq1", [512], BF16)]
        rstd = self.cur.get("rstd") or self.av("rstd", [512], F32)
        if shift_li is not None:
            P.add("dve", lambda e: e.tensor_copy(out=xn[:, :, 0], in_=self.prevcol[:, shift_li, :]),
                  reads=[self.prevcol.r()], writes=[xn.r("prev")])
        for ti, (t0, n) in enumerate(self.tiles()):
            pb = psb[7]
            for kc in range(KC):
                s = sq[kc % 2]
                P.add("act", lambda e, s=s, kc=kc, t0=t0, n=n: e.activation(out=s[:, :n], in_=xT[:, kc, t0:t0 + n], func=AF.Square),
                      reads=[xT.r(ti)], writes=[s.r()])
                P.add("pe", lambda e, s=s, kc=kc, n=n, pb=pb: e.matmul(pb[:, :n], lhsT=self.onesb[:], rhs=s[:, :n], start=(kc == 0), stop=(kc == KC - 1)),
                      reads=[s.r(), self.onesb.r()], writes=[pb.r()])
            P.add("act", lambda e, pb=pb, n=n: e.activation(out=rstd[:, :n], in_=pb[:, :n], func=AF.Sqrt, scale=1.0 / D, bias=1e-6),
                  reads=[pb.r()], writes=[rstd.r()])
            P.add("dve", lambda e, n=n: e.reciprocal(out=rstd[:, :n], in_=rstd[:, :n]), reads=[rstd.r()], writes=[rstd.r()])
            for kc in range(KC):
                P.add("dve", lambda e, kc=kc, t0=t0, n=n: e.scalar_tensor_tensor(
                    out=xn[:, kc, 1 + t0:1 + t0 + n], in0=xT[:, kc, t0:t0 + n], scalar=self.pvc(gname, gi * KC + kc), in1=rstd[:, :n],
                    op0=ALU.mult, op1=ALU.mult),
                    reads=[xT.r(ti), self.pv.r(), rstd.r()], writes=[xn.r(ti)])
            if ti == 1 and last_out is not None:
                go = self.pvoff[gname][0] + gi * KC
                P.add("dve", lambda e: e.tensor_tensor(out=last_out[:], in0=xT[:, :, NT - 1], in1=self.pv[:, go:go + KC], op=ALU.mult),
                      reads=[xT.r(1), self.pv.r()], writes=[last_out.r()])
                P.add("dve", lambda e: e.tensor_scalar(out=last_out[:], in0=last_out[:], scalar1=rstd[:, 511:512], scalar2=None, op0=ALU.mult),
                      reads=[last_out.r(), rstd.r()], writes=[last_out.r()])
        if shift_li is not None:
            P.add("dve", lambda e: e.tensor_copy(out=self.prevcol[:, shift_li, :], in_=xn[:, :, NT]),
                  reads=[xn.r(1)], writes=[self.prevcol.r()])
        if not sample:
            return
        sm = self.small
        P.add("dve", lambda e: e.tensor_tensor(out=sm[:, 0:KC], in0=self.xsT[:], in1=self.xsT[:], op=ALU.mult),
              reads=[self.xsT.r()], writes=[sm.r("a")])
        P.add("dve", lambda e: e.tensor_reduce(out=sm[:, 16:17], in_=sm[:, 0:KC], axis=AX.X, op=ALU.add),
              reads=[sm.r("a")], writes=[sm.r("b")])
        P.add("dve", lambda e: e.tensor_copy(out=sq[0][:, 0:1], in_=sm[:, 16:17]), reads=[sm.r("b")], writes=[sq[0].r()])
        pb = psb[7]
        P.add("pe", lambda e: e.matmul(pb[:, 0:1], lhsT=self.onesb[:], rhs=sq[0][:, 0:1], start=True, stop=True),
              reads=[sq[0].r(), self.onesb.r()], writes=[pb.r()])
        P.add("act", lambda e: e.activation(out=sm[:, 17:18], in_=pb[:, 0:1], func=AF.Sqrt, scale=1.0 / D, bias=1e-6),
              reads=[pb.r()], writes=[sm.r("c")])
        P.add("dve", lambda e: e.reciprocal(out=sm[:, 17:18], in_=sm[:, 17:18]), reads=[sm.r("c")], writes=[sm.r("c")])
        go = self.pvoff[gname][0] + gi * KC
        dst = self.xslast
        P.add("dve", lambda e: e.scalar_tensor_tensor(out=dst[:], in0=self.xsT[:], scalar=sm[:, 17:18], in1=self.pv[:, go:go + KC], op0=ALU.mult, op1=ALU.mult),
              reads=[self.xsT.r(), sm.r("c"), self.pv.r()], writes=[dst.r()])
        P.add("dve", lambda e: e.tensor_copy(out=self.xns[:, :, 1], in_=dst[:]), reads=[dst.r()], writes=[self.xns.r("cur")])

    def ffn(self, li, lj, sample):
        P, d = self.P, self.din
        xT, xn, psb = self.xT, self.xn, self.psb
        self.phase("ffn")
        hq = self.av("hq", [11, NT], BF16)
        hqs = self.av("hqs", [32], BF16)
        wb = [self.av(f"wb{i}", [8192], BF16) for i in range(3)]
        sil = [self.av(f"sil{i}", [512], F32) for i in range(2)]
        fi = li * 2 + lj
        self.rmsnorm("ffn_norm", fi, sample)
        w13v = d["ffn_w13"][fi].rearrange("(kc p) c -> p kc c", p=128)
        w2v = d["ffn_w2"][fi].rearrange("(fc p) c -> p fc c", p=128)
        tiles = self.tiles()
        ps_s = psb[4]
        wi = 0
        for q in range(4):
            for jl in range(11):
                j = q * 11 + jl
                b = wb[wi % 3]
                wi += 1
                bv = b[:, :KC * 256].rearrange("p (k c) -> p k c", k=KC)
                P.add("pool", lambda e, bv=bv, j=j: e.dma_start(out=bv[:, :, 0:128], in_=w13v[:, :, j * 128:(j + 1) * 128]),
                      writes=[b.r()], dma=True)
                P.add("pool", lambda e, bv=bv, j=j: e.dma_start(out=bv[:, :, 128:256], in_=w13v[:, :, DFF + j * 128:DFF + (j + 1) * 128]),
                      writes=[b.r()], dma=True)
                for ti, (t0, n) in enumerate(tiles):
                    pg, pu = psb[ti * 2], psb[ti * 2 + 1]
                    for kc in range(KC):
                        P.add("pe", lambda e, pg=pg, bv=bv, kc=kc, t0=t0, n=n: e.matmul(
                            pg[:, :n], lhsT=bv[:, kc, 0:128], rhs=xn[:, kc, 1 + t0:1 + t0 + n], start=(kc == 0), stop=(kc == KC - 1)),
                            reads=[b.r(), xn.r(ti)], writes=[pg.r()])
                    for kc in range(KC):
                        P.add("pe", lambda e, pu=pu, bv=bv, kc=kc, t0=t0, n=n: e.matmul(
                            pu[:, :n], lhsT=bv[:, kc, 128:256], rhs=xn[:, kc, 1 + t0:1 + t0 + n], start=(kc == 0), stop=(kc == KC - 1)),
                            reads=[b.r(), xn.r(ti)], writes=[pu.r()])
                    s = sil[ti]
                    P.add("act", lambda e, s=s, pg=pg, n=n: e.activation(out=s[:, :n], in_=pg[:, :n], func=AF.Silu),
                          reads=[pg.r()], writes=[s.r()])
                    P.add("dve", lambda e, s=s, pu=pu, jl=jl, t0=t0, n=n: e.tensor_tensor(
                        out=hq[:, jl, t0:t0 + n], in0=s[:, :n], in1=pu[:, :n], op=ALU.mult),
                        reads=[s.r(), pu.r()], writes=[hq.r(ti)])
                if sample:
                    for half in range(2):
                        for kc in range(KC):
                            P.add("pe", lambda e, bv=bv, kc=kc, half=half: e.matmul(
                                ps_s[:, half:half + 1], lhsT=bv[:, kc, half * 128:(half + 1) * 128], rhs=self.xns[:, kc, 1:2],
                                start=(kc == 0), stop=(kc == KC - 1)),
                                reads=[b.r(), self.xns.r("cur")], writes=[ps_s.r()])
                    sm = self.small
                    P.add("act", lambda e: e.activation(out=sm[:, 20:21], in_=ps_s[:, 0:1], func=AF.Silu), reads=[ps_s.r()], writes=[sm.r("s")])
                    P.add("dve", lambda e, jl=jl: e.tensor_tensor(out=hqs[:, jl:jl + 1], in0=sm[:, 20:21], in1=ps_s[:, 1:2], op=ALU.mult),
                          reads=[sm.r("s"), ps_s.r()], writes=[hqs.r()])
            for dq in range(4):
                b = wb[wi % 3]
                wi += 1
                bv = b[:, :11 * 512].rearrange("p (k c) -> p k c", k=11)
                P.add("pool", lambda e, bv=bv, q=q, dq=dq: e.dma_start(out=bv, in_=w2v[:, q * 11:(q + 1) * 11, dq * 512:(dq + 1) * 512]),
                      writes=[b.r()], dma=True)
                for dl in range(4):
                    dc = dq * 4 + dl
                    for ti, (t0, n) in enumerate(tiles):
                        pa = psb[5 + (dc * 2 + ti) % 2]
                        for f in range(11):
                            P.add("pe", lambda e, pa=pa, bv=bv, f=f, dl=dl, t0=t0, n=n: e.matmul(
                                pa[:, :n], lhsT=bv[:, f, dl * 128:(dl + 1) * 128], rhs=hq[:, f, t0:t0 + n], start=(f == 0), stop=(f == 10)),
                                reads=[b.r(), hq.r(ti)], writes=[pa.r()])
                        P.add("dve", lambda e, pa=pa, dc=dc, t0=t0, n=n: e.scalar_tensor_tensor(
                            out=xT[:, dc, t0:t0 + n], in0=pa[:, :n], scalar=0.5, in1=xT[:, dc, t0:t0 + n], op0=ALU.mult, op1=ALU.add),
                            reads=[pa.r(), xT.r(ti)], writes=[xT.r(ti)])
                    if sample:
                        for f in range(11):
                            P.add("pe", lambda e, bv=bv, f=f, dl=dl: e.matmul(
                                ps_s[:, 2:3], lhsT=bv[:, f, dl * 128:(dl + 1) * 128], rhs=hqs[:, f:f + 1], start=(f == 0), stop=(f == 10)),
                                reads=[b.r(), hqs.r()], writes=[ps_s.r()])
                        P.add("dve", lambda e, dc=dc: e.scalar_tensor_tensor(
                            out=self.xsT[:, dc:dc + 1], in0=ps_s[:, 2:3], scalar=0.5, in1=self.xsT[:, dc:dc + 1], op0=ALU.mult, op1=ALU.add),
                            reads=[ps_s.r(), self.xsT.r()], writes=[self.xsT.r()])

    def mem_all(self):
        P, d, o = self.P, self.din, self.dout
        psb = self.psb
        self.phase("mem")
        memT = self.av("memT", [KC, MEMT], F32)
        memn = self.av("memn", [KC, MEMT], BF16)
        xin0 = self.av("xin0", [D], F32)
        xin = [xin0, xin0]
        wbuf = [self.av(f"wm{i}", [KC, 512], BF16) for i in range(2)]
        kf = [self.av(f"kf{i}", [512], F32) for i in range(2)]
        kb16 = [self.av(f"kb{i}", [512], BF16) for i in range(2)]
        sqv = [self.av(f"sqv{i}", [512], F32) for i in range(2)]
        gk = self.av("gk", [128], F32)
        mkT = self.av("mkT", [4, MEMT], BF16)
        sq = [self.av("sq0", [512], BF16), self.av("sq1", [512], BF16)]
        rstd = self.av("rstd", [512], F32)
        for tb in range(MEMT // 128):
            xi = xin[tb % 2]
            P.add("sp", lambda e, xi=xi, tb=tb: e.dma_start(out=xi[:], in_=d["memp"][tb * 128:(tb + 1) * 128, :]), writes=[xi.r()], dma=True)
            for k4 in range(4):
                pb = psb[(tb * 4 + k4) % 2]
                for j in range(4):
                    kc = k4 * 4 + j
                    P.add("pe", lambda e, pb=pb, xi=xi, j=j, kc=kc: e.transpose(
                        out=pb[:, j * 128:(j + 1) * 128], in_=xi[:, kc * 128:(kc + 1) * 128], identity=self.ident[:]),
                        reads=[xi.r(), self.ident.r()], writes=[pb.r()])
                P.add("act", lambda e, pb=pb, k4=k4, tb=tb: e.copy(
                    out=memT[:, k4 * 4:(k4 + 1) * 4, tb * 128:(tb + 1) * 128], in_=pb[:].rearrange("p (a b) -> p a b", a=4)),
                    reads=[pb.r()], writes=[memT.r()])
        pb = psb[7]
        n = MEMT
        for kc in range(KC):
            s = sq[kc % 2]
            P.add("act", lambda e, s=s, kc=kc: e.activation(out=s[:, :n], in_=memT[:, kc, :], func=AF.Square), reads=[memT.r()], writes=[s.r()])
            P.add("pe", lambda e, s=s, kc=kc: e.matmul(pb[:, :n], lhsT=self.onesb[:], rhs=s[:, :n], start=(kc == 0), stop=(kc == KC - 1)),
                  reads=[s.r(), self.onesb.r()], writes=[pb.r()])
        P.add("act", lambda e: e.activation(out=rstd[:, :n], in_=pb[:, :n], func=AF.Sqrt, scale=1.0 / D, bias=1e-6), reads=[pb.r()], writes=[rstd.r()])
        P.add("dve", lambda e: e.reciprocal(out=rstd[:, :n], in_=rstd[:, :n]), reads=[rstd.r()], writes=[rstd.r()])
        for kc in range(KC):
            P.add("dve", lambda e, kc=kc: e.tensor_tensor(out=memT[:, kc, :], in0=memT[:, kc, :], in1=rstd[:, :n], op=ALU.mult),
                  reads=[memT.r(), rstd.r()], writes=[memT.r()])
        sm = self.small
        for li in range(4):
            for kc in range(KC):
                P.add("dve", lambda e, kc=kc, li=li: e.tensor_scalar(out=memn[:, kc, :], in0=memT[:, kc, :], scalar1=self.pvc("mem_norm", li * KC + kc), scalar2=None, op0=ALU.mult),
                      reads=[memT.r(), self.pv.r()], writes=[memn.r()])
            P.add("sp", lambda e, li=li: e.dma_start(out=gk[:], in_=d["mem_k_norm"][li, :].partition_broadcast(128)), writes=[gk.r()], dma=True)
            wv = d["mem_w_kv"][li].rearrange("(kc p) c -> p kc c", p=128)
            for cb in range(2):
                b = wbuf[cb]
                P.add("pool", lambda e, b=b, cb=cb, wv=wv: e.dma_start(out=b[:], in_=wv[:, :, cb * 512:(cb + 1) * 512]), writes=[b.r()], dma=True)
                for mt in range(2):
                    pb = psb[mt]
                    for kc in range(KC):
                        P.add("pe", lambda e, pb=pb, b=b, kc=kc, mt=mt: e.matmul(
                            pb[:, :], lhsT=memn[:, kc, mt * 128:(mt + 1) * 128], rhs=b[:, kc, :], start=(kc == 0), stop=(kc == KC - 1)),
                            reads=[b.r(), memn.r()], writes=[pb.r()])
                    k_f = kf[mt]
                    k_b = kb16[mt]
                    if cb == 0:
                        sv = sqv[mt]
                        c0 = 24 + 4 * mt
                        P.add("act", lambda e, pb=pb, sv=sv: e.activation(out=sv[:, :], in_=pb[:, :], func=AF.Square), reads=[pb.r()], writes=[sv.r()])
                        P.add("dve", lambda e, sv=sv, c0=c0: e.tensor_reduce(out=sm[:, c0:c0 + 4], in_=sv[:, :].rearrange("p (h d) -> p h d", h=4), axis=AX.X, op=ALU.add),
                              reads=[sv.r()], writes=[sm.r(("mk", mt))])
                        P.add("act", lambda e, c0=c0: e.activation(out=sm[:, c0:c0 + 4], in_=sm[:, c0:c0 + 4], func=AF.Sqrt, scale=1.0 / 128, bias=1e-6),
                              reads=[sm.r(("mk", mt))], writes=[sm.r(("mk", mt))])
                        P.add("dve", lambda e, c0=c0: e.reciprocal(out=sm[:, c0:c0 + 4], in_=sm[:, c0:c0 + 4]), reads=[sm.r(("mk", mt))], writes=[sm.r(("mk", mt))])
                        for h in range(4):
                            P.add("dve", lambda e, pb=pb, k_f=k_f, h=h, c0=c0: e.scalar_tensor_tensor(
                                out=k_f[:, h * 128:(h + 1) * 128], in0=pb[:, h * 128:(h + 1) * 128], scalar=sm[:, c0 + h:c0 + h + 1], in1=gk[:, :],
                                op0=ALU.mult, op1=ALU.mult),
                                reads=[pb.r(), sm.r(("mk", mt)), gk.r()], writes=[k_f.r()])
                        P.add("sp", lambda e, k_f=k_f, mt=mt, li=li: e.dma_start(out=o["mkp"][li, mt * 128:(mt + 1) * 128, :], in_=k_f[:, :]), reads=[k_f.r()], writes=[k_f.r("o")], dma=True)
                        P.add("act", lambda e, k_f=k_f, k_b=k_b: e.copy(out=k_b[:, :], in_=k_f[:, :]), reads=[k_f.r()], writes=[k_b.r()])
                        pt = self.psbf(2 + mt)
                        for h in range(4):
                            P.add("pe", lambda e, pt=pt, k_b=k_b, h=h: e.transpose(out=pt[:, h * 128:(h + 1) * 128], in_=k_b[:, h * 128:(h + 1) * 128], identity=self.identb[:]),
                                  reads=[k_b.r(), self.identb.r()], writes=[psb[2 + mt].r()])
                        P.add("act", lambda e, pt=pt, mt=mt: e.copy(out=mkT[:, :, mt * 128:(mt + 1) * 128], in_=pt[:, 0:512].rearrange("p (h m) -> p h m", h=4)),
                              reads=[psb[2 + mt].r()], writes=[mkT.r()])
                    else:
                        P.add("act", lambda e, pb=pb, k_f=k_f: e.copy(out=k_f[:, :], in_=pb[:, :]), reads=[pb.r()], writes=[k_f.r()])
                        P.add("sp", lambda e, k_f=k_f, mt=mt, li=li: e.dma_start(out=o["mvp"][li, mt * 128:(mt + 1) * 128, :], in_=k_f[:, :]), reads=[k_f.r()], writes=[k_f.r("o")], dma=True)
                        P.add("dve", lambda e, k_f=k_f, k_b=k_b: e.tensor_copy(out=k_b[:, :], in_=k_f[:, :]), reads=[k_f.r()], writes=[k_b.r()])
                        P.add("sp", lambda e, k_b=k_b, mt=mt, li=li: e.dma_start(out=self.mv_scr.t.ap()[li, mt * 128:(mt + 1) * 128, :], in_=k_b[:, :]),
                              reads=[k_b.r()], writes=[k_b.r("o"), self.mv_scr.r(li)], dma=True)
            P.add("sp", lambda e, li=li: e.dma_start(out=self.mk_scr.t.ap()[li], in_=mkT[:].rearrange("p h m -> p (h m)")),
                  reads=[mkT.r()], writes=[mkT.r("o"), self.mk_scr.r(li)], dma=True)

    def mem_sample(self):
        P, d = self.P, self.din
        psb = self.psb
        self.phase("mems")
        kb16 = [self.av(f"kb{i}", [512], BF16) for i in range(2)]
        mkT = self.av("mkT", [4, MEMT], BF16)
        for li in range(4):
            for mt in range(2):
                k_b = kb16[mt]
                P.add("pool", lambda e, k_b=k_b, li=li, mt=mt: e.dma_start(out=k_b[:, :], in_=d["cmk"][li, mt * 128:(mt + 1) * 128, :]), writes=[k_b.r()], dma=True)
                pt = self.psbf(2 + mt)
                for h in range(4):
                    P.add("pe", lambda e, pt=pt, k_b=k_b, h=h: e.transpose(out=pt[:, h * 128:(h + 1) * 128], in_=k_b[:, h * 128:(h + 1) * 128], identity=self.identb[:]),
                          reads=[k_b.r(), self.identb.r()], writes=[psb[2 + mt].r()])
                P.add("act", lambda e, pt=pt, mt=mt: e.copy(out=mkT[:, :, mt * 128:(mt + 1) * 128], in_=pt[:, 0:512].rearrange("p (h m) -> p h m", h=4)),
                      reads=[psb[2 + mt].r()], writes=[mkT.r()])
            P.add("sp", lambda e, li=li: e.dma_start(out=self.mks_scr.t.ap()[li], in_=mkT[:].rearrange("p h m -> p (h m)")), reads=[mkT.r()], writes=[self.mks_scr.r(li)], dma=True)
            for mt in range(2):
                k_b = kb16[mt]
                P.add("pool", lambda e, k_b=k_b, li=li, mt=mt: e.dma_start(out=k_b[:, :], in_=d["cmv"][li, mt * 128:(mt + 1) * 128, :]), writes=[k_b.r()], dma=True)
                P.add("sp", lambda e, k_b=k_b, li=li, mt=mt: e.dma_start(out=self.mvs_scr.t.ap()[li, mt * 128:(mt + 1) * 128, :], in_=k_b[:, :]), reads=[k_b.r()], writes=[self.mvs_scr.r(li)], dma=True)

    def mixerA(self, li, blk, sample, smp=False):
        P, d = self.P, self.din
        xn, xT, psb = self.xn, self.xT, self.psb
        self.phase("mixA")
        N = CH if smp else TA
        NC = N // CH
        av = self.av
        wbig = av("wbig", [3 * 32 * 128], BF16)
        wX = [wbig[:, i * 4096:(i + 1) * 4096].rearrange("p (k c) -> p k c", k=32) for i in range(3)]
        l1w = av("l1w", [KC, 96], BF16)
        l1a = av("l1a", [KC, 96], BF16)
        self._xw_off = self.aoff
        class _V:
            pass
        l1gv = _V()
        l1gv_ap = wbig[:, 8192:8192 + KC * 128].rearrange("p (k c) -> p k c", k=KC)
        l1gv.__class__ = type("LV", (), {"__getitem__": lambda s, idx: l1gv_ap[idx], "r": lambda s, key=None: wbig.r(2)})
        xw = [av("xw0", [N], BF16), av("xw1", [N], BF16)]
        xtmp = [av("xt0", [N], BF16), av("xt1", [N], BF16)]
        th = av("th", [N], BF16)
        ah = av("ah", [N], BF16)
        gh = av("gh", [2, N], BF16)
        w2s = av("w2s", [128], BF16)
        a2s = av("a2s", [128], BF16)
        g2s = av("g2s", [2, 128], BF16)
        r_f = av("r_f", [N], F32)
        k_f = av("k_f", [N], F32)
        v_b = av("v_b", [N], BF16)
        ew = av("ew", [N], F32)
        asig = av("asig", [N], F32)
        kk = av("kk", [N], F32)
        kmod = av("kmod", [N], F32)
        bb = av("bb", [N], F32)
        cum = av("cum", [N], F32)
        e1 = av("e1", [N], F32)
        e2 = av("e2", [N], F32)
        rk = av("rk", [N], F32)
        sqk = av("sqk", [N], BF16)
        AR = av("AR", [NC, 128], BF16)
        Bt = av("Bt", [N], BF16)
        Kt = av("Kt", [N], BF16)
        ARh = [av("AR0", [NC, 128], BF16), av("AR1", [NC, 128], BF16)]
        Bth = [av("Bt0", [N], BF16), av("Bt1", [N], BF16)]
        Kth = [av("Kt0", [N], BF16), av("Kt1", [N], BF16)]
        Bh = av("Bh", [N], BF16)
        Kh = av("Kh", [N], BF16)
        tot = av("tot", [NC], F32)
        gtot2 = [av("gtot0", [NC], F32), av("gtot1", [NC], F32)]
        Sb = av("Sb", [64], BF16)
        BhT = av("BhT", [NC, 128], BF16, parts=64)
        KhT = av("KhT", [NC, 128], BF16, parts=64)
        Vt = av("Vt", [NC, 128], BF16, parts=64)
        g_t2 = [av("g_t0", [NC, 128], BF16, parts=64), av("g_t1", [NC, 128], BF16, parts=64)]
        NNs = av("NNs", [NC, 512], BF16, parts=64)
        NabT = av("NabT", [2 * NC, 64], BF16, parts=64)
        Tm = [av("Tm0", [2 * NC, 64], BF16, parts=64), av("Tm1", [2 * NC, 64], BF16, parts=64)]
        Xs = [av("Xa", [2 * NC, 64], BF16, parts=64), av("Xb", [2 * NC, 64], BF16, parts=64)]
        XTs = [av("XTa", [2 * NC, 64], BF16, parts=64), av("XTb", [2 * NC, 64], BF16, parts=64)]
        y_t = av("y_t", [NC, 128], F32, parts=64)
        W1b = av("W1b", [128], BF16, parts=64)
        Ub = av("Ub", [128], BF16, parts=64)
        pt1 = self.P.view("mixA.pt1", self.arena.t[0:64, self._xw_off // 2:self._xw_off // 2 + NC * 256].bitcast(F32).rearrange("p (a b) -> p a b", a=NC))
        bon2 = [av("bon0", [2 * NC], F32, parts=64), av("bon1", [2 * NC], F32, parts=64)]
        st = av("st", [64], F32, parts=64)
        lnw2 = [av("lnw0", [128], F32, parts=64), av("lnw1", [128], F32, parts=64)]
        lnb2 = [av("lnb0", [128], F32, parts=64), av("lnb1", [128], F32, parts=64)]
        mixout = av("mixout", [KC, N], BF16)
        mkT = av("mkT", [4, MEMT], BF16)
        mv = av("mv", [2, 512], BF16)
        qf, qn, pT, rz = r_f, v_b, [Bt, Kt], k_f
        m4 = self.c64[:, 0:512]
        sl8 = self.c64[:, 512:1024]
        i8 = self.c64[:, 1024:1536]
        hsel = self.c128[:, 128:130]
        SC = 128 ** -0.5

        win = d["a_w_in"][li].rearrange("(k p) c -> p k c", p=128)
        mks, mvs = (self.mks_scr, self.mvs_scr) if smp else (self.mk_scr, self.mv_scr)
        P.add("sp", lambda e: e.dma_start(out=mkT[:].rearrange("p h m -> p (h m)"), in_=mks.t.ap()[li]), reads=[mks.r(li)], writes=[mkT.r()], dma=True)
        P.add("sp", lambda e: e.dma_start(out=mv[:], in_=mvs.t.ap()[li].rearrange("(t p) c -> p t c", p=128)), reads=[mvs.r(li)], writes=[mv.r()], dma=True)
        if smp:
            xst = av("xst", [KC, N + 1], BF16)
            tmask = av("tmask", [N], F32)
            so = av("so_s", [12, 128], F32, parts=64)
            xi = av("sxi", [128], F32)
            P.add("dve", lambda e: e.memset(xst[:], 0.0), writes=[xst.r()])
            P.add("dve", lambda e: e.memset(tmask[:], 0.0), writes=[tmask.r()])
            P.add("dve", lambda e: e.memset(tmask[:, 0:1], 1.0), reads=[tmask.r()], writes=[tmask.r()])
            P.add("sp", lambda e: e.dma_start(out=xi[0:KC, :], in_=d["sts"][li]), writes=[xi.r()], dma=True)
            P.add("pe", lambda e: e.transpose(out=psb[2][:, 0:KC], in_=xi[0:KC, 0:128], identity=self.ident[0:KC, 0:KC]), reads=[xi.r(), self.ident.r()], writes=[psb[2].r()])
            P.add("act", lambda e: e.copy(out=xst[:, :, 0], in_=psb[2][:, 0:KC]), reads=[psb[2].r(), xst.r()], writes=[xst.r()])
            P.add("dve", lambda e: e.tensor_copy(out=xst[:, :, 1], in_=self.xns[:, :, 1]), reads=[self.xns.r("cur"), xst.r()], writes=[xst.r()])
            s_in = av("s_in", [24 * 64], F32, parts=64)
            P.add("sp", lambda e: e.dma_start(out=s_in[:].rearrange("p (h j) -> p h j", h=24), in_=d["stw"][li].rearrange("h i j -> i h j")), writes=[s_in.r()], dma=True)
            for gi in range(12):
                pbt = psb[gi % 2]
                P.add("pe", lambda e, gi=gi, pbt=pbt: e.transpose(out=pbt[:, 0:64], in_=s_in[:, gi * 128:(gi + 1) * 128], identity=self.ident[0:64, 0:64]),
                      reads=[s_in.r(), self.ident.r()], writes=[pbt.r()])
                P.add("act", lambda e, gi=gi, pbt=pbt: e.copy(out=self.Sf[:, li, gi, :], in_=pbt[:, 0:64]), reads=[pbt.r(), self.Sf.r((li, gi)), self.Sf.r()], writes=[self.Sf.r((li, gi))])

        def proj(ps, w, xres, cur, prv):
            for kc in range(32):
                rhs = cur(kc) if kc < 16 else prv(kc - 16)
                P.add("pe", lambda e, kc=kc, rhs=rhs: e.matmul(ps, lhsT=w[0][:, kc, :], rhs=rhs, start=(kc == 0), stop=(kc == 31)),
                      reads=[w[1]] + xres, writes=[w[2]])

        def loadw(i, c0):
            P.add("pool", lambda e: e.dma_start(out=wX[i], in_=win[:, :, c0:c0 + 128]), writes=[wbig.r(i)], dma=True)
            P.add("dve", lambda e: e.tensor_tensor(out=wX[i][:, 0:16, :], in0=wX[i][:, 0:16, :], in1=wX[i][:, 16:32, :], op=ALU.subtract),
                  reads=[wbig.r(i)], writes=[wbig.r(i)])

        import os
        dbg = int(os.environ.get('KDBG', '99'))
        if dbg <= 0:
            return
        for ti in range(1 if smp else ((NT // N) if dbg >= 60 else 1)):
            t0 = ti * N
            if smp:
                cur = lambda kc: xst[:, kc, 1:1 + N]
                prv = lambda kc: xst[:, kc, 0:N]
                xres = [xst.r()]
                xkey = 0
            else:
                cur = lambda kc, t0=t0: xn[:, kc, 1 + t0:1 + t0 + N]
                prv = lambda kc, t0=t0: xn[:, kc, t0:t0 + N]
                xres = [xn.r(t0 // 512)] + ([xn.r("prev")] if t0 == 0 else ([xn.r(t0 // 512 - 1)] if t0 % 512 == 0 else []))
                xkey = t0 // 512
            moff = self.pvoff["a_mu"][0] + li * 48
            for which, (wsrc, l1, ncol, dst, func) in enumerate((("a_w1", l1w, 96, th, AF.Tanh), ("a_a1", l1a, 96, ah, AF.Copy), ("a_g1", l1gv, 128, gh, AF.Sigmoid))):
                nh = 2 if which == 2 else 1
                for hh in range(nh):
                    src = d[wsrc][li].rearrange("(k p) c -> p k c", p=128)[:, :, hh * ncol:(hh + 1) * ncol]
                    l1r = wbig.r(2) if which == 2 else l1.r()
                    P.add("pool", lambda e, l1=l1, src=src: e.dma_start(out=l1[:], in_=src), writes=[l1r], dma=True)
                    pb = psb[3]
                    for kc in range(KC):
                        xt_, xw_ = xtmp[kc % 2], xw[kc % 2]
                        mcol = moff + which * 16 + kc
                        ocol = li * 48 + which * 16 + kc
                        pk, ck = prv(kc), cur(kc)
                        P.add("dve", lambda e, xt_=xt_, pk=pk, mcol=mcol: e.tensor_scalar(out=xt_[:], in0=pk, scalar1=self.pv[:, mcol:mcol + 1], scalar2=None, op0=ALU.mult),
                              reads=xres + [self.pv.r()], writes=[xt_.r()])
                        P.add("dve", lambda e, xt_=xt_, xw_=xw_, ck=ck, ocol=ocol: e.scalar_tensor_tensor(out=xw_[:], in0=ck, scalar=self.omu[:, ocol:ocol + 1], in1=xt_[:], op0=ALU.mult, op1=ALU.add),
                              reads=xres + [self.omu.r(), xt_.r()], writes=[xw_.r()])
                        P.add("pe", lambda e, l1=l1, xw_=xw_, kc=kc, ncol=ncol: e.matmul(pb[0:ncol, 0:N], lhsT=l1[:, kc, 0:ncol], rhs=xw_[:], start=(kc == 0), stop=(kc == KC - 1)),
                              reads=[l1r, xw_.r()], writes=[pb.r()])
                    o_ap = dst[0:ncol, :] if which < 2 else dst[:, hh, :]
                    P.add("act", lambda e, o_ap=o_ap, func=func, ncol=ncol: e.activation(out=o_ap, in_=pb[0:ncol, 0:N], func=func), reads=[pb.r()], writes=[dst.r()])
            if dbg <= 1:
                return
            ngrp = 12 if dbg >= 60 else 1
            def stageA(gi):
                g_tA, bonA, gtotA, lnwA, lnbA = g_t2[gi % 2], bon2[gi % 2], gtot2[gi % 2], lnw2[gi % 2], lnb2[gi % 2]
                c0 = gi * 128
                for i in range(3):
                    loadw(i, i * 1536 + c0)
                P.add("pool", lambda e, c0=c0: e.dma_start(out=w2s[0:96, :], in_=d["a_w2"][li][:, c0:c0 + 128]), writes=[w2s.r()], dma=True)
                P.add("pool", lambda e, c0=c0: e.dma_start(out=a2s[0:96, :], in_=d["a_a2"][li][:, c0:c0 + 128]), writes=[a2s.r()], dma=True)
                P.add("pool", lambda e, c0=c0: e.dma_start(out=g2s[:], in_=d["a_g2"][li].rearrange("(k p) c -> p k c", p=128)[:, :, c0:c0 + 128]), writes=[g2s.r()], dma=True)
                P.add("sp", lambda e, c0=c0: e.dma_start(out=lnwA[:], in_=d["a_lnx_w"][li, c0:c0 + 128].partition_broadcast(64)), writes=[lnwA.r()], dma=True)
                P.add("sp", lambda e, c0=c0: e.dma_start(out=lnbA[:], in_=d["a_lnx_b"][li, c0:c0 + 128].partition_broadcast(64)), writes=[lnbA.r()], dma=True)
                for i in range(3):
                    proj(psb[i][:, 0:N], (wX[i], wbig.r(i), psb[i].r()), xres, cur, prv)
                P.add("act", lambda e: e.copy(out=r_f[:], in_=psb[0][:, 0:N]), reads=[psb[0].r()], writes=[r_f.r()])
                P.add("act", lambda e: e.copy(out=k_f[:], in_=psb[1][:, 0:N]), reads=[psb[1].r()], writes=[k_f.r()])
                P.add("act", lambda e: e.copy(out=v_b[:], in_=psb[2][:, 0:N]), reads=[psb[2].r()], writes=[v_b.r()])
                if smp:
                    for bf_ in (r_f, k_f, v_b):
                        P.add("dve", lambda e, bf_=bf_: e.tensor_tensor(out=bf_[:], in0=bf_[:], in1=tmask[:], op=ALU.mult), reads=[bf_.r(), tmask.r()], writes=[bf_.r()])
                yield
                pb = psb[3]
                P.add("pe", lambda e: e.matmul(pb[:, 0:N], lhsT=w2s[0:96, :], rhs=th[0:96, :], start=True, stop=True), reads=[w2s.r(), th.r()], writes=[pb.r()])
                P.add("pe", lambda e: e.matmul(pb[:, N:2 * N], lhsT=a2s[0:96, :], rhs=ah[0:96, :], start=True, stop=True), reads=[a2s.r(), ah.r()], writes=[pb.r()])
                pcol = li * 12 + gi
                P.add("act", lambda e, pcol=pcol: e.activation(out=ew[:], in_=pb[:, 0:N], func=AF.Exp, scale=-1.0, bias=self.neg[:, pcol:pcol + 1]), reads=[pb.r(), self.neg.r()], writes=[ew.r()])
                P.add("act", lambda e: e.activation(out=ew[:], in_=ew[:], func=AF.Ln, bias=1.0), reads=[ew.r()], writes=[ew.r()])
                P.add("act", lambda e: e.activation(out=ew[:], in_=ew[:], func=AF.Exp, scale=-1.0, bias=-0.5), reads=[ew.r()], writes=[ew.r()])
                if smp:
                    P.add("dve", lambda e: e.tensor_tensor(out=ew[:], in0=ew[:], in1=tmask[:], op=ALU.mult), reads=[ew.r(), tmask.r()], writes=[ew.r()])
                P.add("act", lambda e, pcol=pcol: e.activation(out=asig[:], in_=pb[:, N:2 * N], func=AF.Sigmoid, bias=self.pvc("a_a0", pcol)), reads=[pb.r(), self.pv.r()], writes=[asig.r()])
                pg = psb[0]
                for c in range(NC):
                    for k2 in range(2):
                        P.add("pe", lambda e, c=c, k2=k2: e.matmul(pg[0:64, c * 128:(c + 1) * 128], lhsT=gh[:, k2, c * 64:(c + 1) * 64], rhs=g2s[:, k2, :], start=(k2 == 0), stop=(k2 == 1)),
                              reads=[gh.r(), g2s.r()], writes=[pg.r()])
                P.add("act", lambda e: e.copy(out=g_tA[:].rearrange("p c f -> p (c f)"), in_=pg[0:64, 0:NC * 128]), reads=[pg.r()], writes=[g_tA.r()])
                yield
                V = lambda fn, reads, writes: P.add("dve", fn, reads=reads, writes=writes)
                A = lambda fn, reads, writes: P.add("act", fn, reads=reads, writes=writes)
                V(lambda e, pcol=pcol: e.tensor_scalar(out=kk[:], in0=k_f[:], scalar1=self.pvc("a_k_k", pcol), scalar2=None, op0=ALU.mult), [k_f.r(), self.pv.r()], [kk.r()])
                V(lambda e: e.tensor_tensor(out=sqk[:], in0=kk[:], in1=kk[:], op=ALU.mult), [kk.r()], [sqk.r()])
                pn = psb[1]
                P.add("pe", lambda e: e.matmul(pn[:, 0:N], lhsT=self.blk64b[:], rhs=sqk[:], start=True, stop=True), reads=[sqk.r(), self.blk64b.r()], writes=[pn.r()])
                A(lambda e: e.activation(out=e1[:], in_=pn[:, 0:N], func=AF.Sqrt), [pn.r()], [e1.r()])
                V(lambda e: e.tensor_scalar(out=e1[:], in0=e1[:], scalar1=1e-12, scalar2=None, op0=ALU.max), [e1.r()], [e1.r()])
                V(lambda e: e.reciprocal(out=e1[:], in_=e1[:]), [e1.r()], [e1.r()])
                V(lambda e: e.tensor_tensor(out=kk[:], in0=kk[:], in1=e1[:], op=ALU.mult), [kk.r(), e1.r()], [kk.r()])
                V(lambda e, pcol=pcol: e.tensor_scalar(out=e2[:], in0=asig[:], scalar1=self.pvc("a_k_a", pcol), scalar2=self.neg[:, 24 + pcol:25 + pcol], op0=ALU.mult, op1=ALU.add),
                  [asig.r(), self.pv.r(), self.neg.r()], [e2.r()])
                V(lambda e: e.tensor_tensor(out=kmod[:], in0=k_f[:], in1=e2[:], op=ALU.mult), [k_f.r(), e2.r()], [kmod.r()])
                V(lambda e: e.tensor_tensor(out=bb[:], in0=kk[:], in1=asig[:], op=ALU.mult), [kk.r(), asig.r()], [bb.r()])
                V(lambda e, pcol=pcol: e.scalar_tensor_tensor(out=rk[:], in0=r_f[:], scalar=self.pvc("a_r_k", pcol), in1=kmod[:], op0=ALU.mult, op1=ALU.mult),
                  [r_f.r(), kmod.r(), self.pv.r()], [rk.r()])
                for c in range(NC):
                    P.add("pe", lambda e, c=c: e.matmul(pn[0:64, N + 2 * c:N + 2 * c + 2], lhsT=rk[:, c * 64:(c + 1) * 64], rhs=hsel, start=True, stop=True),
                          reads=[rk.r(), self.c128.r()], writes=[pn.r()])
                A(lambda e: e.copy(out=bonA[:], in_=pn[0:64, N:N + 2 * NC]), [pn.r()], [bonA.r()])
                yield
                V(lambda e: e.tensor_tensor_scan(out=cum[:], data0=self.rmask[:, 0:N], data1=ew[:], initial=0.0, op0=ALU.mult, op1=ALU.add), [ew.r(), self.rmask.r()], [cum.r()])
                c3 = lambda b: b[:].rearrange("p (c t) -> p c t", t=CH)
                V(lambda e: e.tensor_copy(out=tot[:], in_=c3(cum)[:, :, CH - 1]), [cum.r()], [tot.r()])
                A(lambda e: e.activation(out=e1[:], in_=cum[:], func=AF.Exp, scale=-1.0), [cum.r()], [e1.r()])
                V(lambda e: e.tensor_tensor(out=AR[:, :, 64:128], in0=c3(r_f), in1=c3(e1), op=ALU.mult), [r_f.r(), e1.r()], [AR.r()])
                A(lambda e: e.activation(out=e2[:], in_=cum[:], func=AF.Exp), [cum.r()], [e2.r()])
                V(lambda e: e.tensor_tensor(out=Bt[:], in0=bb[:], in1=e2[:], op=ALU.mult), [bb.r(), e2.r()], [Bt.r()])
                V(lambda e: e.tensor_tensor(out=Kt[:], in0=kmod[:], in1=e2[:], op=ALU.mult), [kmod.r(), e2.r()], [Kt.r()])
                yield
                V(lambda e: e.tensor_tensor(out=e1[:], in0=ew[:], in1=cum[:], op=ALU.subtract), [ew.r(), cum.r(), AR.r()], [e1.r()])
                A(lambda e: e.activation(out=e1[:], in_=e1[:], func=AF.Exp), [e1.r()], [e1.r()])
                V(lambda e: e.scalar_tensor_tensor(out=AR[:, :, 0:64], in0=c3(kk), scalar=-1.0, in1=c3(e1), op0=ALU.mult, op1=ALU.mult), [kk.r(), e1.r()], [AR.r()])
                V(lambda e: e.tensor_tensor(out=c3(e2), in0=c3(cum), in1=tot[:].unsqueeze(2).to_broadcast([128, NC, CH]), op=ALU.subtract), [cum.r(), tot.r(), Bt.r(), Kt.r()], [e2.r()])
                A(lambda e: e.activation(out=e2[:], in_=e2[:], func=AF.Exp), [e2.r()], [e2.r()])
                V(lambda e: e.tensor_tensor(out=Bh[:], in0=bb[:], in1=e2[:], op=ALU.mult), [bb.r(), e2.r()], [Bh.r()])
                V(lambda e: e.tensor_tensor(out=Kh[:], in0=kmod[:], in1=e2[:], op=ALU.mult), [kmod.r(), e2.r()], [Kh.r()])
                A(lambda e: e.activation(out=gtotA[:], in_=tot[:], func=AF.Exp, scale=-1.0), [tot.r()], [gtotA.r()])
                yield
            pend = stageA(0)
            for _ in pend:
                pass
            for gi in range(ngrp):
                g_t, bon, gtot, lnw, lnb = g_t2[gi % 2], bon2[gi % 2], gtot2[gi % 2], lnw2[gi % 2], lnb2[gi % 2]
                pcol = li * 12 + gi
                V = lambda fn, reads, writes: P.add("dve", fn, reads=reads, writes=writes)
                A = lambda fn, reads, writes: P.add("act", fn, reads=reads, writes=writes)
                c3 = lambda b: b[:].rearrange("p (c t) -> p c t", t=CH)
                nxt = stageA(gi + 1) if gi + 1 < ngrp else None
                for h in range(2):
                    hm = self.c128[:, 128 + h:129 + h]
                    V(lambda e, h=h, hm=hm: e.tensor_scalar(out=ARh[h][:].rearrange("p c f -> p (c f)"), in0=AR[:].rearrange("p c f -> p (c f)"), scalar1=hm, scalar2=None, op0=ALU.mult), [AR.r(), self.c128.r()], [ARh[h].r()])
                    V(lambda e, h=h, hm=hm: e.tensor_scalar(out=Bth[h][:], in0=Bt[:], scalar1=hm, scalar2=None, op0=ALU.mult), [Bt.r(), self.c128.r()], [Bth[h].r()])
                    V(lambda e, h=h, hm=hm: e.tensor_scalar(out=Kth[h][:], in0=Kt[:], scalar1=hm, scalar2=None, op0=ALU.mult), [Kt.r(), self.c128.r()], [Kth[h].r()])
                if dbg <= 3:
                    return
                for (srcb, dstb) in ((Bh, BhT), (Kh, KhT), (v_b, Vt)):
                    ptb = self.psbf(6)
                    for c in range(NC):
                        P.add("pe", lambda e, c=c, srcb=srcb, ptb=ptb: e.transpose(out=ptb[0:64, c * 128:(c + 1) * 128], in_=srcb[:, c * 64:(c + 1) * 64], identity=self.identb[:]),
                              reads=[srcb.r(), self.identb.r()], writes=[psb[6].r()])
                    A(lambda e, dstb=dstb, ptb=ptb: e.copy(out=dstb[:].rearrange("p c f -> p (c f)"), in_=ptb[0:64, 0:NC * 128]), [psb[6].r()], [dstb.r()])
                if dbg <= 4:
                    return
                for c in range(NC):
                    pb7 = psb[7]
                    for h in range(2):
                        hp = slice(64 * h, 64 * h + 64)
                        P.add("pe", lambda e, c=c, h=h, hp=hp: e.matmul(pb7[0:64, (2 * h) * 128:(2 * h + 1) * 128], lhsT=Bth[h][:, c * 64:(c + 1) * 64], rhs=AR[:, c, :], start=True, stop=True),
                              reads=[Bth[h].r(), AR.r()], writes=[pb7.r()])
                        P.add("pe", lambda e, c=c, h=h, hp=hp: e.matmul(pb7[0:64, (2 * h + 1) * 128:(2 * h + 2) * 128], lhsT=Kth[h][:, c * 64:(c + 1) * 64], rhs=AR[:, c, :], start=True, stop=True),
                              reads=[Kth[h].r(), AR.r()], writes=[pb7.r()])
                    V(lambda e, c=c: e.tensor_tensor(out=NNs[:, c, :], in0=pb7[0:64, :], in1=m4, op=ALU.mult), [pb7.r(), self.c64.r()], [NNs.r()])
                pb4 = psb[4]
                for c in range(NC):
                    for h in range(2):
                        hp = slice(64 * h, 64 * h + 64)
                        m = c * 2 + h
                        P.add("pe", lambda e, c=c, h=h, m=m: e.matmul(pb4[0:64, m * 64:(m + 1) * 64], lhsT=ARh[h][:, c, 0:64], rhs=Bt[:, c * 64:(c + 1) * 64], start=True, stop=True),
                              reads=[Bt.r(), ARh[h].r()], writes=[pb4.r()])
                V(lambda e: e.tensor_tensor(out=NabT[:].rearrange("p m s -> p (m s)"), in0=pb4[0:64, 0:2 * NC * 64], in1=sl8[:, 0:2 * NC * 64], op=ALU.mult), [pb4.r(), self.c64.r()], [NabT.r()])
                if dbg <= 5:
                    return
                nm = 2 * NC
                X0 = lambda m: NNs[:, m // 2, (m % 2) * 256:(m % 2) * 256 + 64]
                NN5 = NNs[:].rearrange("p c (h q t) -> p c h q t", h=2, q=4)
                V(lambda e: e.tensor_tensor(out=Tm[0][:].rearrange("p (c h) s -> p c h s", h=2), in0=NN5[:, :, :, 0, :],
                                            in1=i8[:, 0:nm * 64].rearrange("p (c h s) -> p c h s", h=2, s=64), op=ALU.add), [NNs.r(), self.c64.r()], [Tm[0].r()])
                for lvl in range(5):
                    Xc = X0 if lvl == 0 else (lambda m, b=Xs[lvl % 2]: b[:, m, :])
                    XTc = (lambda m: NabT[:, m, :]) if lvl == 0 else (lambda m, b=XTs[lvl % 2]: b[:, m, :])
                    xr = [NNs.r(), NabT.r()] if lvl == 0 else [Xs[lvl % 2].r(), XTs[lvl % 2].r()]
                    Xn, XTn = Xs[(lvl + 1) % 2], XTs[(lvl + 1) % 2]
                    pX, pXT, pT_ = psb[4], psb[5], psb[6]
                    if lvl < 4:
                        for m in range(nm):
                            P.add("pe", lambda e, m=m, Xc=Xc, XTc=XTc: e.matmul(pX[0:64, m * 64:(m + 1) * 64], lhsT=XTc(m), rhs=Xc(m), start=True, stop=True), reads=xr, writes=[pX.r()])
                        A(lambda e, Xn=Xn: e.copy(out=Xn[:].rearrange("p m s -> p (m s)"), in_=pX[0:64, 0:nm * 64]), [pX.r()], [Xn.r()])
                    for m in range(nm):
                        P.add("pe", lambda e, m=m, Xc=Xc, XTc=XTc: e.matmul(pXT[0:64, m * 64:(m + 1) * 64], lhsT=Xc(m), rhs=XTc(m), start=True, stop=True), reads=xr, writes=[pXT.r()])
                    V(lambda e, XTn=XTn: e.tensor_copy(out=XTn[:].rearrange("p m s -> p (m s)"), in_=pXT[0:64, 0:nm * 64]), [pXT.r()], [XTn.r()])
                    Tc, Tn = Tm[lvl % 2], Tm[(lvl + 1) % 2]
                    for m in range(nm):
                        P.add("pe", lambda e, m=m, XTn=XTn, Tc=Tc: e.matmul(pT_[0:64, m * 64:(m + 1) * 64], lhsT=XTn[:, m, :], rhs=Tc[:, m, :], start=True, stop=True),
                              reads=[XTn.r(), Tc.r()], writes=[pT_.r()])
                    V(lambda e, Tc=Tc, Tn=Tn: e.tensor_tensor(out=Tn[:].rearrange("p m s -> p (m s)"), in0=pT_[0:64, 0:nm * 64], in1=Tc[:].rearrange("p m s -> p (m s)"), op=ALU.add),
                      [pT_.r(), Tc.r()], [Tn.r()])
                TF = Tm[1]
                if dbg <= 6:
                    return
                Sg = self.Sf[:, li, gi, :]
                sres = self.Sf.r((li, gi))
                A(lambda e, Sg=Sg: e.copy(out=Sb[:], in_=Sg), [sres, self.Sf.r()], [Sb.r()])
                for c in range(NC):
                    ps1, ps2, psY, psS = psb[4], psb[5], psb[6], psb[7]
                    for h in range(2):
                        hp = slice(64 * h, 64 * h + 64)
                        hs = slice(64 * h, 64 * h + 64)
                        P.add("pe", lambda e, c=c, h=h, hs=hs: e.matmul(ps1[0:64, hs], lhsT=ARh[h][:, c, 0:64], rhs=Sb[:, :], start=True, stop=False), reads=[ARh[h].r(), Sb.r()], writes=[ps1.r()])
                        P.add("pe", lambda e, c=c, h=h, hs=hs: e.matmul(ps1[0:64, hs], lhsT=NNs[:, c, h * 256 + 128:h * 256 + 192], rhs=Vt[:, c, hs], start=False, stop=True), reads=[NNs.r(), Vt.r()], writes=[ps1.r()])
                    A(lambda e: e.copy(out=W1b[:], in_=ps1[0:64, 0:128]), [ps1.r()], [W1b.r()])
                    for h in range(2):
                        hs = slice(64 * h, 64 * h + 64)
                        P.add("pe", lambda e, c=c, h=h, hs=hs: e.matmul(ps2[0:64, hs], lhsT=TF[:, c * 2 + h, :], rhs=W1b[:, hs], start=True, stop=True), reads=[TF.r(), W1b.r()], writes=[ps2.r()])
                    V(lambda e: e.tensor_copy(out=Ub[:], in_=ps2[0:64, 0:128]), [ps2.r()], [Ub.r()])
                    for h in range(2):
                        hp = slice(64 * h, 64 * h + 64)
                        hs = slice(64 * h, 64 * h + 64)
                        P.add("pe", lambda e, c=c, h=h, hs=hs: e.matmul(psY[0:64, hs], lhsT=ARh[h][:, c, 64:128], rhs=Sb[:, :], start=True, stop=False), reads=[ARh[h].r(), Sb.r()], writes=[psY.r()])
                        P.add("pe", lambda e, c=c, h=h, hs=hs: e.matmul(psY[0:64, hs], lhsT=NNs[:, c, h * 256 + 64:h * 256 + 128], rhs=Ub[:, hs], start=False, stop=False), reads=[NNs.r(), Ub.r()], writes=[psY.r()])
                        P.add("pe", lambda e, c=c, h=h, hs=hs: e.matmul(psY[0:64, hs], lhsT=NNs[:, c, h * 256 + 192:h * 256 + 256], rhs=Vt[:, c, hs], start=False, stop=True), reads=[NNs.r(), Vt.r()], writes=[psY.r()])
                    A(lambda e, c=c: e.copy(out=y_t[:, c, :], in_=psY[0:64, 0:128]), [psY.r()], [y_t.r()])
                    P.add("pe", lambda e, c=c: e.matmul(psS[:, 0:128], lhsT=BhT[:, c, :], rhs=Ub[:], start=True, stop=False), reads=[BhT.r(), Ub.r()], writes=[psS.r()])
                    P.add("pe", lambda e, c=c: e.matmul(psS[:, 0:128], lhsT=KhT[:, c, :], rhs=Vt[:, c, :], start=False, stop=True), reads=[KhT.r(), Vt.r()], writes=[psS.r()])
                    for h in range(2):
                        hp = slice(64 * h, 64 * h + 64)
                        V(lambda e, c=c, hp=hp, h=h, gi=gi, gtot=gtot: e.scalar_tensor_tensor(out=self.Sf[hp, li, gi, :], in0=self.Sf[hp, li, gi, :], scalar=gtot[hp, c:c + 1], in1=psS[hp, 64 * h:64 * h + 64], op0=ALU.mult, op1=ALU.add),
                          [sres, psS.r(), gtot.r(), Sb.r()], [sres])
                    A(lambda e, Sg=Sg: e.copy(out=Sb[:], in_=Sg), [sres], [Sb.r()])
                    if nxt is not None:
                        next(nxt, None)
                if nxt is not None:
                    for _ in nxt:
                        pass
                if dbg <= 7:
                    return
                y3 = y_t[:].rearrange("p c (h i) -> p (c h) i", h=2)
                p3 = pt1[:].rearrange("p c (h i) -> p (c h) i", h=2)
                v3 = Vt[:].rearrange("p c (h i) -> p (c h) i", h=2)
                bc = lambda a: a.unsqueeze(2).to_broadcast([64, nm, 64])
                V(lambda e: e.tensor_reduce(out=st[:, 0:nm], in_=y3, axis=AX.X, op=ALU.add), [y_t.r()], [st.r()])
                V(lambda e: e.tensor_tensor(out=p3, in0=y3, in1=y3, op=ALU.mult), [y_t.r()], [pt1.r()])
                V(lambda e: e.tensor_reduce(out=st[:, 8:8 + nm], in_=p3, axis=AX.X, op=ALU.add), [pt1.r(), st.r()], [st.r()])
                V(lambda e: e.tensor_scalar(out=st[:, 16:16 + nm], in0=st[:, 0:nm], scalar1=1.0 / 64, scalar2=None, op0=ALU.mult), [st.r()], [st.r()])
                V(lambda e: e.tensor_tensor(out=st[:, 24:24 + nm], in0=st[:, 16:16 + nm], in1=st[:, 16:16 + nm], op=ALU.mult), [st.r()], [st.r()])
                V(lambda e: e.scalar_tensor_tensor(out=st[:, 32:32 + nm], in0=st[:, 8:8 + nm], scalar=1.0 / 64, in1=st[:, 24:24 + nm], op0=ALU.mult, op1=ALU.subtract), [st.r()], [st.r()])
                A(lambda e: e.activation(out=st[:, 40:40 + nm], in_=st[:, 32:32 + nm], func=AF.Sqrt, bias=64e-5), [st.r()], [st.r()])
                V(lambda e: e.reciprocal(out=st[:, 40:40 + nm], in_=st[:, 40:40 + nm]), [st.r()], [st.r()])
                V(lambda e: e.tensor_tensor(out=y3, in0=y3, in1=bc(st[:, 16:16 + nm]), op=ALU.subtract), [y_t.r(), st.r()], [y_t.r()])
                V(lambda e: e.tensor_tensor(out=y3, in0=y3, in1=bc(st[:, 40:40 + nm]), op=ALU.mult), [y_t.r(), st.r()], [y_t.r()])
                lb = lambda a: a[:].unsqueeze(1).to_broadcast([64, NC, 128])
                V(lambda e, lnw=lnw: e.tensor_tensor(out=y_t[:], in0=y_t[:], in1=lb(lnw), op=ALU.mult), [y_t.r(), lnw.r()], [y_t.r()])
                V(lambda e, lnb=lnb: e.tensor_tensor(out=y_t[:], in0=y_t[:], in1=lb(lnb), op=ALU.add), [y_t.r(), lnb.r()], [y_t.r()])
                V(lambda e, bon=bon: e.tensor_tensor(out=p3, in0=v3, in1=bc(bon[:, 0:nm]), op=ALU.mult), [Vt.r(), bon.r()], [pt1.r()])
                V(lambda e: e.tensor_tensor(out=y_t[:], in0=y_t[:], in1=pt1[:], op=ALU.add), [y_t.r(), pt1.r()], [y_t.r()])
                V(lambda e, g_t=g_t: e.tensor_tensor(out=BhT[:], in0=y_t[:], in1=g_t[:], op=ALU.mult), [y_t.r(), g_t.r()], [BhT.r()])
                ptb = self.psbf(6)
                for c in range(NC):
                    P.add("pe", lambda e, c=c: e.transpose(out=ptb[:, c * 64:(c + 1) * 64], in_=BhT[:, c, :], identity=self.identb[0:64, 0:64]),
                          reads=[BhT.r(), self.identb.r()], writes=[psb[6].r()])
                A(lambda e, gi=gi: e.copy(out=mixout[:, gi, :], in_=ptb[:, 0:N]), [psb[6].r()], [mixout.r()])
                if dbg == 50:
                    for nm_, bf_ in (("r_f", r_f), ("k_f", k_f), ("v_b", v_b), ("ew", ew), ("asig", asig), ("kk", kk), ("kmod", kmod), ("bb", bb), ("cum", cum),
                                     ("AR", AR), ("Bt", Bt), ("Kt", Kt), ("Bh", Bh), ("Kh", Kh), ("NNs", NNs), ("NabT", NabT), ("TF", TF), ("y_t", y_t), ("yo", BhT), ("Vt", Vt), ("th", th), ("ah", ah), ("gh", gh)):
                        self.dump(nm_, bf_)
                    self.dump("Sg", self.Sf, ap=self.Sf[:, li, gi, :])
                    self.dump("mix0", mixout, ap=mixout[:, 0, :])
                    return
            if dbg <= 8:
                return
            for h in range(4):
                loadw(0, 4608 + h * 128)
                proj(psb[0][:, 0:N], (wX[0], wbig.r(0), psb[0].r()), xres, cur, prv)
                self.mem_attend(psb[0][:, 0:N], psb[0].r(), li, h, N, qf, qn, sqk, pT, rz, mkT, mv, mixout[:, 12 + h, :], mixout.r())
            if dbg <= 9:
                return
            if dbg == 60 and ti == 3:
                self.dump('mixout', mixout)
            wo = wbig[:, 0:KC * 512].rearrange("p (k c) -> p k c", k=KC)
            wsrc = d["w_out"][li].rearrange("(k p) c -> p k c", p=128)
            for dq in range(4):
                P.add("pool", lambda e, dq=dq: e.dma_start(out=wo, in_=wsrc[:, :, dq * 512:(dq + 1) * 512]), writes=[wbig.r(0), wbig.r(1)], dma=True)
                for dl in range(4):
                    dc = dq * 4 + dl
                    pa = psb[1 + dc % 2]
                    for kc in range(KC):
                        P.add("pe", lambda e, pa=pa, kc=kc, dl=dl: e.matmul(pa[:, 0:N], lhsT=wo[:, kc, dl * 128:(dl + 1) * 128], rhs=mixout[:, kc, :], start=(kc == 0), stop=(kc == KC - 1)),
                              reads=[wbig.r(0), wbig.r(1), mixout.r()], writes=[pa.r()])
                    if smp:
                        P.add("dve", lambda e, pa=pa, dc=dc: e.tensor_tensor(out=self.xsT[:, dc:dc + 1], in0=pa[:, 0:1], in1=self.xsT[:, dc:dc + 1], op=ALU.add),
                              reads=[pa.r(), self.xsT.r()], writes=[self.xsT.r()])
                    else:
                        P.add("dve", lambda e, pa=pa, dc=dc, t0=t0: e.tensor_tensor(out=xT[:, dc, t0:t0 + N], in0=pa[:, 0:N], in1=xT[:, dc, t0:t0 + N], op=ALU.add),
                              reads=[pa.r(), xT.r(xkey)], writes=[xT.r(xkey)])
        if smp:
            for gi in range(12):
                pbt = psb[gi % 2]
                P.add("pe", lambda e, gi=gi, pbt=pbt: e.transpose(out=pbt[0:64, 0:128], in_=self.Sf[:, li, gi, :], identity=self.ident[:]),
                      reads=[self.Sf.r((li, gi)), self.ident.r()], writes=[pbt.r()])
                P.add("act", lambda e, gi=gi, pbt=pbt: e.copy(out=so[:, gi, :], in_=pbt[0:64, 0:128]), reads=[pbt.r()], writes=[so.r()])
            P.add("sp", lambda e: e.dma_start(out=self.dout["wkvs"][li].rearrange("(g h) i j -> i g h j", h=2), in_=so[:].rearrange("p g (h j) -> p g h j", h=2)), reads=[so.r()], dma=True)

    def mem_attend(self, psq, psq_res, li, h, N, qf, qn, sqb, pT, rz, mkT, mv, out_ap, out_res):
        P, psb = self.P, self.psb
        SC = 128 ** -0.5
        P.add("act", lambda e: e.copy(out=qf[:, 0:N], in_=psq), reads=[psq_res], writes=[qf.r()])
        P.add("dve", lambda e: e.tensor_tensor(out=sqb[:, 0:N], in0=qf[:, 0:N], in1=qf[:, 0:N], op=ALU.mult), reads=[qf.r()], writes=[sqb.r()])
        pn = psb[5]
        P.add("pe", lambda e: e.matmul(pn[:, 0:N], lhsT=self.onesb[:], rhs=sqb[:, 0:N], start=True, stop=True), reads=[sqb.r(), self.onesb.r()], writes=[pn.r()])
        P.add("act", lambda e: e.activation(out=rz[:, 0:N], in_=pn[:, 0:N], func=AF.Sqrt, scale=1.0 / 128, bias=1e-6), reads=[pn.r()], writes=[rz.r()])
        P.add("dve", lambda e: e.reciprocal(out=rz[:, 0:N], in_=rz[:, 0:N]), reads=[rz.r()], writes=[rz.r()])
        P.add("dve", lambda e: e.scalar_tensor_tensor(out=qn[:, 0:N], in0=qf[:, 0:N], scalar=self.pvc("mem_q_norm", li), in1=rz[:, 0:N], op0=ALU.mult, op1=ALU.mult),
              reads=[qf.r(), rz.r(), self.pv.r()], writes=[qn.r()])
        for mt in range(2):
            pS = psb[6 + mt]
            P.add("pe", lambda e, mt=mt, pS=pS: e.matmul(pS[:, 0:N], lhsT=mkT[:, h, mt * 128:(mt + 1) * 128], rhs=qn[:, 0:N], start=True, stop=True), reads=[mkT.r(), qn.r()], writes=[pS.r()])
            P.add("act", lambda e, mt=mt, pS=pS: e.activation(out=pT[mt][:, 0:N], in_=pS[:, 0:N], func=AF.Exp, scale=SC), reads=[pS.r()], writes=[pT[mt].r()])
        pO, pZ = psb[4], psb[5]
        for mt in range(2):
            P.add("pe", lambda e, mt=mt: e.matmul(pO[:, 0:N], lhsT=mv[:, mt, h * 128:(h + 1) * 128], rhs=pT[mt][:, 0:N], start=(mt == 0), stop=(mt == 1)), reads=[mv.r(), pT[mt].r()], writes=[pO.r()])
        for mt in range(2):
            P.add("pe", lambda e, mt=mt: e.matmul(pZ[:, 0:N], lhsT=self.onesb[:], rhs=pT[mt][:, 0:N], start=(mt == 0), stop=(mt == 1)), reads=[self.onesb.r(), pT[mt].r()], writes=[pZ.r()])
        P.add("dve", lambda e: e.reciprocal(out=rz[:, 0:N], in_=pZ[:, 0:N]), reads=[pZ.r()], writes=[rz.r()])
        P.add("dve", lambda e: e.tensor_tensor(out=out_ap, in0=pO[:, 0:N], in1=rz[:, 0:N], op=ALU.mult), reads=[pO.r(), rz.r()], writes=[out_res])

    def lam_setup(self):
        import math
        P, d = self.P, self.din
        sm = self.small
        row = self.colst
        for lj in range(2):
            lam_init = 0.8 - 0.6 * math.exp(-0.3 * (2 + lj))
            P.add("sp", lambda e, lj=lj: e.dma_start(out=row[0:1, 0:128], in_=d["b_lam"][lj:lj + 1, 0:128]), writes=[row.r()], dma=True)
            P.add("sp", lambda e, lj=lj: e.dma_start(out=row[1:2, 0:128], in_=d["b_lam"][lj:lj + 1, 128:256]), writes=[row.r()], dma=True)
            P.add("dve", lambda e: e.tensor_tensor(out=row[0:2, 0:64], in0=row[0:2, 0:64], in1=row[0:2, 64:128], op=ALU.mult), reads=[row.r()], writes=[row.r()])
            P.add("dve", lambda e: e.tensor_reduce(out=sm[0:2, 40:41], in_=row[0:2, 0:64], axis=AX.X, op=ALU.add), reads=[row.r()], writes=[sm.r("lam")])
            P.add("act", lambda e: e.activation(out=sm[0:2, 40:41], in_=sm[0:2, 40:41], func=AF.Exp), reads=[sm.r("lam")], writes=[sm.r("lam")])
            P.add("dve", lambda e: e.memset(sm[0:2, 41:42], 1.0), reads=[sm.r("lam")], writes=[sm.r("lam2")])
            P.add("dve", lambda e: e.memset(self.colst[0:2, 0:128], -1.0), reads=[row.r(), sm.r("lam")], writes=[row.r()])
            P.add("dve", lambda e: e.memset(self.colst[0:1, 0:128], 1.0), reads=[row.r()], writes=[row.r()])
            pb = self.psb[0]
            P.add("pe", lambda e: e.matmul(pb[:, 0:1], lhsT=row[0:2, 0:128], rhs=sm[0:2, 40:41], start=True, stop=True), reads=[row.r(), sm.r("lam")], writes=[pb.r()])
            P.add("dve", lambda e, lj=lj, lam_init=lam_init: e.tensor_scalar(out=self.lamv[:, 2 * lj:2 * lj + 1], in0=pb[:, 0:1], scalar1=-1.0, scalar2=-lam_init, op0=ALU.mult, op1=ALU.add),
                  reads=[pb.r()], writes=[self.lamv.r()])
            P.add("dve", lambda e, lj=lj, lam_init=lam_init: e.memset(self.lamv[:, 2 * lj + 1:2 * lj + 2], 1.0 - lam_init), reads=[self.lamv.r()], writes=[self.lamv.r()])

    def shared_kv(self, blk, sample):
        P, d, o = self.P, self.din, self.dout
        xn, psb = self.xn, self.psb
        self.phase("kv")
        av = self.av
        self.rmsnorm("kv_norm", 0, sample)
        wkv = [av("wkv0", [KC, 512], BF16), av("wkv1", [KC, 512], BF16)]
        kf = [av("kf0", [512], F32), av("kf1", [512], F32)]
        sqv = av("sqv", [512], F32)
        kb = [av("kb0", [512], BF16), av("kb1", [512], BF16)]
        kts = [av("kts0", [512], BF16), av("kts1", [512], BF16)]
        st = av("stk", [16], F32)
        gk = av("gk64", [64], F32)
        cs = av("cs", [8, 16], F32)
        css = av("css", [16], F32)
        rt = av("rt", [8, 16], F32)
        rt2 = av("rt2", [8, 16], F32)
        P.add("sp", lambda e: e.dma_start(out=gk[:], in_=d["k_norm"][0, :].partition_broadcast(128)), writes=[gk.r()], dma=True)
        P.add("sp", lambda e: e.dma_start(out=cs[:], in_=d["cs_tok"][blk * NT:(blk + 1) * NT, :].rearrange("(t p) f -> p t f", p=128)), writes=[cs.r()], dma=True)
        if sample:
            P.add("sp", lambda e: e.dma_start(out=css[0:1, :], in_=d["cs_tok"][4096:4097, :]), writes=[css.r()], dma=True)
        wv = d["kv_w"].rearrange("(kc p) c -> p kc c", p=128)

        def post(pb, npart, cb, csap, kf_, kb_, out_rows_k, out_rows_v, tb):
            V = lambda fn, reads, writes: P.add("dve", fn, reads=reads, writes=writes)
            pp = slice(0, npart)
            if cb < 3:
                P.add("act", lambda e: e.activation(out=sqv[pp, :], in_=pb[pp, :], func=AF.Square), reads=[pb.r()], writes=[sqv.r()])
                V(lambda e: e.tensor_reduce(out=st[pp, 0:8], in_=sqv[pp, :].rearrange("p (g d) -> p g d", g=8), axis=AX.X, op=ALU.add), [sqv.r()], [st.r()])
                P.add("act", lambda e: e.activation(out=st[pp, 0:8], in_=st[pp, 0:8], func=AF.Sqrt, scale=1.0 / 64, bias=1e-6), reads=[st.r()], writes=[st.r()])
                V(lambda e: e.reciprocal(out=st[pp, 0:8], in_=st[pp, 0:8]), [st.r()], [st.r()])
                k3 = kf_[pp, :].rearrange("p (g d) -> p g d", g=8)
                V(lambda e: e.tensor_tensor(out=k3, in0=pb[pp, :].rearrange("p (g d) -> p g d", g=8), in1=st[pp, 0:8].unsqueeze(2).to_broadcast([npart, 8, 64]), op=ALU.mult), [pb.r(), st.r()], [kf_.r()])
                V(lambda e: e.tensor_tensor(out=k3, in0=k3, in1=gk[pp, :].unsqueeze(1).to_broadcast([npart, 8, 64]), op=ALU.mult), [kf_.r(), gk.r()], [kf_.r()])
                cb_ = csap[:, 0:8].unsqueeze(1).to_broadcast([npart, 8, 8])
                sb_ = csap[:, 8:16].unsqueeze(1).to_broadcast([npart, 8, 8])
                x1, x2 = k3[:, :, 0:8], k3[:, :, 8:16]
                V(lambda e: e.tensor_tensor(out=rt[pp, :, 0:8], in0=x1, in1=cb_, op=ALU.mult), [kf_.r(), cs.r(), css.r()], [rt.r()])
                V(lambda e: e.tensor_tensor(out=rt[pp, :, 8:16], in0=x2, in1=cb_, op=ALU.mult), [kf_.r(), cs.r(), css.r()], [rt.r()])
                V(lambda e: e.tensor_tensor(out=rt2[pp, :, 0:8], in0=x2, in1=sb_, op=ALU.mult), [kf_.r(), cs.r(), css.r()], [rt2.r()])
                V(lambda e: e.tensor_tensor(out=rt2[pp, :, 8:16], in0=x1, in1=sb_, op=ALU.mult), [kf_.r(), cs.r(), css.r()], [rt2.r()])
                V(lambda e: e.tensor_tensor(out=k3[:, :, 0:8], in0=rt[pp, :, 0:8], in1=rt2[pp, :, 0:8], op=ALU.subtract), [rt.r(), rt2.r(), kf_.r()], [kf_.r()])
                V(lambda e: e.tensor_tensor(out=k3[:, :, 8:16], in0=rt[pp, :, 8:16], in1=rt2[pp, :, 8:16], op=ALU.add), [rt.r(), rt2.r(), kf_.r()], [kf_.r()])
                P.add("sp", lambda e: e.dma_start(out=out_rows_k[:, cb * 512:(cb + 1) * 512], in_=kf_[pp, :]), reads=[kf_.r()], dma=True)
                if tb is None:
                    P.add("sp", lambda e: e.dma_start(out=self.ks_scr.t.ap()[:, cb * 512:(cb + 1) * 512], in_=kf_[pp, :]), reads=[kf_.r()], writes=[self.ks_scr.r(cb)], dma=True)
                if tb is not None:
                    P.add("act", lambda e: e.copy(out=kb_[:, :], in_=kf_[:, :]), reads=[kf_.r()], writes=[kb_.r()])
                    ptb = self.psbf(4 + tb % 2)
                    kt_ = kts[tb % 2]
                    for hh in range(4):
                        P.add("pe", lambda e, hh=hh: e.transpose(out=ptb[:, hh * 128:(hh + 1) * 128], in_=kb_[:, hh * 128:(hh + 1) * 128], identity=self.identb[:]),
                              reads=[kb_.r(), self.identb.r()], writes=[psb[4 + tb % 2].r()])
                    P.add("act", lambda e: e.copy(out=kt_[:, :], in_=ptb[:, 0:512]), reads=[psb[4 + tb % 2].r()], writes=[kt_.r()])
                    k0 = blk * NT + tb * 128
                    P.add("sp", lambda e: e.dma_start(out=self.kt_scr.t.ap()[cb * 4:(cb + 1) * 4, :, k0:k0 + 128].rearrange("h p k -> p h k"), in_=kt_[:, :].rearrange("p (h k) -> p h k", h=4)),
                          reads=[kt_.r()], writes=[self.kt_scr.r(blk)], dma=True)
            else:
                vc = cb - 3
                P.add("act", lambda e: e.copy(out=kf_[pp, :], in_=pb[pp, :]), reads=[pb.r()], writes=[kf_.r()])
                P.add("sp", lambda e: e.dma_start(out=out_rows_v[:, vc * 512:(vc + 1) * 512], in_=kf_[pp, :]), reads=[kf_.r()], dma=True)
                if tb is None:
                    P.add("sp", lambda e: e.dma_start(out=self.vs_scr.t.ap()[:, vc * 512:(vc + 1) * 512], in_=kf_[pp, :]), reads=[kf_.r()], writes=[self.vs_scr.r(vc)], dma=True)
                if tb is not None:
                    V(lambda e: e.tensor_copy(out=kb_[:, :], in_=kf_[:, :]), [kf_.r()], [kb_.r()])
                    r0 = blk * NT + tb * 128
                    P.add("sp", lambda e: e.dma_start(out=self.v_scr.t.ap()[r0:r0 + 128, vc * 512:(vc + 1) * 512], in_=kb_[:, :]), reads=[kb_.r()], writes=[self.v_scr.r(blk)], dma=True)

        for cb in range(6):
            w = wkv[cb % 2]
            P.add("pool", lambda e, w=w, cb=cb: e.dma_start(out=w[:], in_=wv[:, :, cb * 512:(cb + 1) * 512]), writes=[w.r()], dma=True)
            for tb in range(8):
                pb = psb[tb % 4]
                for kc in range(KC):
                    P.add("pe", lambda e, pb=pb, w=w, kc=kc, tb=tb: e.matmul(pb[:, :], lhsT=xn[:, kc, 1 + tb * 128:1 + (tb + 1) * 128], rhs=w[:, kc, :], start=(kc == 0), stop=(kc == KC - 1)),
                          reads=[w.r(), xn.r(tb // 4)], writes=[pb.r()])
                r0 = blk * NT + tb * 128
                post(pb, 128, cb, cs[:, tb, :], kf[tb % 2], kb[tb % 2], o["krp"][r0:r0 + 128, :], o["vrp"][r0:r0 + 128, :], tb)
            if sample:
                pb = psb[6]
                for kc in range(KC):
                    P.add("pe", lambda e, pb=pb, w=w, kc=kc: e.matmul(pb[0:1, :], lhsT=self.xns[:, kc, 1:2], rhs=w[:, kc, :], start=(kc == 0), stop=(kc == KC - 1)),
                          reads=[w.r(), self.xns.r("cur")], writes=[pb.r()])
                post(pb, 1, cb, css[0:1, :], kf[0], kb[0], o["krs"], o["vrs"], None)

    def mixerB(self, lj, blk, sample):
        P, d = self.P, self.din
        xn, xT, psb = self.xn, self.xT, self.psb
        li = 2 + lj
        self.phase("mixB")
        av = self.av
        N = 512
        nkeys = (blk + 1) * NT
        KT = [av("KT0", [NBLK * NT], BF16), av("KT1", [NBLK * NT], BF16)]
        VV = [av("VV0", [NBLK * 8, 128], BF16), av("VV1", [NBLK * 8, 128], BF16)]
        wq = [av("wq0", [KC, 128], BF16), av("wq1", [KC, 128], BF16)]
        qf = av("qf", [N], F32)
        sqb = av("sqb", [N], BF16)
        rz = av("rz", [N], F32)
        qn = av("qn", [N], BF16)
        rC = av("rC", [N], F32)
        rS = av("rS", [N], F32)
        qm = [av("qm0", [N], BF16), av("qm1", [N], BF16)]
        pT = [av(f"pT{i}", [N], BF16) for i in range(4)]
        o0 = av("o0", [N], F32)
        o1 = av("o1", [N], F32)
        mixout = av("mixout", [KC, N], BF16)
        mkT = av("mkT", [4, MEMT], BF16)
        mv = av("mv", [2, 512], BF16)
        wo = av("wo", [KC, 256], BF16)
        SC = 64 ** -0.5
        V = lambda fn, reads, writes: P.add("dve", fn, reads=reads, writes=writes)
        A = lambda fn, reads, writes: P.add("act", fn, reads=reads, writes=writes)
        P.add("sp", lambda e: e.dma_start(out=mkT[:].rearrange("p h m -> p (h m)"), in_=self.mk_scr.t.ap()[li]), reads=[self.mk_scr.r(li)], writes=[mkT.r()], dma=True)
        P.add("sp", lambda e: e.dma_start(out=mv[:], in_=self.mv_scr.t.ap()[li].rearrange("(t p) c -> p t c", p=128)), reads=[self.mv_scr.r(li)], writes=[mv.r()], dma=True)
        win = d["b_w_in"][lj].rearrange("(k p) c -> p k c", p=128)
        kvres = [self.kt_scr.r(bb_) for bb_ in range(blk + 1)] + [self.v_scr.r(bb_) for bb_ in range(blk + 1)]
        for qt in range(2):
            t0 = qt * N
            q0 = blk * NT + t0
            xres = [xn.r(qt)]
            P.add("sp", lambda e, q0=q0: e.dma_start(out=rC[:], in_=d["ropeC"][:, q0:q0 + N]), writes=[rC.r()], dma=True)
            P.add("sp", lambda e, q0=q0: e.dma_start(out=rS[:], in_=d["ropeS"][:, q0:q0 + N]), writes=[rS.r()], dma=True)
            nkt = (q0 + N) // 128
            for hd in range(12):
                it = qt * 12 + hd
                w = wq[it % 2]
                kt_, vv_ = KT[it % 2], VV[it % 2]
                P.add("pool", lambda e, w=w, hd=hd: e.dma_start(out=w[:], in_=win[:, :, hd * 128:(hd + 1) * 128]), writes=[w.r()], dma=True)
                P.add("sp", lambda e, kt_=kt_, hd=hd, nkt=nkt: e.dma_start(out=kt_[:, 0:nkt * 128], in_=self.kt_scr.t.ap()[hd, :, 0:nkt * 128]), reads=kvres, writes=[kt_.r()], dma=True)
                P.add("sp", lambda e, vv_=vv_, hd=hd, nkt=nkt: e.dma_start(out=vv_[:, 0:nkt, :], in_=self.v_scr.t.ap()[0:nkt * 128, hd * 128:(hd + 1) * 128].rearrange("(t p) c -> p t c", p=128)),
                      reads=kvres, writes=[vv_.r()], dma=True)
                pq = psb[0]
                for kc in range(KC):
                    P.add("pe", lambda e, w=w, kc=kc, t0=t0: e.matmul(pq[:, 0:N], lhsT=w[:, kc, :], rhs=xn[:, kc, 1 + t0:1 + t0 + N], start=(kc == 0), stop=(kc == KC - 1)),
                          reads=[w.r()] + xres, writes=[pq.r()])
                A(lambda e: e.copy(out=qf[:], in_=pq[:, 0:N]), [pq.r()], [qf.r()])
                V(lambda e: e.tensor_tensor(out=sqb[:], in0=qf[:], in1=qf[:], op=ALU.mult), [qf.r()], [sqb.r()])
                pn = psb[1]
                P.add("pe", lambda e: e.matmul(pn[:, 0:N], lhsT=self.blk64b[:], rhs=sqb[:], start=True, stop=True), reads=[sqb.r(), self.blk64b.r()], writes=[pn.r()])
                A(lambda e: e.activation(out=rz[:], in_=pn[:, 0:N], func=AF.Sqrt, scale=1.0 / 64, bias=1e-6), [pn.r()], [rz.r()])
                V(lambda e: e.reciprocal(out=rz[:], in_=rz[:]), [rz.r()], [rz.r()])
                V(lambda e: e.scalar_tensor_tensor(out=qf[:], in0=qf[:], scalar=self.pvc("b_q_norm", lj), in1=rz[:], op0=ALU.mult, op1=ALU.mult), [qf.r(), rz.r(), self.pv.r()], [qf.r()])
                V(lambda e: e.tensor_copy(out=qn[:], in_=qf[:]), [qf.r()], [qn.r()])
                P.add("pe", lambda e: e.matmul(pn[:, 0:N], lhsT=self.rotb[:], rhs=qn[:], start=True, stop=True), reads=[qn.r(), self.rotb.r()], writes=[pn.r()])
                V(lambda e: e.tensor_tensor(out=rz[:], in0=pn[:, 0:N], in1=rS[:], op=ALU.mult), [pn.r(), rS.r(), rz.r()], [rz.r()])
                V(lambda e: e.tensor_tensor(out=qf[:], in0=qf[:], in1=rC[:], op=ALU.mult), [qf.r(), rC.r()], [qf.r()])
                V(lambda e: e.tensor_tensor(out=qf[:], in0=qf[:], in1=rz[:], op=ALU.add), [qf.r(), rz.r()], [qf.r()])
                for c in range(2):
                    V(lambda e, c=c: e.tensor_scalar(out=qm[c][:], in0=qf[:], scalar1=self.c128[:, 128 + c:129 + c], scalar2=None, op0=ALU.mult), [qf.r(), self.c128.r()], [qm[c].r()])
                pO = [psb[2], psb[3]]
                pZ = [psb[4], psb[5]]
                for kt in range(nkt):
                    r = kt - (q0 // 128)
                    for c in range(2):
                        pS = psb[6 + c]
                        pt_ = pT[(kt % 2) * 2 + c]
                        P.add("pe", lambda e, kt=kt, c=c, pS=pS, kt_=kt_: e.matmul(pS[:, 0:N], lhsT=kt_[:, kt * 128:(kt + 1) * 128], rhs=qm[c][:], start=True, stop=True),
                              reads=[kt_.r(), qm[c].r()], writes=[pS.r()])
                        A(lambda e, pS=pS, pt_=pt_: e.activation(out=pt_[:], in_=pS[:, 0:N], func=AF.Exp, scale=SC), [pS.r()], [pt_.r()])
                        if r >= 0:
                            P.add("pool", lambda e, pt_=pt_, r=r: e.tensor_tensor(out=pt_[:], in0=pt_[:], in1=self.dmask[:, r * 512:(r + 1) * 512], op=ALU.mult),
                                  reads=[pt_.r(), self.dmask.r()], writes=[pt_.r()])
                        P.add("pe", lambda e, kt=kt, c=c, pt_=pt_, vv_=vv_, nkt=nkt: e.matmul(pO[c][:, 0:N], lhsT=vv_[:, kt, :], rhs=pt_[:], start=(kt == 0), stop=(kt == nkt - 1)),
                              reads=[vv_.r(), pt_.r()], writes=[pO[c].r()])
                        P.add("pe", lambda e, kt=kt, c=c, pt_=pt_, nkt=nkt: e.matmul(pZ[c][:, 0:N], lhsT=self.onesb[:], rhs=pt_[:], start=(kt == 0), stop=(kt == nkt - 1)),
                              reads=[self.onesb.r(), pt_.r()], writes=[pZ[c].r()])
                V(lambda e: e.reciprocal(out=rz[:], in_=pZ[0][:, 0:N]), [pZ[0].r()], [rz.r()])
                V(lambda e: e.tensor_tensor(out=o0[:], in0=pO[0][:, 0:N], in1=rz[:], op=ALU.mult), [pO[0].r(), rz.r()], [o0.r()])
                V(lambda e: e.reciprocal(out=rz[:], in_=pZ[1][:, 0:N]), [pZ[1].r(), o0.r()], [rz.r()])
                V(lambda e: e.tensor_tensor(out=o1[:], in0=pO[1][:, 0:N], in1=rz[:], op=ALU.mult), [pO[1].r(), rz.r()], [o1.r()])
                V(lambda e: e.scalar_tensor_tensor(out=o0[:], in0=o1[:], scalar=self.lamv[:, 2 * lj:2 * lj + 1], in1=o0[:], op0=ALU.mult, op1=ALU.add), [o0.r(), o1.r(), self.lamv.r()], [o0.r()])
                V(lambda e: e.tensor_tensor(out=sqb[:], in0=o0[:], in1=o0[:], op=ALU.mult), [o0.r()], [sqb.r()])
                P.add("pe", lambda e: e.matmul(pn[:, 0:N], lhsT=self.onesb[:], rhs=sqb[:], start=True, stop=True), reads=[sqb.r(), self.onesb.r()], writes=[pn.r()])
                A(lambda e: e.activation(out=rz[:], in_=pn[:, 0:N], func=AF.Sqrt, scale=1.0 / 128, bias=1e-5), [pn.r()], [rz.r()])
                V(lambda e: e.reciprocal(out=rz[:], in_=rz[:]), [rz.r()], [rz.r()])
                V(lambda e: e.scalar_tensor_tensor(out=o0[:], in0=o0[:], scalar=self.pvc("b_subln", lj), in1=rz[:], op0=ALU.mult, op1=ALU.mult), [o0.r(), rz.r(), self.pv.r()], [o0.r()])
                V(lambda e, hd=hd: e.tensor_scalar(out=mixout[:, hd, :], in0=o0[:], scalar1=self.lamv[:, 2 * lj + 1:2 * lj + 2], scalar2=None, op0=ALU.mult), [o0.r(), self.lamv.r()], [mixout.r()])
                import os
                if int(os.environ.get('KDBG', '99')) == 70:
                    self.dump("qf", qf); self.dump("qm0", qm[0]); self.dump("o0", o0); self.dump("o1", o1); self.dump("lamv", self.lamv); self.dump("KT", kt_, ap=kt_[:, 0:512])
                    self.dump("VV", vv_, ap=vv_[:, 0:4, :]); self.dump("mix0", mixout, ap=mixout[:, 0, :]); self.dump("pT0", pT[0]); self.dump("rz", rz)
                    return
            for h in range(4):
                it = h
                w = wq[it % 2]
                P.add("pool", lambda e, w=w, h=h: e.dma_start(out=w[:], in_=win[:, :, 1536 + h * 128:1536 + (h + 1) * 128]), writes=[w.r()], dma=True)
                pq = psb[0]
                for kc in range(KC):
                    P.add("pe", lambda e, w=w, kc=kc, t0=t0: e.matmul(pq[:, 0:N], lhsT=w[:, kc, :], rhs=xn[:, kc, 1 + t0:1 + t0 + N], start=(kc == 0), stop=(kc == KC - 1)),
                          reads=[w.r()] + xres, writes=[pq.r()])
                self.mem_attend(pq[:, 0:N], pq.r(), li, h, N, qf, qn, sqb, [pT[0], pT[1]], rz, mkT, mv, mixout[:, 12 + h, :], mixout.r())
            wsrc = d["w_out"][li].rearrange("(k p) c -> p k c", p=128)
            for dq in range(8):
                P.add("pool", lambda e, dq=dq: e.dma_start(out=wo[:], in_=wsrc[:, :, dq * 256:(dq + 1) * 256]), writes=[wo.r()], dma=True)
                for dl in range(2):
                    dc = dq * 2 + dl
                    pa = psb[1 + dc % 2]
                    for kc in range(KC):
                        P.add("pe", lambda e, pa=pa, kc=kc, dl=dl: e.matmul(pa[:, 0:N], lhsT=wo[:, kc, dl * 128:(dl + 1) * 128], rhs=mixout[:, kc, :], start=(kc == 0), stop=(kc == KC - 1)),
                              reads=[wo.r(), mixout.r()], writes=[pa.r()])
                    P.add("dve", lambda e, pa=pa, dc=dc, t0=t0: e.tensor_tensor(out=xT[:, dc, t0:t0 + N], in0=pa[:, 0:N], in1=xT[:, dc, t0:t0 + N], op=ALU.add),
                          reads=[pa.r(), xT.r(qt)], writes=[xT.r(qt)])

    def mixerB_s(self, lj):
        P, d = self.P, self.din
        psb = self.psb
        li = 2 + lj
        self.phase("mixBs")
        av = self.av
        N = 64
        NPG = 128
        V = lambda fn, reads, writes: P.add("dve", fn, reads=reads, writes=writes)
        A = lambda fn, reads, writes: P.add("act", fn, reads=reads, writes=writes)
        xst = av("xst", [KC, N], BF16)
        wq = [av("wq0", [KC, 128], BF16), av("wq1", [KC, 128], BF16)]
        qs = av("qs", [16], F32)
        qsb = av("qsb", [16], BF16)
        rz = av("rz", [N], F32)
        rcs = av("rcs", [2], F32)
        qrow = av("qrow", [128], F32)
        qbc = av("qbc", [1536], F32)
        ptb = av("ptb", [128], I32)
        ptf = av("ptf", [128], F32)
        idx = av("idx", [128], I32)
        kp = [av(f"kp{i}", [1536], F32) for i in range(3)]
        prod = av("prod", [1536], F32)
        S = av("S", [NPG + 1, 24], F32)
        mx = av("mx", [24], F32)
        mxT = av("mxT", [128], F32)
        dg = av("dg", [24], F32)
        mxb = av("mxb", [24], F32)
        o24 = av("o24", [12, 128], F32)
        od = av("od", [128], F32)
        zc = av("zc", [2], F32)
        oT = av("oT", [24], F32)
        of = av("of", [16], F32)
        sqb = av("sqb", [N], BF16)
        qf = av("qf", [N], F32)
        qn = av("qn", [N], BF16)
        pT = [av("pT0", [N], BF16), av("pT1", [N], BF16)]
        mixs = av("mixs", [KC, N], BF16)
        mkT = av("mkT", [4, MEMT], BF16)
        mv = av("mv", [2, 512], BF16)
        wo = av("wo", [KC, 256], BF16)
        P.add("sp", lambda e: e.dma_start(out=mkT[:].rearrange("p h m -> p (h m)"), in_=self.mks_scr.t.ap()[li]), reads=[self.mks_scr.r(li)], writes=[mkT.r()], dma=True)
        P.add("sp", lambda e: e.dma_start(out=mv[:], in_=self.mvs_scr.t.ap()[li].rearrange("(t p) c -> p t c", p=128)), reads=[self.mvs_scr.r(li)], writes=[mv.r()], dma=True)
        P.add("sp", lambda e: e.dma_start(out=rcs[:, 0:1], in_=d["ropeC"][:, 4096:4097], allow_slow_non_contiguous=True), writes=[rcs.r()], dma=True)
        P.add("sp", lambda e: e.dma_start(out=rcs[:, 1:2], in_=d["ropeS"][:, 4096:4097], allow_slow_non_contiguous=True), writes=[rcs.r()], dma=True)
        V(lambda e: e.memset(xst[:], 0.0), [], [xst.r()])
        V(lambda e: e.tensor_copy(out=xst[:, :, 0], in_=self.xns[:, :, 1]), [self.xns.r("cur"), xst.r()], [xst.r()])
        V(lambda e: e.memset(mixs[:], 0.0), [], [mixs.r()])
        win = d["b_w_in"][lj].rearrange("(k p) c -> p k c", p=128)
        pq = psb[0]
        for hd in range(12):
            w = wq[hd % 2]
            P.add("pool", lambda e, w=w, hd=hd: e.dma_start(out=w[:], in_=win[:, :, hd * 128:(hd + 1) * 128]), writes=[w.r()], dma=True)
            for kc in range(KC):
                P.add("pe", lambda e, w=w, kc=kc, hd=hd: e.matmul(pq[:, hd:hd + 1], lhsT=w[:, kc, :], rhs=xst[:, kc, 0:1], start=(kc == 0), stop=(kc == KC - 1)),
                      reads=[w.r(), xst.r()], writes=[pq.r()])
        A(lambda e: e.copy(out=qs[:, 0:12], in_=pq[:, 0:12]), [pq.r()], [qs.r()])
        V(lambda e: e.tensor_tensor(out=qsb[:, 0:12], in0=qs[:, 0:12], in1=qs[:, 0:12], op=ALU.mult), [qs.r()], [qsb.r()])
        pn = psb[1]
        P.add("pe", lambda e: e.matmul(pn[:, 0:12], lhsT=self.blk64b[:], rhs=qsb[:, 0:12], start=True, stop=True), reads=[qsb.r(), self.blk64b.r()], writes=[pn.r()])
        A(lambda e: e.activation(out=rz[:, 0:12], in_=pn[:, 0:12], func=AF.Sqrt, scale=1.0 / 64, bias=1e-6), [pn.r()], [rz.r()])
        V(lambda e: e.reciprocal(out=rz[:, 0:12], in_=rz[:, 0:12]), [rz.r()], [rz.r()])
        V(lambda e: e.scalar_tensor_tensor(out=qs[:, 0:12], in0=qs[:, 0:12], scalar=self.pvc("b_q_norm", lj), in1=rz[:, 0:12], op0=ALU.mult, op1=ALU.mult), [qs.r(), rz.r(), self.pv.r()], [qs.r()])
        V(lambda e: e.tensor_copy(out=qsb[:, 0:12], in_=qs[:, 0:12]), [qs.r(), qsb.r()], [qsb.r()])
        P.add("pe", lambda e: e.matmul(pn[:, 0:12], lhsT=self.rotb[:], rhs=qsb[:, 0:12], start=True, stop=True), reads=[qsb.r(), self.rotb.r()], writes=[pn.r()])
        V(lambda e: e.tensor_scalar(out=rz[:, 0:12], in0=pn[:, 0:12], scalar1=rcs[:, 1:2], scalar2=None, op0=ALU.mult), [pn.r(), rcs.r(), rz.r()], [rz.r()])
        V(lambda e: e.scalar_tensor_tensor(out=qs[:, 0:12], in0=qs[:, 0:12], scalar=rcs[:, 0:1], in1=rz[:, 0:12], op0=ALU.mult, op1=ALU.add), [qs.r(), rz.r(), rcs.r()], [qs.r()])
        V(lambda e: e.tensor_scalar(out=qs[:, 0:12], in0=qs[:, 0:12], scalar1=64 ** -0.5, scalar2=None, op0=ALU.mult), [qs.r()], [qs.r()])
        P.add("pe", lambda e: e.transpose(out=pn[0:12, 0:128], in_=qs[:, 0:12], identity=self.ident[:]), reads=[qs.r(), self.ident.r()], writes=[pn.r()])
        A(lambda e: e.copy(out=qrow[0:12, :], in_=pn[0:12, 0:128]), [pn.r()], [qrow.r()])
        P.add("sp", lambda e: e.dma_start(out=self.q_scr.t.ap()[lj], in_=qrow[0:12, :]), reads=[qrow.r()], writes=[self.q_scr.r(lj)], dma=True)
        P.add("sp", lambda e: e.dma_start(out=qbc[:], in_=self.q_scr.t.ap()[lj].rearrange("h f -> (h f)").partition_broadcast(128)), reads=[self.q_scr.r(lj)], writes=[qbc.r()], dma=True)
        P.add("sp", lambda e: e.dma_start(out=ptb[:], in_=d["pt"][0, :].partition_broadcast(128)), writes=[ptb.r()], dma=True)
        V(lambda e: e.tensor_copy(out=ptf[:], in_=ptb[:]), [ptb.r()], [ptf.r()])
        V(lambda e: e.tensor_scalar(out=ptf[:], in0=ptf[:], scalar1=128.0, scalar2=self.c128[:, 130:131], op0=ALU.mult, op1=ALU.add), [ptf.r(), self.c128.r()], [ptf.r()])
        V(lambda e: e.tensor_copy(out=idx[:], in_=ptf[:]), [ptf.r()], [idx.r()])
        q3 = qbc[:].rearrange("p (g d) -> p g d", d=64)
        for pg in range(NPG + 1):
            kb_ = kp[pg % 3]
            if pg < NPG:
                P.add("pool", lambda e, kb_=kb_, pg=pg: e.indirect_dma_start(out=kb_[:], out_offset=None, in_=d["ck"], in_offset=bass.IndirectOffsetOnAxis(ap=idx[:, pg:pg + 1], axis=0)),
                      reads=[idx.r()], writes=[kb_.r()], dma=True)
            else:
                V(lambda e, kb_=kb_: e.memset(kb_[:], 0.0), [], [kb_.r()])
                P.add("sp", lambda e, kb_=kb_: e.dma_start(out=kb_[0:1, :], in_=self.ks_scr.t.ap()), reads=[self.ks_scr.r(0), self.ks_scr.r(1), self.ks_scr.r(2), kb_.r()], writes=[kb_.r()], dma=True)
            V(lambda e, kb_=kb_: e.tensor_tensor(out=prod[:], in0=kb_[:], in1=qbc[:], op=ALU.mult), [kb_.r(), qbc.r()], [prod.r()])
            V(lambda e, pg=pg: e.tensor_reduce(out=S[:, pg, :], in_=prod[:].rearrange("p (g d) -> p g d", d=64), axis=AX.X, op=ALU.add), [prod.r()], [S.r()])
        V(lambda e: e.tensor_scalar(out=S[:, NPG, :], in0=S[:, NPG, :], scalar1=self.c128[:, 131:132], scalar2=None, op0=ALU.add), [S.r(), self.c128.r()], [S.r()])
        V(lambda e: e.tensor_reduce(out=mx[:], in_=S[:].rearrange("p g c -> p c g"), axis=AX.X, op=ALU.max), [S.r()], [mx.r()])
        P.add("pe", lambda e: e.transpose(out=pn[0:24, 0:128], in_=mx[:, 0:24], identity=self.ident[:]), reads=[mx.r(), self.ident.r()], writes=[pn.r()])
        A(lambda e: e.copy(out=mxT[0:24, :], in_=pn[0:24, 0:128]), [pn.r()], [mxT.r()])
        V(lambda e: e.tensor_reduce(out=zc[0:24, 0:1], in_=mxT[0:24, :], axis=AX.X, op=ALU.max), [mxT.r()], [zc.r()])
        V(lambda e: e.tensor_scalar(out=dg[0:24, 0:24], in0=self.ident[0:24, 0:24], scalar1=zc[0:24, 0:1], scalar2=None, op0=ALU.mult), [zc.r(), self.ident.r()], [dg.r()])
        P.add("pe", lambda e: e.matmul(pn[:, 0:24], lhsT=self.onesf[0:24, :], rhs=dg[0:24, 0:24], start=True, stop=True), reads=[dg.r(), self.onesf.r()], writes=[pn.r()])
        A(lambda e: e.copy(out=mxb[:], in_=pn[:, 0:24]), [pn.r()], [mxb.r()])
        V(lambda e: e.tensor_tensor(out=S[:], in0=S[:], in1=mxb[:].unsqueeze(1).to_broadcast([128, NPG + 1, 24]), op=ALU.subtract), [S.r(), mxb.r()], [S.r()])
        A(lambda e: e.activation(out=S[:].rearrange("p g c -> p (g c)"), in_=S[:].rearrange("p g c -> p (g c)"), func=AF.Exp), [S.r()], [S.r()])
        pO = [psb[1], psb[2], psb[3]]
        pZ = psb[4]
        for pg in range(NPG + 1):
            vb_ = kp[pg % 3]
            if pg < NPG:
                P.add("pool", lambda e, vb_=vb_, pg=pg: e.indirect_dma_start(out=vb_[:], out_offset=None, in_=d["cv"], in_offset=bass.IndirectOffsetOnAxis(ap=idx[:, pg:pg + 1], axis=0)),
                      reads=[idx.r()], writes=[vb_.r()], dma=True)
            else:
                V(lambda e, vb_=vb_: e.memset(vb_[:], 0.0), [], [vb_.r()])
                P.add("sp", lambda e, vb_=vb_: e.dma_start(out=vb_[0:1, :], in_=self.vs_scr.t.ap()), reads=[self.vs_scr.r(0), self.vs_scr.r(1), self.vs_scr.r(2), vb_.r()], writes=[vb_.r()], dma=True)
            for cb in range(3):
                P.add("pe", lambda e, vb_=vb_, pg=pg, cb=cb: e.matmul(pO[cb][0:24, :], lhsT=S[:, pg, :], rhs=vb_[:, cb * 512:(cb + 1) * 512], start=(pg == 0), stop=(pg == NPG)),
                      reads=[S.r(), vb_.r()], writes=[pO[cb].r()])
            P.add("pe", lambda e, pg=pg: e.matmul(pZ[0:24, 0:1], lhsT=S[:, pg, :], rhs=self.onesf[:, 0:1], start=(pg == 0), stop=(pg == NPG)),
                  reads=[S.r(), self.onesf.r()], writes=[pZ.r()])
        V(lambda e: e.reciprocal(out=zc[0:24, 1:2], in_=pZ[0:24, 0:1]), [pZ.r(), zc.r()], [zc.r()])
        for cb in range(3):
            A(lambda e, cb=cb: e.copy(out=o24[0:24, cb * 4:(cb + 1) * 4, :], in_=pO[cb][0:24, :].rearrange("p (h d) -> p h d", h=4)), [pO[cb].r()], [o24.r()])
        V(lambda e: e.tensor_tensor(out=o24[0:24, :, :], in0=o24[0:24, :, :], in1=self.c128[0:24, 132:144].unsqueeze(2).to_broadcast([24, 12, 128]), op=ALU.mult), [o24.r(), self.c128.r()], [o24.r()])
        V(lambda e: e.tensor_reduce(out=od[0:24, :], in_=o24[0:24, :, :].rearrange("p h d -> p d h"), axis=AX.X, op=ALU.add), [o24.r()], [od.r()])
        V(lambda e: e.tensor_scalar(out=od[0:24, :], in0=od[0:24, :], scalar1=zc[0:24, 1:2], scalar2=None, op0=ALU.mult), [od.r(), zc.r()], [od.r()])
        P.add("pe", lambda e: e.transpose(out=pn[:, 0:24], in_=od[0:24, :], identity=self.ident[0:24, 0:24]), reads=[od.r(), self.ident.r()], writes=[pn.r()])
        A(lambda e: e.copy(out=oT[:], in_=pn[:, 0:24]), [pn.r()], [oT.r()])
        o3 = oT[:].rearrange("p (h c) -> p h c", c=2)
        V(lambda e: e.scalar_tensor_tensor(out=of[:, 0:12], in0=o3[:, :, 1], scalar=self.lamv[:, 2 * lj:2 * lj + 1], in1=o3[:, :, 0], op0=ALU.mult, op1=ALU.add), [oT.r(), self.lamv.r()], [of.r()])
        V(lambda e: e.tensor_tensor(out=qsb[:, 0:12], in0=of[:, 0:12], in1=of[:, 0:12], op=ALU.mult), [of.r(), qsb.r()], [qsb.r()])
        P.add("pe", lambda e: e.matmul(pn[:, 0:12], lhsT=self.onesb[:], rhs=qsb[:, 0:12], start=True, stop=True), reads=[qsb.r(), self.onesb.r()], writes=[pn.r()])
        A(lambda e: e.activation(out=rz[:, 0:12], in_=pn[:, 0:12], func=AF.Sqrt, scale=1.0 / 128, bias=1e-5), [pn.r()], [rz.r()])
        V(lambda e: e.reciprocal(out=rz[:, 0:12], in_=rz[:, 0:12]), [rz.r()], [rz.r()])
        V(lambda e: e.scalar_tensor_tensor(out=of[:, 0:12], in0=of[:, 0:12], scalar=self.pvc("b_subln", lj), in1=rz[:, 0:12], op0=ALU.mult, op1=ALU.mult), [of.r(), rz.r(), self.pv.r()], [of.r()])
        V(lambda e: e.tensor_scalar(out=mixs[:, 0:12, 0], in0=of[:, 0:12], scalar1=self.lamv[:, 2 * lj + 1:2 * lj + 2], scalar2=None, op0=ALU.mult), [of.r(), self.lamv.r(), mixs.r()], [mixs.r()])
        for h in range(4):
            w = wq[h % 2]
            P.add("pool", lambda e, w=w, h=h: e.dma_start(out=w[:], in_=win[:, :, 1536 + h * 128:1536 + (h + 1) * 128]), writes=[w.r()], dma=True)
            for kc in range(KC):
                P.add("pe", lambda e, w=w, kc=kc: e.matmul(pq[:, 0:N], lhsT=w[:, kc, :], rhs=xst[:, kc, :], start=(kc == 0), stop=(kc == KC - 1)), reads=[w.r(), xst.r()], writes=[pq.r()])
            self.mem_attend(pq[:, 0:N], pq.r(), li, h, N, qf, qn, sqb, pT, rz, mkT, mv, mixs[:, 12 + h, :], mixs.r())
        wsrc = d["w_out"][li].rearrange("(k p) c -> p k c", p=128)
        for dq in range(8):
            P.add("pool", lambda e, dq=dq: e.dma_start(out=wo[:], in_=wsrc[:, :, dq * 256:(dq + 1) * 256]), writes=[wo.r()], dma=True)
            for dl in range(2):
                dc = dq * 2 + dl
                pa = psb[5 + dc % 2]
                for kc in range(KC):
                    P.add("pe", lambda e, pa=pa, kc=kc, dl=dl: e.matmul(pa[:, 0:N], lhsT=wo[:, kc, dl * 128:(dl + 1) * 128], rhs=mixs[:, kc, :], start=(kc == 0), stop=(kc == KC - 1)),
                          reads=[wo.r(), mixs.r()], writes=[pa.r()])
                V(lambda e, pa=pa, dc=dc: e.tensor_tensor(out=self.xsT[:, dc:dc + 1], in0=pa[:, 0:1], in1=self.xsT[:, dc:dc + 1], op=ALU.add), [pa.r(), self.xsT.r()], [self.xsT.r()])

    def dump(self, name, buf, ap=None):
        ap = buf[:] if ap is None else ap
        shape = list(ap.shape)
        dt_ = ap.dtype
        t = self.nc.dram_tensor("dbg_" + name, shape, dt_, kind="ExternalOutput").ap()
        rs = list(buf._res.values()) or [buf.r()]
        self.P.add("sp", lambda e: e.dma_start(out=t, in_=ap), reads=rs, dma=True)

    def store_wkv(self, li):
        P, o = self.P, self.dout
        self.phase("wkv")
        so = self.av("so", [12, 128], F32, parts=64)
        for gi in range(12):
            pb = self.psb[gi % 2]
            P.add("pe", lambda e, gi=gi, pb=pb: e.transpose(out=pb[0:64, 0:128], in_=self.Sf[:, li, gi, :], identity=self.ident[:]),
                  reads=[self.Sf.r((li, gi)), self.Sf.r(), self.ident.r()], writes=[pb.r()])
            P.add("act", lambda e, gi=gi, pb=pb: e.copy(out=so[:, gi, :], in_=pb[0:64, 0:128]), reads=[pb.r()], writes=[so.r()])
        P.add("sp", lambda e: e.dma_start(out=o["wkvp"][li].rearrange("(g h) i j -> i g h j", h=2), in_=so[:].rearrange("p g (h j) -> p g h j", h=2)), reads=[so.r()], dma=True)


def build_program(pvoff, npv, stage):
    kb = KB(pvoff, npv, stage)
    P = kb.P
    kb.setup()
    kb.mem_all()
    kb.mem_sample()
    for blk in range(NBLK):
        last = blk == NBLK - 1
        sample = last
        kb.load_x(blk)
        for li in range(2):
            kb.ffn(li, 0, sample)
            kb.phase("norm")
            kb.rmsnorm("mix_norm", li, sample, last_out=kb.xlast if last else None, shift_li=li)
            if last:
                kb.store_col(kb.xlast, kb.dout["shp"][li], kb.xlast.r())
                kb.store_col(kb.xslast, kb.dout["shs"][li], kb.xslast.r())
            kb.mixerA(li, blk, False)
            if last:
                kb.store_wkv(li)
                kb.mixerA(li, blk, False, smp=True)
            kb.ffn(li, 1, sample)
        kb.shared_kv(blk, sample)
        for lj in range(2):
            kb.ffn(2 + lj, 0, sample)
            kb.phase("norm")
            kb.rmsnorm("mix_norm", 2 + lj, sample)
            kb.mixerB(lj, blk, False)
            if last:
                kb.mixerB_s(lj)
            kb.ffn(2 + lj, 1, sample)
        kb.store_x(blk)
    P.barrier()
    P.emit()
    P.close()
    return kb.nc


STAGE = 3


def kernel(**inp):
    inp = {k: np.asarray(v) for k, v in inp.items()}
    pvp = pack_params(inp)
    pv = pvp.build()
    nc = build_program(pvp.off, pvp.n, STAGE)
    hc = host_consts()
    w13 = inp["ffn_w13"].reshape(8, D, 2 * DFF)
    w2 = inp["ffn_w2"].reshape(8, DFF, D)
    shared = dict(
        pv=pv, ident=hc["ident"], c64=hc["c64"], c128=hc["c128"], ffn_w13=w13, ffn_w2=w2,
        mem_w_kv=inp["mem_w_kv"], mem_k_norm=inp["mem_k_norm"],
        a_w_in=inp["a_w_in"], a_w1=inp["a_w1"], a_w2=inp["a_w2"], a_a1=inp["a_a1"], a_a2=inp["a_a2"],
        a_g1=inp["a_g1"], a_g2=inp["a_g2"], a_lnx_w=inp["a_lnx_w"], a_lnx_b=inp["a_lnx_b"], w_out=inp["w_out"],
        kv_w=inp["kv_w"], k_norm=inp["k_norm"].reshape(1, 64), b_w_in=inp["b_w_in"], b_lam=inp["b_lam"].reshape(2, 256),
        cs_tok=hc["cs_tok"], ropeC=hc["ropeC"], ropeS=hc["ropeS"], rot=hc["rot"], dmask=hc["dmask"],
        ck=inp["cache_k"].reshape(1280 * 128, 1536), cv=inp["cache_v"].reshape(1280 * 128, 1536),
    )
    xps = [np.ascontiguousarray(inp["x_prompt"][b]) for b in range(2)]
    mps = [np.ascontiguousarray(inp["mem_prompt"][b]) for b in range(2)]
    in_maps = []
    for c in range(NCORE):
        b = c % 2
        m = dict(shared)
        m.update(xp=xps[b], xs=np.ascontiguousarray(inp["x_sample"][c, 0].reshape(KC, 128)), memp=mps[b],
                 stw=np.ascontiguousarray(inp["state_wkv"][:, c]), sts=np.ascontiguousarray(inp["state_shift"][:, c].reshape(2, KC, 128)),
                 cmk=np.ascontiguousarray(inp["cache_mem_k"][:, c].reshape(4, MEMT, 512)), cmv=np.ascontiguousarray(inp["cache_mem_v"][:, c].reshape(4, MEMT, 512)),
                 pt=np.ascontiguousarray(inp["page_table"][c].reshape(1, 128).astype(np.int32)))
        in_maps.append(m)
    res = run_bass_kernel_spmd(nc, in_maps, core_ids=list(range(NCORE)))
    R = res.results
    f32 = np.float32
    y_prompt = np.stack([R[b]["yp"] for b in range(2)]).astype(f32)
    y_sample = np.stack([R[c]["ys"].reshape(1, D) for c in range(8)]).astype(f32)
    wkv_prompt = np.stack([np.stack([R[b]["wkvp"][l] for b in range(2)]) for l in range(2)]).astype(f32)
    shift_prompt = np.stack([np.stack([R[b]["shp"][l].reshape(D) for b in range(2)]) for l in range(2)]).astype(f32)
    wkv_sample = np.stack([np.stack([R[c]["wkvs"][l] for c in range(8)]) for l in range(2)]).astype(f32)
    shift_sample = np.stack([np.stack([R[c]["shs"][l].reshape(D) for c in range(8)]) for l in range(2)]).astype(f32)
    k_rows_p = np.stack([R[b]["krp"].reshape(4096, 12, 2, 64) for b in range(2)]).astype(f32)
    v_rows_p = np.stack([R[b]["vrp"].reshape(4096, 12, 128) for b in range(2)]).astype(f32)
    k_rows_s = np.stack([R[c]["krs"].reshape(1, 12, 2, 64) for c in range(8)]).astype(f32)
    v_rows_s = np.stack([R[c]["vrs"].reshape(1, 12, 128) for c in range(8)]).astype(f32)
    mem_k = np.stack([np.stack([R[b]["mkp"][l].reshape(MEMT, 4, 128) for b in range(2)]) for l in range(4)]).astype(f32)
    mem_v = np.stack([np.stack([R[b]["mvp"][l].reshape(MEMT, 4, 128) for b in range(2)]) for l in range(4)]).astype(f32)
    return (y_prompt, y_sample, wkv_prompt, shift_prompt, wkv_sample, shift_sample,
            k_rows_p, v_rows_p, k_rows_s, v_rows_s, mem_k, mem_v)
```

```python
import numpy as np
from contextlib import ExitStack
import concourse.bass as bass
import concourse.mybir as mybir

F32 = mybir.dt.float32
BF16 = mybir.dt.bfloat16
I32 = mybir.dt.int32
AF = mybir.ActivationFunctionType
ALU = mybir.AluOpType
AX = mybir.AxisListType

ENGS = ("pe", "act", "dve", "pool", "sp")
DMA_POOL = {"sp": 24, "pool": 24, "act": 8}


class Res:
    __slots__ = ("name", "w", "rs")

    def __init__(self, name):
        self.name = name
        self.w = None
        self.rs = {}


class Buf:
    def __init__(self, name, t):
        self.name = name
        self.t = t
        self._res = {}

    def r(self, key=None):
        x = self._res.get(key)
        if x is None:
            x = Res(f"{self.name}[{key}]")
            self._res[key] = x
        return x

    def __getitem__(self, idx):
        return self.t[idx]


class Prog:
    def __init__(self, nc):
        self.nc = nc
        self.es = ExitStack()
        self.ops = {e: [] for e in ENGS}
        self.cnt = {e: 0 for e in ENGS}
        self.dcnt = {e: 0 for e in DMA_POOL}
        self.waited = {e: {} for e in ENGS}
        self.sems = {}
        self.final = {}
        self.nbuf = 0
        self.bar = {}

    def sb(self, name, shape, dt=F32):
        t = self.es.enter_context(self.nc.sbuf_tensor(name, list(shape), dt))
        return Buf(name, t)

    def ps(self, name, shape, dt=F32):
        t = self.es.enter_context(self.nc.psum_tensor(name, list(shape), dt))
        return Buf(name, t)

    def dram(self, name, shape, dt=F32, kind="Internal"):
        t = self.nc.dram_tensor(name, list(shape), dt, kind=kind)
        return Buf(name, t)

    def _sem(self, key):
        s = self.sems.get(key)
        if s is None:
            nm = "s_" + "_".join(str(k) for k in key) if isinstance(key, tuple) else "s_" + str(key)
            s = self.es.enter_context(self.nc.semaphore(nm))
            self.sems[key] = s
        return s

    def add(self, eng, fn, reads=(), writes=(), dma=False, inc=None):
        deps = {}

        def need(k, v):
            if deps.get(k, 0) < v:
                deps[k] = v

        for k, v in self.bar.items():
            need(k, v)
        for r in reads:
            if r.w is not None:
                need(*r.w)
        for r in writes:
            if r.w is not None:
                need(*r.w)
            for k, v in r.rs.items():
                need(k, v)
        if dma:
            n = self.dcnt[eng]
            self.dcnt[eng] = n + 1
            P = DMA_POOL[eng]
            key = ("d", eng, n % P)
            val = 16 * (n // P + 1)
            if n >= P:
                need(key, val - 16)
            tag = (key, val)
            self.final[key] = val
        else:
            self.cnt[eng] += 1
            key = ("e", eng)
            tag = (key, self.cnt[eng])
        wl = []
        wd = self.waited[eng]
        for k, v in deps.items():
            if k == ("e", "pe") and eng == "pe" and not dma:
                continue
            if wd.get(k, 0) >= v:
                continue
            wd[k] = v
            wl.append((k, v))
        for r in writes:
            r.w = tag
            r.rs = {}
        for r in reads:
            if r.rs.get(tag[0], 0) < tag[1]:
                r.rs[tag[0]] = tag[1]
        self.ops[eng].append((fn, wl, tag, dma))
        return tag

    def barrier(self):
        for e in ENGS:
            if self.cnt[e]:
                self.bar[("e", e)] = self.cnt[e]
        for k, v in self.final.items():
            self.bar[k] = v

    def view(self, name, ap):
        return Buf(name, ap)

    def emit(self):
        nc = self.nc
        for k in list(self.final):
            self._sem(k)
        for e in ENGS:
            self._sem(("e", e))
        for e in ENGS:
            for (_, wl, tag, _) in self.ops[e]:
                for k, _ in wl:
                    self._sem(k)
        engobj = {"pe": "tensor", "act": "scalar", "dve": "vector", "pool": "gpsimd", "sp": "sync"}
        with nc.Block() as block:
            for e in ENGS:
                ops = self.ops[e]
                finals = dict(self.final) if e == "sp" else {}

                def body(eng, ops=ops, e=e, finals=finals):
                    for (fn, wl, tag, dma) in ops:
                        for k, v in wl:
                            eng.wait_ge(self.sems[k], v)
                        ins = fn(eng)
                        if dma:
                            ins.then_inc(self.sems[tag[0]], 16)
                        else:
                            ins.then_inc(self.sems[tag[0]], 1)
                    for k, v in finals.items():
                        eng.wait_ge(self.sems[k], v)

                getattr(block, engobj[e])(body)

    def close(self):
        self.es.close()


from concourse.bass_utils import run_bass_kernel_spmd

D = 2048
KC = 16
DFF = 5632
NT = 1024
NCORE = 8
MEMT = 256
NBLK = 4
TA = 256
CH = 64
ARENA = 90112


def fm(v):
    v = np.asarray(v, dtype=np.float32)
    lead = v.shape[:-1]
    n = v.shape[-1] // 128
    w = v.reshape(lead + (n, 128))
    w = np.moveaxis(w, -1, 0)
    return np.ascontiguousarray(w.reshape(128, -1))


class PV:
    def __init__(self):
        self.cols, self.off, self.n = [], {}, 0

    def add(self, name, arr):
        arr = np.asarray(arr, dtype=np.float32)
        assert arr.shape[0] == 128
        self.off[name] = (self.n, arr.shape[1])
        self.cols.append(arr)
        self.n += arr.shape[1]

    def build(self):
        return np.ascontiguousarray(np.concatenate(self.cols, axis=1))


def pack_params(inp):
    pv = PV()
    pv.add("ffn_norm", fm(inp["ffn_norm"].reshape(8, D)))
    pv.add("mix_norm", fm(inp["mix_norm"]))
    pv.add("mem_norm", fm(inp["mem_norm"]))
    pv.add("kv_norm", fm(inp["kv_norm"].reshape(1, D)))
    pv.add("mem_q_norm", fm(inp["mem_q_norm"]))
    pv.add("a_mu", fm(inp["a_mu"].reshape(6, D)))
    for nm in ("a_w0", "a_a0", "a_k_k", "a_k_a"):
        pv.add(nm, fm(inp[nm]))
    pv.add("a_r_k", fm(inp["a_r_k"].reshape(2, 1536)))
    pv.add("a_lnx_w", fm(inp["a_lnx_w"]))
    pv.add("a_lnx_b", fm(inp["a_lnx_b"]))
    pv.add("b_q_norm", np.ascontiguousarray(np.tile(np.asarray(inp["b_q_norm"], np.float32), (1, 2)).T))
    pv.add("b_subln", np.ascontiguousarray(np.asarray(inp["b_subln"], np.float32).T))
    return pv


def host_consts():
    c = {}
    c["ident"] = np.eye(128, dtype=np.float32)
    su = np.triu(np.ones((64, 64), np.float32), 1)
    iu = np.triu(np.ones((64, 64), np.float32), 0)
    m4 = np.tile(np.concatenate([su, iu], 1), (1, 4))
    sl8 = np.tile(su.T, (1, 8))
    i8 = np.tile(np.eye(64, dtype=np.float32), (1, 8))
    c["c64"] = np.ascontiguousarray(np.concatenate([m4, sl8, i8], 1))
    blk = np.zeros((128, 128), np.float32)
    blk[:64, :64] = 1
    blk[64:, 64:] = 1
    hs = np.zeros((128, 2), np.float32)
    hs[:64, 0] = 1
    hs[64:, 1] = 1
    pidx = np.arange(128, dtype=np.float32)[:, None]
    negm = np.full((128, 1), -1e30, np.float32)
    negm[0, 0] = 0.0
    bm = np.zeros((128, 12), np.float32)
    for r_ in range(24):
        bm[r_, r_ // 2] = 1.0
    c["c128"] = np.ascontiguousarray(np.concatenate([blk, hs, pidx, negm, bm], 1))
    pos = np.concatenate([np.arange(4096), [16384]]).astype(np.float32)
    inv = (np.float32(500000.0) ** (-np.arange(0, 16, 2, dtype=np.float32) / np.float32(16))).astype(np.float32)
    ang = (pos[:, None] * inv[None, :]).astype(np.float32)
    cs_, sn_ = np.cos(ang).astype(np.float32), np.sin(ang).astype(np.float32)
    c["cs_tok"] = np.ascontiguousarray(np.concatenate([cs_, sn_], 1))
    C = np.ones((128, 4097), np.float32)
    S = np.zeros((128, 4097), np.float32)
    rot = np.zeros((128, 128), np.float32)
    for p in range(128):
        dd = p % 64
        if dd < 8:
            C[p] = cs_[:, dd]; S[p] = -sn_[:, dd]; rot[p + 8, p] = 1
        elif dd < 16:
            C[p] = cs_[:, dd - 8]; S[p] = sn_[:, dd - 8]; rot[p - 8, p] = 1
    c["ropeC"], c["ropeS"], c["rot"] = C, S, rot
    dm = np.zeros((128, 4, 512), np.float32)
    for r in range(4):
        dm[:, r, :] = (np.arange(128)[:, None] + 128 * r <= np.arange(512)[None, :])
    c["dmask"] = np.ascontiguousarray(dm.reshape(128, 2048))
    return c


class KB:
    def __init__(self, pvoff, npv, stage):
        self.nc = nc = bass.Bass("TRN2", target_bir_lowering=False)
        self.P = P = Prog(nc)
        self.pvoff = pvoff
        self.stage = stage
        dt_in = lambda name, shape, dt=F32: nc.dram_tensor(name, list(shape), dt, kind="ExternalInput").ap()
        dt_out = lambda name, shape, dt=F32: nc.dram_tensor(name, list(shape), dt, kind="ExternalOutput").ap()
        self.din = dict(
            xp=dt_in("xp", [NBLK * NT, D]), xs=dt_in("xs", [KC, 128]), memp=dt_in("memp", [MEMT, D]),
            pv=dt_in("pv", [128, npv]), ident=dt_in("ident", [128, 128]), c64=dt_in("c64", [64, 1536]),
            c128=dt_in("c128", [128, 144]),
            ffn_w13=dt_in("ffn_w13", [8, D, 2 * DFF]), ffn_w2=dt_in("ffn_w2", [8, DFF, D]),
            mem_w_kv=dt_in("mem_w_kv", [4, D, 1024]), mem_k_norm=dt_in("mem_k_norm", [4, 128]),
            a_w_in=dt_in("a_w_in", [2, 2 * D, 5120]), a_w1=dt_in("a_w1", [2, D, 96]), a_w2=dt_in("a_w2", [2, 96, 1536]),
            a_a1=dt_in("a_a1", [2, D, 96]), a_a2=dt_in("a_a2", [2, 96, 1536]),
            a_g1=dt_in("a_g1", [2, D, 256]), a_g2=dt_in("a_g2", [2, 256, 1536]),
            a_lnx_w=dt_in("a_lnx_w", [2, 1536]), a_lnx_b=dt_in("a_lnx_b", [2, 1536]),
            w_out=dt_in("w_out", [4, D, D]),
            kv_w=dt_in("kv_w", [D, 3072]), k_norm=dt_in("k_norm", [1, 64]), b_w_in=dt_in("b_w_in", [2, D, D]), b_lam=dt_in("b_lam", [2, 256]),
            cs_tok=dt_in("cs_tok", [4097, 16]), ropeC=dt_in("ropeC", [128, 4097]), ropeS=dt_in("ropeS", [128, 4097]),
            rot=dt_in("rot", [128, 128]), dmask=dt_in("dmask", [128, 2048]),
            stw=dt_in("stw", [2, 24, 64, 64]), sts=dt_in("sts", [2, KC, 128]),
            cmk=dt_in("cmk", [4, MEMT, 512]), cmv=dt_in("cmv", [4, MEMT, 512]),
            ck=dt_in("ck", [1280 * 128, 1536]), cv=dt_in("cv", [1280 * 128, 1536]), pt=dt_in("pt", [1, 128], I32),
        )
        self.dout = dict(
            yp=dt_out("yp", [NBLK * NT, D]), ys=dt_out("ys", [KC, 128]),
            shp=dt_out("shp", [2, KC, 128]), shs=dt_out("shs", [2, KC, 128]),
            mkp=dt_out("mkp", [4, MEMT, 512]), mvp=dt_out("mvp", [4, MEMT, 512]),
            wkvp=dt_out("wkvp", [2, 24, 64, 64]),
            krp=dt_out("krp", [NBLK * NT, 1536]), vrp=dt_out("vrp", [NBLK * NT, 1536]),
            krs=dt_out("krs", [1, 1536]), vrs=dt_out("vrs", [1, 1536]),
            wkvs=dt_out("wkvs", [2, 24, 64, 64]),
        )
        self.mk_scr = P.dram("mk_scr", [4, 128, 4 * MEMT], BF16)
        self.mv_scr = P.dram("mv_scr", [4, MEMT, 512], BF16)
        self.mks_scr = P.dram("mks_scr", [4, 128, 4 * MEMT], BF16)
        self.mvs_scr = P.dram("mvs_scr", [4, MEMT, 512], BF16)
        self.ks_scr = P.dram("ks_scr", [1, 1536], F32)
        self.vs_scr = P.dram("vs_scr", [1, 1536], F32)
        self.q_scr = P.dram("q_scr", [2, 12, 128], F32)
        self.kt_scr = P.dram("kt_scr", [12, 128, NBLK * NT], BF16)
        self.v_scr = P.dram("v_scr", [NBLK * NT, 1536], BF16)
        self.xT = P.sb("xT", [128, KC, NT], F32)
        self.xsT = P.sb("xsT", [128, KC], F32)
        self.xn = P.sb("xn", [128, KC, NT + 1], BF16)
        self.xns = P.sb("xns", [128, KC, 2], BF16)
        self.pv = P.sb("pv_sb", [128, npv], F32)
        self.omu = P.sb("omu", [128, 96], F32)
        self.neg = P.sb("negp", [128, 48], F32)
        self.ident = P.sb("ident_sb", [128, 128], F32)
        self.identb = P.sb("identb", [128, 128], BF16)
        self.onesb = P.sb("onesb", [128, 128], BF16)
        self.c128 = P.sb("c128_sb", [128, 144], F32)
        self.onesf = P.sb("onesf", [128, 128], F32)
        self.blk64b = P.sb("blk64b", [128, 128], BF16)
        self.c64 = P.sb("c64_sb", [64, 1536], BF16)
        self.rmask = P.sb("rmask", [128, TA], F32)
        self.prevcol = P.sb("prevcol", [128, 2, KC], BF16)
        self.xlast = P.sb("xlast", [128, KC], F32)
        self.xslast = P.sb("xslast", [128, KC], F32)
        self.Sf = P.sb("Sf", [128, 2, 12, 64], F32)
        self.small = P.sb("small", [128, 64], F32)
        self.rotb = P.sb("rotb", [128, 128], BF16)
        self.dmask = P.sb("dmask_sb", [128, 2048], BF16)
        self.lamv = P.sb("lamv", [128, 4], F32)
        self.colst = P.sb("colst", [KC, 128], F32)
        self.psb = [P.ps(f"ps{i}", [128, 512], F32) for i in range(8)]
        self.arena = P.sb("arena", [128, ARENA // 2], BF16)
        self.aoff = 0
        self.cur = {}
        self.wi = 0

    def phase(self, name):
        self.P.barrier()
        self.aoff = 0
        self.cur = {}
        self.pname = name

    def av(self, name, shape, dt=F32, parts=128):
        esz = 4 if dt in (F32, I32) else 2
        n = int(np.prod(shape))
        nbytes = ((n * esz + 63) // 64) * 64
        assert self.aoff + nbytes <= ARENA, (self.pname, name, self.aoff, nbytes)
        o2 = self.aoff // 2
        ap = self.arena.t[0:parts, o2:o2 + (n * esz) // 2]
        if dt in (F32, I32):
            ap = ap.bitcast(dt)
        if len(shape) == 2:
            ap = ap.rearrange("p (a b) -> p a b", a=shape[0])
        elif len(shape) == 3:
            ap = ap.rearrange("p (a b c) -> p a b c", a=shape[0], b=shape[1])
        self.aoff += nbytes
        b = self.P.view(f"{self.pname}.{name}", ap)
        self.cur[name] = b
        return b

    def pvc(self, name, i=0, n=1):
        o, w = self.pvoff[name]
        return self.pv[:, o + i:o + i + n]

    def psbf(self, i):
        return self.psb[i].t[:].bitcast(BF16)

    def setup(self):
        P, d = self.P, self.din
        P.add("sp", lambda e: e.dma_start(out=self.ident[:], in_=d["ident"]), writes=[self.ident.r()], dma=True)
        P.add("sp", lambda e: e.dma_start(out=self.pv[:], in_=d["pv"]), writes=[self.pv.r()], dma=True)
        P.add("pool", lambda e: e.dma_start(out=self.c64[:], in_=d["c64"]), writes=[self.c64.r()], dma=True)
        P.add("sp", lambda e: e.dma_start(out=self.c128[:], in_=d["c128"]), writes=[self.c128.r()], dma=True)
        P.add("dve", lambda e: e.memset(self.onesb[:], 1.0), writes=[self.onesb.r()])
        P.add("dve", lambda e: e.memset(self.onesf[:], 1.0), writes=[self.onesf.r()])
        P.add("dve", lambda e: e.tensor_copy(out=self.identb[:], in_=self.ident[:]), reads=[self.ident.r()], writes=[self.identb.r()])
        P.add("dve", lambda e: e.tensor_copy(out=self.blk64b[:], in_=self.c128[:, 0:128]), reads=[self.c128.r()], writes=[self.blk64b.r()])
        P.add("dve", lambda e: e.memset(self.prevcol[:], 0.0), writes=[self.prevcol.r()])
        P.add("dve", lambda e: e.memset(self.Sf[:], 0.0), writes=[self.Sf.r()])
        P.add("dve", lambda e: e.memset(self.rmask[:], 1.0), writes=[self.rmask.r()])
        P.add("dve", lambda e: e.memset(self.rmask[:].rearrange("p (c t) -> p c t", t=CH)[:, :, 0:1], 0.0), reads=[self.rmask.r()], writes=[self.rmask.r()])
        P.add("pool", lambda e: e.dma_start(out=self.rotb[:], in_=d["rot"]), writes=[self.rotb.r()], dma=True)
        P.add("pool", lambda e: e.dma_start(out=self.dmask[:], in_=d["dmask"]), writes=[self.dmask.r()], dma=True)
        self.lam_setup()
        mo = self.pvoff["a_mu"][0]
        P.add("dve", lambda e: e.tensor_scalar(out=self.omu[:], in0=self.pv[:, mo:mo + 96], scalar1=-1.0, scalar2=1.0, op0=ALU.mult, op1=ALU.add),
              reads=[self.pv.r()], writes=[self.omu.r()])
        wo = self.pvoff["a_w0"][0]
        ko = self.pvoff["a_k_a"][0]
        P.add("dve", lambda e: e.tensor_scalar(out=self.neg[:, 0:24], in0=self.pv[:, wo:wo + 24], scalar1=-1.0, scalar2=None, op0=ALU.mult),
              reads=[self.pv.r()], writes=[self.neg.r()])
        P.add("dve", lambda e: e.tensor_scalar(out=self.neg[:, 24:48], in0=self.pv[:, ko:ko + 24], scalar1=-1.0, scalar2=1.0, op0=ALU.mult, op1=ALU.add),
              reads=[self.pv.r()], writes=[self.neg.r()])

    def load_x(self, blk):
        P, d = self.P, self.din
        xT, psb = self.xT, self.psb
        self.phase("io")
        xin = [self.av(f"xin{i}", [D], F32) for i in range(2)]
        for tb in range(NT // 128):
            xi = xin[tb % 2]
            r0 = blk * NT + tb * 128
            P.add("sp", lambda e, xi=xi, r0=r0: e.dma_start(out=xi[:], in_=d["xp"][r0:r0 + 128, :]), writes=[xi.r()], dma=True)
            for k4 in range(4):
                pb = psb[(tb * 4 + k4) % 2]
                for j in range(4):
                    kc = k4 * 4 + j
                    P.add("pe", lambda e, pb=pb, xi=xi, j=j, kc=kc: e.transpose(
                        out=pb[:, j * 128:(j + 1) * 128], in_=xi[:, kc * 128:(kc + 1) * 128], identity=self.ident[:]),
                        reads=[xi.r(), self.ident.r()], writes=[pb.r()])
                P.add("act", lambda e, pb=pb, k4=k4, tb=tb: e.copy(
                    out=xT[:, k4 * 4:(k4 + 1) * 4, tb * 128:(tb + 1) * 128],
                    in_=pb[:].rearrange("p (a b) -> p a b", a=4)),
                    reads=[pb.r()], writes=[xT.r(tb // 4)])
        if blk == 0:
            xi = xin[0]
            P.add("sp", lambda e: e.dma_start(out=xi[0:KC, 0:128], in_=d["xs"]), writes=[xi.r()], dma=True)
            pb = psb[2]
            P.add("pe", lambda e: e.transpose(out=pb[:, 0:KC], in_=xi[0:KC, 0:128], identity=self.ident[0:KC, 0:KC]),
                  reads=[xi.r(), self.ident.r()], writes=[pb.r()])
            P.add("act", lambda e: e.copy(out=self.xsT[:], in_=pb[:, 0:KC]), reads=[pb.r()], writes=[self.xsT.r()])

    def store_x(self, blk):
        P, o = self.P, self.dout
        xT, psb = self.xT, self.psb
        self.phase("io")
        xin = [self.av(f"xin{i}", [D], F32) for i in range(2)]
        for tb in range(NT // 128):
            xi = xin[tb % 2]
            for k4 in range(4):
                pb = psb[(tb * 4 + k4) % 2]
                for j in range(4):
                    kc = k4 * 4 + j
                    P.add("pe", lambda e, pb=pb, j=j, kc=kc, tb=tb: e.transpose(
                        out=pb[:, j * 128:(j + 1) * 128], in_=xT[:, kc, tb * 128:(tb + 1) * 128], identity=self.ident[:]),
                        reads=[xT.r(tb // 4), self.ident.r()], writes=[pb.r()])
                P.add("act", lambda e, pb=pb, k4=k4, xi=xi: e.copy(out=xi[:, k4 * 512:(k4 + 1) * 512], in_=pb[:]),
                      reads=[pb.r()], writes=[xi.r()])
            r0 = blk * NT + tb * 128
            P.add("sp", lambda e, xi=xi, r0=r0: e.dma_start(out=o["yp"][r0:r0 + 128, :], in_=xi[:]), reads=[xi.r()], dma=True)
        if blk == NBLK - 1:
            self.store_col(self.xsT, o["ys"], self.xsT.r())

    def store_col(self, src, dst, res):
        P = self.P
        pb = self.psb[2]
        cs = self.colst
        P.add("pe", lambda e: e.transpose(out=pb[0:KC, 0:128], in_=src[:, 0:KC], identity=self.ident[:]),
              reads=[res, self.ident.r()], writes=[pb.r()])
        P.add("act", lambda e: e.copy(out=cs[0:KC, 0:128], in_=pb[0:KC, 0:128]), reads=[pb.r()], writes=[cs.r()])
        P.add("sp", lambda e: e.dma_start(out=dst, in_=cs[0:KC, 0:128]), reads=[cs.r()], writes=[cs.r("o")], dma=True)

    def tiles(self):
        return [(0, 512), (512, 512)]

    def rmsnorm(self, gname, gi, sample, last_out=None, shift_li=None):
        P = self.P
        xT, xn, psb = self.xT, self.xn, self.psb
        sq = [self.cur.get("sq0") or self.av("sq0", [512], BF16), self.cur.get("sq1") or self.av("sq1", [512], BF16)]
        rstd = self.cur.get("rstd") or self.av("rstd", [512], F32)
        if shift_li is not None:
            P.add("dve", lambda e: e.tensor_copy(out=xn[:, :, 0], in_=self.prevcol[:, shift_li, :]),
                  reads=[self.prevcol.r()], writes=[xn.r("prev")])
        for ti, (t0, n) in enumerate(self.tiles()):
            pb = psb[7]
            for kc in range(KC):
                s = sq[kc % 2]
                P.add("act", lambda e, s=s, kc=kc, t0=t0, n=n: e.activation(out=s[:, :n], in_=xT[:, kc, t0:t0 + n], func=AF.Square),
                      reads=[xT.r(ti)], writes=[s.r()])
                P.add("pe", lambda e, s=s, kc=kc, n=n, pb=pb: e.matmul(pb[:, :n], lhsT=self.onesb[:], rhs=s[:, :n], start=(kc == 0), stop=(kc == KC - 1)),
                      reads=[s.r(), self.onesb.r()], writes=[pb.r()])
            P.add("act", lambda e, pb=pb, n=n: e.activation(out=rstd[:, :n], in_=pb[:, :n], func=AF.Sqrt, scale=1.0 / D, bias=1e-6),
                  reads=[pb.r()], writes=[rstd.r()])
            P.add("dve", lambda e, n=n: e.reciprocal(out=rstd[:, :n], in_=rstd[:, :n]), reads=[rstd.r()], writes=[rstd.r()])
            for kc in range(KC):
                P.add("dve", lambda e, kc=kc, t0=t0, n=n: e.scalar_tensor_tensor(
                    out=xn[:, kc, 1 + t0:1 + t0 + n], in0=xT[:, kc, t0:t0 + n], scalar=self.pvc(gname, gi * KC + kc), in1=rstd[:, :n],
                    op0=ALU.mult, op1=ALU.mult),
                    reads=[xT.r(ti), self.pv.r(), rstd.r()], writes=[xn.r(ti)])
            if ti == 1 and last_out is not None:
                go = self.pvoff[gname][0] + gi * KC
                P.add("dve", lambda e: e.tensor_tensor(out=last_out[:], in0=xT[:, :, NT - 1], in1=self.pv[:, go:go + KC], op=ALU.mult),
                      reads=[xT.r(1), self.pv.r()], writes=[last_out.r()])
                P.add("dve", lambda e: e.tensor_scalar(out=last_out[:], in0=last_out[:], scalar1=rstd[:, 511:512], scalar2=None, op0=ALU.mult),
                      reads=[last_out.r(), rstd.r()], writes=[last_out.r()])
        if shift_li is not None:
            P.add("dve", lambda e: e.tensor_copy(out=self.prevcol[:, shift_li, :], in_=xn[:, :, NT]),
                  reads=[xn.r(1)], writes=[self.prevcol.r()])
        if not sample:
            return
        sm = self.small
        P.add("dve", lambda e: e.tensor_tensor(out=sm[:, 0:KC], in0=self.xsT[:], in1=self.xsT[:], op=ALU.mult),
              reads=[self.xsT.r()], writes=[sm.r("a")])
        P.add("dve", lambda e: e.tensor_reduce(out=sm[:, 16:17], in_=sm[:, 0:KC], axis=AX.X, op=ALU.add),
              reads=[sm.r("a")], writes=[sm.r("b")])
        P.add("dve", lambda e: e.tensor_copy(out=sq[0][:, 0:1], in_=sm[:, 16:17]), reads=[sm.r("b")], writes=[sq[0].r()])
        pb = psb[7]
        P.add("pe", lambda e: e.matmul(pb[:, 0:1], lhsT=self.onesb[:], rhs=sq[0][:, 0:1], start=True, stop=True),
              reads=[sq[0].r(), self.onesb.r()], writes=[pb.r()])
        P.add("act", lambda e: e.activation(out=sm[:, 17:18], in_=pb[:, 0:1], func=AF.Sqrt, scale=1.0 / D, bias=1e-6),
              reads=[pb.r()], writes=[sm.r("c")])
        P.add("dve", lambda e: e.reciprocal(out=sm[:, 17:18], in_=sm[:, 17:18]), reads=[sm.r("c")], writes=[sm.r("c")])
        go = self.pvoff[gname][0] + gi * KC
        dst = self.xslast
        P.add("dve", lambda e: e.scalar_tensor_tensor(out=dst[:], in0=self.xsT[:], scalar=sm[:, 17:18], in1=self.pv[:, go:go + KC], op0=ALU.mult, op1=ALU.mult),
              reads=[self.xsT.r(), sm.r("c"), self.pv.r()], writes=[dst.r()])
        P.add("dve", lambda e: e.tensor_copy(out=self.xns[:, :, 1], in_=dst[:]), reads=[dst.r()], writes=[self.xns.r("cur")])

    def ffn(self, li, lj, sample):
        P, d = self.P, self.din
        xT, xn, psb = self.xT, self.xn, self.psb
        self.phase("ffn")
        hq = self.av("hq", [11, NT], BF16)
        hqs = self.av("hqs", [32], BF16)
        wb = [self.av(f"wb{i}", [8192], BF16) for i in range(3)]
        sil = [self.av(f"sil{i}", [512], F32) for i in range(2)]
        fi = li * 2 + lj
        self.rmsnorm("ffn_norm", fi, sample)
        w13v = d["ffn_w13"][fi].rearrange("(kc p) c -> p kc c", p=128)
        w2v = d["ffn_w2"][fi].rearrange("(fc p) c -> p fc c", p=128)
        tiles = self.tiles()
        ps_s = psb[4]
        wi = 0
        for q in range(4):
            for jl in range(11):
                j = q * 11 + jl
                b = wb[wi % 3]
                wi += 1
                bv = b[:, :KC * 256].rearrange("p (k c) -> p k c", k=KC)
                P.add("pool", lambda e, bv=bv, j=j: e.dma_start(out=bv[:, :, 0:128], in_=w13v[:, :, j * 128:(j + 1) * 128]),
                      writes=[b.r()], dma=True)
                P.add("pool", lambda e, bv=bv, j=j: e.dma_start(out=bv[:, :, 128:256], in_=w13v[:, :, DFF + j * 128:DFF + (j + 1) * 128]),
                      writes=[b.r()], dma=True)
                for ti, (t0, n) in enumerate(tiles):
                    pg, pu = psb[ti * 2], psb[ti * 2 + 1]
                    for kc in range(KC):
                        P.add("pe", lambda e, pg=pg, bv=bv, kc=kc, t0=t0, n=n: e.matmul(
                            pg[:, :n], lhsT=bv[:, kc, 0:128], rhs=xn[:, kc, 1 + t0:1 + t0 + n], start=(kc == 0), stop=(kc == KC - 1)),
                            reads=[b.r(), xn.r(ti)], writes=[pg.r()])
                    for kc in range(KC):
                        P.add("pe", lambda e, pu=pu, bv=bv, kc=kc, t0=t0, n=n: e.matmul(
                            pu[:, :n], lhsT=bv[:, kc, 128:256], rhs=xn[:, kc, 1 + t0:1 + t0 + n], start=(kc == 0), stop=(kc == KC - 1)),
                            reads=[b.r(), xn.r(ti)], writes=[pu.r()])
                    s = sil[ti]
                    P.add("act", lambda e, s=s, pg=pg, n=n: e.activation(out=s[:, :n], in_=pg[:, :n], func=AF.Silu),
                          reads=[pg.r()], writes=[s.r()])
                    P.add("dve", lambda e, s=s, pu=pu, jl=jl, t0=t0, n=n: e.tensor_tensor(
                        out=hq[:, jl, t0:t0 + n], in0=s[:, :n], in1=pu[:, :n], op=ALU.mult),
                        reads=[s.r(), pu.r()], writes=[hq.r(ti)])
                if sample:
                    for half in range(2):
                        for kc in range(KC):
                            P.add("pe", lambda e, bv=bv, kc=kc, half=half: e.matmul(
                                ps_s[:, half:half + 1], lhsT=bv[:, kc, half * 128:(half + 1) * 128], rhs=self.xns[:, kc, 1:2],
                                start=(kc == 0), stop=(kc == KC - 1)),
                                reads=[b.r(), self.xns.r("cur")], writes=[ps_s.r()])
                    sm = self.small
                    P.add("act", lambda e: e.activation(out=sm[:, 20:21], in_=ps_s[:, 0:1], func=AF.Silu), reads=[ps_s.r()], writes=[sm.r("s")])
                    P.add("dve", lambda e, jl=jl: e.tensor_tensor(out=hqs[:, jl:jl + 1], in0=sm[:, 20:21], in1=ps_s[:, 1:2], op=ALU.mult),
                          reads=[sm.r("s"), ps_s.r()], writes=[hqs.r()])
            for dq in range(4):
                b = wb[wi % 3]
                wi += 1
                bv = b[:, :11 * 512].rearrange("p (k c) -> p k c", k=11)
                P.add("pool", lambda e, bv=bv, q=q, dq=dq: e.dma_start(out=bv, in_=w2v[:, q * 11:(q + 1) * 11, dq * 512:(dq + 1) * 512]),
                      writes=[b.r()], dma=True)
                for dl in range(4):
                    dc = dq * 4 + dl
                    for ti, (t0, n) in enumerate(tiles):
                        pa = psb[5 + (dc * 2 + ti) % 2]
                        for f in range(11):
                            P.add("pe", lambda e, pa=pa, bv=bv, f=f, dl=dl, t0=t0, n=n: e.matmul(
                                pa[:, :n], lhsT=bv[:, f, dl * 128:(dl + 1) * 128], rhs=hq[:, f, t0:t0 + n], start=(f == 0), stop=(f == 10)),
                                reads=[b.r(), hq.r(ti)], writes=[pa.r()])
                        P.add("dve", lambda e, pa=pa, dc=dc, t0=t0, n=n: e.scalar_tensor_tensor(
                            out=xT[:, dc, t0:t0 + n], in0=pa[:, :n], scalar=0.5, in1=xT[:, dc, t0:t0 + n], op0=ALU.mult, op1=ALU.add),
                            reads=[pa.r(), xT.r(ti)], writes=[xT.r(ti)])
                    if sample:
                        for f in range(11):
                            P.add("pe", lambda e, bv=bv, f=f, dl=dl: e.matmul(
                                ps_s[:, 2:3], lhsT=bv[:, f, dl * 128:(dl + 1) * 128], rhs=hqs[:, f:f + 1], start=(f == 0), stop=(f == 10)),
                                reads=[b.r(), hqs.r()], writes=[ps_s.r()])
                        P.add("dve", lambda e, dc=dc: e.scalar_tensor_tensor(
                            out=self.xsT[:, dc:dc + 1], in0=ps_s[:, 2:3], scalar=0.5, in1=self.xsT[:, dc:dc + 1], op0=ALU.mult, op1=ALU.add),
                            reads=[ps_s.r(), self.xsT.r()], writes=[self.xsT.r()])

    def mem_all(self):
        P, d, o = self.P, self.din, self.dout
        psb = self.psb
        self.phase("mem")
        memT = self.av("memT", [KC, MEMT], F32)
        memn = self.av("memn", [KC, MEMT], BF16)
        xin0 = self.av("xin0", [D], F32)
        xin = [xin0, xin0]
        wbuf = [self.av(f"wm{i}", [KC, 512], BF16) for i in range(2)]
        kf = [self.av(f"kf{i}", [512], F32) for i in range(2)]
        kb16 = [self.av(f"kb{i}", [512], BF16) for i in range(2)]
        sqv = [self.av(f"sqv{i}", [512], F32) for i in range(2)]
        gk = self.av("gk", [128], F32)
        mkT = self.av("mkT", [4, MEMT], BF16)
        sq = [self.av("sq0", [512], BF16), self.av("sq1", [512], BF16)]
        rstd = self.av("rstd", [512], F32)
        for tb in range(MEMT // 128):
            xi = xin[tb % 2]
            P.add("sp", lambda e, xi=xi, tb=tb: e.dma_start(out=xi[:], in_=d["memp"][tb * 128:(tb + 1) * 128, :]), writes=[xi.r()], dma=True)
            for k4 in range(4):
                pb = psb[(tb * 4 + k4) % 2]
                for j in range(4):
                    kc = k4 * 4 + j
                    P.add("pe", lambda e, pb=pb, xi=xi, j=j, kc=kc: e.transpose(
                        out=pb[:, j * 128:(j + 1) * 128], in_=xi[:, kc * 128:(kc + 1) * 128], identity=self.ident[:]),
                        reads=[xi.r(), self.ident.r()], writes=[pb.r()])
                P.add("act", lambda e, pb=pb, k4=k4, tb=tb: e.copy(
                    out=memT[:, k4 * 4:(k4 + 1) * 4, tb * 128:(tb + 1) * 128], in_=pb[:].rearrange("p (a b) -> p a b", a=4)),
                    reads=[pb.r()], writes=[memT.r()])
        pb = psb[7]
        n = MEMT
        for kc in range(KC):
            s = sq[kc % 2]
            P.add("act", lambda e, s=s, kc=kc: e.activation(out=s[:, :n], in_=memT[:, kc, :], func=AF.Square), reads=[memT.r()], writes=[s.r()])
            P.add("pe", lambda e, s=s, kc=kc: e.matmul(pb[:, :n], lhsT=self.onesb[:], rhs=s[:, :n], start=(kc == 0), stop=(kc == KC - 1)),
                  reads=[s.r(), self.onesb.r()], writes=[pb.r()])
        P.add("act", lambda e: e.activation(out=rstd[:, :n], in_=pb[:, :n], func=AF.Sqrt, scale=1.0 / D, bias=1e-6), reads=[pb.r()], writes=[rstd.r()])
        P.add("dve", lambda e: e.reciprocal(out=rstd[:, :n], in_=rstd[:, :n]), reads=[rstd.r()], writes=[rstd.r()])
        for kc in range(KC):
            P.add("dve", lambda e, kc=kc: e.tensor_tensor(out=memT[:, kc, :], in0=memT[:, kc, :], in1=rstd[:, :n], op=ALU.mult),
                  reads=[memT.r(), rstd.r()], writes=[memT.r()])
        sm = self.small
        for li in range(4):
            for kc in range(KC):
                P.add("dve", lambda e, kc=kc, li=li: e.tensor_scalar(out=memn[:, kc, :], in0=memT[:, kc, :], scalar1=self.pvc("mem_norm", li * KC + kc), scalar2=None, op0=ALU.mult),
                      reads=[memT.r(), self.pv.r()], writes=[memn.r()])
            P.add("sp", lambda e, li=li: e.dma_start(out=gk[:], in_=d["mem_k_norm"][li, :].partition_broadcast(128)), writes=[gk.r()], dma=True)
            wv = d["mem_w_kv"][li].rearrange("(kc p) c -> p kc c", p=128)
            for cb in range(2):
                b = wbuf[cb]
                P.add("pool", lambda e, b=b, cb=cb, wv=wv: e.dma_start(out=b[:], in_=wv[:, :, cb * 512:(cb + 1) * 512]), writes=[b.r()], dma=True)
                for mt in range(2):
                    pb = psb[mt]
                    for kc in range(KC):
                        P.add("pe", lambda e, pb=pb, b=b, kc=kc, mt=mt: e.matmul(
                            pb[:, :], lhsT=memn[:, kc, mt * 128:(mt + 1) * 128], rhs=b[:, kc, :], start=(kc == 0), stop=(kc == KC - 1)),
                            reads=[b.r(), memn.r()], writes=[pb.r()])
                    k_f = kf[mt]
                    k_b = kb16[mt]
                    if cb == 0:
                        sv = sqv[mt]
                        c0 = 24 + 4 * mt
                        P.add("act", lambda e, pb=pb, sv=sv: e.activation(out=sv[:, :], in_=pb[:, :], func=AF.Square), reads=[pb.r()], writes=[sv.r()])
                        P.add("dve", lambda e, sv=sv, c0=c0: e.tensor_reduce(out=sm[:, c0:c0 + 4], in_=sv[:, :].rearrange("p (h d) -> p h d", h=4), axis=AX.X, op=ALU.add),
                              reads=[sv.r()], writes=[sm.r(("mk", mt))])
                        P.add("act", lambda e, c0=c0: e.activation(out=sm[:, c0:c0 + 4], in_=sm[:, c0:c0 + 4], func=AF.Sqrt, scale=1.0 / 128, bias=1e-6),
                              reads=[sm.r(("mk", mt))], writes=[sm.r(("mk", mt))])
                        P.add("dve", lambda e, c0=c0: e.reciprocal(out=sm[:, c0:c0 + 4], in_=sm[:, c0:c0 + 4]), reads=[sm.r(("mk", mt))], writes=[sm.r(("mk", mt))])
                        for h in range(4):
                            P.add("dve", lambda e, pb=pb, k_f=k_f, h=h, c0=c0: e.scalar_tensor_tensor(
                                out=k_f[:, h * 128:(h + 1) * 128], in0=pb[:, h * 128:(h + 1) * 128], scalar=sm[:, c0 + h:c0 + h + 1], in1=gk[:, :],
                                op0=ALU.mult, op1=ALU.mult),
                                reads=[pb.r(), sm.r(("mk", mt)), gk.r()], writes=[k_f.r()])
                        P.add("sp", lambda e, k_f=k_f, mt=mt, li=li: e.dma_start(out=o["mkp"][li, mt * 128:(mt + 1) * 128, :], in_=k_f[:, :]), reads=[k_f.r()], writes=[k_f.r("o")], dma=True)
                        P.add("act", lambda e, k_f=k_f, k_b=k_b: e.copy(out=k_b[:, :], in_=k_f[:, :]), reads=[k_f.r()], writes=[k_b.r()])
                        pt = self.psbf(2 + mt)
                        for h in range(4):
                            P.add("pe", lambda e, pt=pt, k_b=k_b, h=h: e.transpose(out=pt[:, h * 128:(h + 1) * 128], in_=k_b[:, h * 128:(h + 1) * 128], identity=self.identb[:]),
                                  reads=[k_b.r(), self.identb.r()], writes=[psb[2 + mt].r()])
                        P.add("act", lambda e, pt=pt, mt=mt: e.copy(out=mkT[:, :, mt * 128:(mt + 1) * 128], in_=pt[:, 0:512].rearrange("p (h m) -> p h m", h=4)),
                              reads=[psb[2 + mt].r()], writes=[mkT.r()])
                    else:
                        P.add("act", lambda e, pb=pb, k_f=k_f: e.copy(out=k_f[:, :], in_=pb[:, :]), reads=[pb.r()], writes=[k_f.r()])
                        P.add("sp", lambda e, k_f=k_f, mt=mt, li=li: e.dma_start(out=o["mvp"][li, mt * 128:(mt + 1) * 128, :], in_=k_f[:, :]), reads=[k_f.r()], writes=[k_f.r("o")], dma=True)
                        P.add("dve", lambda e, k_f=k_f, k_b=k_b: e.tensor_copy(out=k_b[:, :], in_=k_f[:, :]), reads=[k_f.r()], writes=[k_b.r()])
                        P.add("sp", lambda e, k_b=k_b, mt=mt, li=li: e.dma_start(out=self.mv_scr.t.ap()[li, mt * 128:(mt + 1) * 128, :], in_=k_b[:, :]),
                              reads=[k_b.r()], writes=[k_b.r("o"), self.mv_scr.r(li)], dma=True)
            P.add("sp", lambda e, li=li: e.dma_start(out=self.mk_scr.t.ap()[li], in_=mkT[:].rearrange("p h m -> p (h m)")),
                  reads=[mkT.r()], writes=[mkT.r("o"), self.mk_scr.r(li)], dma=True)

    def mem_sample(self):
        P, d = self.P, self.din
        psb = self.psb
        self.phase("mems")
        kb16 = [self.av(f"kb{i}", [512], BF16) for i in range(2)]
        mkT = self.av("mkT", [4, MEMT], BF16)
        for li in range(4):
            for mt in range(2):
                k_b = kb16[mt]
                P.add("pool", lambda e, k_b=k_b, li=li, mt=mt: e.dma_start(out=k_b[:, :], in_=d["cmk"][li, mt * 128:(mt + 1) * 128, :]), writes=[k_b.r()], dma=True)
                pt = self.psbf(2 + mt)
                for h in range(4):
                    P.add("pe", lambda e, pt=pt, k_b=k_b, h=h: e.transpose(out=pt[:, h * 128:(h + 1) * 128], in_=k_b[:, h * 128:(h + 1) * 128], identity=self.identb[:]),
                          reads=[k_b.r(), self.identb.r()], writes=[psb[2 + mt].r()])
                P.add("act", lambda e, pt=pt, mt=mt: e.copy(out=mkT[:, :, mt * 128:(mt + 1) * 128], in_=pt[:, 0:512].rearrange("p (h m) -> p h m", h=4)),
                      reads=[psb[2 + mt].r()], writes=[mkT.r()])
            P.add("sp", lambda e, li=li: e.dma_start(out=self.mks_scr.t.ap()[li], in_=mkT[:].rearrange("p h m -> p (h m)")), reads=[mkT.r()], writes=[self.mks_scr.r(li)], dma=True)
            for mt in range(2):
                k_b = kb16[mt]
                P.add("pool", lambda e, k_b=k_b, li=li, mt=mt: e.dma_start(out=k_b[:, :], in_=d["cmv"][li, mt * 128:(mt + 1) * 128, :]), writes=[k_b.r()], dma=True)
                P.add("sp", lambda e, k_b=k_b, li=li, mt=mt: e.dma_start(out=self.mvs_scr.t.ap()[li, mt * 128:(mt + 1) * 128, :], in_=k_b[:, :]), reads=[k_b.r()], writes=[self.mvs_scr.r(li)], dma=True)

    def mixerA(self, li, blk, sample, smp=False):
        P, d = self.P, self.din
        xn, xT, psb = self.xn, self.xT, self.psb
        self.phase("mixA")
        N = CH if smp else TA
        NC = N // CH
        av = self.av
        wbig = av("wbig", [3 * 32 * 128], BF16)
        wX = [wbig[:, i * 4096:(i + 1) * 4096].rearrange("p (k c) -> p k c", k=32) for i in range(3)]
        l1w = av("l1w", [KC, 96], BF16)
        l1a = av("l1a", [KC, 96], BF16)
        self._xw_off = self.aoff
        class _V:
            pass
        l1gv = _V()
        l1gv_ap = wbig[:, 8192:8192 + KC * 128].rearrange("p (k c) -> p k c", k=KC)
        l1gv.__class__ = type("LV", (), {"__getitem__": lambda s, idx: l1gv_ap[idx], "r": lambda s, key=None: wbig.r(2)})
        xw = [av("xw0", [N], BF16), av("xw1", [N], BF16)]
        xtmp = [av("xt0", [N], BF16), av("xt1", [N], BF16)]
        th = av("th", [N], BF16)
        ah = av("ah", [N], BF16)
        gh = av("gh", [2, N], BF16)
        w2s = av("w2s", [128], BF16)
        a2s = av("a2s", [128], BF16)
        g2s = av("g2s", [2, 128], BF16)
        r_f = av("r_f", [N], F32)
        k_f = av("k_f", [N], F32)
        v_b = av("v_b", [N], BF16)
        ew = av("ew", [N], F32)
        asig = av("asig", [N], F32)
        kk = av("kk", [N], F32)
        kmod = av("kmod", [N], F32)
        bb = av("bb", [N], F32)
        cum = av("cum", [N], F32)
        e1 = av("e1", [N], F32)
        e2 = av("e2", [N], F32)
        rk = av("rk", [N], F32)
        sqk = av("sqk", [N], BF16)
        AR = av("AR", [NC, 128], BF16)
        Bt = av("Bt", [N], BF16)
        Kt = av("Kt", [N], BF16)
        ARh = [av("AR0", [NC, 128], BF16), av("AR1", [NC, 128], BF16)]
        Bth = [av("Bt0", [N], BF16), av("Bt1", [N], BF16)]
        Kth = [av("Kt0", [N], BF16), av("Kt1", [N], BF16)]
        Bh = av("Bh", [N], BF16)
        Kh = av("Kh", [N], BF16)
        tot = av("tot", [NC], F32)
        gtot2 = [av("gtot0", [NC], F32), av("gtot1", [NC], F32)]
        Sb = av("Sb", [64], BF16)
        BhT = av("BhT", [NC, 128], BF16, parts=64)
        KhT = av("KhT", [NC, 128], BF16, parts=64)
        Vt = av("Vt", [NC, 128], BF16, parts=64)
        g_t2 = [av("g_t0", [NC, 128], BF16, parts=64), av("g_t1", [NC, 128], BF16, parts=64)]
        NNs = av("NNs", [NC, 512], BF16, parts=64)
        NabT = av("NabT", [2 * NC, 64], BF16, parts=64)
        Tm = [av("Tm0", [2 * NC, 64], BF16, parts=64), av("Tm1", [2 * NC, 64], BF16, parts=64)]
        Xs = [av("Xa", [2 * NC, 64], BF16, parts=64), av("Xb", [2 * NC, 64], BF16, parts=64)]
        XTs = [av("XTa", [2 * NC, 64], BF16, parts=64), av("XTb", [2 * NC, 64], BF16, parts=64)]
        y_t = av("y_t", [NC, 128], F32, parts=64)
        W1b = av("W1b", [128], BF16, parts=64)
        Ub = av("Ub", [128], BF16, parts=64)
        pt1 = self.P.view("mixA.pt1", self.arena.t[0:64, self._xw_off // 2:self._xw_off // 2 + NC * 256].bitcast(F32).rearrange("p (a b) -> p a b", a=NC))
        bon2 = [av("bon0", [2 * NC], F32, parts=64), av("bon1", [2 * NC], F32, parts=64)]
        st = av("st", [64], F32, parts=64)
        lnw2 = [av("lnw0", [128], F32, parts=64), av("lnw1", [128], F32, parts=64)]
        lnb2 = [av("lnb0", [128], F32, parts=64), av("lnb1", [128], F32, parts=64)]
        mixout = av("mixout", [KC, N], BF16)
        mkT = av("mkT", [4, MEMT], BF16)
        mv = av("mv", [2, 512], BF16)
        qf, qn, pT, rz = r_f, v_b, [Bt, Kt], k_f
        m4 = self.c64[:, 0:512]
        sl8 = self.c64[:, 512:1024]
        i8 = self.c64[:, 1024:1536]
        hsel = self.c128[:, 128:130]
        SC = 128 ** -0.5

        win = d["a_w_in"][li].rearrange("(k p) c -> p k c", p=128)
        mks, mvs = (self.mks_scr, self.mvs_scr) if smp else (self.mk_scr, self.mv_scr)
        P.add("sp", lambda e: e.dma_start(out=mkT[:].rearrange("p h m -> p (h m)"), in_=mks.t.ap()[li]), reads=[mks.r(li)], writes=[mkT.r()], dma=True)
        P.add("sp", lambda e: e.dma_start(out=mv[:], in_=mvs.t.ap()[li].rearrange("(t p) c -> p t c", p=128)), reads=[mvs.r(li)], writes=[mv.r()], dma=True)
        if smp:
            xst = av("xst", [KC, N + 1], BF16)
            tmask = av("tmask", [N], F32)
            so = av("so_s", [12, 128], F32, parts=64)
            xi = av("sxi", [128], F32)
            P.add("dve", lambda e: e.memset(xst[:], 0.0), writes=[xst.r()])
            P.add("dve", lambda e: e.memset(tmask[:], 0.0), writes=[tmask.r()])
            P.add("dve", lambda e: e.memset(tmask[:, 0:1], 1.0), reads=[tmask.r()], writes=[tmask.r()])
            P.add("sp", lambda e: e.dma_start(out=xi[0:KC, :], in_=d["sts"][li]), writes=[xi.r()], dma=True)
            P.add("pe", lambda e: e.transpose(out=psb[2][:, 0:KC], in_=xi[0:KC, 0:128], identity=self.ident[0:KC, 0:KC]), reads=[xi.r(), self.ident.r()], writes=[psb[2].r()])
            P.add("act", lambda e: e.copy(out=xst[:, :, 0], in_=psb[2][:, 0:KC]), reads=[psb[2].r(), xst.r()], writes=[xst.r()])
            P.add("dve", lambda e: e.tensor_copy(out=xst[:, :, 1], in_=self.xns[:, :, 1]), reads=[self.xns.r("cur"), xst.r()], writes=[xst.r()])
            s_in = av("s_in", [24 * 64], F32, parts=64)
            P.add("sp", lambda e: e.dma_start(out=s_in[:].rearrange("p (h j) -> p h j", h=24), in_=d["stw"][li].rearrange("h i j -> i h j")), writes=[s_in.r()], dma=True)
            for gi in range(12):
                pbt = psb[gi % 2]
                P.add("pe", lambda e, gi=gi, pbt=pbt: e.transpose(out=pbt[:, 0:64], in_=s_in[:, gi * 128:(gi + 1) * 128], identity=self.ident[0:64, 0:64]),
                      reads=[s_in.r(), self.ident.r()], writes=[pbt.r()])
                P.add("act", lambda e, gi=gi, pbt=pbt: e.copy(out=self.Sf[:, li, gi, :], in_=pbt[:, 0:64]), reads=[pbt.r(), self.Sf.r((li, gi)), self.Sf.r()], writes=[self.Sf.r((li, gi))])

        def proj(ps, w, xres, cur, prv):
            for kc in range(32):
                rhs = cur(kc) if kc < 16 else prv(kc - 16)
                P.add("pe", lambda e, kc=kc, rhs=rhs: e.matmul(ps, lhsT=w[0][:, kc, :], rhs=rhs, start=(kc == 0), stop=(kc == 31)),
                      reads=[w[1]] + xres, writes=[w[2]])

        def loadw(i, c0):
            P.add("pool", lambda e: e.dma_start(out=wX[i], in_=win[:, :, c0:c0 + 128]), writes=[wbig.r(i)], dma=True)
            P.add("dve", lambda e: e.tensor_tensor(out=wX[i][:, 0:16, :], in0=wX[i][:, 0:16, :], in1=wX[i][:, 16:32, :], op=ALU.subtract),
                  reads=[wbig.r(i)], writes=[wbig.r(i)])

        import os
        dbg = int(os.environ.get('KDBG', '99'))
        if dbg <= 0:
            return
        for ti in range(1 if smp else ((NT // N) if dbg >= 60 else 1)):
            t0 = ti * N
            if smp:
                cur = lambda kc: xst[:, kc, 1:1 + N]
                prv = lambda kc: xst[:, kc, 0:N]
                xres = [xst.r()]
                xkey = 0
            else:
                cur = lambda kc, t0=t0: xn[:, kc, 1 + t0:1 + t0 + N]
                prv = lambda kc, t0=t0: xn[:, kc, t0:t0 + N]
                xres = [xn.r(t0 // 512)] + ([xn.r("prev")] if t0 == 0 else ([xn.r(t0 // 512 - 1)] if t0 % 512 == 0 else []))
                xkey = t0 // 512
            moff = self.pvoff["a_mu"][0] + li * 48
            for which, (wsrc, l1, ncol, dst, func) in enumerate((("a_w1", l1w, 96, th, AF.Tanh), ("a_a1", l1a, 96, ah, AF.Copy), ("a_g1", l1gv, 128, gh, AF.Sigmoid))):
                nh = 2 if which == 2 else 1
                for hh in range(nh):
                    src = d[wsrc][li].rearrange("(k p) c -> p k c", p=128)[:, :, hh * ncol:(hh + 1) * ncol]
                    l1r = wbig.r(2) if which == 2 else l1.r()
                    P.add("pool", lambda e, l1=l1, src=src: e.dma_start(out=l1[:], in_=src), writes=[l1r], dma=True)
                    pb = psb[3]
                    for kc in range(KC):
                        xt_, xw_ = xtmp[kc % 2], xw[kc % 2]
                        mcol = moff + which * 16 + kc
                        ocol = li * 48 + which * 16 + kc
                        pk, ck = prv(kc), cur(kc)
                        P.add("dve", lambda e, xt_=xt_, pk=pk, mcol=mcol: e.tensor_scalar(out=xt_[:], in0=pk, scalar1=self.pv[:, mcol:mcol + 1], scalar2=None, op0=ALU.mult),
                              reads=xres + [self.pv.r()], writes=[xt_.r()])
                        P.add("dve", lambda e, xt_=xt_, xw_=xw_, ck=ck, ocol=ocol: e.scalar_tensor_tensor(out=xw_[:], in0=ck, scalar=self.omu[:, ocol:ocol + 1], in1=xt_[:], op0=ALU.mult, op1=ALU.add),
                              reads=xres + [self.omu.r(), xt_.r()], writes=[xw_.r()])
                        P.add("pe", lambda e, l1=l1, xw_=xw_, kc=kc, ncol=ncol: e.matmul(pb[0:ncol, 0:N], lhsT=l1[:, kc, 0:ncol], rhs=xw_[:], start=(kc == 0), stop=(kc == KC - 1)),
                              reads=[l1r, xw_.r()], writes=[pb.r()])
                    o_ap = dst[0:ncol, :] if which < 2 else dst[:, hh, :]
                    P.add("act", lambda e, o_ap=o_ap, func=func, ncol=ncol: e.activation(out=o_ap, in_=pb[0:ncol, 0:N], func=func), reads=[pb.r()], writes=[dst.r()])
            if dbg <= 1:
                return
            ngrp = 12 if dbg >= 60 else 1
            def stageA(gi):
                g_tA, bonA, gtotA, lnwA, lnbA = g_t2[gi % 2], bon2[gi % 2], gtot2[gi % 2], lnw2[gi % 2], lnb2[gi % 2]
                c0 = gi * 128
                for i in range(3):
                    loadw(i, i * 1536 + c0)
                P.add("pool", lambda e, c0=c0: e.dma_start(out=w2s[0:96, :], in_=d["a_w2"][li][:, c0:c0 + 128]), writes=[w2s.r()], dma=True)
                P.add("pool", lambda e, c0=c0: e.dma_start(out=a2s[0:96, :], in_=d["a_a2"][li][:, c0:c0 + 128]), writes=[a2s.r()], dma=True)
                P.add("pool", lambda e, c0=c0: e.dma_start(out=g2s[:], in_=d["a_g2"][li].rearrange("(k p) c -> p k c", p=128)[:, :, c0:c0 + 128]), writes=[g2s.r()], dma=True)
                P.add("sp", lambda e, c0=c0: e.dma_start(out=lnwA[:], in_=d["a_lnx_w"][li, c0:c0 + 128].partition_broadcast(64)), writes=[lnwA.r()], dma=True)
                P.add("sp", lambda e, c0=c0: e.dma_start(out=lnbA[:], in_=d["a_lnx_b"][li, c0:c0 + 128].partition_broadcast(64)), writes=[lnbA.r()], dma=True)
                for i in range(3):
                    proj(psb[i][:, 0:N], (wX[i], wbig.r(i), psb[i].r()), xres, cur, prv)
                P.add("act", lambda e: e.copy(out=r_f[:], in_=psb[0][:, 0:N]), reads=[psb[0].r()], writes=[r_f.r()])
                P.add("act", lambda e: e.copy(out=k_f[:], in_=psb[1][:, 0:N]), reads=[psb[1].r()], writes=[k_f.r()])
                P.add("act", lambda e: e.copy(out=v_b[:], in_=psb[2][:, 0:N]), reads=[psb[2].r()], writes=[v_b.r()])
                if smp:
                    for bf_ in (r_f, k_f, v_b):
                        P.add("dve", lambda e, bf_=bf_: e.tensor_tensor(out=bf_[:], in0=bf_[:], in1=tmask[:], op=ALU.mult), reads=[bf_.r(), tmask.r()], writes=[bf_.r()])
                yield
                pb = psb[3]
                P.add("pe", lambda e: e.matmul(pb[:, 0:N], lhsT=w2s[0:96, :], rhs=th[0:96, :], start=True, stop=True), reads=[w2s.r(), th.r()], writes=[pb.r()])
                P.add("pe", lambda e: e.matmul(pb[:, N:2 * N], lhsT=a2s[0:96, :], rhs=ah[0:96, :], start=True, stop=True), reads=[a2s.r(), ah.r()], writes=[pb.r()])
                pcol = li * 12 + gi
                P.add("act", lambda e, pcol=pcol: e.activation(out=ew[:], in_=pb[:, 0:N], func=AF.Exp, scale=-1.0, bias=self.neg[:, pcol:pcol + 1]), reads=[pb.r(), self.neg.r()], writes=[ew.r()])
                P.add("act", lambda e: e.activation(out=ew[:], in_=ew[:], func=AF.Ln, bias=1.0), reads=[ew.r()], writes=[ew.r()])
                P.add("act", lambda e: e.activation(out=ew[:], in_=ew[:], func=AF.Exp, scale=-1.0, bias=-0.5), reads=[ew.r()], writes=[ew.r()])
                if smp:
                    P.add("dve", lambda e: e.tensor_tensor(out=ew[:], in0=ew[:], in1=tmask[:], op=ALU.mult), reads=[ew.r(), tmask.r()], writes=[ew.r()])
                P.add("act", lambda e, pcol=pcol: e.activation(out=asig[:], in_=pb[:, N:2 * N], func=AF.Sigmoid, bias=self.pvc("a_a0", pcol)), reads=[pb.r(), self.pv.r()], writes=[asig.r()])
                pg = psb[0]
                for c in range(NC):
                    for k2 in range(2):
                        P.add("pe", lambda e, c=c, k2=k2: e.matmul(pg[0:64, c * 128:(c + 1) * 128], lhsT=gh[:, k2, c * 64:(c + 1) * 64], rhs=g2s[:, k2, :], start=(k2 == 0), stop=(k2 == 1)),
                              reads=[gh.r(), g2s.r()], writes=[pg.r()])
                P.add("act", lambda e: e.copy(out=g_tA[:].rearrange("p c f -> p (c f)"), in_=pg[0:64, 0:NC * 128]), reads=[pg.r()], writes=[g_tA.r()])
                yield
                V = lambda fn, reads, writes: P.add("dve", fn, reads=reads, writes=writes)
                A = lambda fn, reads, writes: P.add("act", fn, reads=reads, writes=writes)
                V(lambda e, pcol=pcol: e.tensor_scalar(out=kk[:], in0=k_f[:], scalar1=self.pvc("a_k_k", pcol), scalar2=None, op0=ALU.mult), [k_f.r(), self.pv.r()], [kk.r()])
                V(lambda e: e.tensor_tensor(out=sqk[:], in0=kk[:], in1=kk[:], op=ALU.mult), [kk.r()], [sqk.r()])
                pn = psb[1]
                P.add("pe", lambda e: e.matmul(pn[:, 0:N], lhsT=self.blk64b[:], rhs=sqk[:], start=True, stop=True), reads=[sqk.r(), self.blk64b.r()], writes=[pn.r()])
                A(lambda e: e.activation(out=e1[:], in_=pn[:, 0:N], func=AF.Sqrt), [pn.r()], [e1.r()])
                V(lambda e: e.tensor_scalar(out=e1[:], in0=e1[:], scalar1=1e-12, scalar2=None, op0=ALU.max), [e1.r()], [e1.r()])
                V(lambda e: e.reciprocal(out=e1[:], in_=e1[:]), [e1.r()], [e1.r()])
                V(lambda e: e.tensor_tensor(out=kk[:], in0=kk[:], in1=e1[:], op=ALU.mult), [kk.r(), e1.r()], [kk.r()])
                V(lambda e, pcol=pcol: e.tensor_scalar(out=e2[:], in0=asig[:], scalar1=self.pvc("a_k_a", pcol), scalar2=self.neg[:, 24 + pcol:25 + pcol], op0=ALU.mult, op1=ALU.add),
                  [asig.r(), self.pv.r(), self.neg.r()], [e2.r()])
                V(lambda e: e.tensor_tensor(out=kmod[:], in0=k_f[:], in1=e2[:], op=ALU.mult), [k_f.r(), e2.r()], [kmod.r()])
                V(lambda e: e.tensor_tensor(out=bb[:], in0=kk[:], in1=asig[:], op=ALU.mult), [kk.r(), asig.r()], [bb.r()])
                V(lambda e, pcol=pcol: e.scalar_tensor_tensor(out=rk[:], in0=r_f[:], scalar=self.pvc("a_r_k", pcol), in1=kmod[:], op0=ALU.mult, op1=ALU.mult),
                  [r_f.r(), kmod.r(), self.pv.r()], [rk.r()])
                for c in range(NC):
                    P.add("pe", lambda e, c=c: e.matmul(pn[0:64, N + 2 * c:N + 2 * c + 2], lhsT=rk[:, c * 64:(c + 1) * 64], rhs=hsel, start=True, stop=True),
                          reads=[rk.r(), self.c128.r()], writes=[pn.r()])
                A(lambda e: e.copy(out=bonA[:], in_=pn[0:64, N:N + 2 * NC]), [pn.r()], [bonA.r()])
                yield
                V(lambda e: e.tensor_tensor_scan(out=cum[:], data0=self.rmask[:, 0:N], data1=ew[:], initial=0.0, op0=ALU.mult, op1=ALU.add), [ew.r(), self.rmask.r()], [cum.r()])
                c3 = lambda b: b[:].rearrange("p (c t) -> p c t", t=CH)
                V(lambda e: e.tensor_copy(out=tot[:], in_=c3(cum)[:, :, CH - 1]), [cum.r()], [tot.r()])
                A(lambda e: e.activation(out=e1[:], in_=cum[:], func=AF.Exp, scale=-1.0), [cum.r()], [e1.r()])
                V(lambda e: e.tensor_tensor(out=AR[:, :, 64:128], in0=c3(r_f), in1=c3(e1), op=ALU.mult), [r_f.r(), e1.r()], [AR.r()])
                A(lambda e: e.activation(out=e2[:], in_=cum[:], func=AF.Exp), [cum.r()], [e2.r()])
                V(lambda e: e.tensor_tensor(out=Bt[:], in0=bb[:], in1=e2[:], op=ALU.mult), [bb.r(), e2.r()], [Bt.r()])
                V(lambda e: e.tensor_tensor(out=Kt[:], in0=kmod[:], in1=e2[:], op=ALU.mult), [kmod.r(), e2.r()], [Kt.r()])
                yield
                V(lambda e: e.tensor_tensor(out=e1[:], in0=ew[:], in1=cum[:], op=ALU.subtract), [ew.r(), cum.r(), AR.r()], [e1.r()])
                A(lambda e: e.activation(out=e1[:], in_=e1[:], func=AF.Exp), [e1.r()], [e1.r()])
                V(lambda e: e.scalar_tensor_tensor(out=AR[:, :, 0:64], in0=c3(kk), scalar=-1.0, in1=c3(e1), op0=ALU.mult, op1=ALU.mult), [kk.r(), e1.r()], [AR.r()])
                V(lambda e: e.tensor_tensor(out=c3(e2), in0=c3(cum), in1=tot[:].unsqueeze(2).to_broadcast([128, NC, CH]), op=ALU.subtract), [cum.r(), tot.r(), Bt.r(), Kt.r()], [e2.r()])
                A(lambda e: e.activation(out=e2[:], in_=e2[:], func=AF.Exp), [e2.r()], [e2.r()])
                V(lambda e: e.tensor_tensor(out=Bh[:], in0=bb[:], in1=e2[:], op=ALU.mult), [bb.r(), e2.r()], [Bh.r()])
                V(lambda e: e.tensor_tensor(out=Kh[:], in0=kmod[:], in1=e2[:], op=ALU.mult), [kmod.r(), e2.r()], [Kh.r()])
                A(lambda e: e.activation(out=gtotA[:], in_=tot[:], func=AF.Exp, scale=-1.0), [tot.r()], [gtotA.r()])
                yield
            pend = stageA(0)
            for _ in pend:
                pass
            for gi in range(ngrp):
                g_t, bon, gtot, lnw, lnb = g_t2[gi % 2], bon2[gi % 2], gtot2[gi % 2], lnw2[gi % 2], lnb2[gi % 2]
                pcol = li * 12 + gi
                V = lambda fn, reads, writes: P.add("dve", fn, reads=reads, writes=writes)
                A = lambda fn, reads, writes: P.add("act", fn, reads=reads, writes=writes)
                c3 = lambda b: b[:].rearrange("p (c t) -> p c t", t=CH)
                nxt = stageA(gi + 1) if gi + 1 < ngrp else None
                for h in range(2):
                    hm = self.c128[:, 128 + h:129 + h]
                    V(lambda e, h=h, hm=hm: e.tensor_scalar(out=ARh[h][:].rearrange("p c f -> p (c f)"), in0=AR[:].rearrange("p c f -> p (c f)"), scalar1=hm, scalar2=None, op0=ALU.mult), [AR.r(), self.c128.r()], [ARh[h].r()])
                    V(lambda e, h=h, hm=hm: e.tensor_scalar(out=Bth[h][:], in0=Bt[:], scalar1=hm, scalar2=None, op0=ALU.mult), [Bt.r(), self.c128.r()], [Bth[h].r()])
                    V(lambda e, h=h, hm=hm: e.tensor_scalar(out=Kth[h][:], in0=Kt[:], scalar1=hm, scalar2=None, op0=ALU.mult), [Kt.r(), self.c128.r()], [Kth[h].r()])
                if dbg <= 3:
                    return
                for (srcb, dstb) in ((Bh, BhT), (Kh, KhT), (v_b, Vt)):
                    ptb = self.psbf(6)
                    for c in range(NC):
                        P.add("pe", lambda e, c=c, srcb=srcb, ptb=ptb: e.transpose(out=ptb[0:64, c * 128:(c + 1) * 128], in_=srcb[:, c * 64:(c + 1) * 64], identity=self.identb[:]),
                              reads=[srcb.r(), self.identb.r()], writes=[psb[6].r()])
                    A(lambda e, dstb=dstb, ptb=ptb: e.copy(out=dstb[:].rearrange("p c f -> p (c f)"), in_=ptb[0:64, 0:NC * 128]), [psb[6].r()], [dstb.r()])
                if dbg <= 4:
                    return
                for c in range(NC):
                    pb7 = psb[7]
                    for h in range(2):
                        hp = slice(64 * h, 64 * h + 64)
                        P.add("pe", lambda e, c=c, h=h, hp=hp: e.matmul(pb7[0:64, (2 * h) * 128:(2 * h + 1) * 128], lhsT=Bth[h][:, c * 64:(c + 1) * 64], rhs=AR[:, c, :], start=True, stop=True),
                              reads=[Bth[h].r(), AR.r()], writes=[pb7.r()])
                        P.add("pe", lambda e, c=c, h=h, hp=hp: e.matmul(pb7[0:64, (2 * h + 1) * 128:(2 * h + 2) * 128], lhsT=Kth[h][:, c * 64:(c + 1) * 64], rhs=AR[:, c, :], start=True, stop=True),
                              reads=[Kth[h].r(), AR.r()], writes=[pb7.r()])
                    V(lambda e, c=c: e.tensor_tensor(out=NNs[:, c, :], in0=pb7[0:64, :], in1=m4, op=ALU.mult), [pb7.r(), self.c64.r()], [NNs.r()])
                pb4 = psb[4]
                for c in range(NC):
                    for h in range(2):
                        hp = slice(64 * h, 64 * h + 64)
                        m = c * 2 + h
                        P.add("pe", lambda e, c=c, h=h, m=m: e.matmul(pb4[0:64, m * 64:(m + 1) * 64], lhsT=ARh[h][:, c, 0:64], rhs=Bt[:, c * 64:(c + 1) * 64], start=True, stop=True),
                              reads=[Bt.r(), ARh[h].r()], writes=[pb4.r()])
                V(lambda e: e.tensor_tensor(out=NabT[:].rearrange("p m s -> p (m s)"), in0=pb4[0:64, 0:2 * NC * 64], in1=sl8[:, 0:2 * NC * 64], op=ALU.mult), [pb4.r(), self.c64.r()], [NabT.r()])
                if dbg <= 5:
                    return
                nm = 2 * NC
                X0 = lambda m: NNs[:, m // 2, (m % 2) * 256:(m % 2) * 256 + 64]
                NN5 = NNs[:].rearrange("p c (h q t) -> p c h q t", h=2, q=4)
                V(lambda e: e.tensor_tensor(out=Tm[0][:].rearrange("p (c h) s -> p c h s", h=2), in0=NN5[:, :, :, 0, :],
                                            in1=i8[:, 0:nm * 64].rearrange("p (c h s) -> p c h s", h=2, s=64), op=ALU.add), [NNs.r(), self.c64.r()], [Tm[0].r()])
                for lvl in range(5):
                    Xc = X0 if lvl == 0 else (lambda m, b=Xs[lvl % 2]: b[:, m, :])
                    XTc = (lambda m: NabT[:, m, :]) if lvl == 0 else (lambda m, b=XTs[lvl % 2]: b[:, m, :])
                    xr = [NNs.r(), NabT.r()] if lvl == 0 else [Xs[lvl % 2].r(), XTs[lvl % 2].r()]
                    Xn, XTn = Xs[(lvl + 1) % 2], XTs[(lvl + 1) % 2]
                    pX, pXT, pT_ = psb[4], psb[5], psb[6]
                    if lvl < 4:
                        for m in range(nm):
                            P.add("pe", lambda e, m=m, Xc=Xc, XTc=XTc: e.matmul(pX[0:64, m * 64:(m + 1) * 64], lhsT=XTc(m), rhs=Xc(m), start=True, stop=True), reads=xr, writes=[pX.r()])
                        A(lambda e, Xn=Xn: e.copy(out=Xn[:].rearrange("p m s -> p (m s)"), in_=pX[0:64, 0:nm * 64]), [pX.r()], [Xn.r()])
                    for m in range(nm):
                        P.add("pe", lambda e, m=m, Xc=Xc, XTc=XTc: e.matmul(pXT[0:64, m * 64:(m + 1) * 64], lhsT=Xc(m), rhs=XTc(m), start=True, stop=True), reads=xr, writes=[pXT.r()])
                    V(lambda e, XTn=XTn: e.tensor_copy(out=XTn[:].rearrange("p m s -> p (m s)"), in_=pXT[0:64, 0:nm * 64]), [pXT.r()], [XTn.r()])
                    Tc, Tn = Tm[lvl % 2], Tm[(lvl + 1) % 2]
                    for m in range(nm):
                        P.add("pe", lambda e, m=m, XTn=XTn, Tc=Tc: e.matmul(pT_[0:64, m * 64:(m + 1) * 64], lhsT=XTn[:, m, :], rhs=Tc[:, m, :], start=True, stop=True),
                              reads=[XTn.r(), Tc.r()], writes=[pT_.r()])
                    V(lambda e, Tc=Tc, Tn=Tn: e.tensor_tensor(out=Tn[:].rearrange("p m s -> p (m s)"), in0=pT_[0:64, 0:nm * 64], in1=Tc[:].rearrange("p m s -> p (m s)"), op=ALU.add),
                      [pT_.r(), Tc.r()], [Tn.r()])
                TF = Tm[1]
                if dbg <= 6:
                    return
                Sg = self.Sf[:, li, gi, :]
                sres = self.Sf.r((li, gi))
                A(lambda e, Sg=Sg: e.copy(out=Sb[:], in_=Sg), [sres, self.Sf.r()], [Sb.r()])
                for c in range(NC):
                    ps1, ps2, psY, psS = psb[4], psb[5], psb[6], psb[7]
                    for h in range(2):
                        hp = slice(64 * h, 64 * h + 64)
                        hs = slice(64 * h, 64 * h + 64)
                        P.add("pe", lambda e, c=c, h=h, hs=hs: e.matmul(ps1[0:64, hs], lhsT=ARh[h][:, c, 0:64], rhs=Sb[:, :], start=True, stop=False), reads=[ARh[h].r(), Sb.r()], writes=[ps1.r()])
                        P.add("pe", lambda e, c=c, h=h, hs=hs: e.matmul(ps1[0:64, hs], lhsT=NNs[:, c, h * 256 + 128:h * 256 + 192], rhs=Vt[:, c, hs], start=False, stop=True), reads=[NNs.r(), Vt.r()], writes=[ps1.r()])
                    A(lambda e: e.copy(out=W1b[:], in_=ps1[0:64, 0:128]), [ps1.r()], [W1b.r()])
                    for h in range(2):
                        hs = slice(64 * h, 64 * h + 64)
                        P.add("pe", lambda e, c=c, h=h, hs=hs: e.matmul(ps2[0:64, hs], lhsT=TF[:, c * 2 + h, :], rhs=W1b[:, hs], start=True, stop=True), reads=[TF.r(), W1b.r()], writes=[ps2.r()])
                    V(lambda e: e.tensor_copy(out=Ub[:], in_=ps2[0:64, 0:128]), [ps2.r()], [Ub.r()])
                    for h in range(2):
                        hp = slice(64 * h, 64 * h + 64)
                        hs = slice(64 * h, 64 * h + 64)
                        P.add("pe", lambda e, c=c, h=h, hs=hs: e.matmul(psY[0:64, hs], lhsT=ARh[h][:, c, 64:128], rhs=Sb[:, :], start=True, stop=False), reads=[ARh[h].r(), Sb.r()], writes=[psY.r()])
                        P.add("pe", lambda e, c=c, h=h, hs=hs: e.matmul(psY[0:64, hs], lhsT=NNs[:, c, h * 256 + 64:h * 256 + 128], rhs=Ub[:, hs], start=False, stop=False), reads=[NNs.r(), Ub.r()], writes=[psY.r()])
                        P.add("pe", lambda e, c=c, h=h, hs=hs: e.matmul(psY[0:64, hs], lhsT=NNs[:, c, h * 256 + 192:h * 256 + 256], rhs=Vt[:, c, hs], start=False, stop=True), reads=[NNs.r(), Vt.r()], writes=[psY.r()])
                    A(lambda e, c=c: e.copy(out=y_t[:, c, :], in_=psY[0:64, 0:128]), [psY.r()], [y_t.r()])
                    P.add("pe", lambda e, c=c: e.matmul(psS[:, 0:128], lhsT=BhT[:, c, :], rhs=Ub[:], start=True, stop=False), reads=[BhT.r(), Ub.r()], writes=[psS.r()])
                    P.add("pe", lambda e, c=c: e.matmul(psS[:, 0:128], lhsT=KhT[:, c, :], rhs=Vt[:, c, :], start=False, stop=True), reads=[KhT.r(), Vt.r()], writes=[psS.r()])
                    for h in range(2):
                        hp = slice(64 * h, 64 * h + 64)
                        V(lambda e, c=c, hp=hp, h=h, gi=gi, gtot=gtot: e.scalar_tensor_tensor(out=Sb[hp, :], in0=self.Sf[hp, li, gi, :], scalar=gtot[hp, c:c + 1], in1=psS[hp, 64 * h:64 * h + 64], op0=ALU.mult, op1=ALU.add),
                          [sres, psS.r(), gtot.r(), Sb.r()], [Sb.r()])
                    for h in range(2):
                        hp = slice(64 * h, 64 * h + 64)
                        V(lambda e, c=c, hp=hp, h=h, gi=gi, gtot=gtot: e.scalar_tensor_tensor(out=self.Sf[hp, li, gi, :], in0=self.Sf[hp, li, gi, :], scalar=gtot[hp, c:c + 1], in1=psS[hp, 64 * h:64 * h + 64], op0=ALU.mult, op1=ALU.add),
                          [sres, psS.r(), gtot.r()], [sres])
                    if nxt is not None:
                        next(nxt, None)
                if nxt is not None:
                    for _ in nxt:
                        pass
                if dbg <= 7:
                    return
                y3 = y_t[:].rearrange("p c (h i) -> p (c h) i", h=2)
                p3 = pt1[:].rearrange("p c (h i) -> p (c h) i", h=2)
                v3 = Vt[:].rearrange("p c (h i) -> p (c h) i", h=2)
                bc = lambda a: a.unsqueeze(2).to_broadcast([64, nm, 64])
                V(lambda e: e.tensor_reduce(out=st[:, 0:nm], in_=y3, axis=AX.X, op=ALU.add), [y_t.r()], [st.r()])
                V(lambda e: e.tensor_tensor(out=p3, in0=y3, in1=y3, op=ALU.mult), [y_t.r()], [pt1.r()])
                V(lambda e: e.tensor_reduce(out=st[:, 8:8 + nm], in_=p3, axis=AX.X, op=ALU.add), [pt1.r(), st.r()], [st.r()])
                V(lambda e: e.tensor_scalar(out=st[:, 16:16 + nm], in0=st[:, 0:nm], scalar1=1.0 / 64, scalar2=None, op0=ALU.mult), [st.r()], [st.r()])
                V(lambda e: e.tensor_tensor(out=st[:, 24:24 + nm], in0=st[:, 16:16 + nm], in1=st[:, 16:16 + nm], op=ALU.mult), [st.r()], [st.r()])
                V(lambda e: e.scalar_tensor_tensor(out=st[:, 32:32 + nm], in0=st[:, 8:8 + nm], scalar=1.0 / 64, in1=st[:, 24:24 + nm], op0=ALU.mult, op1=ALU.subtract), [st.r()], [st.r()])
                A(lambda e: e.activation(out=st[:, 40:40 + nm], in_=st[:, 32:32 + nm], func=AF.Sqrt, bias=64e-5), [st.r()], [st.r()])
                V(lambda e: e.reciprocal(out=st[:, 40:40 + nm], in_=st[:, 40:40 + nm]), [st.r()], [st.r()])
                V(lambda e: e.tensor_tensor(out=y3, in0=y3, in1=bc(st[:, 16:16 + nm]), op=ALU.subtract), [y_t.r(), st.r()], [y_t.r()])
                V(lambda e: e.tensor_tensor(out=y3, in0=y3, in1=bc(st[:, 40:40 + nm]), op=ALU.mult), [y_t.r(), st.r()], [y_t.r()])
                lb = lambda a: a[:].unsqueeze(1).to_broadcast([64, NC, 128])
                V(lambda e, lnw=lnw: e.tensor_tensor(out=y_t[:], in0=y_t[:], in1=lb(lnw), op=ALU.mult), [y_t.r(), lnw.r()], [y_t.r()])
                V(lambda e, lnb=lnb: e.tensor_tensor(out=y_t[:], in0=y_t[:], in1=lb(lnb), op=ALU.add), [y_t.r(), lnb.r()], [y_t.r()])
                V(lambda e, bon=bon: e.tensor_tensor(out=p3, in0=v3, in1=bc(bon[:, 0:nm]), op=ALU.mult), [Vt.r(), bon.r()], [pt1.r()])
                V(lambda e: e.tensor_tensor(out=y_t[:], in0=y_t[:], in1=pt1[:], op=ALU.add), [y_t.r(), pt1.r()], [y_t.r()])
                V(lambda e, g_t=g_t: e.tensor_tensor(out=BhT[:], in0=y_t[:], in1=g_t[:], op=ALU.mult), [y_t.r(), g_t.r()], [BhT.r()])
                ptb = self.psbf(6)
                for c in range(NC):
                    P.add("pe", lambda e, c=c: e.transpose(out=ptb[:, c * 64:(c + 1) * 64], in_=BhT[:, c, :], identity=self.identb[0:64, 0:64]),
                          reads=[BhT.r(), self.identb.r()], writes=[psb[6].r()])
                A(lambda e, gi=gi: e.copy(out=mixout[:, gi, :], in_=ptb[:, 0:N]), [psb[6].r()], [mixout.r()])
                if dbg == 50:
                    for nm_, bf_ in (("r_f", r_f), ("k_f", k_f), ("v_b", v_b), ("ew", ew), ("asig", asig), ("kk", kk), ("kmod", kmod), ("bb", bb), ("cum", cum),
                                     ("AR", AR), ("Bt", Bt), ("Kt", Kt), ("Bh", Bh), ("Kh", Kh), ("NNs", NNs), ("NabT", NabT), ("TF", TF), ("y_t", y_t), ("yo", BhT), ("Vt", Vt), ("th", th), ("ah", ah), ("gh", gh)):
                        self.dump(nm_, bf_)
                    self.dump("Sg", self.Sf, ap=self.Sf[:, li, gi, :])
                    self.dump("mix0", mixout, ap=mixout[:, 0, :])
                    return
            if dbg <= 8:
                return
            for h in range(4):
                loadw(0, 4608 + h * 128)
                proj(psb[0][:, 0:N], (wX[0], wbig.r(0), psb[0].r()), xres, cur, prv)
                self.mem_attend(psb[0][:, 0:N], psb[0].r(), li, h, N, qf, qn, sqk, pT, rz, mkT, mv, mixout[:, 12 + h, :], mixout.r())
            if dbg <= 9:
                return
            if dbg == 60 and ti == 3:
                self.dump('mixout', mixout)
            wo = wbig[:, 0:KC * 512].rearrange("p (k c) -> p k c", k=KC)
            wsrc = d["w_out"][li].rearrange("(k p) c -> p k c", p=128)
            for dq in range(4):
                P.add("pool", lambda e, dq=dq: e.dma_start(out=wo, in_=wsrc[:, :, dq * 512:(dq + 1) * 512]), writes=[wbig.r(0), wbig.r(1)], dma=True)
                for dl in range(4):
                    dc = dq * 4 + dl
                    pa = psb[1 + dc % 2]
                    for kc in range(KC):
                        P.add("pe", lambda e, pa=pa, kc=kc, dl=dl: e.matmul(pa[:, 0:N], lhsT=wo[:, kc, dl * 128:(dl + 1) * 128], rhs=mixout[:, kc, :], start=(kc == 0), stop=(kc == KC - 1)),
                              reads=[wbig.r(0), wbig.r(1), mixout.r()], writes=[pa.r()])
                    if smp:
                        P.add("dve", lambda e, pa=pa, dc=dc: e.tensor_tensor(out=self.xsT[:, dc:dc + 1], in0=pa[:, 0:1], in1=self.xsT[:, dc:dc + 1], op=ALU.add),
                              reads=[pa.r(), self.xsT.r()], writes=[self.xsT.r()])
                    else:
                        P.add("dve", lambda e, pa=pa, dc=dc, t0=t0: e.tensor_tensor(out=xT[:, dc, t0:t0 + N], in0=pa[:, 0:N], in1=xT[:, dc, t0:t0 + N], op=ALU.add),
                              reads=[pa.r(), xT.r(xkey)], writes=[xT.r(xkey)])
        if smp:
            for gi in range(12):
                pbt = psb[gi % 2]
                P.add("pe", lambda e, gi=gi, pbt=pbt: e.transpose(out=pbt[0:64, 0:128], in_=self.Sf[:, li, gi, :], identity=self.ident[:]),
                      reads=[self.Sf.r((li, gi)), self.ident.r()], writes=[pbt.r()])
                P.add("act", lambda e, gi=gi, pbt=pbt: e.copy(out=so[:, gi, :], in_=pbt[0:64, 0:128]), reads=[pbt.r()], writes=[so.r()])
            P.add("sp", lambda e: e.dma_start(out=self.dout["wkvs"][li].rearrange("(g h) i j -> i g h j", h=2), in_=so[:].rearrange("p g (h j) -> p g h j", h=2)), reads=[so.r()], dma=True)

    def mem_attend(self, psq, psq_res, li, h, N, qf, qn, sqb, pT, rz, mkT, mv, out_ap, out_res):
        P, psb = self.P, self.psb
        SC = 128 ** -0.5
        P.add("act", lambda e: e.copy(out=qf[:, 0:N], in_=psq), reads=[psq_res], writes=[qf.r()])
        P.add("dve", lambda e: e.tensor_tensor(out=sqb[:, 0:N], in0=qf[:, 0:N], in1=qf[:, 0:N], op=ALU.mult), reads=[qf.r()], writes=[sqb.r()])
        pn = psb[5]
        P.add("pe", lambda e: e.matmul(pn[:, 0:N], lhsT=self.onesb[:], rhs=sqb[:, 0:N], start=True, stop=True), reads=[sqb.r(), self.onesb.r()], writes=[pn.r()])
        P.add("act", lambda e: e.activation(out=rz[:, 0:N], in_=pn[:, 0:N], func=AF.Sqrt, scale=1.0 / 128, bias=1e-6), reads=[pn.r()], writes=[rz.r()])
        P.add("dve", lambda e: e.reciprocal(out=rz[:, 0:N], in_=rz[:, 0:N]), reads=[rz.r()], writes=[rz.r()])
        P.add("dve", lambda e: e.scalar_tensor_tensor(out=qn[:, 0:N], in0=qf[:, 0:N], scalar=self.pvc("mem_q_norm", li), in1=rz[:, 0:N], op0=ALU.mult, op1=ALU.mult),
              reads=[qf.r(), rz.r(), self.pv.r()], writes=[qn.r()])
        for mt in range(2):
            pS = psb[6 + mt]
            P.add("pe", lambda e, mt=mt, pS=pS: e.matmul(pS[:, 0:N], lhsT=mkT[:, h, mt * 128:(mt + 1) * 128], rhs=qn[:, 0:N], start=True, stop=True), reads=[mkT.r(), qn.r()], writes=[pS.r()])
            P.add("act", lambda e, mt=mt, pS=pS: e.activation(out=pT[mt][:, 0:N], in_=pS[:, 0:N], func=AF.Exp, scale=SC), reads=[pS.r()], writes=[pT[mt].r()])
        pO, pZ = psb[4], psb[5]
        for mt in range(2):
            P.add("pe", lambda e, mt=mt: e.matmul(pO[:, 0:N], lhsT=mv[:, mt, h * 128:(h + 1) * 128], rhs=pT[mt][:, 0:N], start=(mt == 0), stop=(mt == 1)), reads=[mv.r(), pT[mt].r()], writes=[pO.r()])
        for mt in range(2):
            P.add("pe", lambda e, mt=mt: e.matmul(pZ[:, 0:N], lhsT=self.onesb[:], rhs=pT[mt][:, 0:N], start=(mt == 0), stop=(mt == 1)), reads=[self.onesb.r(), pT[mt].r()], writes=[pZ.r()])
        P.add("dve", lambda e: e.reciprocal(out=rz[:, 0:N], in_=pZ[:, 0:N]), reads=[pZ.r()], writes=[rz.r()])
        P.add("dve", lambda e: e.tensor_tensor(out=out_ap, in0=pO[:, 0:N], in1=rz[:, 0:N], op=ALU.mult), reads=[pO.r(), rz.r()], writes=[out_res])

    def lam_setup(self):
        import math
        P, d = self.P, self.din
        sm = self.small
        row = self.colst
        for lj in range(2):
            lam_init = 0.8 - 0.6 * math.exp(-0.3 * (2 + lj))
            P.add("sp", lambda e, lj=lj: e.dma_start(out=row[0:1, 0:128], in_=d["b_lam"][lj:lj + 1, 0:128]), writes=[row.r()], dma=True)
            P.add("sp", lambda e, lj=lj: e.dma_start(out=row[1:2, 0:128], in_=d["b_lam"][lj:lj + 1, 128:256]), writes=[row.r()], dma=True)
            P.add("dve", lambda e: e.tensor_tensor(out=row[0:2, 0:64], in0=row[0:2, 0:64], in1=row[0:2, 64:128], op=ALU.mult), reads=[row.r()], writes=[row.r()])
            P.add("dve", lambda e: e.tensor_reduce(out=sm[0:2, 40:41], in_=row[0:2, 0:64], axis=AX.X, op=ALU.add), reads=[row.r()], writes=[sm.r("lam")])
            P.add("act", lambda e: e.activation(out=sm[0:2, 40:41], in_=sm[0:2, 40:41], func=AF.Exp), reads=[sm.r("lam")], writes=[sm.r("lam")])
            P.add("dve", lambda e: e.memset(sm[0:2, 41:42], 1.0), reads=[sm.r("lam")], writes=[sm.r("lam2")])
            P.add("dve", lambda e: e.memset(self.colst[0:2, 0:128], -1.0), reads=[row.r(), sm.r("lam")], writes=[row.r()])
            P.add("dve", lambda e: e.memset(self.colst[0:1, 0:128], 1.0), reads=[row.r()], writes=[row.r()])
            pb = self.psb[0]
            P.add("pe", lambda e: e.matmul(pb[:, 0:1], lhsT=row[0:2, 0:128], rhs=sm[0:2, 40:41], start=True, stop=True), reads=[row.r(), sm.r("lam")], writes=[pb.r()])
            P.add("dve", lambda e, lj=lj, lam_init=lam_init: e.tensor_scalar(out=self.lamv[:, 2 * lj:2 * lj + 1], in0=pb[:, 0:1], scalar1=-1.0, scalar2=-lam_init, op0=ALU.mult, op1=ALU.add),
                  reads=[pb.r()], writes=[self.lamv.r()])
            P.add("dve", lambda e, lj=lj, lam_init=lam_init: e.memset(self.lamv[:, 2 * lj + 1:2 * lj + 2], 1.0 - lam_init), reads=[self.lamv.r()], writes=[self.lamv.r()])

    def shared_kv(self, blk, sample):
        P, d, o = self.P, self.din, self.dout
        xn, psb = self.xn, self.psb
        self.phase("kv")
        av = self.av
        self.rmsnorm("kv_norm", 0, sample)
        wkv = [av("wkv0", [KC, 512], BF16), av("wkv1", [KC, 512], BF16)]
        kf = [av("kf0", [512], F32), av("kf1", [512], F32)]
        sqv = av("sqv", [512], F32)
        kb = [av("kb0", [512], BF16), av("kb1", [512], BF16)]
        kts = [av("kts0", [512], BF16), av("kts1", [512], BF16)]
        st = av("stk", [16], F32)
        gk = av("gk64", [64], F32)
        cs = av("cs", [8, 16], F32)
        css = av("css", [16], F32)
        rt = av("rt", [8, 16], F32)
        rt2 = av("rt2", [8, 16], F32)
        P.add("sp", lambda e: e.dma_start(out=gk[:], in_=d["k_norm"][0, :].partition_broadcast(128)), writes=[gk.r()], dma=True)
        P.add("sp", lambda e: e.dma_start(out=cs[:], in_=d["cs_tok"][blk * NT:(blk + 1) * NT, :].rearrange("(t p) f -> p t f", p=128)), writes=[cs.r()], dma=True)
        if sample:
            P.add("sp", lambda e: e.dma_start(out=css[0:1, :], in_=d["cs_tok"][4096:4097, :]), writes=[css.r()], dma=True)
        wv = d["kv_w"].rearrange("(kc p) c -> p kc c", p=128)

        def post(pb, npart, cb, csap, kf_, kb_, out_rows_k, out_rows_v, tb):
            V = lambda fn, reads, writes: P.add("dve", fn, reads=reads, writes=writes)
            pp = slice(0, npart)
            if cb < 3:
                P.add("act", lambda e: e.activation(out=sqv[pp, :], in_=pb[pp, :], func=AF.Square), reads=[pb.r()], writes=[sqv.r()])
                V(lambda e: e.tensor_reduce(out=st[pp, 0:8], in_=sqv[pp, :].rearrange("p (g d) -> p g d", g=8), axis=AX.X, op=ALU.add), [sqv.r()], [st.r()])
                P.add("act", lambda e: e.activation(out=st[pp, 0:8], in_=st[pp, 0:8], func=AF.Sqrt, scale=1.0 / 64, bias=1e-6), reads=[st.r()], writes=[st.r()])
                V(lambda e: e.reciprocal(out=st[pp, 0:8], in_=st[pp, 0:8]), [st.r()], [st.r()])
                k3 = kf_[pp, :].rearrange("p (g d) -> p g d", g=8)
                V(lambda e: e.tensor_tensor(out=k3, in0=pb[pp, :].rearrange("p (g d) -> p g d", g=8), in1=st[pp, 0:8].unsqueeze(2).to_broadcast([npart, 8, 64]), op=ALU.mult), [pb.r(), st.r()], [kf_.r()])
                V(lambda e: e.tensor_tensor(out=k3, in0=k3, in1=gk[pp, :].unsqueeze(1).to_broadcast([npart, 8, 64]), op=ALU.mult), [kf_.r(), gk.r()], [kf_.r()])
                cb_ = csap[:, 0:8].unsqueeze(1).to_broadcast([npart, 8, 8])
                sb_ = csap[:, 8:16].unsqueeze(1).to_broadcast([npart, 8, 8])
                x1, x2 = k3[:, :, 0:8], k3[:, :, 8:16]
                V(lambda e: e.tensor_tensor(out=rt[pp, :, 0:8], in0=x1, in1=cb_, op=ALU.mult), [kf_.r(), cs.r(), css.r()], [rt.r()])
                V(lambda e: e.tensor_tensor(out=rt[pp, :, 8:16], in0=x2, in1=cb_, op=ALU.mult), [kf_.r(), cs.r(), css.r()], [rt.r()])
                V(lambda e: e.tensor_tensor(out=rt2[pp, :, 0:8], in0=x2, in1=sb_, op=ALU.mult), [kf_.r(), cs.r(), css.r()], [rt2.r()])
                V(lambda e: e.tensor_tensor(out=rt2[pp, :, 8:16], in0=x1, in1=sb_, op=ALU.mult), [kf_.r(), cs.r(), css.r()], [rt2.r()])
                V(lambda e: e.tensor_tensor(out=k3[:, :, 0:8], in0=rt[pp, :, 0:8], in1=rt2[pp, :, 0:8], op=ALU.subtract), [rt.r(), rt2.r(), kf_.r()], [kf_.r()])
                V(lambda e: e.tensor_tensor(out=k3[:, :, 8:16], in0=rt[pp, :, 8:16], in1=rt2[pp, :, 8:16], op=ALU.add), [rt.r(), rt2.r(), kf_.r()], [kf_.r()])
                P.add("sp", lambda e: e.dma_start(out=out_rows_k[:, cb * 512:(cb + 1) * 512], in_=kf_[pp, :]), reads=[kf_.r()], dma=True)
                if tb is None:
                    P.add("sp", lambda e: e.dma_start(out=self.ks_scr.t.ap()[:, cb * 512:(cb + 1) * 512], in_=kf_[pp, :]), reads=[kf_.r()], writes=[self.ks_scr.r(cb)], dma=True)
                if tb is not None:
                    P.add("act", lambda e: e.copy(out=kb_[:, :], in_=kf_[:, :]), reads=[kf_.r()], writes=[kb_.r()])
                    ptb = self.psbf(4 + tb % 2)
                    kt_ = kts[tb % 2]
                    for hh in range(4):
                        P.add("pe", lambda e, hh=hh: e.transpose(out=ptb[:, hh * 128:(hh + 1) * 128], in_=kb_[:, hh * 128:(hh + 1) * 128], identity=self.identb[:]),
                              reads=[kb_.r(), self.identb.r()], writes=[psb[4 + tb % 2].r()])
                    P.add("act", lambda e: e.copy(out=kt_[:, :], in_=ptb[:, 0:512]), reads=[psb[4 + tb % 2].r()], writes=[kt_.r()])
                    k0 = blk * NT + tb * 128
                    P.add("sp", lambda e: e.dma_start(out=self.kt_scr.t.ap()[cb * 4:(cb + 1) * 4, :, k0:k0 + 128].rearrange("h p k -> p h k"), in_=kt_[:, :].rearrange("p (h k) -> p h k", h=4)),
                          reads=[kt_.r()], writes=[self.kt_scr.r(blk)], dma=True)
            else:
                vc = cb - 3
                P.add("act", lambda e: e.copy(out=kf_[pp, :], in_=pb[pp, :]), reads=[pb.r()], writes=[kf_.r()])
                P.add("sp", lambda e: e.dma_start(out=out_rows_v[:, vc * 512:(vc + 1) * 512], in_=kf_[pp, :]), reads=[kf_.r()], dma=True)
                if tb is None:
                    P.add("sp", lambda e: e.dma_start(out=self.vs_scr.t.ap()[:, vc * 512:(vc + 1) * 512], in_=kf_[pp, :]), reads=[kf_.r()], writes=[self.vs_scr.r(vc)], dma=True)
                if tb is not None:
                    V(lambda e: e.tensor_copy(out=kb_[:, :], in_=kf_[:, :]), [kf_.r()], [kb_.r()])
                    r0 = blk * NT + tb * 128
                    P.add("sp", lambda e: e.dma_start(out=self.v_scr.t.ap()[r0:r0 + 128, vc * 512:(vc + 1) * 512], in_=kb_[:, :]), reads=[kb_.r()], writes=[self.v_scr.r(blk)], dma=True)

        for cb in range(6):
            w = wkv[cb % 2]
            P.add("pool", lambda e, w=w, cb=cb: e.dma_start(out=w[:], in_=wv[:, :, cb * 512:(cb + 1) * 512]), writes=[w.r()], dma=True)
            for tb in range(8):
                pb = psb[tb % 4]
                for kc in range(KC):
                    P.add("pe", lambda e, pb=pb, w=w, kc=kc, tb=tb: e.matmul(pb[:, :], lhsT=xn[:, kc, 1 + tb * 128:1 + (tb + 1) * 128], rhs=w[:, kc, :], start=(kc == 0), stop=(kc == KC - 1)),
                          reads=[w.r(), xn.r(tb // 4)], writes=[pb.r()])
                r0 = blk * NT + tb * 128
                post(pb, 128, cb, cs[:, tb, :], kf[tb % 2], kb[tb % 2], o["krp"][r0:r0 + 128, :], o["vrp"][r0:r0 + 128, :], tb)
            if sample:
                pb = psb[6]
                for kc in range(KC):
                    P.add("pe", lambda e, pb=pb, w=w, kc=kc: e.matmul(pb[0:1, :], lhsT=self.xns[:, kc, 1:2], rhs=w[:, kc, :], start=(kc == 0), stop=(kc == KC - 1)),
                          reads=[w.r(), self.xns.r("cur")], writes=[pb.r()])
                post(pb, 1, cb, css[0:1, :], kf[0], kb[0], o["krs"], o["vrs"], None)

    def mixerB(self, lj, blk, sample):
        P, d = self.P, self.din
        xn, xT, psb = self.xn, self.xT, self.psb
        li = 2 + lj
        self.phase("mixB")
        av = self.av
        N = 512
        nkeys = (blk + 1) * NT
        KT = [av("KT0", [NBLK * NT], BF16), av("KT1", [NBLK * NT], BF16)]
        VV = [av("VV0", [NBLK * 8, 128], BF16), av("VV1", [NBLK * 8, 128], BF16)]
        wq = [av("wq0", [KC, 128], BF16), av("wq1", [KC, 128], BF16)]
        qf = av("qf", [N], F32)
        sqb = av("sqb", [N], BF16)
        rz = av("rz", [N], F32)
        qn = av("qn", [N], BF16)
        rC = av("rC", [N], F32)
        rS = av("rS", [N], F32)
        qm = [av("qm0", [N], BF16), av("qm1", [N], BF16)]
        pT = [av(f"pT{i}", [N], BF16) for i in range(4)]
        o0 = av("o0", [N], F32)
        o1 = av("o1", [N], F32)
        mixout = av("mixout", [KC, N], BF16)
        mkT = av("mkT", [4, MEMT], BF16)
        mv = av("mv", [2, 512], BF16)
        wo = av("wo", [KC, 256], BF16)
        SC = 64 ** -0.5
        V = lambda fn, reads, writes: P.add("dve", fn, reads=reads, writes=writes)
        A = lambda fn, reads, writes: P.add("act", fn, reads=reads, writes=writes)
        P.add("sp", lambda e: e.dma_start(out=mkT[:].rearrange("p h m -> p (h m)"), in_=self.mk_scr.t.ap()[li]), reads=[self.mk_scr.r(li)], writes=[mkT.r()], dma=True)
        P.add("sp", lambda e: e.dma_start(out=mv[:], in_=self.mv_scr.t.ap()[li].rearrange("(t p) c -> p t c", p=128)), reads=[self.mv_scr.r(li)], writes=[mv.r()], dma=True)
        win = d["b_w_in"][lj].rearrange("(k p) c -> p k c", p=128)
        kvres = [self.kt_scr.r(bb_) for bb_ in range(blk + 1)] + [self.v_scr.r(bb_) for bb_ in range(blk + 1)]
        for qt in range(2):
            t0 = qt * N
            q0 = blk * NT + t0
            xres = [xn.r(qt)]
            P.add("sp", lambda e, q0=q0: e.dma_start(out=rC[:], in_=d["ropeC"][:, q0:q0 + N]), writes=[rC.r()], dma=True)
            P.add("sp", lambda e, q0=q0: e.dma_start(out=rS[:], in_=d["ropeS"][:, q0:q0 + N]), writes=[rS.r()], dma=True)
            nkt = (q0 + N) // 128
            for hd in range(12):
                it = qt * 12 + hd
                w = wq[it % 2]
                kt_, vv_ = KT[it % 2], VV[it % 2]
                P.add("pool", lambda e, w=w, hd=hd: e.dma_start(out=w[:], in_=win[:, :, hd * 128:(hd + 1) * 128]), writes=[w.r()], dma=True)
                P.add("sp", lambda e, kt_=kt_, hd=hd, nkt=nkt: e.dma_start(out=kt_[:, 0:nkt * 128], in_=self.kt_scr.t.ap()[hd, :, 0:nkt * 128]), reads=kvres, writes=[kt_.r()], dma=True)
                P.add("sp", lambda e, vv_=vv_, hd=hd, nkt=nkt: e.dma_start(out=vv_[:, 0:nkt, :], in_=self.v_scr.t.ap()[0:nkt * 128, hd * 128:(hd + 1) * 128].rearrange("(t p) c -> p t c", p=128)),
                      reads=kvres, writes=[vv_.r()], dma=True)
                pq = psb[0]
                for kc in range(KC):
                    P.add("pe", lambda e, w=w, kc=kc, t0=t0: e.matmul(pq[:, 0:N], lhsT=w[:, kc, :], rhs=xn[:, kc, 1 + t0:1 + t0 + N], start=(kc == 0), stop=(kc == KC - 1)),
                          reads=[w.r()] + xres, writes=[pq.r()])
                A(lambda e: e.copy(out=qf[:], in_=pq[:, 0:N]), [pq.r()], [qf.r()])
                V(lambda e: e.tensor_tensor(out=sqb[:], in0=qf[:], in1=qf[:], op=ALU.mult), [qf.r()], [sqb.r()])
                pn = psb[1]
                P.add("pe", lambda e: e.matmul(pn[:, 0:N], lhsT=self.blk64b[:], rhs=sqb[:], start=True, stop=True), reads=[sqb.r(), self.blk64b.r()], writes=[pn.r()])
                A(lambda e: e.activation(out=rz[:], in_=pn[:, 0:N], func=AF.Sqrt, scale=1.0 / 64, bias=1e-6), [pn.r()], [rz.r()])
                V(lambda e: e.reciprocal(out=rz[:], in_=rz[:]), [rz.r()], [rz.r()])
                V(lambda e: e.scalar_tensor_tensor(out=qf[:], in0=qf[:], scalar=self.pvc("b_q_norm", lj), in1=rz[:], op0=ALU.mult, op1=ALU.mult), [qf.r(), rz.r(), self.pv.r()], [qf.r()])
                V(lambda e: e.tensor_copy(out=qn[:], in_=qf[:]), [qf.r()], [qn.r()])
                P.add("pe", lambda e: e.matmul(pn[:, 0:N], lhsT=self.rotb[:], rhs=qn[:], start=True, stop=True), reads=[qn.r(), self.rotb.r()], writes=[pn.r()])
                V(lambda e: e.tensor_tensor(out=rz[:], in0=pn[:, 0:N], in1=rS[:], op=ALU.mult), [pn.r(), rS.r(), rz.r()], [rz.r()])
                V(lambda e: e.tensor_tensor(out=qf[:], in0=qf[:], in1=rC[:], op=ALU.mult), [qf.r(), rC.r()], [qf.r()])
                V(lambda e: e.tensor_tensor(out=qf[:], in0=qf[:], in1=rz[:], op=ALU.add), [qf.r(), rz.r()], [qf.r()])
                for c in range(2):
                    V(lambda e, c=c: e.tensor_scalar(out=qm[c][:], in0=qf[:], scalar1=self.c128[:, 128 + c:129 + c], scalar2=None, op0=ALU.mult), [qf.r(), self.c128.r()], [qm[c].r()])
                pO = [psb[2], psb[3]]
                pZ = [psb[4], psb[5]]
                for kt in range(nkt):
                    r = kt - (q0 // 128)
                    for c in range(2):
                        pS = psb[6 + c]
                        pt_ = pT[(kt % 2) * 2 + c]
                        P.add("pe", lambda e, kt=kt, c=c, pS=pS, kt_=kt_: e.matmul(pS[:, 0:N], lhsT=kt_[:, kt * 128:(kt + 1) * 128], rhs=qm[c][:], start=True, stop=True),
                              reads=[kt_.r(), qm[c].r()], writes=[pS.r()])
                        A(lambda e, pS=pS, pt_=pt_: e.activation(out=pt_[:], in_=pS[:, 0:N], func=AF.Exp, scale=SC), [pS.r()], [pt_.r()])
                        if r >= 0:
                            P.add("pool", lambda e, pt_=pt_, r=r: e.tensor_tensor(out=pt_[:], in0=pt_[:], in1=self.dmask[:, r * 512:(r + 1) * 512], op=ALU.mult),
                                  reads=[pt_.r(), self.dmask.r()], writes=[pt_.r()])
                        P.add("pe", lambda e, kt=kt, c=c, pt_=pt_, vv_=vv_, nkt=nkt: e.matmul(pO[c][:, 0:N], lhsT=vv_[:, kt, :], rhs=pt_[:], start=(kt == 0), stop=(kt == nkt - 1)),
                              reads=[vv_.r(), pt_.r()], writes=[pO[c].r()])
                        P.add("pe", lambda e, kt=kt, c=c, pt_=pt_, nkt=nkt: e.matmul(pZ[c][:, 0:N], lhsT=self.onesb[:], rhs=pt_[:], start=(kt == 0), stop=(kt == nkt - 1)),
                              reads=[self.onesb.r(), pt_.r()], writes=[pZ[c].r()])
                V(lambda e: e.reciprocal(out=rz[:], in_=pZ[0][:, 0:N]), [pZ[0].r()], [rz.r()])
                V(lambda e: e.tensor_tensor(out=o0[:], in0=pO[0][:, 0:N], in1=rz[:], op=ALU.mult), [pO[0].r(), rz.r()], [o0.r()])
                V(lambda e: e.reciprocal(out=rz[:], in_=pZ[1][:, 0:N]), [pZ[1].r(), o0.r()], [rz.r()])
                V(lambda e: e.tensor_tensor(out=o1[:], in0=pO[1][:, 0:N], in1=rz[:], op=ALU.mult), [pO[1].r(), rz.r()], [o1.r()])
                V(lambda e: e.scalar_tensor_tensor(out=o0[:], in0=o1[:], scalar=self.lamv[:, 2 * lj:2 * lj + 1], in1=o0[:], op0=ALU.mult, op1=ALU.add), [o0.r(), o1.r(), self.lamv.r()], [o0.r()])
                V(lambda e: e.tensor_tensor(out=sqb[:], in0=o0[:], in1=o0[:], op=ALU.mult), [o0.r()], [sqb.r()])
                P.add("pe", lambda e: e.matmul(pn[:, 0:N], lhsT=self.onesb[:], rhs=sqb[:], start=True, stop=True), reads=[sqb.r(), self.onesb.r()], writes=[pn.r()])
                A(lambda e: e.activation(out=rz[:], in_=pn[:, 0:N], func=AF.Sqrt, scale=1.0 / 128, bias=1e-5), [pn.r()], [rz.r()])
                V(lambda e: e.reciprocal(out=rz[:], in_=rz[:]), [rz.r()], [rz.r()])
                V(lambda e: e.scalar_tensor_tensor(out=o0[:], in0=o0[:], scalar=self.pvc("b_subln", lj), in1=rz[:], op0=ALU.mult, op1=ALU.mult), [o0.r(), rz.r(), self.pv.r()], [o0.r()])
                V(lambda e, hd=hd: e.tensor_scalar(out=mixout[:, hd, :], in0=o0[:], scalar1=self.lamv[:, 2 * lj + 1:2 * lj + 2], scalar2=None, op0=ALU.mult), [o0.r(), self.lamv.r()], [mixout.r()])
                import os
                if int(os.environ.get('KDBG', '99')) == 70:
                    self.dump("qf", qf); self.dump("qm0", qm[0]); self.dump("o0", o0); self.dump("o1", o1); self.dump("lamv", self.lamv); self.dump("KT", kt_, ap=kt_[:, 0:512])
                    self.dump("VV", vv_, ap=vv_[:, 0:4, :]); self.dump("mix0", mixout, ap=mixout[:, 0, :]); self.dump("pT0", pT[0]); self.dump("rz", rz)
                    return
            for h in range(4):
                it = h
                w = wq[it % 2]
                P.add("pool", lambda e, w=w, h=h: e.dma_start(out=w[:], in_=win[:, :, 1536 + h * 128:1536 + (h + 1) * 128]), writes=[w.r()], dma=True)
                pq = psb[0]
                for kc in range(KC):
                    P.add("pe", lambda e, w=w, kc=kc, t0=t0: e.matmul(pq[:, 0:N], lhsT=w[:, kc, :], rhs=xn[:, kc, 1 + t0:1 + t0 + N], start=(kc == 0), stop=(kc == KC - 1)),
                          reads=[w.r()] + xres, writes=[pq.r()])
                self.mem_attend(pq[:, 0:N], pq.r(), li, h, N, qf, qn, sqb, [pT[0], pT[1]], rz, mkT, mv, mixout[:, 12 + h, :], mixout.r())
            wsrc = d["w_out"][li].rearrange("(k p) c -> p k c", p=128)
            for dq in range(8):
                P.add("pool", lambda e, dq=dq: e.dma_start(out=wo[:], in_=wsrc[:, :, dq * 256:(dq + 1) * 256]), writes=[wo.r()], dma=True)
                for dl in range(2):
                    dc = dq * 2 + dl
                    pa = psb[1 + dc % 2]
                    for kc in range(KC):
                        P.add("pe", lambda e, pa=pa, kc=kc, dl=dl: e.matmul(pa[:, 0:N], lhsT=wo[:, kc, dl * 128:(dl + 1) * 128], rhs=mixout[:, kc, :], start=(kc == 0), stop=(kc == KC - 1)),
                              reads=[wo.r(), mixout.r()], writes=[pa.r()])
                    P.add("dve", lambda e, pa=pa, dc=dc, t0=t0: e.tensor_tensor(out=xT[:, dc, t0:t0 + N], in0=pa[:, 0:N], in1=xT[:, dc, t0:t0 + N], op=ALU.add),
                          reads=[pa.r(), xT.r(qt)], writes=[xT.r(qt)])

    def mixerB_s(self, lj):
        P, d = self.P, self.din
        psb = self.psb
        li = 2 + lj
        self.phase("mixBs")
        av = self.av
        N = 64
        NPG = 128
        V = lambda fn, reads, writes: P.add("dve", fn, reads=reads, writes=writes)
        A = lambda fn, reads, writes: P.add("act", fn, reads=reads, writes=writes)
        xst = av("xst", [KC, N], BF16)
        wq = [av("wq0", [KC, 128], BF16), av("wq1", [KC, 128], BF16)]
        qs = av("qs", [16], F32)
        qsb = av("qsb", [16], BF16)
        rz = av("rz", [N], F32)
        rcs = av("rcs", [2], F32)
        qrow = av("qrow", [128], F32)
        qbc = av("qbc", [1536], F32)
        ptb = av("ptb", [128], I32)
        ptf = av("ptf", [128], F32)
        idx = av("idx", [128], I32)
        kp = [av(f"kp{i}", [1536], F32) for i in range(3)]
        prod = av("prod", [1536], F32)
        S = av("S", [NPG + 1, 24], F32)
        mx = av("mx", [24], F32)
        mxT = av("mxT", [128], F32)
        dg = av("dg", [24], F32)
        mxb = av("mxb", [24], F32)
        o24 = av("o24", [12, 128], F32)
        od = av("od", [128], F32)
        zc = av("zc", [2], F32)
        oT = av("oT", [24], F32)
        of = av("of", [16], F32)
        sqb = av("sqb", [N], BF16)
        qf = av("qf", [N], F32)
        qn = av("qn", [N], BF16)
        pT = [av("pT0", [N], BF16), av("pT1", [N], BF16)]
        mixs = av("mixs", [KC, N], BF16)
        mkT = av("mkT", [4, MEMT], BF16)
        mv = av("mv", [2, 512], BF16)
        wo = av("wo", [KC, 256], BF16)
        P.add("sp", lambda e: e.dma_start(out=mkT[:].rearrange("p h m -> p (h m)"), in_=self.mks_scr.t.ap()[li]), reads=[self.mks_scr.r(li)], writes=[mkT.r()], dma=True)
        P.add("sp", lambda e: e.dma_start(out=mv[:], in_=self.mvs_scr.t.ap()[li].rearrange("(t p) c -> p t c", p=128)), reads=[self.mvs_scr.r(li)], writes=[mv.r()], dma=True)
        P.add("sp", lambda e: e.dma_start(out=rcs[:, 0:1], in_=d["ropeC"][:, 4096:4097], allow_slow_non_contiguous=True), writes=[rcs.r()], dma=True)
        P.add("sp", lambda e: e.dma_start(out=rcs[:, 1:2], in_=d["ropeS"][:, 4096:4097], allow_slow_non_contiguous=True), writes=[rcs.r()], dma=True)
        V(lambda e: e.memset(xst[:], 0.0), [], [xst.r()])
        V(lambda e: e.tensor_copy(out=xst[:, :, 0], in_=self.xns[:, :, 1]), [self.xns.r("cur"), xst.r()], [xst.r()])
        V(lambda e: e.memset(mixs[:], 0.0), [], [mixs.r()])
        win = d["b_w_in"][lj].rearrange("(k p) c -> p k c", p=128)
        pq = psb[0]
        for hd in range(12):
            w = wq[hd % 2]
            P.add("pool", lambda e, w=w, hd=hd: e.dma_start(out=w[:], in_=win[:, :, hd * 128:(hd + 1) * 128]), writes=[w.r()], dma=True)
            for kc in range(KC):
                P.add("pe", lambda e, w=w, kc=kc, hd=hd: e.matmul(pq[:, hd:hd + 1], lhsT=w[:, kc, :], rhs=xst[:, kc, 0:1], start=(kc == 0), stop=(kc == KC - 1)),
                      reads=[w.r(), xst.r()], writes=[pq.r()])
        A(lambda e: e.copy(out=qs[:, 0:12], in_=pq[:, 0:12]), [pq.r()], [qs.r()])
        V(lambda e: e.tensor_tensor(out=qsb[:, 0:12], in0=qs[:, 0:12], in1=qs[:, 0:12], op=ALU.mult), [qs.r()], [qsb.r()])
        pn = psb[1]
        P.add("pe", lambda e: e.matmul(pn[:, 0:12], lhsT=self.blk64b[:], rhs=qsb[:, 0:12], start=True, stop=True), reads=[qsb.r(), self.blk64b.r()], writes=[pn.r()])
        A(lambda e: e.activation(out=rz[:, 0:12], in_=pn[:, 0:12], func=AF.Sqrt, scale=1.0 / 64, bias=1e-6), [pn.r()], [rz.r()])
        V(lambda e: e.reciprocal(out=rz[:, 0:12], in_=rz[:, 0:12]), [rz.r()], [rz.r()])
        V(lambda e: e.scalar_tensor_tensor(out=qs[:, 0:12], in0=qs[:, 0:12], scalar=self.pvc("b_q_norm", lj), in1=rz[:, 0:12], op0=ALU.mult, op1=ALU.mult), [qs.r(), rz.r(), self.pv.r()], [qs.r()])
        V(lambda e: e.tensor_copy(out=qsb[:, 0:12], in_=qs[:, 0:12]), [qs.r(), qsb.r()], [qsb.r()])
        P.add("pe", lambda e: e.matmul(pn[:, 0:12], lhsT=self.rotb[:], rhs=qsb[:, 0:12], start=True, stop=True), reads=[qsb.r(), self.rotb.r()], writes=[pn.r()])
        V(lambda e: e.tensor_scalar(out=rz[:, 0:12], in0=pn[:, 0:12], scalar1=rcs[:, 1:2], scalar2=None, op0=ALU.mult), [pn.r(), rcs.r(), rz.r()], [rz.r()])
        V(lambda e: e.scalar_tensor_tensor(out=qs[:, 0:12], in0=qs[:, 0:12], scalar=rcs[:, 0:1], in1=rz[:, 0:12], op0=ALU.mult, op1=ALU.add), [qs.r(), rz.r(), rcs.r()], [qs.r()])
        V(lambda e: e.tensor_scalar(out=qs[:, 0:12], in0=qs[:, 0:12], scalar1=64 ** -0.5, scalar2=None, op0=ALU.mult), [qs.r()], [qs.r()])
        P.add("pe", lambda e: e.transpose(out=pn[0:12, 0:128], in_=qs[:, 0:12], identity=self.ident[:]), reads=[qs.r(), self.ident.r()], writes=[pn.r()])
        A(lambda e: e.copy(out=qrow[0:12, :], in_=pn[0:12, 0:128]), [pn.r()], [qrow.r()])
        P.add("sp", lambda e: e.dma_start(out=self.q_scr.t.ap()[lj], in_=qrow[0:12, :]), reads=[qrow.r()], writes=[self.q_scr.r(lj)], dma=True)
        P.add("sp", lambda e: e.dma_start(out=qbc[:], in_=self.q_scr.t.ap()[lj].rearrange("h f -> (h f)").partition_broadcast(128)), reads=[self.q_scr.r(lj)], writes=[qbc.r()], dma=True)
        P.add("sp", lambda e: e.dma_start(out=ptb[:], in_=d["pt"][0, :].partition_broadcast(128)), writes=[ptb.r()], dma=True)
        V(lambda e: e.tensor_copy(out=ptf[:], in_=ptb[:]), [ptb.r()], [ptf.r()])
        V(lambda e: e.tensor_scalar(out=ptf[:], in0=ptf[:], scalar1=128.0, scalar2=self.c128[:, 130:131], op0=ALU.mult, op1=ALU.add), [ptf.r(), self.c128.r()], [ptf.r()])
        V(lambda e: e.tensor_copy(out=idx[:], in_=ptf[:]), [ptf.r()], [idx.r()])
        q3 = qbc[:].rearrange("p (g d) -> p g d", d=64)
        for pg in range(NPG + 1):
            kb_ = kp[pg % 3]
            if pg < NPG:
                P.add("pool", lambda e, kb_=kb_, pg=pg: e.indirect_dma_start(out=kb_[:], out_offset=None, in_=d["ck"], in_offset=bass.IndirectOffsetOnAxis(ap=idx[:, pg:pg + 1], axis=0)),
                      reads=[idx.r()], writes=[kb_.r()], dma=True)
            else:
                V(lambda e, kb_=kb_: e.memset(kb_[:], 0.0), [], [kb_.r()])
                P.add("sp", lambda e, kb_=kb_: e.dma_start(out=kb_[0:1, :], in_=self.ks_scr.t.ap()), reads=[self.ks_scr.r(0), self.ks_scr.r(1), self.ks_scr.r(2), kb_.r()], writes=[kb_.r()], dma=True)
            V(lambda e, kb_=kb_: e.tensor_tensor(out=prod[:], in0=kb_[:], in1=qbc[:], op=ALU.mult), [kb_.r(), qbc.r()], [prod.r()])
            V(lambda e, pg=pg: e.tensor_reduce(out=S[:, pg, :], in_=prod[:].rearrange("p (g d) -> p g d", d=64), axis=AX.X, op=ALU.add), [prod.r()], [S.r()])
        V(lambda e: e.tensor_scalar(out=S[:, NPG, :], in0=S[:, NPG, :], scalar1=self.c128[:, 131:132], scalar2=None, op0=ALU.add), [S.r(), self.c128.r()], [S.r()])
        V(lambda e: e.tensor_reduce(out=mx[:], in_=S[:].rearrange("p g c -> p c g"), axis=AX.X, op=ALU.max), [S.r()], [mx.r()])
        P.add("pe", lambda e: e.transpose(out=pn[0:24, 0:128], in_=mx[:, 0:24], identity=self.ident[:]), reads=[mx.r(), self.ident.r()], writes=[pn.r()])
        A(lambda e: e.copy(out=mxT[0:24, :], in_=pn[0:24, 0:128]), [pn.r()], [mxT.r()])
        V(lambda e: e.tensor_reduce(out=zc[0:24, 0:1], in_=mxT[0:24, :], axis=AX.X, op=ALU.max), [mxT.r()], [zc.r()])
        V(lambda e: e.tensor_scalar(out=dg[0:24, 0:24], in0=self.ident[0:24, 0:24], scalar1=zc[0:24, 0:1], scalar2=None, op0=ALU.mult), [zc.r(), self.ident.r()], [dg.r()])
        P.add("pe", lambda e: e.matmul(pn[:, 0:24], lhsT=self.onesf[0:24, :], rhs=dg[0:24, 0:24], start=True, stop=True), reads=[dg.r(), self.onesf.r()], writes=[pn.r()])
        A(lambda e: e.copy(out=mxb[:], in_=pn[:, 0:24]), [pn.r()], [mxb.r()])
        V(lambda e: e.tensor_tensor(out=S[:], in0=S[:], in1=mxb[:].unsqueeze(1).to_broadcast([128, NPG + 1, 24]), op=ALU.subtract), [S.r(), mxb.r()], [S.r()])
        A(lambda e: e.activation(out=S[:].rearrange("p g c -> p (g c)"), in_=S[:].rearrange("p g c -> p (g c)"), func=AF.Exp), [S.r()], [S.r()])
        pO = [psb[1], psb[2], psb[3]]
        pZ = psb[4]
        for pg in range(NPG + 1):
            vb_ = kp[pg % 3]
            if pg < NPG:
                P.add("pool", lambda e, vb_=vb_, pg=pg: e.indirect_dma_start(out=vb_[:], out_offset=None, in_=d["cv"], in_offset=bass.IndirectOffsetOnAxis(ap=idx[:, pg:pg + 1], axis=0)),
                      reads=[idx.r()], writes=[vb_.r()], dma=True)
            else:
                V(lambda e, vb_=vb_: e.memset(vb_[:], 0.0), [], [vb_.r()])
                P.add("sp", lambda e, vb_=vb_: e.dma_start(out=vb_[0:1, :], in_=self.vs_scr.t.ap()), reads=[self.vs_scr.r(0), self.vs_scr.r(1), self.vs_scr.r(2), vb_.r()], writes=[vb_.r()], dma=True)
            for cb in range(3):
                P.add("pe", lambda e, vb_=vb_, pg=pg, cb=cb: e.matmul(pO[cb][0:24, :], lhsT=S[:, pg, :], rhs=vb_[:, cb * 512:(cb + 1) * 512], start=(pg == 0), stop=(pg == NPG)),
                      reads=[S.r(), vb_.r()], writes=[pO[cb].r()])
            P.add("pe", lambda e, pg=pg: e.matmul(pZ[0:24, 0:1], lhsT=S[:, pg, :], rhs=self.onesf[:, 0:1], start=(pg == 0), stop=(pg == NPG)),
                  reads=[S.r(), self.onesf.r()], writes=[pZ.r()])
        V(lambda e: e.reciprocal(out=zc[0:24, 1:2], in_=pZ[0:24, 0:1]), [pZ.r(), zc.r()], [zc.r()])
        for cb in range(3):
            A(lambda e, cb=cb: e.copy(out=o24[0:24, cb * 4:(cb + 1) * 4, :], in_=pO[cb][0:24, :].rearrange("p (h d) -> p h d", h=4)), [pO[cb].r()], [o24.r()])
        V(lambda e: e.tensor_tensor(out=o24[0:24, :, :], in0=o24[0:24, :, :], in1=self.c128[0:24, 132:144].unsqueeze(2).to_broadcast([24, 12, 128]), op=ALU.mult), [o24.r(), self.c128.r()], [o24.r()])
        V(lambda e: e.tensor_reduce(out=od[0:24, :], in_=o24[0:24, :, :].rearrange("p h d -> p d h"), axis=AX.X, op=ALU.add), [o24.r()], [od.r()])
        V(lambda e: e.tensor_scalar(out=od[0:24, :], in0=od[0:24, :], scalar1=zc[0:24, 1:2], scalar2=None, op0=ALU.mult), [od.r(), zc.r()], [od.r()])
        P.add("pe", lambda e: e.transpose(out=pn[:, 0:24], in_=od[0:24, :], identity=self.ident[0:24, 0:24]), reads=[od.r(), self.ident.r()], writes=[pn.r()])
        A(lambda e: e.copy(out=oT[:], in_=pn[:, 0:24]), [pn.r()], [oT.r()])
        o3 = oT[:].rearrange("p (h c) -> p h c", c=2)
        V(lambda e: e.scalar_tensor_tensor(out=of[:, 0:12], in0=o3[:, :, 1], scalar=self.lamv[:, 2 * lj:2 * lj + 1], in1=o3[:, :, 0], op0=ALU.mult, op1=ALU.add), [oT.r(), self.lamv.r()], [of.r()])
        V(lambda e: e.tensor_tensor(out=qsb[:, 0:12], in0=of[:, 0:12], in1=of[:, 0:12], op=ALU.mult), [of.r(), qsb.r()], [qsb.r()])
        P.add("pe", lambda e: e.matmul(pn[:, 0:12], lhsT=self.onesb[:], rhs=qsb[:, 0:12], start=True, stop=True), reads=[qsb.r(), self.onesb.r()], writes=[pn.r()])
        A(lambda e: e.activation(out=rz[:, 0:12], in_=pn[:, 0:12], func=AF.Sqrt, scale=1.0 / 128, bias=1e-5), [pn.r()], [rz.r()])
        V(lambda e: e.reciprocal(out=rz[:, 0:12], in_=rz[:, 0:12]), [rz.r()], [rz.r()])
        V(lambda e: e.scalar_tensor_tensor(out=of[:, 0:12], in0=of[:, 0:12], scalar=self.pvc("b_subln", lj), in1=rz[:, 0:12], op0=ALU.mult, op1=ALU.mult), [of.r(), rz.r(), self.pv.r()], [of.r()])
        V(lambda e: e.tensor_scalar(out=mixs[:, 0:12, 0], in0=of[:, 0:12], scalar1=self.lamv[:, 2 * lj + 1:2 * lj + 2], scalar2=None, op0=ALU.mult), [of.r(), self.lamv.r(), mixs.r()], [mixs.r()])
        for h in range(4):
            w = wq[h % 2]
            P.add("pool", lambda e, w=w, h=h: e.dma_start(out=w[:], in_=win[:, :, 1536 + h * 128:1536 + (h + 1) * 128]), writes=[w.r()], dma=True)
            for kc in range(KC):
                P.add("pe", lambda e, w=w, kc=kc: e.matmul(pq[:, 0:N], lhsT=w[:, kc, :], rhs=xst[:, kc, :], start=(kc == 0), stop=(kc == KC - 1)), reads=[w.r(), xst.r()], writes=[pq.r()])
            self.mem_attend(pq[:, 0:N], pq.r(), li, h, N, qf, qn, sqb, pT, rz, mkT, mv, mixs[:, 12 + h, :], mixs.r())
        wsrc = d["w_out"][li].rearrange("(k p) c -> p k c", p=128)
        for dq in range(8):
            P.add("pool", lambda e, dq=dq: e.dma_start(out=wo[:], in_=wsrc[:, :, dq * 256:(dq + 1) * 256]), writes=[wo.r()], dma=True)
            for dl in range(2):
                dc = dq * 2 + dl
                pa = psb[5 + dc % 2]
                for kc in range(KC):
                    P.add("pe", lambda e, pa=pa, kc=kc, dl=dl: e.matmul(pa[:, 0:N], lhsT=wo[:, kc, dl * 128:(dl + 1) * 128], rhs=mixs[:, kc, :], start=(kc == 0), stop=(kc == KC - 1)),
                          reads=[wo.r(), mixs.r()], writes=[pa.r()])
                V(lambda e, pa=pa, dc=dc: e.tensor_tensor(out=self.xsT[:, dc:dc + 1], in0=pa[:, 0:1], in1=self.xsT[:, dc:dc + 1], op=ALU.add), [pa.r(), self.xsT.r()], [self.xsT.r()])

    def dump(self, name, buf, ap=None):
        ap = buf[:] if ap is None else ap
        shape = list(ap.shape)
        dt_ = ap.dtype
        t = self.nc.dram_tensor("dbg_" + name, shape, dt_, kind="ExternalOutput").ap()
        rs = list(buf._res.values()) or [buf.r()]
        self.P.add("sp", lambda e: e.dma_start(out=t, in_=ap), reads=rs, dma=True)

    def store_wkv(self, li):
        P, o = self.P, self.dout
        self.phase("wkv")
        so = self.av("so", [12, 128], F32, parts=64)
        for gi in range(12):
            pb = self.psb[gi % 2]
            P.add("pe", lambda e, gi=gi, pb=pb: e.transpose(out=pb[0:64, 0:128], in_=self.Sf[:, li, gi, :], identity=self.ident[:]),
                  reads=[self.Sf.r((li, gi)), self.Sf.r(), self.ident.r()], writes=[pb.r()])
            P.add("act", lambda e, gi=gi, pb=pb: e.copy(out=so[:, gi, :], in_=pb[0:64, 0:128]), reads=[pb.r()], writes=[so.r()])
        P.add("sp", lambda e: e.dma_start(out=o["wkvp"][li].rearrange("(g h) i j -> i g h j", h=2), in_=so[:].rearrange("p g (h j) -> p g h j", h=2)), reads=[so.r()], dma=True)


def build_program(pvoff, npv, stage):
    kb = KB(pvoff, npv, stage)
    P = kb.P
    kb.setup()
    kb.mem_all()
    kb.mem_sample()
    for blk in range(NBLK):
        last = blk == NBLK - 1
        sample = last
        kb.load_x(blk)
        for li in range(2):
            kb.ffn(li, 0, sample)
            kb.phase("norm")
            kb.rmsnorm("mix_norm", li, sample, last_out=kb.xlast if last else None, shift_li=li)
            if last:
                kb.store_col(kb.xlast, kb.dout["shp"][li], kb.xlast.r())
                kb.store_col(kb.xslast, kb.dout["shs"][li], kb.xslast.r())
            kb.mixerA(li, blk, False)
            if last:
                kb.store_wkv(li)
                kb.mixerA(li, blk, False, smp=True)
            kb.ffn(li, 1, sample)
        kb.shared_kv(blk, sample)
        for lj in range(2):
            kb.ffn(2 + lj, 0, sample)
            kb.phase("norm")
            kb.rmsnorm("mix_norm", 2 + lj, sample)
            kb.mixerB(lj, blk, False)
            if last:
                kb.mixerB_s(lj)
            kb.ffn(2 + lj, 1, sample)
        kb.store_x(blk)
    P.barrier()
    P.emit()
    P.close()
    return kb.nc


STAGE = 3


def kernel(**inp):
    inp = {k: np.asarray(v) for k, v in inp.items()}
    pvp = pack_params(inp)
    pv = pvp.build()
    nc = build_program(pvp.off, pvp.n, STAGE)
    hc = host_consts()
    w13 = inp["ffn_w13"].reshape(8, D, 2 * DFF)
    w2 = inp["ffn_w2"].reshape(8, DFF, D)
    shared = dict(
        pv=pv, ident=hc["ident"], c64=hc["c64"], c128=hc["c128"], ffn_w13=w13, ffn_w2=w2,
        mem_w_kv=inp["mem_w_kv"], mem_k_norm=inp["mem_k_norm"],
        a_w_in=inp["a_w_in"], a_w1=inp["a_w1"], a_w2=inp["a_w2"], a_a1=inp["a_a1"], a_a2=inp["a_a2"],
        a_g1=inp["a_g1"], a_g2=inp["a_g2"], a_lnx_w=inp["a_lnx_w"], a_lnx_b=inp["a_lnx_b"], w_out=inp["w_out"],
        kv_w=inp["kv_w"], k_norm=inp["k_norm"].reshape(1, 64), b_w_in=inp["b_w_in"], b_lam=inp["b_lam"].reshape(2, 256),
        cs_tok=hc["cs_tok"], ropeC=hc["ropeC"], ropeS=hc["ropeS"], rot=hc["rot"], dmask=hc["dmask"],
        ck=inp["cache_k"].reshape(1280 * 128, 1536), cv=inp["cache_v"].reshape(1280 * 128, 1536),
    )
    xps = [np.ascontiguousarray(inp["x_prompt"][b]) for b in range(2)]
    mps = [np.ascontiguousarray(inp["mem_prompt"][b]) for b in range(2)]
    in_maps = []
    for c in range(NCORE):
        b = c % 2
        m = dict(shared)
        m.update(xp=xps[b], xs=np.ascontiguousarray(inp["x_sample"][c, 0].reshape(KC, 128)), memp=mps[b],
                 stw=np.ascontiguousarray(inp["state_wkv"][:, c]), sts=np.ascontiguousarray(inp["state_shift"][:, c].reshape(2, KC, 128)),
                 cmk=np.ascontiguousarray(inp["cache_mem_k"][:, c].reshape(4, MEMT, 512)), cmv=np.ascontiguousarray(inp["cache_mem_v"][:, c].reshape(4, MEMT, 512)),
                 pt=np.ascontiguousarray(inp["page_table"][c].reshape(1, 128).astype(np.int32)))
        in_maps.append(m)
    res = run_bass_kernel_spmd(nc, in_maps, core_ids=list(range(NCORE)))
    R = res.results
    f32 = np.float32
    y_prompt = np.stack([R[b]["yp"] for b in range(2)]).astype(f32)
    y_sample = np.stack([R[c]["ys"].reshape(1, D) for c in range(8)]).astype(f32)
    wkv_prompt = np.stack([np.stack([R[b]["wkvp"][l] for b in range(2)]) for l in range(2)]).astype(f32)
    shift_prompt = np.stack([np.stack([R[b]["shp"][l].reshape(D) for b in range(2)]) for l in range(2)]).astype(f32)
    wkv_sample = np.stack([np.stack([R[c]["wkvs"][l] for c in range(8)]) for l in range(2)]).astype(f32)
    shift_sample = np.stack([np.stack([R[c]["shs"][l].reshape(D) for c in range(8)]) for l in range(2)]).astype(f32)
    k_rows_p = np.stack([R[b]["krp"].reshape(4096, 12, 2, 64) for b in range(2)]).astype(f32)
    v_rows_p = np.stack([R[b]["vrp"].reshape(4096, 12, 128) for b in range(2)]).astype(f32)
    k_rows_s = np.stack([R[c]["krs"].reshape(1, 12, 2, 64) for c in range(8)]).astype(f32)
    v_rows_s = np.stack([R[c]["vrs"].reshape(1, 12, 128) for c in range(8)]).astype(f32)
    mem_k = np.stack([np.stack([R[b]["mkp"][l].reshape(MEMT, 4, 128) for b in range(2)]) for l in range(4)]).astype(f32)
    mem_v = np.stack([np.stack([R[b]["mvp"][l].reshape(MEMT, 4, 128) for b in range(2)]) for l in range(4)]).astype(f32)
    return (y_prompt, y_sample, wkv_prompt, shift_prompt, wkv_sample, shift_sample,
            k_rows_p, v_rows_p, k_rows_s, v_rows_s, mem_k, mem_v)
```
